# Optimizing a Trainium2 kernel written in Bass

```python
import math
import jax, jax.numpy as jnp
from jax import lax
import numpy as np

D_MODEL = 1024
BATCH = 8
SEQ = 2048
DEPTH = 1
DEC_BATCH = 128
DEC_SEQ = 1
PAST_LEN = 16384
PAGE_SIZE = 128

ATTN_HEADS = 8
ATTN_KV_HEADS = 2
ATTN_HEAD_DIM = 64
WINDOW = 128
ATTN_BLOCK = 128
REL_BUCKETS = 32
REL_MAX_DIST = 128
RET_HEADS = 4
RET_DK = 128
RET_DV = 128
RET_CHUNK = 128
ROPE_BASE = 10000.0
D_FF = 2816
LN_EPS = 1e-5
GN_EPS = 1e-6
ALPHA = (2.0 * DEPTH) ** 0.25
BETA = (8.0 * DEPTH) ** -0.25

ATTN_Q = ATTN_HEADS * ATTN_HEAD_DIM
ATTN_KV = ATTN_KV_HEADS * ATTN_HEAD_DIM
RET_QK = RET_HEADS * RET_DK
RET_V = RET_HEADS * RET_DV
IN_COLS = ATTN_Q + 2 * ATTN_KV + 2 * RET_QK + 2 * RET_V + 2 * D_MODEL

kernel_name = "hybrid_swa_retention_macaron_step"


def layer_norm(x, g, b):
    xf = x.astype(jnp.float32)
    mu = xf.mean(-1, keepdims=True)
    var = jnp.square(xf - mu).mean(-1, keepdims=True)
    return ((xf - mu) * lax.rsqrt(var + LN_EPS) * g.astype(jnp.float32) + b.astype(jnp.float32)).astype(x.dtype)


def swiglu(x, w_up, w_down):
    gate, up = jnp.split(x @ w_up, 2, axis=-1)
    return (jax.nn.silu(gate) * up) @ w_down


def rel_bucket(dist):
    n = jnp.maximum(dist, 0)
    max_exact = REL_BUCKETS // 2
    ratio = jnp.maximum(n, 1).astype(jnp.float32) / max_exact
    large = max_exact + (jnp.log(jnp.maximum(ratio, 1.0)) / math.log(REL_MAX_DIST / max_exact)
                         * (REL_BUCKETS - max_exact)).astype(jnp.int32)
    large = jnp.minimum(large, REL_BUCKETS - 1)
    return jnp.where(n < max_exact, n, large)


def rotary(x, pos):
    half = x.shape[-1] // 2
    inv = ROPE_BASE ** (-jnp.arange(half, dtype=jnp.float32) / half)
    ang = pos.astype(jnp.float32)[:, None] * inv[None, :]
    cos = jnp.cos(ang)[None, :, None, :]
    sin = jnp.sin(ang)[None, :, None, :]
    xf = x.astype(jnp.float32)
    x1, x2 = xf[..., :half], xf[..., half:]
    return jnp.concatenate([x1 * cos - x2 * sin, x2 * cos + x1 * sin], axis=-1).astype(x.dtype)


def project(x, w_in, pos):
    B, L, _ = x.shape
    sizes = (ATTN_Q, ATTN_KV, ATTN_KV, RET_QK, RET_QK, RET_V, RET_V, D_MODEL, D_MODEL)
    idx = [int(i) for i in np.cumsum(sizes)[:-1]]
    qa, ka, va, qr, kr, vr, gr, ga, gb = jnp.split(x @ w_in, idx, axis=-1)
    qa = qa.reshape(B, L, ATTN_HEADS, ATTN_HEAD_DIM)
    ka = ka.reshape(B, L, ATTN_KV_HEADS, ATTN_HEAD_DIM)
    va = va.reshape(B, L, ATTN_KV_HEADS, ATTN_HEAD_DIM)
    qr = rotary(qr.reshape(B, L, RET_HEADS, RET_DK), pos)
    kr = rotary(kr.reshape(B, L, RET_HEADS, RET_DK), pos) * (RET_DK ** -0.5)
    vr = vr.reshape(B, L, RET_HEADS, RET_DV)
    return qa, ka, va, qr, kr, vr, gr, ga, gb


def window_attention(q, k, v, q_pos, k_pos, sinks, rel_bias):
    B, NB, Lq, H, hd = q.shape
    Lk = k.shape[2]
    G = H // ATTN_KV_HEADS
    qg = q.reshape(B, NB, Lq, ATTN_KV_HEADS, G, hd)
    s = jnp.einsum('bnqhgd,bnshd->bnhgqs', qg, k).astype(jnp.float32) * (hd ** -0.5)
    dist = q_pos[:, :, None] - k_pos[:, None, :]
    allowed = (dist >= 0) & (dist < WINDOW) & (k_pos[:, None, :] >= 0)
    bias = rel_bias.astype(jnp.float32)[rel_bucket(dist)]
    bias = bias.transpose(0, 3, 1, 2).reshape(NB, ATTN_KV_HEADS, G, Lq, Lk)
    s = jnp.where(allowed[None, :, None, None], s + bias[None], -jnp.inf)
    sink = sinks.astype(jnp.float32).reshape(ATTN_KV_HEADS, G)[None, None, :, :, None, None]
    m = jnp.maximum(s.max(-1, keepdims=True), sink)
    p = jnp.exp(s - m)
    p = p / (p.sum(-1, keepdims=True) + jnp.exp(sink - m))
    o = jnp.einsum('bnhgqs,bnshd->bnqhgd', p.astype(v.dtype), v)
    return o.reshape(B, NB, Lq, H * hd)


def retention_chunks(q, k, v, s0):
    C = q.shape[2]
    lg = jnp.log1p(-(2.0 ** (-5.0 - jnp.arange(RET_HEADS, dtype=jnp.float32))))
    i = jnp.arange(C, dtype=jnp.float32)
    diff = i[:, None] - i[None, :]
    dmask = jnp.where(diff[None] >= 0, jnp.exp(jnp.maximum(diff, 0.0)[None] * lg[:, None, None]), 0.0)
    scores = jnp.einsum('bnihd,bnjhd->bnhij', q, k) * dmask
    inner = jnp.einsum('bnhij,bnjhe->bnihe', scores, v)
    k_tail = k * jnp.exp((C - 1 - i)[:, None] * lg[None, :])[None, None, :, :, None]
    u = jnp.einsum('bnjhd,bnjhe->nbhde', k_tail, v)
    chunk_decay = jnp.exp(C * lg)[None, :, None, None]

    def step(s, u_n):
        return chunk_decay * s + u_n, s

    s_final, s_prev = lax.scan(step, s0, u)
    q_dec = q * jnp.exp((i + 1)[:, None] * lg[None, :])[None, None, :, :, None]
    cross = jnp.einsum('bnihd,nbhde->bnihe', q_dec, s_prev)
    return inner + cross, s_final


def head_group_norm(o):
    mu = o.mean(-1, keepdims=True)
    var = jnp.square(o - mu).mean(-1, keepdims=True)
    return (o - mu) * lax.rsqrt(var + GN_EPS)


def merge_branches(o_attn, o_ret, gr, ga, gb, w_attn_out, w_ret_out, w_o):
    B, L, _ = o_attn.shape
    r = head_group_norm(o_ret).reshape(B, L, RET_V).astype(gr.dtype) * jax.nn.silu(gr)
    merged = jax.nn.sigmoid(ga) * (o_attn @ w_attn_out) + jax.nn.sigmoid(gb) * (r @ w_ret_out)
    return merged @ w_o


def prompt_mixers(x, w_in, sinks, rel_bias, w_attn_out, w_ret_out, w_o, state_dtype):
    B, S, _ = x.shape
    pos = jnp.arange(S, dtype=jnp.int32)
    qa, ka, va, qr, kr, vr, gr, ga, gb = project(x, w_in, pos)
    nb = S // ATTN_BLOCK

    def band(t):
        tp = jnp.pad(t, ((0, 0), (ATTN_BLOCK, 0), (0, 0), (0, 0)))
        tp = tp.reshape(B, nb + 1, ATTN_BLOCK, ATTN_KV_HEADS, ATTN_HEAD_DIM)
        return jnp.concatenate([tp[:, :-1], tp[:, 1:]], axis=2)

    q_pos = pos.reshape(nb, ATTN_BLOCK)
    k_pos = jnp.concatenate([q_pos - ATTN_BLOCK, q_pos], axis=1)
    q_blk = qa.reshape(B, nb, ATTN_BLOCK, ATTN_HEADS, ATTN_HEAD_DIM)
    o_attn = window_attention(q_blk, band(ka), band(va), q_pos, k_pos, sinks, rel_bias).reshape(B, S, ATTN_Q)
    nc = S // RET_CHUNK

    def chunk(t):
        return t.astype(jnp.float32).reshape(B, nc, RET_CHUNK, RET_HEADS, t.shape[-1])

    s0 = jnp.zeros((B, RET_HEADS, RET_DK, RET_DV), jnp.float32)
    o_ret, s_fin = retention_chunks(chunk(qr), chunk(kr), chunk(vr), s0)
    o_ret = o_ret.reshape(B, S, RET_HEADS, RET_DV)
    y = merge_branches(o_attn, o_ret, gr, ga, gb, w_attn_out, w_ret_out, w_o)
    return y, ka[:, -WINDOW:], va[:, -WINDOW:], s_fin.astype(state_dtype)


def sample_mixers(x, k_buf, v_buf, s_prev, w_in, sinks, rel_bias, w_attn_out, w_ret_out, w_o):
    B, L, _ = x.shape
    pos = PAST_LEN + jnp.arange(L, dtype=jnp.int32)
    qa, ka, va, qr, kr, vr, gr, ga, gb = project(x, w_in, pos)
    w = k_buf.shape[1]
    k_all = jnp.concatenate([k_buf.astype(ka.dtype), ka], axis=1)
    v_all = jnp.concatenate([v_buf.astype(va.dtype), va], axis=1)
    k_pos = jnp.concatenate([PAST_LEN - w + jnp.arange(w, dtype=jnp.int32), pos])
    o_attn = window_attention(qa[:, None], k_all[:, None], v_all[:, None], pos[None], k_pos[None],
                              sinks, rel_bias).reshape(B, L, ATTN_Q)

    def one_chunk(t):
        return t.astype(jnp.float32)[:, None]

    o_ret, s_new = retention_chunks(one_chunk(qr), one_chunk(kr), one_chunk(vr), s_prev.astype(jnp.float32))
    o_ret = o_ret.reshape(B, L, RET_HEADS, RET_DV)
    y = merge_branches(o_attn, o_ret, gr, ga, gb, w_attn_out, w_ret_out, w_o)
    return y, k_all[:, -WINDOW:], v_all[:, -WINDOW:], s_new.astype(s_prev.dtype)


def setup_inputs(seed: int = 0) -> dict:
    key = jax.random.key(seed)
    ks = jax.random.split(key, 24)
    f32 = jnp.float32

    def nrm(k, shape, scale):
        return jax.random.normal(k, shape, f32) * scale

    return {
        "x_prompt": nrm(ks[0], (BATCH, SEQ, D_MODEL), 1.0),
        "x_sample": nrm(ks[1], (DEC_BATCH, DEC_SEQ, D_MODEL), 1.0),
        "cache_k_win": nrm(ks[2], (DEPTH, DEC_BATCH, WINDOW, ATTN_KV_HEADS, ATTN_HEAD_DIM), 1.0),
        "cache_v_win": nrm(ks[3], (DEPTH, DEC_BATCH, WINDOW, ATTN_KV_HEADS, ATTN_HEAD_DIM), 1.0),
        "state_ret": nrm(ks[4], (DEPTH, DEC_BATCH, RET_HEADS, RET_DK, RET_DV), 0.3),
        "rel_bias": nrm(ks[5], (REL_BUCKETS, ATTN_HEADS), 0.5),
        "w_in": nrm(ks[6], (DEPTH, D_MODEL, IN_COLS), D_MODEL ** -0.5),
        "attn_sinks": nrm(ks[7], (DEPTH, ATTN_HEADS), 0.5),
        "w_attn_out": nrm(ks[8], (DEPTH, ATTN_Q, D_MODEL), BETA * ATTN_Q ** -0.5),
        "w_ret_out": nrm(ks[9], (DEPTH, RET_V, D_MODEL), BETA * RET_V ** -0.5),
        "w_o": nrm(ks[10], (DEPTH, D_MODEL, D_MODEL), BETA * D_MODEL ** -0.5),
        "ffn1_w_up": nrm(ks[11], (DEPTH, D_MODEL, 2 * D_FF), D_MODEL ** -0.5),
        "ffn1_w_down": nrm(ks[12], (DEPTH, D_FF, D_MODEL), BETA * D_FF ** -0.5),
        "ffn2_w_up": nrm(ks[13], (DEPTH, D_MODEL, 2 * D_FF), D_MODEL ** -0.5),
        "ffn2_w_down": nrm(ks[14], (DEPTH, D_FF, D_MODEL), BETA * D_FF ** -0.5),
        "ln1_g": 1.0 + nrm(ks[15], (DEPTH, D_MODEL), 0.01),
        "ln1_b": nrm(ks[16], (DEPTH, D_MODEL), 0.01),
        "ln2_g": 1.0 + nrm(ks[17], (DEPTH, D_MODEL), 0.01),
        "ln2_b": nrm(ks[18], (DEPTH, D_MODEL), 0.01),
        "ln3_g": 1.0 + nrm(ks[19], (DEPTH, D_MODEL), 0.01),
        "ln3_b": nrm(ks[20], (DEPTH, D_MODEL), 0.01),
    }


def reference(x_prompt, x_sample, cache_k_win, cache_v_win, state_ret, rel_bias, w_in, attn_sinks,
              w_attn_out, w_ret_out, w_o, ffn1_w_up, ffn1_w_down, ffn2_w_up, ffn2_w_down,
              ln1_g, ln1_b, ln2_g, ln2_b, ln3_g, ln3_b):
    hp, hs = x_prompt, x_sample
    kp_l, vp_l, sp_l, ks_l, vs_l, ss_l = [], [], [], [], [], []
    for l in range(DEPTH):
        hp = layer_norm(ALPHA * hp + 0.5 * swiglu(hp, ffn1_w_up[l], ffn1_w_down[l]), ln1_g[l], ln1_b[l])
        hs = layer_norm(ALPHA * hs + 0.5 * swiglu(hs, ffn1_w_up[l], ffn1_w_down[l]), ln1_g[l], ln1_b[l])
        mp, kp, vp, sp = prompt_mixers(hp, w_in[l], attn_sinks[l], rel_bias, w_attn_out[l], w_ret_out[l], w_o[l],
                                       state_ret.dtype)
        ms, kbs, vbs, ss = sample_mixers(hs, cache_k_win[l], cache_v_win[l], state_ret[l], w_in[l], attn_sinks[l],
                                         rel_bias, w_attn_out[l], w_ret_out[l], w_o[l])
        hp = layer_norm(ALPHA * hp + mp, ln2_g[l], ln2_b[l])
        hs = layer_norm(ALPHA * hs + ms, ln2_g[l], ln2_b[l])
        hp = layer_norm(ALPHA * hp + 0.5 * swiglu(hp, ffn2_w_up[l], ffn2_w_down[l]), ln3_g[l], ln3_b[l])
        hs = layer_norm(ALPHA * hs + 0.5 * swiglu(hs, ffn2_w_up[l], ffn2_w_down[l]), ln3_g[l], ln3_b[l])
        kp_l.append(kp); vp_l.append(vp); sp_l.append(sp)
        ks_l.append(kbs); vs_l.append(vbs); ss_l.append(ss)
    k_win_prompt = jnp.stack(kp_l)
    v_win_prompt = jnp.stack(vp_l)
    state_ret_prompt = jnp.stack(sp_l)
    k_win_sample = jnp.stack(ks_l)
    v_win_sample = jnp.stack(vs_l)
    state_ret_sample = jnp.stack(ss_l)
    return (hp, hs, k_win_prompt, v_win_prompt, state_ret_prompt, k_win_sample, v_win_sample, state_ret_sample)
```

```python
import numpy as np
from contextlib import ExitStack
import concourse.bass as bass
import concourse.mybir as mybir
from concourse.bass_utils import run_bass_kernel_spmd

F32 = mybir.dt.float32
BF16 = mybir.dt.bfloat16
AF = mybir.ActivationFunctionType
ALU = mybir.AluOpType

NCORES = 8
D = 1024
SEQ = 2048
DFF = 2816
NJ = DFF // 128
NS = 16
TB = 1024
NCOL = TB + NS
PAST = 16384
ALPHA = 2.0 ** 0.25
LN_EPS = 1e-5
GN_EPS = 1e-6
MASKV = -30000.0
SB_BASE = 16512
SB_END = 229376

C_ID = 0
C_J = 128
C_CAUS = 256
C_COS = 384
C_SIN = C_COS + 17 * 64
C_NSIN = C_SIN + 17 * 64
C_DQ = C_NSIN + 17 * 64
C_DK = C_DQ + 8
C_GC = C_DK + 8
C_EYER = C_GC + 4
C_OH1 = C_EYER + 256
C_OH2 = C_OH1 + 128
C_MH = C_OH2 + 128
C_ONE = C_MH + 1
NCONST = ((C_ONE + 1 + 7) // 8) * 8
GAMMAS = [1.0 - 2.0 ** (-5.0 - h) for h in range(4)]


def _bucket(d):
    d = np.asarray(d)
    n = np.maximum(d, 0)
    ratio = np.maximum(n, 1).astype(np.float32) / np.float32(16)
    large = 16 + (np.log(np.maximum(ratio, np.float32(1.0))).astype(np.float32)
                  / np.float32(np.log(128 / 16)) * np.float32(16)).astype(np.int32)
    large = np.minimum(large, 31)
    return np.where(n < 16, n, large)


def make_consts():
    c = np.zeros((128, NCONST), np.float32)
    c[:, C_ID:C_ID + 128] = np.eye(128, dtype=np.float32)
    c[:, C_J:C_J + 128] = np.eye(128, dtype=np.float32)[::-1]
    jj = np.arange(128)
    c[:, C_CAUS:C_CAUS + 128] = (jj[None, :] >= jj[:, None]).astype(np.float32)
    inv = (np.float32(10000.0) ** (-(np.arange(64, dtype=np.float32) / np.float32(64)))).astype(np.float32)
    for t in range(17):
        pos = (t * 128 + np.arange(128)) if t < 16 else np.full(128, PAST)
        ang = (pos.astype(np.float32)[:, None] * inv[None, :]).astype(np.float32)
        c[:, C_COS + t * 64:C_COS + (t + 1) * 64] = np.cos(ang.astype(np.float64)).astype(np.float32)
        c[:, C_SIN + t * 64:C_SIN + (t + 1) * 64] = np.sin(ang.astype(np.float64)).astype(np.float32)
        c[:, C_NSIN + t * 64:C_NSIN + (t + 1) * 64] = -np.sin(ang.astype(np.float64)).astype(np.float32)
    p = np.arange(128, dtype=np.float64)
    for h in range(4):
        lg = np.log1p(-(2.0 ** (-5.0 - h)))
        c[:, C_DQ + h] = np.exp((p + 1) * lg)
        c[:, C_DK + h] = (128.0 ** -0.5) * np.exp(-(p + 1) * lg)
        c[:, C_DQ + 4 + h] = 1.0
        c[:, C_DK + 4 + h] = 128.0 ** -0.5
        c[:, C_GC + h] = np.exp(128 * lg)
    c[:, C_EYER:C_EYER + 256] = np.eye(16, dtype=np.float32).reshape(1, 256)
    b1 = _bucket(np.arange(128))
    b2 = _bucket(127 - np.arange(128))
    for r in range(128):
        c[b1[r], C_OH1 + r] = 1.0
        c[b2[r], C_OH2 + r] = 1.0
    c[:, C_MH] = -0.5
    c[:, C_ONE] = 1.0
    return c


def _rnd_tile(x):
    return 32 if x <= 32 else (64 if x <= 64 else 128)


class _FakePE:
    def __init__(self):
        self.mode = None

    def matmul(self, out, lhsT=None, rhs=None, **kw):
        self.mode = (_rnd_tile(lhsT.shape[0]), _rnd_tile(int(np.prod(lhsT.shape[1:]))))

    def transpose(self, out=None, in_=None, identity=None):
        self.mode = (_rnd_tile(in_.shape[0]), _rnd_tile(int(np.prod(in_.shape[1:]))))


class Sched:
    ENGS = ("pe", "act", "dve", "pool", "sp")

    def __init__(self):
        self.ops = []
        self.last_w = {}
        self.readers = {}
        self.dma_cnt = {}
        self.bar = None
        self.bar_passed = set()
        self.last_eng = {}
        self.last_dma = {}
        self.pe_sync = False
        self.pe_sync_once = False

    def _stream(self, idx):
        o = self.ops[idx]
        return ("dma", o["dma"]) if o["dma"] is not None else ("eng", o["eng"])

    def op(self, eng, fn, r=(), w=(), dma=None):
        idx = len(self.ops)
        deps = {}

        def add(d):
            s = self._stream(d)
            if deps.get(s, -1) < d:
                deps[s] = d
        for t in r:
            lw = self.last_w.get(t)
            if lw is not None:
                add(lw)
        for t in w:
            lw = self.last_w.get(t)
            if lw is not None:
                add(lw)
            for d in self.readers.get(t, {}).values():
                add(d)
        if self.bar is not None and eng not in self.bar_passed:
            for d in self.bar:
                add(d)
            self.bar_passed.add(eng)
        force = False
        if eng == "pe" and dma is None and (self.pe_sync or self.pe_sync_once):
            self.pe_sync_once = self.pe_sync
            if "pe" in self.last_eng:
                add(self.last_eng["pe"])
                force = True
        o = dict(eng=eng, fn=fn, deps=deps, dma=dma, need_inc=False, ev=None, force=force)
        if dma is not None:
            self.dma_cnt[dma] = self.dma_cnt.get(dma, 0) + 1
            o["ev"] = ("dma", dma, 16 * self.dma_cnt[dma])
            self.last_dma[dma] = idx
        else:
            self.last_eng[eng] = idx
        self.ops.append(o)
        me = ("dma", dma) if dma is not None else ("eng", eng)
        for t in r:
            self.readers.setdefault(t, {})[me] = idx
        for t in w:
            self.last_w[t] = idx
            self.readers[t] = {}
        return idx

    def barrier(self):
        self.bar = set(self.last_eng.values()) | set(self.last_dma.values())
        self.bar_passed = set()

    def finalize(self):
        ops = self.ops
        for o in ops:
            real = []
            for s, d in o["deps"].items():
                od = ops[d]
                if s == ("eng", "pe") and o["eng"] == "pe" and o["dma"] is None and not o["force"]:
                    continue
                real.append(d)
                if od["dma"] is None:
                    od["need_inc"] = True
            o["deps"] = real
        cnt = {e: 0 for e in self.ENGS}
        for o in ops:
            if o["dma"] is None and o["need_inc"]:
                cnt[o["eng"]] += 1
                o["ev"] = ("eng", o["eng"], cnt[o["eng"]])
        seen = {e: {} for e in self.ENGS}
        for o in ops:
            waits = {}
            for d in o["deps"]:
                kind, key, val = ops[d]["ev"]
                k = (kind, key)
                if seen[o["eng"]].get(k, 0) >= val:
                    continue
                waits[k] = max(waits.get(k, 0), val)
            for k, v in waits.items():
                seen[o["eng"]][k] = v
            o["waits"] = waits

    def emit(self, nc, es):
        self.finalize()
        sems = {}
        n = [0]

        def sem(k):
            if k not in sems:
                n[0] += 1
                sems[k] = es.enter_context(nc.semaphore("s%d" % n[0]))
            return sems[k]
        for o in self.ops:
            for k in o["waits"]:
                sem(k)
            if o["ev"] is not None:
                sem((o["ev"][0], o["ev"][1]))
        block = es.enter_context(nc.Block())
        ops = self.ops

        import os as _os2
        pad = int(_os2.environ.get("K_PAD", "0"))

        def run(engname, eng):
            for _ in range(pad):
                eng.nop()
            for o in ops:
                if o["eng"] != engname:
                    continue
                for k, v in o["waits"].items():
                    eng.wait_ge(sems[k], v)
                ins = o["fn"](eng)
                if o["dma"] is not None:
                    ins.then_inc(sems[("dma", o["dma"])], 16)
                elif o["need_inc"]:
                    ins.then_inc(sems[("eng", engname)], 1)

        @block.tensor
        def _(e):
            run("pe", e)

        @block.scalar
        def _(e):
            run("act", e)

        @block.vector
        def _(e):
            run("dve", e)

        @block.gpsimd
        def _(e):
            run("pool", e)

        @block.sync
        def _(e):
            run("sp", e)
        self.nsems = len(sems)


class Arena:
    def __init__(self, nc, base, end):
        self.nc, self.top, self.end = nc, base, end
        self.n = 0
        self.peak = base

    def alloc(self, shape, dtype):
        sz = int(np.prod(shape[1:])) * (2 if dtype == BF16 else 4)
        off = (self.top + 31) // 32 * 32
        assert off + sz <= self.end, ("SBUF overflow", off + sz, self.end)
        self.top = off + sz
        self.peak = max(self.peak, self.top)
        self.n += 1
        return self.nc.alloc_sbuf_tensor_at("sb%d" % self.n, list(shape), dtype, offset=off)


def build_nc(debug=False, max_stage=99, sub=99):
    nc = bass.Bass("TRN2", target_bir_lowering=False)

    def din(name, shape):
        return nc.dram_tensor(name, list(shape), F32, kind="ExternalInput").ap()

    def dout(name, shape):
        return nc.dram_tensor(name, list(shape), F32, kind="ExternalOutput").ap()
    xp = din("xp", [SEQ, D])
    xs = din("xs", [NS, D])
    ck = din("ck", [NS, 128, 128])
    cv = din("cv", [NS, 128, 128])
    st_in = din("st", [NS, 4, 128, 128])
    relb = din("relb", [32, 8])
    w_in = din("w_in", [D, 4864])
    sinks = din("sinks", [8])
    w_ao = din("w_ao", [512, D])
    w_ro = din("w_ro", [512, D])
    w_o = din("w_o", [D, D])
    wup_d = [din("f1u", [D, 2 * DFF]), din("f2u", [D, 2 * DFF])]
    wdn_d = [din("f1d", [DFF, D]), din("f2d", [DFF, D])]
    lng = [din("ln%dg" % i, [D]) for i in (1, 2, 3)]
    lnb = [din("ln%db" % i, [D]) for i in (1, 2, 3)]
    consts_d = din("consts", [128, NCONST])
    yp = dout("yp", [SEQ, D])
    ys = dout("ys", [NS, D])
    kwp = dout("kwp", [128, 128])
    vwp = dout("vwp", [128, 128])
    sp_out = dout("sp", [4, 128, 128])
    kws = dout("kws", [NS, 128, 128])
    vws = dout("vws", [NS, 128, 128])
    ss_out = dout("ss", [NS, 4, 128, 128])
    scr = nc.dram_tensor("scr", [2, 8, 256], F32, kind="Internal").ap()
    dbg = {}
    if debug:
        dbg["h1"] = dout("dbg_h1", [9, 128, D])
        dbg["h2"] = dout("dbg_h2", [9, 128, D])

    es = ExitStack()
    with es:
        S = Sched()
        A = Arena(nc, SB_BASE, SB_END)
        ps = es.enter_context(nc.psum_tensor("ps", [128, 8, 512], F32))

        h_tok = A.alloc([128, 9, D], F32)
        hT = A.alloc([128, 8, NCOL], BF16)
        cst = A.alloc([128, NCONST], F32)
        BT = A.alloc([128, 2, 8, 128], F32)
        lg_t = A.alloc([128, D], F32)
        lb_t = A.alloc([128, D], F32)
        Sst = A.alloc([128, 4, 128], F32)
        Sb = A.alloc([128, 4, 128], BF16)
        bias_s = A.alloc([128, 8], F32)
        sinkexp = A.alloc([128, 8], F32)
        stt = [A.alloc([128, 4, 6], F32) for _ in range(2)]
        mvt = [A.alloc([128, 4, 8], F32) for _ in range(2)]
        kprev = A.alloc([128, 128], BF16)
        vprev = A.alloc([128, 2, 65], BF16)
        phase_base = A.top

        ident = cst[:, C_ID:C_ID + 128]
        Jm = cst[:, C_J:C_J + 128]
        caus = cst[:, C_CAUS:C_CAUS + 128]
        mhalf = cst[:, C_MH:C_MH + 1]

        def PSB(b, P=128, n=512):
            return ps[0:P, b, 0:n]

        def PS2(b, P=128):
            return ps[0:P, b:b + 2, :].rearrange("p a b -> p (a b)")

        S.op("sp", lambda e: e.dma_start(out=cst[:], in_=consts_d), w=["cst"], dma="cst")
        S.op("sp", lambda e: e.dma_start(out=sinkexp[:], in_=sinks.partition_broadcast(128)), w=["sinkexp"], dma="c2")
        S.op("act", lambda e: e.activation(out=sinkexp[:], in_=sinkexp[:], func=AF.Exp), r=["sinkexp"], w=["sinkexp"])
        S.op("dve", lambda e: e.memset(Sst[:], 0.0), w=["S"])
        S.op("dve", lambda e: e.memset(Sb[:], 0.0), w=["Sb"])
        A0 = A.top
        rb = A.alloc([32, 8], F32)
        TTs = A.alloc([8, 128], F32)
        Lsb = A.alloc([8, 2, 256], F32)
        G = A.alloc([128, 2, 8, 128], F32)
        S.op("sp", lambda e: e.dma_start(out=rb[:], in_=relb), w=["rb"], dma="c3")
        S.op("pe", lambda e: e.matmul(ps[0:8, 0, 0:128], lhsT=rb[:], rhs=cst[0:32, C_OH1:C_OH1 + 128], start=True, stop=True),
             r=["rb", "cst"], w=[("ps", 0)])
        S.op("act", lambda e: e.copy(out=TTs[:], in_=ps[0:8, 0, 0:128]), r=[("ps", 0)], w=["TTs"])
        S.op("dve", lambda e: e.memset(Lsb[:], MASKV), w=["Lsb"])
        S.op("dve", lambda e: e.tensor_copy(out=Lsb[:, 0, 0:127], in_=TTs[:, 1:128]), r=["TTs", "Lsb"], w=["Lsb"])
        S.op("dve", lambda e: e.tensor_copy(out=Lsb[:, 1, 127:255], in_=TTs[:, 0:128]), r=["TTs", "Lsb"], w=["Lsb"])
        S.op("sp", lambda e: e.dma_start(out=scr.rearrange("t h u -> h t u"), in_=Lsb[:]), r=["Lsb"], w=["scr"], dma="c4")
        for tab in range(2):
            hank = bass.AP(tensor=scr.tensor, offset=tab * 2048, ap=[[1, 128], [256, 8], [1, 128]])
            S.op("sp", lambda e, tab=tab, hank=hank: e.dma_start(out=G[:, tab, :, :], in_=hank), r=["scr"], w=["G"], dma="c5")
        for tab in range(2):
            for hf in range(2):
                b = 1 + tab * 2 + hf
                S.op("pe", lambda e, tab=tab, hf=hf, b=b: e.matmul(
                    ps[:, b, :], lhsT=Jm, rhs=G[:, tab, hf * 4:(hf + 1) * 4, :].rearrange("p a b -> p (a b)"),
                    start=True, stop=True), r=["cst", "G"], w=[("ps", b)])
                S.op("act", lambda e, tab=tab, hf=hf, b=b: e.copy(
                    out=BT[:, tab, hf * 4:(hf + 1) * 4, :].rearrange("p a b -> p (a b)"), in_=ps[:, b, :]),
                    r=[("ps", b)], w=["BT"])
        S.op("pe", lambda e: e.matmul(ps[:, 5, 0:8], lhsT=cst[0:32, C_OH2:C_OH2 + 128], rhs=rb[:], start=True, stop=True),
             r=["rb", "cst"], w=[("ps", 5)])
        S.op("act", lambda e: e.copy(out=bias_s[:], in_=ps[:, 5, 0:8]), r=[("ps", 5)], w=["bias_s"])
        A.top = A0

        def tile_geom(t):
            return (128, t * 128) if t < 8 else (NS, TB)

        def hT_tok(c0, n):
            if c0 >= TB:
                return [("hT", 8)]
            return [("hT", t) for t in range(c0 // 128, (c0 + n + 127) // 128)]

        def transpose_to_hT(t, pbank):
            P, c0 = tile_geom(t)
            S.pe_sync = (t == 8)
            for kc in range(8):
                S.op("pe", lambda e, kc=kc: e.transpose(out=ps[:, pbank + kc // 4, (kc % 4) * 128:(kc % 4) * 128 + P],
                                                        in_=h_tok[0:P, t, kc * 128:(kc + 1) * 128], identity=ident[0:P, 0:P]),
                     r=[("h", t), "cst"], w=[("ps", pbank + kc // 4)])
            S.op("act", lambda e: e.copy(out=hT[:, :, c0:c0 + P],
                                         in_=ps[:, pbank:pbank + 2, :].rearrange("p a (b c) -> p (a b) c", c=128)[:, :, 0:P]),
                 r=[("ps", pbank), ("ps", pbank + 1)], w=[("hT", t)])
            S.pe_sync = False

        def load_ln(i):
            S.op("sp", lambda e: e.dma_start(out=lg_t[:], in_=lng[i].partition_broadcast(128)), w=["lng"], dma="lng")
            S.op("sp", lambda e: e.dma_start(out=lb_t[:], in_=lnb[i].partition_broadcast(128)), w=["lnb"], dma="lnb")

        def ln_elem(t, src, src_tok, cscale, eps_eff):
            P, c0 = tile_geom(t)
            h = h_tok[0:P, t, :]
            st, mv = stt[t % 2], mvt[t % 2]
            tk = ("lnt", t % 2)
            S.op("dve", lambda e: e.scalar_tensor_tensor(out=h, in0=src, scalar=cscale, in1=h, op0=ALU.mult, op1=ALU.add),
                 r=list(src_tok) + [("h", t)], w=[("h", t)])
            for a in range(2):
                S.op("dve", lambda e, a=a: e.bn_stats(out=st[0:P, a, :], in_=h_tok[0:P, t, a * 512:(a + 1) * 512]),
                     r=[("h", t)], w=[tk])
            S.op("dve", lambda e: e.bn_aggr(out=mv[0:P, 0, 0:2], in_=st[0:P, 0:2, :].rearrange("p a b -> p (a b)")),
                 r=[tk], w=[tk])
            S.op("dve", lambda e: e.tensor_scalar(out=mv[0:P, 0, 2:3], in0=mv[0:P, 0, 1:2], scalar1=eps_eff, scalar2=None,
                                                  op0=ALU.add), r=[tk], w=[tk])
            S.op("pool", lambda e: e.tensor_tensor(out=mv[0:P, 0, 3:4], in0=mv[0:P, 0, 2:3], in1=mhalf[0:P], op=ALU.pow),
                 r=[tk, "cst"], w=[tk])
            S.op("dve", lambda e: e.scalar_tensor_tensor(out=mv[0:P, 0, 4:5], in0=mv[0:P, 0, 0:1], scalar=-1.0,
                                                         in1=mv[0:P, 0, 3:4], op0=ALU.mult, op1=ALU.mult), r=[tk], w=[tk])
            S.op("act", lambda e: e.activation(out=h, in_=h, func=AF.Identity, scale=mv[0:P, 0, 3:4], bias=mv[0:P, 0, 4:5]),
                 r=[tk, ("h", t)], w=[("h", t)])
            S.op("dve", lambda e: e.tensor_tensor(out=h, in0=h, in1=lg_t[0:P], op=ALU.mult), r=[("h", t), "lng"], w=[("h", t)])
            S.op("dve", lambda e: e.tensor_tensor(out=h, in0=h, in1=lb_t[0:P], op=ALU.add), r=[("h", t), "lnb"], w=[("h", t)])

        def ffn_phase(blk, which, ln_i, final):
            tiles = list(range(8)) + ([8] if blk == 0 else [])
            ntiles = [(0, 512), (512, 512)] + ([(TB, NS)] if blk == 0 else [])
            S.barrier()
            A.top = phase_base
            actT = A.alloc([128, NJ, NCOL], BF16)
            wdn = A.alloc([128, NJ, D], BF16)
            wup = [A.alloc([128, 8, 2, 128], BF16) for _ in range(3)]
            sg = [A.alloc([128, 512], F32) for _ in range(2)]
            wu_v = wup_d[which].rearrange("(kc p) (two j c) -> p kc two j c", p=128, two=2, c=128)
            wd_v = wdn_d[which].rearrange("(j p) c -> p j c", p=128)

            def load_wup(j):
                for half in range(2):
                    S.op("pool", lambda e, half=half: e.dma_start(out=wup[j % 3][:, :, half, :], in_=wu_v[:, :, half, j, :]),
                         w=[("wup", j % 3)], dma=("wup", j % 3))
            load_ln(ln_i)
            load_wup(0)
            load_wup(1)
            cnt = 0
            for j in range(NJ):
                if j + 2 < NJ:
                    load_wup(j + 2)
                S.op("pool", lambda e, j=j: e.dma_start(out=wdn[:, j, :], in_=wd_v[:, j, :]), w=["wdn"], dma="wdn")
                for ni, (c0, n) in enumerate(ntiles):
                    S.pe_sync = (c0 >= TB)
                    bg, bu = cnt % 2, 2 + cnt % 2
                    sgt = sg[cnt % 2]
                    sgk = ("sg", cnt % 2)
                    cnt += 1
                    for half, bank in ((0, bg), (1, bu)):
                        for kc in range(8):
                            S.op("pe", lambda e, kc=kc, half=half, bank=bank, j=j, c0=c0, n=n: e.matmul(
                                ps[:, bank, 0:n], lhsT=wup[j % 3][:, kc, half, :], rhs=hT[:, kc, c0:c0 + n],
                                start=(kc == 0), stop=(kc == 7)),
                                r=[("wup", j % 3)] + hT_tok(c0, n), w=[("ps", bank)])
                    S.op("act", lambda e, bg=bg, n=n, sgt=sgt: e.activation(out=sgt[:, 0:n], in_=ps[:, bg, 0:n], func=AF.Silu),
                         r=[("ps", bg)], w=[sgk])
                    S.op("dve", lambda e, bu=bu, n=n, sgt=sgt, j=j, c0=c0: e.tensor_tensor(
                        out=actT[:, j, c0:c0 + n], in0=sgt[:, 0:n], in1=ps[:, bu, 0:n], op=ALU.mult),
                        r=[sgk, ("ps", bu)], w=[("actT", j, ni)])
            S.pe_sync = False
            pend = None
            for ti, t in enumerate(tiles):
                P, c0 = tile_geom(t)
                S.pe_sync = (t == 8)
                pb = (ti % 2) * 2
                ni = 2 if t == 8 else t // 4
                for dh in range(2):
                    for j in range(NJ):
                        S.op("pe", lambda e, j=j, dh=dh, pb=pb, P=P, c0=c0: e.matmul(
                            ps[0:P, pb + dh, :], lhsT=actT[:, j, c0:c0 + P], rhs=wdn[:, j, dh * 512:(dh + 1) * 512],
                            start=(j == 0), stop=(j == NJ - 1)),
                            r=[("actT", j, ni), "wdn"], w=[("ps", pb + dh)])
                S.pe_sync = False
                ln_elem(t, PS2(pb, P), [("ps", pb), ("ps", pb + 1)], 0.5 / ALPHA, LN_EPS / (ALPHA * ALPHA))
                if final:
                    dst = yp[(blk * 8 + t) * 128:(blk * 8 + t + 1) * 128, :] if t < 8 else ys
                    S.op("sp", lambda e, t=t, P=P, dst=dst: e.dma_start(out=dst, in_=h_tok[0:P, t, :]),
                         r=[("h", t)], dma=("yout", t))
                else:
                    if debug and blk == 0:
                        S.op("sp", lambda e, t=t: e.dma_start(out=dbg["h1"][t], in_=h_tok[:, t, :]), r=[("h", t)],
                             dma=("dbg", t))
                    if pend is not None:
                        transpose_to_hT(pend[0], pend[1])
                    pend = (t, 4 + (ti % 2) * 2)
            if pend is not None and not final:
                transpose_to_hT(pend[0], pend[1])

        def mixer_phase(blk):
            tiles = list(range(8)) + ([8] if blk == 0 else [])
            ntiles = [(0, 512), (512, 512)] + ([(TB, NS)] if blk == 0 else [])
            S.barrier()
            A.top = phase_base
            oaT = A.alloc([128, 4, NCOL], BF16)
            rgn = A.alloc([128, 9, 512], F32)
            b12_base = A.top
            w1 = A.alloc([128, 8, 2304], BF16)
            w1_end = A.top
            qaT = A.alloc([128, 4, NCOL], BF16)
            kaT = A.alloc([128, 128 + NCOL], BF16)
            vaug = A.alloc([128, 10, 2, 65], BF16)
            qs = A.alloc([128, 4, 128], F32)
            tA = A.alloc([128, 4, 128], F32)
            tB = A.alloc([128, 4, 128], F32)
            qh = A.alloc([128, 4, 128], F32)
            kh = A.alloc([128, 4, 128], F32)
            khb = A.alloc([128, 4, 128], BF16)
            vrb = A.alloc([128, 4, 128], BF16)
            qhT = A.alloc([128, 4, 128], BF16)
            khT = A.alloc([128, 4, 128], BF16)
            et = [A.alloc([128, 512], F32) for _ in range(2)]
            pT = [A.alloc([128, 2, 4, 128], BF16) for _ in range(2)]
            scm = A.alloc([128, 4, 128], BF16)
            o_n = A.alloc([128, 8, 64], F32)
            den = A.alloc([128, 8], F32)
            kvout = A.alloc([128, 256], F32)
            qhT32 = A.alloc([128, 4, NS], F32)
            vr32 = A.alloc([128, 4, 128], F32)

            S.op("pool", lambda e: e.dma_start(out=w1[:, :, 512:768],
                                               in_=w_in[:, 512:768].rearrange("(kc p) c -> p kc c", p=128)),
                 w=["w1"], dma="w1")
            for kc in range(8):
                for hh in range(4):
                    S.op("pool", lambda e, kc=kc, hh=hh: e.dma_start(
                        out=w1[:, kc, hh * 128:(hh + 1) * 128].rearrange("p (g d) -> p g d", g=2),
                        in_=w_in[kc * 128:(kc + 1) * 128, 0:512].rearrange("p (g hh d) -> p hh g d", g=2, hh=4)[:, hh, :, :]),
                        w=["w1"], dma="w1")
            S.op("pool", lambda e: e.dma_start(out=w1[:, :, 768:2304],
                                               in_=w_in[:, 768:2304].rearrange("(kc p) c -> p kc c", p=128)),
                 w=["w1"], dma="w1")
            S.op("dve", lambda e: e.memset(vaug[:, :, :, 64:65], 1.0), w=["vaug_ones"])
            if blk == 0:
                S.op("sp", lambda e: e.dma_start(out=kws[:, 0:127, :], in_=ck[:, 1:128, :]), w=["kws_a"], dma="kws_a")
                S.op("sp", lambda e: e.dma_start(out=vws[:, 0:127, :], in_=cv[:, 1:128, :]), w=["vws_a"], dma="vws_a")
            else:
                S.op("dve", lambda e: e.tensor_copy(out=kaT[:, 0:128], in_=kprev[:]), r=["kprev"], w=["kaT_prev"])
                S.op("dve", lambda e: e.tensor_copy(out=vaug[:, 0, :, :], in_=vprev[:]), r=["vprev", "vaug_ones"],
                     w=[("vaug", 0)])

            cnt = 0
            for c in range(5):
                for ni, (c0, n) in enumerate(ntiles):
                    S.pe_sync = (c0 >= TB)
                    bank = 6 + cnt % 2
                    for kc in range(8):
                        S.op("pe", lambda e, c=c, kc=kc, c0=c0, n=n, bank=bank: e.matmul(
                            ps[:, bank, 0:n], lhsT=w1[:, kc, c * 128:(c + 1) * 128], rhs=hT[:, kc, c0:c0 + n],
                            start=(kc == 0), stop=(kc == 7)), r=["w1"] + hT_tok(c0, n), w=[("ps", bank)])
                    if c < 4:
                        dst, wt = qaT[:, c, c0:c0 + n], [("qaT", c, ni)]
                    else:
                        dst, wt = kaT[:, 128 + c0:128 + c0 + n], [("kaT", ni), "kaT_all"]
                    eng = "act" if cnt % 2 == 0 else "dve"
                    if eng == "act":
                        S.op("act", lambda e, dst=dst, bank=bank, n=n: e.copy(out=dst, in_=ps[:, bank, 0:n]),
                             r=[("ps", bank)], w=wt)
                    else:
                        S.op("dve", lambda e, dst=dst, bank=bank, n=n: e.tensor_copy(out=dst, in_=ps[:, bank, 0:n]),
                             r=[("ps", bank)], w=wt)
                    cnt += 1
            S.pe_sync = False
            qa_tok = lambda ni: [("qaT", c, ni) for c in range(4)]

            def gn_store(t, P, heads):
                st, mv = stt[t % 2], mvt[t % 2]
                tk = ("lnt", t % 2)
                for h, (v, vt) in enumerate(heads):
                    S.op("dve", lambda e, h=h, v=v: e.bn_stats(out=st[0:P, h, :], in_=v), r=vt, w=[tk])
                for h in range(4):
                    S.op("dve", lambda e, h=h: e.bn_aggr(out=mv[0:P, h, 0:2], in_=st[0:P, h, :]), r=[tk], w=[tk])
                S.op("dve", lambda e: e.tensor_scalar(out=mv[0:P, :, 2:3], in0=mv[0:P, :, 1:2], scalar1=GN_EPS, scalar2=None,
                                                      op0=ALU.add), r=[tk], w=[tk])
                S.op("pool", lambda e: e.tensor_tensor(out=mv[0:P, :, 3:4], in0=mv[0:P, :, 2:3],
                                                       in1=mhalf[0:P].unsqueeze(1).to_broadcast([P, 4, 1]), op=ALU.pow),
                     r=[tk, "cst"], w=[tk])
                for h, (v, vt) in enumerate(heads):
                    S.op("dve", lambda e, h=h, v=v: e.tensor_scalar(
                        out=rgn[0:P, t, h * 128:(h + 1) * 128], in0=v, scalar1=mv[0:P, h, 0:1], scalar2=mv[0:P, h, 3:4],
                        op0=ALU.subtract, op1=ALU.mult), r=vt + [tk], w=[("rgn", t)])

            def attn_finish(t, P, c0):
                for g, bank in ((0, 0), (1, 3)):
                    ov = ps[0:P, bank, 0:260].rearrange("p (h d) -> p h d", d=65)
                    S.op("dve", lambda e, g=g, ov=ov: e.tensor_tensor(
                        out=den[0:P, g * 4:(g + 1) * 4], in0=ov[:, :, 64], in1=sinkexp[0:P, g * 4:(g + 1) * 4], op=ALU.add),
                        r=[("ps", bank), "sinkexp"], w=[("den", g)])
                    S.op("dve", lambda e, g=g: e.reciprocal(out=den[0:P, g * 4:(g + 1) * 4], in_=den[0:P, g * 4:(g + 1) * 4]),
                         r=[("den", g)], w=[("den", g)])
                    S.op("dve", lambda e, g=g, ov=ov: e.tensor_tensor(
                        out=o_n[0:P, g * 4:(g + 1) * 4, :], in0=ov[:, :, 0:64],
                        in1=den[0:P, g * 4:(g + 1) * 4].unsqueeze(2).to_broadcast([P, 4, 64]), op=ALU.mult),
                        r=[("ps", bank), ("den", g)], w=[("o_n", g)])
                for c in range(4):
                    S.op("pe", lambda e, c=c: e.transpose(out=ps[:, 4, c * 128:c * 128 + P],
                                                          in_=o_n[0:P, 2 * c:2 * c + 2, :].rearrange("p a b -> p (a b)"),
                                                          identity=ident[0:P, 0:P]),
                         r=[("o_n", c // 2), "cst"], w=[("ps", 4)])
                S.op("act", lambda e: e.copy(out=oaT[:, :, c0:c0 + P],
                                             in_=ps[:, 4, :].rearrange("p (a b) -> p a b", b=128)[:, :, 0:P]),
                     r=[("ps", 4)], w=[("oaT", t)])

            def b1_tile(t):
                P, c0 = tile_geom(t)
                gt = blk * 8 + t if t < 8 else 16
                smp = (t == 8)
                srow = 4 if smp else 0
                for bank, (co, n) in enumerate(((512, 256), (768, 512), (1280, 512), (1792, 512))):
                    for kc in range(8):
                        S.op("pe", lambda e, kc=kc, bank=bank, co=co, n=n: e.matmul(
                            ps[0:P, bank, 0:n], lhsT=hT[:, kc, c0:c0 + P], rhs=w1[:, kc, co:co + n],
                            start=(kc == 0), stop=(kc == 7)), r=["w1", ("hT", t)], w=[("ps", bank)])
                if smp and sub < 4.52:
                    return
                slot = 1 + t
                if not (smp and (KX & 1)):
                    S.op("act", lambda e, slot=slot: e.copy(out=vaug[0:P, slot, :, 0:64],
                                                            in_=ps[0:P, 0, 128:256].rearrange("p (g d) -> p g d", g=2)),
                         r=[("ps", 0)], w=[("vaug", slot)])
                if (smp and not (KX & 2)) or (blk == 1 and t == 7):
                    S.op("act", lambda e: e.copy(out=kvout[0:P, :], in_=ps[0:P, 0, 0:256]), r=[("ps", 0)], w=["kvout"])
                    if smp and not (KX & 4):
                        S.op("sp", lambda e: e.dma_start(out=kws[:, 127, :], in_=kvout[0:NS, 0:128]), r=["kvout"],
                             w=["kws_b"], dma="kws_b")
                        S.op("sp", lambda e: e.dma_start(out=vws[:, 127, :], in_=kvout[0:NS, 128:256]), r=["kvout"],
                             w=["vws_b"], dma="vws_b")
                    elif not smp:
                        S.op("sp", lambda e: e.dma_start(out=kwp, in_=kvout[:, 0:128]), r=["kvout"], dma="kwp")
                        S.op("sp", lambda e: e.dma_start(out=vwp, in_=kvout[:, 128:256]), r=["kvout"], dma="vwp")
                if not (smp and (KX & 8)):
                    S.op("act", lambda e: e.copy(out=vrb[0:P].rearrange("p a b -> p (a b)"), in_=ps[0:P, 3, :]),
                         r=[("ps", 3)], w=["vrb"])
                if smp and not (KX & 16):
                    S.op("act", lambda e: e.copy(out=vr32[0:NS].rearrange("p a b -> p (a b)"), in_=ps[0:NS, 3, :]),
                         r=[("ps", 3)], w=["vr32"])
                if smp and sub < 4.53:
                    return
                cosv = cst[0:P, C_COS + gt * 64:C_COS + (gt + 1) * 64].unsqueeze(1).unsqueeze(1).to_broadcast([P, 4, 2, 64])
                sinv = cst[0:P, C_SIN + gt * 64:C_SIN + (gt + 1) * 64].unsqueeze(1).to_broadcast([P, 4, 64])
                nsinv = cst[0:P, C_NSIN + gt * 64:C_NSIN + (gt + 1) * 64].unsqueeze(1).to_broadcast([P, 4, 64])
                for which, bank, dcol, dsth in (("q", 1, C_DQ, qh), ("k", 2, C_DK, kh)):
                    dv = cst[0:P, dcol + srow:dcol + srow + 4].unsqueeze(2).to_broadcast([P, 4, 128])
                    S.op("dve", lambda e, bank=bank, dv=dv: e.tensor_tensor(
                        out=qs[0:P], in0=ps[0:P, bank, :].rearrange("p (a b) -> p a b", b=128), in1=dv, op=ALU.mult),
                        r=[("ps", bank), "cst"], w=["qs"])
                    S.op("pool", lambda e: e.tensor_tensor(
                        out=tA[0:P].rearrange("p h (two d) -> p h two d", two=2),
                        in0=qs[0:P].rearrange("p h (two d) -> p h two d", two=2), in1=cosv, op=ALU.mult),
                        r=["qs", "cst"], w=["tA"])
                    S.op("pool", lambda e: e.tensor_tensor(out=tB[0:P, :, 0:64], in0=qs[0:P, :, 64:128], in1=nsinv, op=ALU.mult),
                         r=["qs", "cst"], w=["tB0"])
                    S.op("pool", lambda e: e.tensor_tensor(out=tB[0:P, :, 64:128], in0=qs[0:P, :, 0:64], in1=sinv, op=ALU.mult),
                         r=["qs", "cst"], w=["tB1"])
                    S.op("pool", lambda e, dsth=dsth: e.tensor_tensor(out=dsth[0:P], in0=tA[0:P], in1=tB[0:P], op=ALU.add),
                         r=["tA", "tB0", "tB1"], w=[which + "h"])
                if smp and sub < 4.54:
                    return
                S.op("act", lambda e: e.copy(out=khb[0:P], in_=kh[0:P]), r=["kh"], w=["khb"])
                for which, src, bank, dstT in (("q", qh, 4, qhT), ("k", kh, 5, khT)):
                    for h in range(4):
                        S.op("pe", lambda e, h=h, src=src, bank=bank: e.transpose(
                            out=ps[:, bank, h * 128:h * 128 + P], in_=src[0:P, h, :], identity=ident[0:P, 0:P]),
                            r=[which + "h", "cst"], w=[("ps", bank)])
                    S.op("act", lambda e, bank=bank, dstT=dstT: e.copy(
                        out=dstT[:, :, 0:P], in_=ps[:, bank, :].rearrange("p (a b) -> p a b", b=128)[:, :, 0:P]),
                        r=[("ps", bank)], w=[which + "hT"])
                    if smp and which == "q":
                        S.op("act", lambda e, bank=bank: e.copy(
                            out=qhT32[:], in_=ps[:, bank, :].rearrange("p (a b) -> p a b", b=128)[:, :, 0:NS]),
                            r=[("ps", bank)], w=["qhT32"])

                if sub < 2:
                    return
                if not smp:
                    first = (blk == 0 and t == 0)
                    kbs = [1] if first else [0, 1]
                    sc_cnt = 0
                    for g in range(2):
                        for kb in kbs:
                            kcol = (t + kb) * 128
                            sbank = 6 + sc_cnt % 2
                            e_t = et[sc_cnt % 2]
                            ek = ("et", sc_cnt % 2)
                            sc_cnt += 1
                            ktok = ["kaT_prev"] if (kb == 0 and t == 0) else [("kaT", (t + kb - 1) // 4)]
                            S.op("pe", lambda e, g=g, kcol=kcol, sbank=sbank: e.matmul(
                                ps[:, sbank, :], lhsT=kaT[g * 64:(g + 1) * 64, kcol:kcol + 128],
                                rhs=qaT[g * 64:(g + 1) * 64, :, c0:c0 + 128], start=True, stop=True),
                                r=ktok + qa_tok(t // 4), w=[("ps", sbank)])
                            S.op("dve", lambda e, g=g, kb=kb, sbank=sbank, e_t=e_t: e.scalar_tensor_tensor(
                                out=e_t[:], in0=ps[:, sbank, :], scalar=0.125,
                                in1=BT[:, kb, g * 4:(g + 1) * 4, :].rearrange("p a b -> p (a b)"), op0=ALU.mult, op1=ALU.add),
                                r=[("ps", sbank), "BT"], w=[ek])
                            S.op("act", lambda e, g=g, kb=kb, e_t=e_t: e.activation(
                                out=pT[g][:, kb, :, :].rearrange("p a b -> p (a b)"), in_=e_t[:], func=AF.Exp),
                                r=[ek], w=[("pT", g, kb)])
                        obank = 0 if g == 0 else 3
                        for hh in range(4):
                            for i, kb in enumerate(kbs):
                                vslot = t + kb
                                S.op("pe", lambda e, g=g, hh=hh, kb=kb, vslot=vslot, obank=obank, i=i: e.matmul(
                                    ps[:, obank, hh * 65:(hh + 1) * 65], lhsT=pT[g][:, kb, hh, :], rhs=vaug[:, vslot, g, :],
                                    start=(i == 0), stop=(i == len(kbs) - 1)),
                                    r=[("pT", g, kb), ("vaug", vslot), "vaug_ones"], w=[("ps", obank)])
                    attn_finish(t, P, c0)
                    if sub < 3:
                        return
                    for h in range(4):
                        S.op("pe", lambda e, h=h: e.matmul(ps[:, 1, h * 128:(h + 1) * 128], lhsT=khT[:, h, :], rhs=qhT[:, h, :],
                                                           start=True, stop=True), r=["khT", "qhT"], w=[("ps", 1)])
                    S.op("dve", lambda e: e.tensor_tensor(
                        out=scm[:], in0=ps[:, 1, :].rearrange("p (a b) -> p a b", b=128),
                        in1=caus.unsqueeze(1).to_broadcast([128, 4, 128]), op=ALU.mult), r=[("ps", 1), "cst"], w=["scm"])
                    for h in range(4):
                        S.op("pe", lambda e, h=h: e.matmul(ps[:, 2, h * 128:(h + 1) * 128], lhsT=scm[:, h, :], rhs=vrb[:, h, :],
                                                           start=True, stop=first), r=["scm", "vrb"], w=[("ps", 2)])
                        if not first:
                            S.op("pe", lambda e, h=h: e.matmul(ps[:, 2, h * 128:(h + 1) * 128], lhsT=qhT[:, h, :], rhs=Sb[:, h, :],
                                                               start=False, stop=True), r=["qhT", "Sb"], w=[("ps", 2)])
                    gn_store(t, P, [(ps[:, 2, h * 128:(h + 1) * 128], [("ps", 2)]) for h in range(4)])
                    for h in range(4):
                        S.op("pe", lambda e, h=h: e.matmul(ps[:, 5, h * 128:(h + 1) * 128], lhsT=khb[:, h, :], rhs=vrb[:, h, :],
                                                           start=True, stop=True), r=["khb", "vrb"], w=[("ps", 5)])
                    S.op("dve", lambda e: e.tensor_tensor(out=Sst[:].rearrange("p a b -> p (a b)"), in0=ps[:, 5, :],
                                                          in1=Sst[:].rearrange("p a b -> p (a b)"), op=ALU.add),
                         r=[("ps", 5), "S"], w=["S"])
                    S.op("dve", lambda e: e.tensor_tensor(out=Sst[:], in0=Sst[:],
                                                          in1=cst[:, C_GC:C_GC + 4].unsqueeze(2).to_broadcast([128, 4, 128]),
                                                          op=ALU.mult), r=["S", "cst"], w=["S"])
                    S.op("act", lambda e: e.copy(out=Sb[:], in_=Sst[:]), r=["S"], w=["Sb"])
                    if blk == 1 and t == 7:
                        S.op("sp", lambda e: e.dma_start(out=sp_out.rearrange("h k v -> k h v"), in_=Sst[:]), r=["S"], dma="spout")
                else:
                    if sub < 5:
                        return
                    A_save = A.top
                    A.top = b12_base
                    Wk32 = A.alloc([128, NS, 128], F32)
                    WkT = A.alloc([128, NS, 128], BF16)
                    Wva = A.alloc([128, NS, 2, 65], BF16)
                    oTs = A.alloc([128, 8, NS], F32)
                    e_s = A.alloc([128, NS, 8], F32)
                    pTs = A.alloc([128, NS, 8], BF16)
                    Sp = [A.alloc([128, 4, 128], F32) for _ in range(2)]
                    Sn = [A.alloc([128, 4, 128], F32) for _ in range(2)]
                    Vm = [A.alloc([128, 4, 128], F32) for _ in range(2)]
                    QTm = A.alloc([128, 4, NS, NS], F32)
                    assert A.top <= w1_end, (A.top, w1_end)
                    S.op("sp", lambda e: e.dma_start(out=Wk32[:], in_=kws.rearrange("b r c -> r b c")),
                         r=["kws_a", "kws_b"], w=["Wk32", "w1"], dma="wk32")
                    for g in range(2):
                        S.op("pool", lambda e, g=g: e.dma_start(out=Wva[:, :, g, 0:64],
                                                                in_=vws[:, :, g * 64:(g + 1) * 64].rearrange("b r d -> r b d")),
                             r=["vws_a", "vws_b"], w=["Wva", "w1"], dma="wva")
                    S.op("dve", lambda e: e.memset(Wva[:, :, :, 64:65], 1.0), w=["Wva1", "w1"])
                    for b in range(NS):
                        bank = 6 + (b // 4) % 2
                        S.op("pe", lambda e, b=b, bank=bank: e.transpose(out=ps[:, bank, (b % 4) * 128:(b % 4 + 1) * 128],
                                                                        in_=Wk32[:, b, :], identity=ident),
                             r=["Wk32", "cst"], w=[("ps", bank)])
                        if b % 4 == 3:
                            S.op("act", lambda e, b=b, bank=bank: e.copy(
                                out=WkT[:, b - 3:b + 1, :], in_=ps[:, bank, :].rearrange("p (a b) -> p a b", b=128)),
                                r=[("ps", bank)], w=["WkT", "w1"])
                    for b in range(NS):
                        for g in range(2):
                            S.op("pe", lambda e, b=b, g=g: e.matmul(
                                ps[:, 0, b * 8 + g * 4:b * 8 + g * 4 + 4], lhsT=WkT[g * 64:(g + 1) * 64, b, :],
                                rhs=qaT[g * 64:(g + 1) * 64, :, TB + b], start=True, stop=True),
                                r=["WkT"] + qa_tok(2), w=[("ps", 0)])
                    S.op("dve", lambda e: e.scalar_tensor_tensor(
                        out=e_s[:], in0=ps[:, 0, 0:128].rearrange("p (b h) -> p b h", h=8), scalar=0.125,
                        in1=bias_s[:].unsqueeze(1).to_broadcast([128, NS, 8]), op0=ALU.mult, op1=ALU.add),
                        r=[("ps", 0), "bias_s"], w=["e_s", "w1"])
                    S.op("act", lambda e: e.activation(out=pTs[:], in_=e_s[:], func=AF.Exp), r=["e_s"], w=["pTs", "w1"])
                    if sub < 5.2:
                        A.top = A_save
                        return
                    for b in range(NS):
                        for g in range(2):
                            S.op("pe", lambda e, b=b, g=g: e.matmul(
                                ps[0:65, 3, b * 8 + g * 4:b * 8 + g * 4 + 4], lhsT=Wva[:, b, g, :],
                                rhs=pTs[:, b, g * 4:(g + 1) * 4], start=True, stop=True),
                                r=["Wva", "Wva1", "pTs"], w=[("ps", 3)])
                    S.op("act", lambda e: e.copy(out=oTs[0:65], in_=ps[0:65, 3, 0:128].rearrange("p (b h) -> p h b", h=8)),
                         r=[("ps", 3)], w=["oTs", "w1"])
                    for h in range(8):
                        bank = 0 if h < 4 else 3
                        S.op("pe", lambda e, h=h, bank=bank: e.transpose(
                            out=ps[0:NS, bank, (h % 4) * 65:(h % 4 + 1) * 65], in_=oTs[0:65, h, :], identity=ident[0:65, 0:65]),
                            r=["oTs", "cst"], w=[("ps", bank)])
                    attn_finish(t, P, c0)
                    if sub < 5.3:
                        A.top = A_save
                        return
                    S.op("dve", lambda e: e.tensor_tensor(
                        out=QTm[:], in0=qhT32[:].unsqueeze(3).to_broadcast([128, 4, NS, NS]),
                        in1=cst[:, C_EYER:C_EYER + 256].rearrange("p (a b) -> p a b", b=NS).unsqueeze(1).to_broadcast([128, 4, NS, NS]),
                        op=ALU.mult), r=["qhT32", "cst"], w=["QTm", "w1"])
                    for b in range(NS):
                        i2 = b % 2
                        S.op("sp", lambda e, b=b, i2=i2: e.dma_start(out=Sp[i2][:], in_=st_in[b].rearrange("h k v -> k h v")),
                             w=[("Sp", i2), "w1"], dma=("Sp", i2))
                        S.op("pool", lambda e, b=b, i2=i2: e.tensor_scalar(
                            out=Vm[i2][0:NS].rearrange("p a b -> p (a b)"), in0=vr32[0:NS].rearrange("p a b -> p (a b)"),
                            scalar1=ident[0:NS, b:b + 1], scalar2=None, op0=ALU.mult),
                            r=["vr32", "cst"], w=[("Vm", i2), "w1"])
                        ub = 1 + i2
                        for h in range(4):
                            S.op("pe", lambda e, h=h, ub=ub, i2=i2: e.matmul(
                                ps[:, ub, h * 128:(h + 1) * 128], lhsT=kh[0:NS, h, :], rhs=Vm[i2][0:NS, h, :],
                                start=True, stop=True), r=["kh", ("Vm", i2)], w=[("ps", ub)])
                        for h in range(4):
                            S.op("dve", lambda e, h=h, ub=ub, i2=i2: e.scalar_tensor_tensor(
                                out=Sn[i2][:, h, :], in0=Sp[i2][:, h, :], scalar=float(GAMMAS[h]),
                                in1=ps[:, ub, h * 128:(h + 1) * 128], op0=ALU.mult, op1=ALU.add),
                                r=[("Sp", i2), ("ps", ub)], w=[("Sn", i2), "w1"])
                        S.op("sp", lambda e, b=b, i2=i2: e.dma_start(out=ss_out[b].rearrange("h k v -> k h v"), in_=Sn[i2][:]),
                             r=[("Sn", i2)], dma=("ssout", i2))
                        for h in range(4):
                            if sub < 5.4:
                                break
                            S.op("pe", lambda e, h=h, b=b, i2=i2: e.matmul(
                                ps[0:NS, 4 + h, 0:128], lhsT=QTm[:, h, b, :], rhs=Sn[i2][:, h, :],
                                start=(b == 0), stop=(b == NS - 1)), r=["QTm", ("Sn", i2)], w=[("ps", 4 + h)])
                    if sub >= 5.4:
                        gn_store(t, P, [(ps[0:NS, 4 + h, 0:128], [("ps", 4 + h)]) for h in range(4)])
                    A.top = A_save

            if sub < 1:
                return
            for t in tiles:
                if t == 8 and (sub < 4.5 or (KX & 32)):
                    continue
                if t > 0 and sub < 4:
                    continue
                S.pe_sync = (t == 8)
                b1_tile(t)
                S.pe_sync = False
            if sub < 6:
                return
            if blk == 0:
                S.op("dve", lambda e: e.tensor_copy(out=kprev[:], in_=kaT[:, TB:TB + 128]), r=[("kaT", 1)], w=["kprev"])
                S.op("dve", lambda e: e.tensor_copy(out=vprev[:], in_=vaug[:, 8, :, :]), r=[("vaug", 8), "vaug_ones"],
                     w=["vprev"])

            S.barrier()
            A.top = b12_base
            w2 = A.alloc([128, 8, 2560], BF16)
            wao = A.alloc([128, 4, D], BF16)
            wro = A.alloc([128, 4, D], BF16)
            wo = A.alloc([128, 8, D], BF16)
            sgr = A.alloc([128, 512], F32)
            rr = A.alloc([128, 512], F32)
            rT = A.alloc([128, 4, 128], BF16)
            sa = A.alloc([128, D], F32)
            m1 = A.alloc([128, D], F32)
            mT = A.alloc([128, 8, 128], BF16)
            def ld_w2(lo, hi, tok):
                S.op("pool", lambda e: e.dma_start(out=w2[:, :, lo:hi],
                                                   in_=w_in[:, 2304 + lo:2304 + hi].rearrange("(kc p) c -> p kc c", p=128)),
                     w=[tok], dma=tok)
            ld_w2(0, 512, "w2a")
            S.op("pool", lambda e: e.dma_start(out=wao[:], in_=w_ao.rearrange("(kc p) c -> p kc c", p=128)), w=["wao"], dma="wao")
            S.op("pool", lambda e: e.dma_start(out=wro[:], in_=w_ro.rearrange("(kc p) c -> p kc c", p=128)), w=["wro"], dma="wro")
            ld_w2(512, 1536, "w2b")
            ld_w2(1536, 2560, "w2c")
            S.op("pool", lambda e: e.dma_start(out=wo[:], in_=w_o.rearrange("(kc p) c -> p kc c", p=128)), w=["wo"], dma="wo")
            load_ln(1)
            def b2_tile(t):
                P, c0 = tile_geom(t)
                for kc in range(8):
                    S.op("pe", lambda e, kc=kc: e.matmul(ps[0:P, 0, :], lhsT=hT[:, kc, c0:c0 + P], rhs=w2[:, kc, 0:512],
                                                         start=(kc == 0), stop=(kc == 7)), r=["w2a", ("hT", t)], w=[("ps", 0)])
                S.op("act", lambda e: e.activation(out=sgr[0:P], in_=ps[0:P, 0, :], func=AF.Silu), r=[("ps", 0)], w=["sgr"])
                S.op("dve", lambda e: e.tensor_tensor(out=rr[0:P], in0=sgr[0:P], in1=rgn[0:P, t, :], op=ALU.mult),
                     r=["sgr", ("rgn", t)], w=["rr"])
                for c in range(4):
                    S.op("pe", lambda e, c=c: e.transpose(out=ps[:, 1, c * 128:c * 128 + P], in_=rr[0:P, c * 128:(c + 1) * 128],
                                                          identity=ident[0:P, 0:P]), r=["rr", "cst"], w=[("ps", 1)])
                S.op("act", lambda e: e.copy(out=rT[:, :, 0:P], in_=ps[:, 1, :].rearrange("p (a b) -> p a b", b=128)[:, :, 0:P]),
                     r=[("ps", 1)], w=["rT"])
                if t == 8 and (KX & 64):
                    return
                for dh in range(2):
                    for c in range(4):
                        S.op("pe", lambda e, c=c, dh=dh: e.matmul(ps[0:P, 2 + dh, :], lhsT=oaT[:, c, c0:c0 + P],
                                                                  rhs=wao[:, c, dh * 512:(dh + 1) * 512],
                                                                  start=(c == 0), stop=(c == 3)),
                             r=["wao", ("oaT", t)], w=[("ps", 2 + dh)])
                for dh in range(2):
                    for c in range(4):
                        S.op("pe", lambda e, c=c, dh=dh: e.matmul(ps[0:P, 4 + dh, :], lhsT=rT[:, c, 0:P],
                                                                  rhs=wro[:, c, dh * 512:(dh + 1) * 512],
                                                                  start=(c == 0), stop=(c == 3)),
                             r=["wro", "rT"], w=[("ps", 4 + dh)])
                if t == 8 and (KX & 128):
                    return
                for gi, goff in enumerate((512, 1536)):
                    for dh in range(2):
                        for kc in range(8):
                            S.op("pe", lambda e, kc=kc, dh=dh, goff=goff: e.matmul(
                                ps[0:P, 6 + dh, :], lhsT=hT[:, kc, c0:c0 + P], rhs=w2[:, kc, goff + dh * 512:goff + (dh + 1) * 512],
                                start=(kc == 0), stop=(kc == 7)), r=["w2b" if gi == 0 else "w2c", ("hT", t)], w=[("ps", 6 + dh)])
                    S.op("act", lambda e: e.activation(out=sa[0:P], in_=PS2(6, P), func=AF.Tanh, scale=0.5),
                         r=[("ps", 6), ("ps", 7)], w=["sa"])
                    if gi == 0:
                        S.op("dve", lambda e: e.scalar_tensor_tensor(out=m1[0:P], in0=sa[0:P], scalar=1.0, in1=PS2(2, P),
                                                                     op0=ALU.add, op1=ALU.mult),
                             r=["sa", ("ps", 2), ("ps", 3)], w=["m1"])
                    else:
                        S.op("dve", lambda e: e.scalar_tensor_tensor(out=sa[0:P], in0=sa[0:P], scalar=1.0, in1=PS2(4, P),
                                                                     op0=ALU.add, op1=ALU.mult),
                             r=["sa", ("ps", 4), ("ps", 5)], w=["sa"])
                        S.op("dve", lambda e: e.tensor_tensor(out=m1[0:P], in0=m1[0:P], in1=sa[0:P], op=ALU.add),
                             r=["sa", "m1"], w=["m1"])
                if t == 8 and (KX & 256):
                    return
                for kc in range(8):
                    S.op("pe", lambda e, kc=kc: e.transpose(out=ps[:, kc // 4, (kc % 4) * 128:(kc % 4) * 128 + P],
                                                            in_=m1[0:P, kc * 128:(kc + 1) * 128], identity=ident[0:P, 0:P]),
                         r=["m1", "cst"], w=[("ps", kc // 4)])
                S.op("act", lambda e: e.copy(out=mT[:, :, 0:P],
                                             in_=ps[:, 0:2, :].rearrange("p a (b c) -> p (a b) c", c=128)[:, :, 0:P]),
                     r=[("ps", 0), ("ps", 1)], w=["mT"])
                for dh in range(2):
                    for kc in range(8):
                        S.op("pe", lambda e, kc=kc, dh=dh: e.matmul(ps[0:P, 2 + dh, :], lhsT=mT[:, kc, 0:P],
                                                                    rhs=wo[:, kc, dh * 512:(dh + 1) * 512],
                                                                    start=(kc == 0), stop=(kc == 7)),
                             r=["wo", "mT"], w=[("ps", 2 + dh)])
                ln_elem(t, PS2(2, P), [("ps", 2), ("ps", 3)], 0.5 / ALPHA, LN_EPS / (ALPHA * ALPHA))
                if debug and blk == 0:
                    S.op("sp", lambda e, t=t: e.dma_start(out=dbg["h2"][t], in_=h_tok[:, t, :]), r=[("h", t)], dma=("dbg2", t))

            pend = None
            for t in tiles:
                if t == 8 and (KX & 32):
                    continue
                S.pe_sync = (t == 8)
                b2_tile(t)
                S.pe_sync = False
                if pend is not None:
                    transpose_to_hT(pend, 6)
                pend = t
            transpose_to_hT(pend, 6)

        stage = 0
        for blk in range(2):
            tiles = list(range(8)) + ([8] if blk == 0 else [])
            if stage >= max_stage:
                break
            stage += 1
            for t in tiles:
                P, c0 = tile_geom(t)
                src = xp[(blk * 8 + t) * 128:(blk * 8 + t + 1) * 128, :] if t < 8 else xs
                S.op("sp", lambda e, t=t, P=P, src=src: e.dma_start(out=h_tok[0:P, t, :], in_=src), w=[("h", t)], dma=("x", t))
            for ti, t in enumerate(tiles):
                transpose_to_hT(t, 4 + (ti % 2) * 2)
            if stage >= max_stage:
                break
            stage += 1
            ffn_phase(blk, 0, 0, final=False)
            if stage >= max_stage:
                break
            stage += 1
            mixer_phase(blk)
            if stage >= max_stage:
                break
            stage += 1
            ffn_phase(blk, 1, 2, final=True)
        S.barrier()
        S.op("sp", lambda e: e.nop(), r=[], w=[])
        S.emit(nc, es)
        build_nc.info = dict(nops=len(S.ops), nsems=S.nsems, sbuf_peak=A.peak)
    return nc


_CACHE = {}
KX = 0


def kernel(x_prompt, x_sample, cache_k_win, cache_v_win, state_ret, rel_bias, w_in, attn_sinks,
           w_attn_out, w_ret_out, w_o, ffn1_w_up, ffn1_w_down, ffn2_w_up, ffn2_w_down,
           ln1_g, ln1_b, ln2_g, ln2_b, ln3_g, ln3_b, _debug=False):
    f = lambda a: np.ascontiguousarray(np.asarray(a, dtype=np.float32))
    import os as _os
    _ms = int(_os.environ.get("K_STAGES", "99"))
    _sub = float(_os.environ.get("K_SUB", "99"))
    global KX
    KX = int(_os.environ.get("K_X", "0"))
    key = ("nc", bool(_debug), _ms, _sub, KX, _os.environ.get("K_PAD", "0"))
    if key not in _CACHE:
        _CACHE[key] = build_nc(debug=_debug, max_stage=_ms, sub=_sub)
    nc = _CACHE[key]
    consts = make_consts()
    shared = dict(relb=f(rel_bias), w_in=f(w_in)[0], sinks=f(attn_sinks)[0], w_ao=f(w_attn_out)[0], w_ro=f(w_ret_out)[0],
                  w_o=f(w_o)[0], f1u=f(ffn1_w_up)[0], f2u=f(ffn2_w_up)[0], f1d=f(ffn1_w_down)[0], f2d=f(ffn2_w_down)[0],
                  ln1g=f(ln1_g)[0], ln1b=f(ln1_b)[0], ln2g=f(ln2_g)[0], ln2b=f(ln2_b)[0], ln3g=f(ln3_g)[0], ln3b=f(ln3_b)[0],
                  consts=consts)
    xp, xs = f(x_prompt), f(x_sample)
    ckf, cvf, stf = f(cache_k_win), f(cache_v_win), f(state_ret)
    in_maps = []
    for c in range(NCORES):
        m = dict(shared)
        sl = slice(c * NS, (c + 1) * NS)
        m["xp"] = xp[c]
        m["xs"] = xs[sl, 0, :]
        m["ck"] = ckf[0, sl].reshape(NS, 128, 128)
        m["cv"] = cvf[0, sl].reshape(NS, 128, 128)
        m["st"] = stf[0, sl]
        in_maps.append(m)
    _nco = int(_os.environ.get("K_CORES", str(NCORES)))
    res = run_bass_kernel_spmd(nc, in_maps[:_nco], core_ids=list(range(_nco)))
    R = list(res.results)
    while len(R) < NCORES:
        R.append({k: np.zeros_like(v) for k, v in R[0].items()})
    y_p = np.stack([R[c]["yp"] for c in range(NCORES)], 0)
    y_s = np.concatenate([R[c]["ys"] for c in range(NCORES)], 0).reshape(128, 1, D)
    kwp = np.stack([R[c]["kwp"] for c in range(NCORES)], 0).reshape(1, 8, 128, 2, 64)
    vwp = np.stack([R[c]["vwp"] for c in range(NCORES)], 0).reshape(1, 8, 128, 2, 64)
    spo = np.stack([R[c]["sp"] for c in range(NCORES)], 0).reshape(1, 8, 4, 128, 128)
    kws = np.concatenate([R[c]["kws"] for c in range(NCORES)], 0).reshape(1, 128, 128, 2, 64)
    vws = np.concatenate([R[c]["vws"] for c in range(NCORES)], 0).reshape(1, 128, 128, 2, 64)
    sso = np.concatenate([R[c]["ss"] for c in range(NCORES)], 0).reshape(1, 128, 4, 128, 128)
    outs = tuple(np.ascontiguousarray(a.astype(np.float32)) for a in (y_p, y_s, kwp, vwp, spo, kws, vws, sso))
    if _debug:
        kernel.dbg = [{k: v for k, v in R[c].items() if k.startswith("dbg")} for c in range(NCORES)]
    return outs
```

```python
import numpy as np
from contextlib import ExitStack
import concourse.bass as bass
import concourse.mybir as mybir
from concourse.bass_utils import run_bass_kernel_spmd

F32 = mybir.dt.float32
BF16 = mybir.dt.bfloat16
AF = mybir.ActivationFunctionType
ALU = mybir.AluOpType

NCORES = 8
D = 1024
SEQ = 2048
DFF = 2816
NJ = DFF // 128
NS = 16
TB = 1024
NCOL = TB + NS
PAST = 16384
ALPHA = 2.0 ** 0.25
LN_EPS = 1e-5
GN_EPS = 1e-6
MASKV = -30000.0
SB_BASE = 16512
SB_END = 229376

C_ID = 0
C_J = 128
C_CAUS = 256
C_COS = 384
C_SIN = C_COS + 17 * 64
C_NSIN = C_SIN + 17 * 64
C_DQ = C_NSIN + 17 * 64
C_DK = C_DQ + 8
C_GC = C_DK + 8
C_EYER = C_GC + 4
C_OH1 = C_EYER + 256
C_OH2 = C_OH1 + 128
C_MH = C_OH2 + 128
C_ONE = C_MH + 1
NCONST = ((C_ONE + 1 + 7) // 8) * 8
GAMMAS = [1.0 - 2.0 ** (-5.0 - h) for h in range(4)]


def _bucket(d):
    d = np.asarray(d)
    n = np.maximum(d, 0)
    ratio = np.maximum(n, 1).astype(np.float32) / np.float32(16)
    large = 16 + (np.log(np.maximum(ratio, np.float32(1.0))).astype(np.float32)
                  / np.float32(np.log(128 / 16)) * np.float32(16)).astype(np.int32)
    large = np.minimum(large, 31)
    return np.where(n < 16, n, large)


def make_consts():
    c = np.zeros((128, NCONST), np.float32)
    c[:, C_ID:C_ID + 128] = np.eye(128, dtype=np.float32)
    c[:, C_J:C_J + 128] = np.eye(128, dtype=np.float32)[::-1]
    jj = np.arange(128)
    c[:, C_CAUS:C_CAUS + 128] = (jj[None, :] >= jj[:, None]).astype(np.float32)
    inv = (np.float32(10000.0) ** (-(np.arange(64, dtype=np.float32) / np.float32(64)))).astype(np.float32)
    for t in range(17):
        pos = (t * 128 + np.arange(128)) if t < 16 else np.full(128, PAST)
        ang = (pos.astype(np.float32)[:, None] * inv[None, :]).astype(np.float32)
        c[:, C_COS + t * 64:C_COS + (t + 1) * 64] = np.cos(ang.astype(np.float64)).astype(np.float32)
        c[:, C_SIN + t * 64:C_SIN + (t + 1) * 64] = np.sin(ang.astype(np.float64)).astype(np.float32)
        c[:, C_NSIN + t * 64:C_NSIN + (t + 1) * 64] = -np.sin(ang.astype(np.float64)).astype(np.float32)
    p = np.arange(128, dtype=np.float64)
    for h in range(4):
        lg = np.log1p(-(2.0 ** (-5.0 - h)))
        c[:, C_DQ + h] = np.exp((p + 1) * lg)
        c[:, C_DK + h] = (128.0 ** -0.5) * np.exp(-(p + 1) * lg)
        c[:, C_DQ + 4 + h] = 1.0
        c[:, C_DK + 4 + h] = 128.0 ** -0.5
        c[:, C_GC + h] = np.exp(128 * lg)
    c[:, C_EYER:C_EYER + 256] = np.eye(16, dtype=np.float32).reshape(1, 256)
    b1 = _bucket(np.arange(128))
    b2 = _bucket(127 - np.arange(128))
    for r in range(128):
        c[b1[r], C_OH1 + r] = 1.0
        c[b2[r], C_OH2 + r] = 1.0
    c[:, C_MH] = -0.5
    c[:, C_ONE] = 1.0
    return c


def _rnd_tile(x):
    return 32 if x <= 32 else (64 if x <= 64 else 128)


class _FakePE:
    def __init__(self):
        self.mode = None

    def matmul(self, out, lhsT=None, rhs=None, **kw):
        self.mode = (_rnd_tile(lhsT.shape[0]), _rnd_tile(int(np.prod(lhsT.shape[1:]))))

    def transpose(self, out=None, in_=None, identity=None):
        self.mode = (_rnd_tile(in_.shape[0]), _rnd_tile(int(np.prod(in_.shape[1:]))))


class Sched:
    ENGS = ("pe", "act", "dve", "pool", "sp")

    def __init__(self):
        self.ops = []
        self.last_w = {}
        self.readers = {}
        self.dma_cnt = {}
        self.bar = None
        self.bar_passed = set()
        self.last_eng = {}
        self.last_dma = {}
        self.pe_sync = False
        self.pe_sync_once = False

    def _stream(self, idx):
        o = self.ops[idx]
        return ("dma", o["dma"]) if o["dma"] is not None else ("eng", o["eng"])

    def op(self, eng, fn, r=(), w=(), dma=None):
        idx = len(self.ops)
        deps = {}

        def add(d):
            s = self._stream(d)
            if deps.get(s, -1) < d:
                deps[s] = d
        for t in r:
            lw = self.last_w.get(t)
            if lw is not None:
                add(lw)
        for t in w:
            lw = self.last_w.get(t)
            if lw is not None:
                add(lw)
            for d in self.readers.get(t, {}).values():
                add(d)
        if self.bar is not None and eng not in self.bar_passed:
            for d in self.bar:
                add(d)
            self.bar_passed.add(eng)
        force = False
        if eng == "pe" and dma is None and (self.pe_sync or self.pe_sync_once):
            self.pe_sync_once = self.pe_sync
            if "pe" in self.last_eng:
                add(self.last_eng["pe"])
                force = True
        o = dict(eng=eng, fn=fn, deps=deps, dma=dma, need_inc=False, ev=None, force=force)
        if dma is not None:
            self.dma_cnt[dma] = self.dma_cnt.get(dma, 0) + 1
            o["ev"] = ("dma", dma, 16 * self.dma_cnt[dma])
            self.last_dma[dma] = idx
        else:
            self.last_eng[eng] = idx
        self.ops.append(o)
        me = ("dma", dma) if dma is not None else ("eng", eng)
        for t in r:
            self.readers.setdefault(t, {})[me] = idx
        for t in w:
            self.last_w[t] = idx
            self.readers[t] = {}
        return idx

    def barrier(self):
        self.bar = set(self.last_eng.values()) | set(self.last_dma.values())
        self.bar_passed = set()

    def finalize(self):
        ops = self.ops
        for o in ops:
            real = []
            for s, d in o["deps"].items():
                od = ops[d]
                if s == ("eng", "pe") and o["eng"] == "pe" and o["dma"] is None and not o["force"]:
                    continue
                real.append(d)
                if od["dma"] is None:
                    od["need_inc"] = True
            o["deps"] = real
        cnt = {e: 0 for e in self.ENGS}
        for o in ops:
            if o["dma"] is None and o["need_inc"]:
                cnt[o["eng"]] += 1
                o["ev"] = ("eng", o["eng"], cnt[o["eng"]])
        seen = {e: {} for e in self.ENGS}
        for o in ops:
            waits = {}
            for d in o["deps"]:
                kind, key, val = ops[d]["ev"]
                k = (kind, key)
                if seen[o["eng"]].get(k, 0) >= val:
                    continue
                waits[k] = max(waits.get(k, 0), val)
            for k, v in waits.items():
                seen[o["eng"]][k] = v
            o["waits"] = waits

    def emit(self, nc, es):
        self.finalize()
        sems = {}
        n = [0]

        def sem(k):
            if k not in sems:
                n[0] += 1
                sems[k] = es.enter_context(nc.semaphore("s%d" % n[0]))
            return sems[k]
        for o in self.ops:
            for k in o["waits"]:
                sem(k)
            if o["ev"] is not None:
                sem((o["ev"][0], o["ev"][1]))
        block = es.enter_context(nc.Block())
        ops = self.ops

        import os as _os2
        pad = int(_os2.environ.get("K_PAD", "0"))

        def run(engname, eng):
            for _ in range(pad):
                eng.nop()
            for o in ops:
                if o["eng"] != engname:
                    continue
                for k, v in o["waits"].items():
                    eng.wait_ge(sems[k], v)
                ins = o["fn"](eng)
                if o["dma"] is not None:
                    ins.then_inc(sems[("dma", o["dma"])], 16)
                elif o["need_inc"]:
                    ins.then_inc(sems[("eng", engname)], 1)

        @block.tensor
        def _(e):
            run("pe", e)

        @block.scalar
        def _(e):
            run("act", e)

        @block.vector
        def _(e):
            run("dve", e)

        @block.gpsimd
        def _(e):
            run("pool", e)

        @block.sync
        def _(e):
            run("sp", e)
        self.nsems = len(sems)


class Arena:
    def __init__(self, nc, base, end):
        self.nc, self.top, self.end = nc, base, end
        self.n = 0
        self.peak = base

    def alloc(self, shape, dtype):
        sz = int(np.prod(shape[1:])) * (2 if dtype == BF16 else 4)
        off = (self.top + 31) // 32 * 32
        assert off + sz <= self.end, ("SBUF overflow", off + sz, self.end)
        self.top = off + sz
        self.peak = max(self.peak, self.top)
        self.n += 1
        return self.nc.alloc_sbuf_tensor_at("sb%d" % self.n, list(shape), dtype, offset=off)


def build_nc(debug=False, max_stage=99, sub=99):
    nc = bass.Bass("TRN2", target_bir_lowering=False)

    def din(name, shape):
        return nc.dram_tensor(name, list(shape), F32, kind="ExternalInput").ap()

    def dout(name, shape):
        return nc.dram_tensor(name, list(shape), F32, kind="ExternalOutput").ap()
    xp = din("xp", [SEQ, D])
    xs = din("xs", [NS, D])
    ck = din("ck", [NS, 128, 128])
    cv = din("cv", [NS, 128, 128])
    st_in = din("st", [NS, 4, 128, 128])
    relb = din("relb", [32, 8])
    w_in = din("w_in", [D, 4864])
    sinks = din("sinks", [8])
    w_ao = din("w_ao", [512, D])
    w_ro = din("w_ro", [512, D])
    w_o = din("w_o", [D, D])
    wup_d = [din("f1u", [D, 2 * DFF]), din("f2u", [D, 2 * DFF])]
    wdn_d = [din("f1d", [DFF, D]), din("f2d", [DFF, D])]
    lng = [din("ln%dg" % i, [D]) for i in (1, 2, 3)]
    lnb = [din("ln%db" % i, [D]) for i in (1, 2, 3)]
    consts_d = din("consts", [128, NCONST])
    yp = dout("yp", [SEQ, D])
    ys = dout("ys", [NS, D])
    kwp = dout("kwp", [128, 128])
    vwp = dout("vwp", [128, 128])
    sp_out = dout("sp", [4, 128, 128])
    kws = dout("kws", [NS, 128, 128])
    vws = dout("vws", [NS, 128, 128])
    ss_out = dout("ss", [NS, 4, 128, 128])
    scr = nc.dram_tensor("scr", [2, 8, 256], F32, kind="Internal").ap()
    dbg = {}
    if debug:
        dbg["h1"] = dout("dbg_h1", [9, 128, D])
        dbg["h2"] = dout("dbg_h2", [9, 128, D])

    es = ExitStack()
    with es:
        S = Sched()
        A = Arena(nc, SB_BASE, SB_END)
        ps = es.enter_context(nc.psum_tensor("ps", [128, 8, 512], F32))

        h_tok = A.alloc([128, 9, D], F32)
        hT = A.alloc([128, 8, NCOL], BF16)
        cst = A.alloc([128, NCONST], F32)
        BT = A.alloc([128, 2, 8, 128], F32)
        lg_t = A.alloc([128, D], F32)
        lb_t = A.alloc([128, D], F32)
        Sst = A.alloc([128, 4, 128], F32)
        Sb = A.alloc([128, 4, 128], BF16)
        bias_s = A.alloc([128, 8], F32)
        sinkexp = A.alloc([128, 8], F32)
        stt = [A.alloc([128, 4, 6], F32) for _ in range(2)]
        mvt = [A.alloc([128, 4, 8], F32) for _ in range(2)]
        kprev = A.alloc([128, 128], BF16)
        vprev = A.alloc([128, 2, 65], BF16)
        phase_base = A.top

        ident = cst[:, C_ID:C_ID + 128]
        Jm = cst[:, C_J:C_J + 128]
        caus = cst[:, C_CAUS:C_CAUS + 128]
        mhalf = cst[:, C_MH:C_MH + 1]

        def PSB(b, P=128, n=512):
            return ps[0:P, b, 0:n]

        def PS2(b, P=128):
            return ps[0:P, b:b + 2, :].rearrange("p a b -> p (a b)")

        S.op("sp", lambda e: e.dma_start(out=cst[:], in_=consts_d), w=["cst"], dma="cst")
        S.op("sp", lambda e: e.dma_start(out=sinkexp[:], in_=sinks.partition_broadcast(128)), w=["sinkexp"], dma="c2")
        S.op("act", lambda e: e.activation(out=sinkexp[:], in_=sinkexp[:], func=AF.Exp), r=["sinkexp"], w=["sinkexp"])
        S.op("dve", lambda e: e.memset(Sst[:], 0.0), w=["S"])
        S.op("dve", lambda e: e.memset(Sb[:], 0.0), w=["Sb"])
        A0 = A.top
        rb = A.alloc([32, 8], F32)
        TTs = A.alloc([8, 128], F32)
        Lsb = A.alloc([8, 2, 256], F32)
        G = A.alloc([128, 2, 8, 128], F32)
        S.op("sp", lambda e: e.dma_start(out=rb[:], in_=relb), w=["rb"], dma="c3")
        S.op("pe", lambda e: e.matmul(ps[0:8, 0, 0:128], lhsT=rb[:], rhs=cst[0:32, C_OH1:C_OH1 + 128], start=True, stop=True),
             r=["rb", "cst"], w=[("ps", 0)])
        S.op("act", lambda e: e.copy(out=TTs[:], in_=ps[0:8, 0, 0:128]), r=[("ps", 0)], w=["TTs"])
        S.op("dve", lambda e: e.memset(Lsb[:], MASKV), w=["Lsb"])
        S.op("dve", lambda e: e.tensor_copy(out=Lsb[:, 0, 0:127], in_=TTs[:, 1:128]), r=["TTs", "Lsb"], w=["Lsb"])
        S.op("dve", lambda e: e.tensor_copy(out=Lsb[:, 1, 127:255], in_=TTs[:, 0:128]), r=["TTs", "Lsb"], w=["Lsb"])
        S.op("sp", lambda e: e.dma_start(out=scr.rearrange("t h u -> h t u"), in_=Lsb[:]), r=["Lsb"], w=["scr"], dma="c4")
        for tab in range(2):
            hank = bass.AP(tensor=scr.tensor, offset=tab * 2048, ap=[[1, 128], [256, 8], [1, 128]])
            S.op("sp", lambda e, tab=tab, hank=hank: e.dma_start(out=G[:, tab, :, :], in_=hank), r=["scr"], w=["G"], dma="c5")
        for tab in range(2):
            for hf in range(2):
                b = 1 + tab * 2 + hf
                S.op("pe", lambda e, tab=tab, hf=hf, b=b: e.matmul(
                    ps[:, b, :], lhsT=Jm, rhs=G[:, tab, hf * 4:(hf + 1) * 4, :].rearrange("p a b -> p (a b)"),
                    start=True, stop=True), r=["cst", "G"], w=[("ps", b)])
                S.op("act", lambda e, tab=tab, hf=hf, b=b: e.copy(
                    out=BT[:, tab, hf * 4:(hf + 1) * 4, :].rearrange("p a b -> p (a b)"), in_=ps[:, b, :]),
                    r=[("ps", b)], w=["BT"])
        S.op("pe", lambda e: e.matmul(ps[:, 5, 0:8], lhsT=cst[0:32, C_OH2:C_OH2 + 128], rhs=rb[:], start=True, stop=True),
             r=["rb", "cst"], w=[("ps", 5)])
        S.op("act", lambda e: e.copy(out=bias_s[:], in_=ps[:, 5, 0:8]), r=[("ps", 5)], w=["bias_s"])
        A.top = A0

        def tile_geom(t):
            return (128, t * 128) if t < 8 else (NS, TB)

        def hT_tok(c0, n):
            if c0 >= TB:
                return [("hT", 8)]
            return [("hT", t) for t in range(c0 // 128, (c0 + n + 127) // 128)]

        def transpose_to_hT(t, pbank):
            P, c0 = tile_geom(t)
            S.pe_sync = (t == 8)
            for kc in range(8):
                S.op("pe", lambda e, kc=kc: e.transpose(out=ps[:, pbank + kc // 4, (kc % 4) * 128:(kc % 4) * 128 + P],
                                                        in_=h_tok[0:P, t, kc * 128:(kc + 1) * 128], identity=ident[0:P, 0:P]),
                     r=[("h", t), "cst"], w=[("ps", pbank + kc // 4)])
            S.op("act", lambda e: e.copy(out=hT[:, :, c0:c0 + P],
                                         in_=ps[:, pbank:pbank + 2, :].rearrange("p a (b c) -> p (a b) c", c=128)[:, :, 0:P]),
                 r=[("ps", pbank), ("ps", pbank + 1)], w=[("hT", t)])
            S.pe_sync = False

        def load_ln(i):
            S.op("sp", lambda e: e.dma_start(out=lg_t[:], in_=lng[i].partition_broadcast(128)), w=["lng"], dma="lng")
            S.op("sp", lambda e: e.dma_start(out=lb_t[:], in_=lnb[i].partition_broadcast(128)), w=["lnb"], dma="lnb")

        def ln_elem(t, src, src_tok, cscale, eps_eff):
            P, c0 = tile_geom(t)
            h = h_tok[0:P, t, :]
            st, mv = stt[t % 2], mvt[t % 2]
            tk = ("lnt", t % 2)
            S.op("dve", lambda e: e.scalar_tensor_tensor(out=h, in0=src, scalar=cscale, in1=h, op0=ALU.mult, op1=ALU.add),
                 r=list(src_tok) + [("h", t)], w=[("h", t)])
            for a in range(2):
                S.op("dve", lambda e, a=a: e.bn_stats(out=st[0:P, a, :], in_=h_tok[0:P, t, a * 512:(a + 1) * 512]),
                     r=[("h", t)], w=[tk])
            S.op("dve", lambda e: e.bn_aggr(out=mv[0:P, 0, 0:2], in_=st[0:P, 0:2, :].rearrange("p a b -> p (a b)")),
                 r=[tk], w=[tk])
            S.op("dve", lambda e: e.tensor_scalar(out=mv[0:P, 0, 2:3], in0=mv[0:P, 0, 1:2], scalar1=eps_eff, scalar2=None,
                                                  op0=ALU.add), r=[tk], w=[tk])
            S.op("pool", lambda e: e.tensor_tensor(out=mv[0:P, 0, 3:4], in0=mv[0:P, 0, 2:3], in1=mhalf[0:P], op=ALU.pow),
                 r=[tk, "cst"], w=[tk])
            S.op("dve", lambda e: e.scalar_tensor_tensor(out=mv[0:P, 0, 4:5], in0=mv[0:P, 0, 0:1], scalar=-1.0,
                                                         in1=mv[0:P, 0, 3:4], op0=ALU.mult, op1=ALU.mult), r=[tk], w=[tk])
            S.op("act", lambda e: e.activation(out=h, in_=h, func=AF.Identity, scale=mv[0:P, 0, 3:4], bias=mv[0:P, 0, 4:5]),
                 r=[tk, ("h", t)], w=[("h", t)])
            S.op("dve", lambda e: e.tensor_tensor(out=h, in0=h, in1=lg_t[0:P], op=ALU.mult), r=[("h", t), "lng"], w=[("h", t)])
            S.op("dve", lambda e: e.tensor_tensor(out=h, in0=h, in1=lb_t[0:P], op=ALU.add), r=[("h", t), "lnb"], w=[("h", t)])

        w1t_holder = {}

        def ffn_phase(blk, which, ln_i, final):
            tiles = list(range(8)) + ([8] if blk == 0 else [])
            ntiles = [(0, 512), (512, 512)] + ([(TB, NS)] if blk == 0 else [])
            S.barrier()
            A.top = phase_base
            actT = A.alloc([128, NJ, NCOL], BF16)
            wdn = A.alloc([128, NJ, D], BF16)
            if "t" not in w1t_holder:
                off = (A.top + 31) // 32 * 32
                assert off + 8 * 1792 * 2 <= SB_END
                w1t_holder["t"] = nc.alloc_sbuf_tensor_at("w1top", [128, 8, 1792], BF16, offset=off)
                w1t_holder["off"] = off
            wup = [A.alloc([128, 8, 2, 128], BF16) for _ in range(3)]
            sg = [A.alloc([128, 512], F32) for _ in range(2)]
            wu_v = wup_d[which].rearrange("(kc p) (two j c) -> p kc two j c", p=128, two=2, c=128)
            wd_v = wdn_d[which].rearrange("(j p) c -> p j c", p=128)

            def load_wup(j):
                for half in range(2):
                    S.op("pool", lambda e, half=half: e.dma_start(out=wup[j % 3][:, :, half, :], in_=wu_v[:, :, half, j, :]),
                         w=[("wup", j % 3)], dma=("wup", j % 3))
            load_ln(ln_i)
            load_wup(0)
            load_wup(1)
            cnt = 0
            for j in range(NJ):
                if j + 2 < NJ:
                    load_wup(j + 2)
                S.op("pool", lambda e, j=j: e.dma_start(out=wdn[:, j, :], in_=wd_v[:, j, :]), w=["wdn"], dma="wdn")
                for ni, (c0, n) in enumerate(ntiles):
                    S.pe_sync = (c0 >= TB)
                    bg, bu = cnt % 2, 2 + cnt % 2
                    sgt = sg[cnt % 2]
                    sgk = ("sg", cnt % 2)
                    cnt += 1
                    for half, bank in ((0, bg), (1, bu)):
                        for kc in range(8):
                            S.op("pe", lambda e, kc=kc, half=half, bank=bank, j=j, c0=c0, n=n: e.matmul(
                                ps[:, bank, 0:n], lhsT=wup[j % 3][:, kc, half, :], rhs=hT[:, kc, c0:c0 + n],
                                start=(kc == 0), stop=(kc == 7)),
                                r=[("wup", j % 3)] + hT_tok(c0, n), w=[("ps", bank)])
                    S.op("act", lambda e, bg=bg, n=n, sgt=sgt: e.activation(out=sgt[:, 0:n], in_=ps[:, bg, 0:n], func=AF.Silu),
                         r=[("ps", bg)], w=[sgk])
                    S.op("dve", lambda e, bu=bu, n=n, sgt=sgt, j=j, c0=c0: e.tensor_tensor(
                        out=actT[:, j, c0:c0 + n], in0=sgt[:, 0:n], in1=ps[:, bu, 0:n], op=ALU.mult),
                        r=[sgk, ("ps", bu)], w=[("actT", j, ni)])
            if which == 0:
                w1t = w1t_holder["t"]
                alias = [("wup", 0), ("wup", 1), ("wup", 2), ("sg", 0), ("sg", 1)]
                S.op("pool", lambda e: e.dma_start(out=w1t[:, :, 0:256],
                                                   in_=w_in[:, 512:768].rearrange("(kc p) c -> p kc c", p=128)),
                     w=alias + ["w1a"], dma="w1a")
                S.op("pool", lambda e: e.dma_start(out=w1t[:, :, 256:1792],
                                                   in_=w_in[:, 768:2304].rearrange("(kc p) c -> p kc c", p=128)),
                     w=alias + ["w1r"], dma="w1r")
            S.pe_sync = False
            pend = None
            for ti, t in enumerate(tiles):
                P, c0 = tile_geom(t)
                S.pe_sync = (t == 8)
                pb = (ti % 2) * 2
                ni = 2 if t == 8 else t // 4
                for dh in range(2):
                    for j in range(NJ):
                        S.op("pe", lambda e, j=j, dh=dh, pb=pb, P=P, c0=c0: e.matmul(
                            ps[0:P, pb + dh, :], lhsT=actT[:, j, c0:c0 + P], rhs=wdn[:, j, dh * 512:(dh + 1) * 512],
                            start=(j == 0), stop=(j == NJ - 1)),
                            r=[("actT", j, ni), "wdn"], w=[("ps", pb + dh)])
                S.pe_sync = False
                ln_elem(t, PS2(pb, P), [("ps", pb), ("ps", pb + 1)], 0.5 / ALPHA, LN_EPS / (ALPHA * ALPHA))
                if final:
                    dst = yp[(blk * 8 + t) * 128:(blk * 8 + t + 1) * 128, :] if t < 8 else ys
                    S.op("sp", lambda e, t=t, P=P, dst=dst: e.dma_start(out=dst, in_=h_tok[0:P, t, :]),
                         r=[("h", t)], dma=("yout", t))
                else:
                    if debug and blk == 0:
                        S.op("sp", lambda e, t=t: e.dma_start(out=dbg["h1"][t], in_=h_tok[:, t, :]), r=[("h", t)],
                             dma=("dbg", t))
                    if pend is not None:
                        transpose_to_hT(pend[0], pend[1])
                    pend = (t, 4 + (ti % 2) * 2)
            if pend is not None and not final:
                transpose_to_hT(pend[0], pend[1])

        def mixer_phase(blk):
            tiles = list(range(8)) + ([8] if blk == 0 else [])
            ntiles = [(0, 512), (512, 512)] + ([(TB, NS)] if blk == 0 else [])
            S.barrier()
            A.top = phase_base
            oaT = A.alloc([128, 4, NCOL], BF16)
            rgn = A.alloc([128, 9, 512], F32)
            b12_base = A.top
            w1q = A.alloc([128, 8, 512], BF16)
            w1t = w1t_holder["t"]
            qaT = A.alloc([128, 4, NCOL], BF16)
            kaT = A.alloc([128, 128 + NCOL], BF16)
            vaug = A.alloc([128, 10, 2, 65], BF16)
            qs = A.alloc([128, 4, 128], F32)
            tA = A.alloc([128, 4, 128], F32)
            tB = A.alloc([128, 4, 128], F32)
            qh = A.alloc([128, 4, 128], F32)
            kh = A.alloc([128, 4, 128], F32)
            khb = A.alloc([128, 4, 128], BF16)
            vrb = A.alloc([128, 4, 128], BF16)
            qhT = A.alloc([128, 4, 128], BF16)
            khT = A.alloc([128, 4, 128], BF16)
            et = [A.alloc([128, 512], F32) for _ in range(2)]
            pT = [A.alloc([128, 2, 4, 128], BF16) for _ in range(2)]
            scm = A.alloc([128, 4, 128], BF16)
            o_n = A.alloc([128, 8, 64], F32)
            den = A.alloc([128, 8], F32)
            kvout = A.alloc([128, 256], F32)
            qhT32 = A.alloc([128, 4, NS], F32)
            vr32 = A.alloc([128, 4, 128], F32)
            assert A.top <= w1t_holder["off"], (A.top, w1t_holder["off"])

            for kc in range(8):
                for hh in range(4):
                    S.op("pool", lambda e, kc=kc, hh=hh: e.dma_start(
                        out=w1q[:, kc, hh * 128:(hh + 1) * 128].rearrange("p (g d) -> p g d", g=2),
                        in_=w_in[kc * 128:(kc + 1) * 128, 0:512].rearrange("p (g hh d) -> p hh g d", g=2, hh=4)[:, hh, :, :]),
                        w=["w1q"], dma="w1q")
            S.op("dve", lambda e: e.memset(vaug[:, :, :, 64:65], 1.0), w=["vaug_ones"])
            if blk == 0:
                S.op("sp", lambda e: e.dma_start(out=kws[:, 0:127, :], in_=ck[:, 1:128, :]), w=["kws_a"], dma="kws_a")
                S.op("sp", lambda e: e.dma_start(out=vws[:, 0:127, :], in_=cv[:, 1:128, :]), w=["vws_a"], dma="vws_a")
            else:
                S.op("dve", lambda e: e.tensor_copy(out=kaT[:, 0:128], in_=kprev[:]), r=["kprev"], w=["kaT_prev"])
                S.op("dve", lambda e: e.tensor_copy(out=vaug[:, 0, :, :], in_=vprev[:]), r=["vprev", "vaug_ones"],
                     w=[("vaug", 0)])

            def b0_all():
              cnt = 0
              for c in range(5):
                    for ni, (c0, n) in enumerate(ntiles):
                        S.pe_sync = (c0 >= TB)
                        bank = 6 + cnt % 2
                        for kc in range(8):
                            S.op("pe", lambda e, c=c, kc=kc, c0=c0, n=n, bank=bank: e.matmul(
                                ps[:, bank, 0:n], lhsT=(w1q[:, kc, c * 128:(c + 1) * 128] if c < 4 else w1t[:, kc, 0:128]),
                                rhs=hT[:, kc, c0:c0 + n],
                                start=(kc == 0), stop=(kc == 7)), r=["w1q" if c < 4 else "w1a"] + hT_tok(c0, n), w=[("ps", bank)])
                        if c < 4:
                            dst, wt = qaT[:, c, c0:c0 + n], [("qaT", c, ni)]
                        else:
                            dst, wt = kaT[:, 128 + c0:128 + c0 + n], [("kaT", ni), "kaT_all"]
                        eng = "act" if cnt % 2 == 0 else "dve"
                        if eng == "act":
                            S.op("act", lambda e, dst=dst, bank=bank, n=n: e.copy(out=dst, in_=ps[:, bank, 0:n]),
                                 r=[("ps", bank)], w=wt)
                        else:
                            S.op("dve", lambda e, dst=dst, bank=bank, n=n: e.tensor_copy(out=dst, in_=ps[:, bank, 0:n]),
                                 r=[("ps", bank)], w=wt)
                        cnt += 1
            S.pe_sync = False
            qa_tok = lambda ni: [("qaT", c, ni) for c in range(4)]

            def gn_store(t, P, heads):
                st, mv = stt[t % 2], mvt[t % 2]
                tk = ("lnt", t % 2)
                for h, (v, vt) in enumerate(heads):
                    S.op("dve", lambda e, h=h, v=v: e.bn_stats(out=st[0:P, h, :], in_=v), r=vt, w=[tk])
                for h in range(4):
                    S.op("dve", lambda e, h=h: e.bn_aggr(out=mv[0:P, h, 0:2], in_=st[0:P, h, :]), r=[tk], w=[tk])
                S.op("dve", lambda e: e.tensor_scalar(out=mv[0:P, :, 2:3], in0=mv[0:P, :, 1:2], scalar1=GN_EPS, scalar2=None,
                                                      op0=ALU.add), r=[tk], w=[tk])
                S.op("pool", lambda e: e.tensor_tensor(out=mv[0:P, :, 3:4], in0=mv[0:P, :, 2:3],
                                                       in1=mhalf[0:P].unsqueeze(1).to_broadcast([P, 4, 1]), op=ALU.pow),
                     r=[tk, "cst"], w=[tk])
                for h, (v, vt) in enumerate(heads):
                    S.op("dve", lambda e, h=h, v=v: e.tensor_scalar(
                        out=rgn[0:P, t, h * 128:(h + 1) * 128], in0=v, scalar1=mv[0:P, h, 0:1], scalar2=mv[0:P, h, 3:4],
                        op0=ALU.subtract, op1=ALU.mult), r=vt + [tk], w=[("rgn", t)])

            def attn_finish(t, P, c0):
                for g, bank in ((0, 0), (1, 3)):
                    ov = ps[0:P, bank, 0:260].rearrange("p (h d) -> p h d", d=65)
                    S.op("dve", lambda e, g=g, ov=ov: e.tensor_tensor(
                        out=den[0:P, g * 4:(g + 1) * 4], in0=ov[:, :, 64], in1=sinkexp[0:P, g * 4:(g + 1) * 4], op=ALU.add),
                        r=[("ps", bank), "sinkexp"], w=[("den", g)])
                    S.op("dve", lambda e, g=g: e.reciprocal(out=den[0:P, g * 4:(g + 1) * 4], in_=den[0:P, g * 4:(g + 1) * 4]),
                         r=[("den", g)], w=[("den", g)])
                    S.op("dve", lambda e, g=g, ov=ov: e.tensor_tensor(
                        out=o_n[0:P, g * 4:(g + 1) * 4, :], in0=ov[:, :, 0:64],
                        in1=den[0:P, g * 4:(g + 1) * 4].unsqueeze(2).to_broadcast([P, 4, 64]), op=ALU.mult),
                        r=[("ps", bank), ("den", g)], w=[("o_n", g)])
                for c in range(4):
                    S.op("pe", lambda e, c=c: e.transpose(out=ps[:, 4, c * 128:c * 128 + P],
                                                          in_=o_n[0:P, 2 * c:2 * c + 2, :].rearrange("p a b -> p (a b)"),
                                                          identity=ident[0:P, 0:P]),
                         r=[("o_n", c // 2), "cst"], w=[("ps", 4)])
                S.op("act", lambda e: e.copy(out=oaT[:, :, c0:c0 + P],
                                             in_=ps[:, 4, :].rearrange("p (a b) -> p a b", b=128)[:, :, 0:P]),
                     r=[("ps", 4)], w=[("oaT", t)])

            def b1_tile(t, part):
                P, c0 = tile_geom(t)
                gt = blk * 8 + t if t < 8 else 16
                smp = (t == 8)
                srow = 4 if smp else 0
                if part == "A":
                    for bank, (co, n) in enumerate(((512, 256), (768, 512), (1280, 512), (1792, 512))):
                        for kc in range(8):
                            S.op("pe", lambda e, kc=kc, bank=bank, co=co, n=n: e.matmul(
                                ps[0:P, bank, 0:n], lhsT=hT[:, kc, c0:c0 + P], rhs=w1t[:, kc, co - 512:co - 512 + n],
                                start=(kc == 0), stop=(kc == 7)), r=["w1a" if bank == 0 else "w1r", ("hT", t)], w=[("ps", bank)])
                    if smp and sub < 4.52:
                        return
                    slot = 1 + t
                    if not (smp and (KX & 1)):
                        S.op("act", lambda e, slot=slot: e.copy(out=vaug[0:P, slot, :, 0:64],
                                                                in_=ps[0:P, 0, 128:256].rearrange("p (g d) -> p g d", g=2)),
                             r=[("ps", 0)], w=[("vaug", slot)])
                    if (smp and not (KX & 2)) or (blk == 1 and t == 7):
                        S.op("act", lambda e: e.copy(out=kvout[0:P, :], in_=ps[0:P, 0, 0:256]), r=[("ps", 0)], w=["kvout"])
                        if smp and not (KX & 4):
                            S.op("sp", lambda e: e.dma_start(out=kws[:, 127, :], in_=kvout[0:NS, 0:128]), r=["kvout"],
                                 w=["kws_b"], dma="kws_b")
                            S.op("sp", lambda e: e.dma_start(out=vws[:, 127, :], in_=kvout[0:NS, 128:256]), r=["kvout"],
                                 w=["vws_b"], dma="vws_b")
                        elif not smp:
                            S.op("sp", lambda e: e.dma_start(out=kwp, in_=kvout[:, 0:128]), r=["kvout"], dma="kwp")
                            S.op("sp", lambda e: e.dma_start(out=vwp, in_=kvout[:, 128:256]), r=["kvout"], dma="vwp")
                    if not (smp and (KX & 8)):
                        S.op("act", lambda e: e.copy(out=vrb[0:P].rearrange("p a b -> p (a b)"), in_=ps[0:P, 3, :]),
                             r=[("ps", 3)], w=["vrb"])
                    if smp and not (KX & 16):
                        S.op("act", lambda e: e.copy(out=vr32[0:NS].rearrange("p a b -> p (a b)"), in_=ps[0:NS, 3, :]),
                             r=[("ps", 3)], w=["vr32"])
                    if smp and sub < 4.53:
                        return
                    cosv = cst[0:P, C_COS + gt * 64:C_COS + (gt + 1) * 64].unsqueeze(1).unsqueeze(1).to_broadcast([P, 4, 2, 64])
                    sinv = cst[0:P, C_SIN + gt * 64:C_SIN + (gt + 1) * 64].unsqueeze(1).to_broadcast([P, 4, 64])
                    nsinv = cst[0:P, C_NSIN + gt * 64:C_NSIN + (gt + 1) * 64].unsqueeze(1).to_broadcast([P, 4, 64])
                    for which, bank, dcol, dsth in (("q", 1, C_DQ, qh), ("k", 2, C_DK, kh)):
                        dv = cst[0:P, dcol + srow:dcol + srow + 4].unsqueeze(2).to_broadcast([P, 4, 128])
                        S.op("dve", lambda e, bank=bank, dv=dv: e.tensor_tensor(
                            out=qs[0:P], in0=ps[0:P, bank, :].rearrange("p (a b) -> p a b", b=128), in1=dv, op=ALU.mult),
                            r=[("ps", bank), "cst"], w=["qs"])
                        S.op("pool", lambda e: e.tensor_tensor(
                            out=tA[0:P].rearrange("p h (two d) -> p h two d", two=2),
                            in0=qs[0:P].rearrange("p h (two d) -> p h two d", two=2), in1=cosv, op=ALU.mult),
                            r=["qs", "cst"], w=["tA"])
                        S.op("pool", lambda e: e.tensor_tensor(out=tB[0:P, :, 0:64], in0=qs[0:P, :, 64:128], in1=nsinv, op=ALU.mult),
                             r=["qs", "cst"], w=["tB0"])
                        S.op("pool", lambda e: e.tensor_tensor(out=tB[0:P, :, 64:128], in0=qs[0:P, :, 0:64], in1=sinv, op=ALU.mult),
                             r=["qs", "cst"], w=["tB1"])
                        S.op("pool", lambda e, dsth=dsth: e.tensor_tensor(out=dsth[0:P], in0=tA[0:P], in1=tB[0:P], op=ALU.add),
                             r=["tA", "tB0", "tB1"], w=[which + "h"])
                    if smp and sub < 4.54:
                        return
                    S.op("act", lambda e: e.copy(out=khb[0:P], in_=kh[0:P]), r=["kh"], w=["khb"])
                    for which, src, bank, dstT in (("q", qh, 4, qhT), ("k", kh, 5, khT)):
                        for h in range(4):
                            S.op("pe", lambda e, h=h, src=src, bank=bank: e.transpose(
                                out=ps[:, bank, h * 128:h * 128 + P], in_=src[0:P, h, :], identity=ident[0:P, 0:P]),
                                r=[which + "h", "cst"], w=[("ps", bank)])
                        S.op("act", lambda e, bank=bank, dstT=dstT: e.copy(
                            out=dstT[:, :, 0:P], in_=ps[:, bank, :].rearrange("p (a b) -> p a b", b=128)[:, :, 0:P]),
                            r=[("ps", bank)], w=[which + "hT"])
                        if smp and which == "q":
                            S.op("act", lambda e, bank=bank: e.copy(
                                out=qhT32[:], in_=ps[:, bank, :].rearrange("p (a b) -> p a b", b=128)[:, :, 0:NS]),
                                r=[("ps", bank)], w=["qhT32"])

                    return
                if sub < 2:
                    return
                if not smp:
                    first = (blk == 0 and t == 0)
                    kbs = [1] if first else [0, 1]
                    sc_cnt = 0
                    for g in range(2):
                        for kb in kbs:
                            kcol = (t + kb) * 128
                            sbank = 6 + sc_cnt % 2
                            e_t = et[sc_cnt % 2]
                            ek = ("et", sc_cnt % 2)
                            sc_cnt += 1
                            ktok = ["kaT_prev"] if (kb == 0 and t == 0) else [("kaT", (t + kb - 1) // 4)]
                            S.op("pe", lambda e, g=g, kcol=kcol, sbank=sbank: e.matmul(
                                ps[:, sbank, :], lhsT=kaT[g * 64:(g + 1) * 64, kcol:kcol + 128],
                                rhs=qaT[g * 64:(g + 1) * 64, :, c0:c0 + 128], start=True, stop=True),
                                r=ktok + qa_tok(t // 4), w=[("ps", sbank)])
                            S.op("dve", lambda e, g=g, kb=kb, sbank=sbank, e_t=e_t: e.scalar_tensor_tensor(
                                out=e_t[:], in0=ps[:, sbank, :], scalar=0.125,
                                in1=BT[:, kb, g * 4:(g + 1) * 4, :].rearrange("p a b -> p (a b)"), op0=ALU.mult, op1=ALU.add),
                                r=[("ps", sbank), "BT"], w=[ek])
                            S.op("act", lambda e, g=g, kb=kb, e_t=e_t: e.activation(
                                out=pT[g][:, kb, :, :].rearrange("p a b -> p (a b)"), in_=e_t[:], func=AF.Exp),
                                r=[ek], w=[("pT", g, kb)])
                        obank = 0 if g == 0 else 3
                        for hh in range(4):
                            for i, kb in enumerate(kbs):
                                vslot = t + kb
                                S.op("pe", lambda e, g=g, hh=hh, kb=kb, vslot=vslot, obank=obank, i=i: e.matmul(
                                    ps[:, obank, hh * 65:(hh + 1) * 65], lhsT=pT[g][:, kb, hh, :], rhs=vaug[:, vslot, g, :],
                                    start=(i == 0), stop=(i == len(kbs) - 1)),
                                    r=[("pT", g, kb), ("vaug", vslot), "vaug_ones"], w=[("ps", obank)])
                    attn_finish(t, P, c0)
                    if sub < 3:
                        return
                    for h in range(4):
                        S.op("pe", lambda e, h=h: e.matmul(ps[:, 1, h * 128:(h + 1) * 128], lhsT=khT[:, h, :], rhs=qhT[:, h, :],
                                                           start=True, stop=True), r=["khT", "qhT"], w=[("ps", 1)])
                    S.op("dve", lambda e: e.tensor_tensor(
                        out=scm[:], in0=ps[:, 1, :].rearrange("p (a b) -> p a b", b=128),
                        in1=caus.unsqueeze(1).to_broadcast([128, 4, 128]), op=ALU.mult), r=[("ps", 1), "cst"], w=["scm"])
                    for h in range(4):
                        S.op("pe", lambda e, h=h: e.matmul(ps[:, 2, h * 128:(h + 1) * 128], lhsT=scm[:, h, :], rhs=vrb[:, h, :],
                                                           start=True, stop=first), r=["scm", "vrb"], w=[("ps", 2)])
                        if not first:
                            S.op("pe", lambda e, h=h: e.matmul(ps[:, 2, h * 128:(h + 1) * 128], lhsT=qhT[:, h, :], rhs=Sb[:, h, :],
                                                               start=False, stop=True), r=["qhT", "Sb"], w=[("ps", 2)])
                    gn_store(t, P, [(ps[:, 2, h * 128:(h + 1) * 128], [("ps", 2)]) for h in range(4)])
                    for h in range(4):
                        S.op("pe", lambda e, h=h: e.matmul(ps[:, 5, h * 128:(h + 1) * 128], lhsT=khb[:, h, :], rhs=vrb[:, h, :],
                                                           start=True, stop=True), r=["khb", "vrb"], w=[("ps", 5)])
                    S.op("dve", lambda e: e.tensor_tensor(out=Sst[:].rearrange("p a b -> p (a b)"), in0=ps[:, 5, :],
                                                          in1=Sst[:].rearrange("p a b -> p (a b)"), op=ALU.add),
                         r=[("ps", 5), "S"], w=["S"])
                    S.op("dve", lambda e: e.tensor_tensor(out=Sst[:], in0=Sst[:],
                                                          in1=cst[:, C_GC:C_GC + 4].unsqueeze(2).to_broadcast([128, 4, 128]),
                                                          op=ALU.mult), r=["S", "cst"], w=["S"])
                    S.op("act", lambda e: e.copy(out=Sb[:], in_=Sst[:]), r=["S"], w=["Sb"])
                    if blk == 1 and t == 7:
                        S.op("sp", lambda e: e.dma_start(out=sp_out.rearrange("h k v -> k h v"), in_=Sst[:]), r=["S"], dma="spout")
                else:
                    if sub < 5:
                        return
                    A_save = A.top
                    Wk32 = A.alloc([128, NS, 128], F32)
                    WkT = A.alloc([128, NS, 128], BF16)
                    Wva = A.alloc([128, NS, 2, 65], BF16)
                    oTs = A.alloc([128, 8, NS], F32)
                    e_s = A.alloc([128, NS, 8], F32)
                    pTs = A.alloc([128, NS, 8], BF16)
                    Sp = [A.alloc([128, 4, 128], F32) for _ in range(2)]
                    Sn = [A.alloc([128, 4, 128], F32) for _ in range(2)]
                    Vm = [A.alloc([128, 4, 128], F32) for _ in range(2)]
                    QTm = A.alloc([128, 4, NS, NS], F32)
                    S.op("sp", lambda e: e.dma_start(out=Wk32[:], in_=kws.rearrange("b r c -> r b c")),
                         r=["kws_a", "kws_b"], w=["Wk32", "w1a", "w1r", "w1q"], dma="wk32")
                    for g in range(2):
                        S.op("pool", lambda e, g=g: e.dma_start(out=Wva[:, :, g, 0:64],
                                                                in_=vws[:, :, g * 64:(g + 1) * 64].rearrange("b r d -> r b d")),
                             r=["vws_a", "vws_b"], w=["Wva", "w1a", "w1r", "w1q"], dma="wva")
                    S.op("dve", lambda e: e.memset(Wva[:, :, :, 64:65], 1.0), w=["Wva1", "w1a", "w1r", "w1q"])
                    for b in range(NS):
                        bank = 6 + (b // 4) % 2
                        S.op("pe", lambda e, b=b, bank=bank: e.transpose(out=ps[:, bank, (b % 4) * 128:(b % 4 + 1) * 128],
                                                                        in_=Wk32[:, b, :], identity=ident),
                             r=["Wk32", "cst"], w=[("ps", bank)])
                        if b % 4 == 3:
                            S.op("act", lambda e, b=b, bank=bank: e.copy(
                                out=WkT[:, b - 3:b + 1, :], in_=ps[:, bank, :].rearrange("p (a b) -> p a b", b=128)),
                                r=[("ps", bank)], w=["WkT", "w1a", "w1r", "w1q"])
                    for b in range(NS):
                        for g in range(2):
                            S.op("pe", lambda e, b=b, g=g: e.matmul(
                                ps[:, 0, b * 8 + g * 4:b * 8 + g * 4 + 4], lhsT=WkT[g * 64:(g + 1) * 64, b, :],
                                rhs=qaT[g * 64:(g + 1) * 64, :, TB + b], start=True, stop=True),
                                r=["WkT"] + qa_tok(2), w=[("ps", 0)])
                    S.op("dve", lambda e: e.scalar_tensor_tensor(
                        out=e_s[:], in0=ps[:, 0, 0:128].rearrange("p (b h) -> p b h", h=8), scalar=0.125,
                        in1=bias_s[:].unsqueeze(1).to_broadcast([128, NS, 8]), op0=ALU.mult, op1=ALU.add),
                        r=[("ps", 0), "bias_s"], w=["e_s", "w1a", "w1r", "w1q"])
                    S.op("act", lambda e: e.activation(out=pTs[:], in_=e_s[:], func=AF.Exp), r=["e_s"], w=["pTs", "w1a", "w1r", "w1q"])
                    if sub < 5.2:
                        A.top = A_save
                        return
                    for b in range(NS):
                        for g in range(2):
                            S.op("pe", lambda e, b=b, g=g: e.matmul(
                                ps[0:65, 3, b * 8 + g * 4:b * 8 + g * 4 + 4], lhsT=Wva[:, b, g, :],
                                rhs=pTs[:, b, g * 4:(g + 1) * 4], start=True, stop=True),
                                r=["Wva", "Wva1", "pTs"], w=[("ps", 3)])
                    S.op("act", lambda e: e.copy(out=oTs[0:65], in_=ps[0:65, 3, 0:128].rearrange("p (b h) -> p h b", h=8)),
                         r=[("ps", 3)], w=["oTs", "w1a", "w1r", "w1q"])
                    for h in range(8):
                        bank = 0 if h < 4 else 3
                        S.op("pe", lambda e, h=h, bank=bank: e.transpose(
                            out=ps[0:NS, bank, (h % 4) * 65:(h % 4 + 1) * 65], in_=oTs[0:65, h, :], identity=ident[0:65, 0:65]),
                            r=["oTs", "cst"], w=[("ps", bank)])
                    attn_finish(t, P, c0)
                    if sub < 5.3:
                        A.top = A_save
                        return
                    S.op("dve", lambda e: e.tensor_tensor(
                        out=QTm[:], in0=qhT32[:].unsqueeze(3).to_broadcast([128, 4, NS, NS]),
                        in1=cst[:, C_EYER:C_EYER + 256].rearrange("p (a b) -> p a b", b=NS).unsqueeze(1).to_broadcast([128, 4, NS, NS]),
                        op=ALU.mult), r=["qhT32", "cst"], w=["QTm", "w1a", "w1r", "w1q"])
                    for b in range(NS):
                        i2 = b % 2
                        S.op("sp", lambda e, b=b, i2=i2: e.dma_start(out=Sp[i2][:], in_=st_in[b].rearrange("h k v -> k h v")),
                             w=[("Sp", i2), "w1a", "w1r", "w1q"], dma=("Sp", i2))
                        S.op("pool", lambda e, b=b, i2=i2: e.tensor_scalar(
                            out=Vm[i2][0:NS].rearrange("p a b -> p (a b)"), in0=vr32[0:NS].rearrange("p a b -> p (a b)"),
                            scalar1=ident[0:NS, b:b + 1], scalar2=None, op0=ALU.mult),
                            r=["vr32", "cst"], w=[("Vm", i2), "w1a", "w1r", "w1q"])
                        ub = 1 + i2
                        for h in range(4):
                            S.op("pe", lambda e, h=h, ub=ub, i2=i2: e.matmul(
                                ps[:, ub, h * 128:(h + 1) * 128], lhsT=kh[0:NS, h, :], rhs=Vm[i2][0:NS, h, :],
                                start=True, stop=True), r=["kh", ("Vm", i2)], w=[("ps", ub)])
                        for h in range(4):
                            S.op("dve", lambda e, h=h, ub=ub, i2=i2: e.scalar_tensor_tensor(
                                out=Sn[i2][:, h, :], in0=Sp[i2][:, h, :], scalar=float(GAMMAS[h]),
                                in1=ps[:, ub, h * 128:(h + 1) * 128], op0=ALU.mult, op1=ALU.add),
                                r=[("Sp", i2), ("ps", ub)], w=[("Sn", i2), "w1a", "w1r", "w1q"])
                        S.op("sp", lambda e, b=b, i2=i2: e.dma_start(out=ss_out[b].rearrange("h k v -> k h v"), in_=Sn[i2][:]),
                             r=[("Sn", i2)], dma=("ssout", i2))
                        for h in range(4):
                            if sub < 5.4:
                                break
                            S.op("pe", lambda e, h=h, b=b, i2=i2: e.matmul(
                                ps[0:NS, 4 + h, 0:128], lhsT=QTm[:, h, b, :], rhs=Sn[i2][:, h, :],
                                start=(b == 0), stop=(b == NS - 1)), r=["QTm", ("Sn", i2)], w=[("ps", 4 + h)])
                    if sub >= 5.4:
                        gn_store(t, P, [(ps[0:NS, 4 + h, 0:128], [("ps", 4 + h)]) for h in range(4)])
                    A.top = A_save

            if sub < 1:
                return
            for t in tiles:
                if t == 8 and (sub < 4.5 or (KX & 32)):
                    continue
                if t > 0 and sub < 4:
                    continue
                S.pe_sync = (t == 8)
                b1_tile(t, "A")
                S.pe_sync = False
                if t == 0:
                    b0_all()
                    S.pe_sync = False
                S.pe_sync = (t == 8)
                b1_tile(t, "B")
                S.pe_sync = False
            if sub < 6:
                return
            if blk == 0:
                S.op("dve", lambda e: e.tensor_copy(out=kprev[:], in_=kaT[:, TB:TB + 128]), r=[("kaT", 1)], w=["kprev"])
                S.op("dve", lambda e: e.tensor_copy(out=vprev[:], in_=vaug[:, 8, :, :]), r=[("vaug", 8), "vaug_ones"],
                     w=["vprev"])

            S.barrier()
            A.top = b12_base
            w2 = A.alloc([128, 8, 2560], BF16)
            wao = A.alloc([128, 4, D], BF16)
            wro = A.alloc([128, 4, D], BF16)
            wo = A.alloc([128, 8, D], BF16)
            sgr = A.alloc([128, 512], F32)
            rr = A.alloc([128, 512], F32)
            rT = A.alloc([128, 4, 128], BF16)
            sa = A.alloc([128, D], F32)
            m1 = A.alloc([128, D], F32)
            mT = A.alloc([128, 8, 128], BF16)
            def ld_w2(lo, hi, tok):
                S.op("pool", lambda e: e.dma_start(out=w2[:, :, lo:hi],
                                                   in_=w_in[:, 2304 + lo:2304 + hi].rearrange("(kc p) c -> p kc c", p=128)),
                     w=[tok], dma=tok)
            ld_w2(0, 512, "w2a")
            S.op("pool", lambda e: e.dma_start(out=wao[:], in_=w_ao.rearrange("(kc p) c -> p kc c", p=128)), w=["wao"], dma="wao")
            S.op("pool", lambda e: e.dma_start(out=wro[:], in_=w_ro.rearrange("(kc p) c -> p kc c", p=128)), w=["wro"], dma="wro")
            ld_w2(512, 1536, "w2b")
            ld_w2(1536, 2560, "w2c")
            S.op("pool", lambda e: e.dma_start(out=wo[:], in_=w_o.rearrange("(kc p) c -> p kc c", p=128)), w=["wo"], dma="wo")
            load_ln(1)
            def b2_tile(t):
                P, c0 = tile_geom(t)
                for kc in range(8):
                    S.op("pe", lambda e, kc=kc: e.matmul(ps[0:P, 0, :], lhsT=hT[:, kc, c0:c0 + P], rhs=w2[:, kc, 0:512],
                                                         start=(kc == 0), stop=(kc == 7)), r=["w2a", ("hT", t)], w=[("ps", 0)])
                S.op("act", lambda e: e.activation(out=sgr[0:P], in_=ps[0:P, 0, :], func=AF.Silu), r=[("ps", 0)], w=["sgr"])
                S.op("dve", lambda e: e.tensor_tensor(out=rr[0:P], in0=sgr[0:P], in1=rgn[0:P, t, :], op=ALU.mult),
                     r=["sgr", ("rgn", t)], w=["rr"])
                for c in range(4):
                    S.op("pe", lambda e, c=c: e.transpose(out=ps[:, 1, c * 128:c * 128 + P], in_=rr[0:P, c * 128:(c + 1) * 128],
                                                          identity=ident[0:P, 0:P]), r=["rr", "cst"], w=[("ps", 1)])
                S.op("act", lambda e: e.copy(out=rT[:, :, 0:P], in_=ps[:, 1, :].rearrange("p (a b) -> p a b", b=128)[:, :, 0:P]),
                     r=[("ps", 1)], w=["rT"])
                if t == 8 and (KX & 64):
                    return
                for dh in range(2):
                    for c in range(4):
                        S.op("pe", lambda e, c=c, dh=dh: e.matmul(ps[0:P, 2 + dh, :], lhsT=oaT[:, c, c0:c0 + P],
                                                                  rhs=wao[:, c, dh * 512:(dh + 1) * 512],
                                                                  start=(c == 0), stop=(c == 3)),
                             r=["wao", ("oaT", t)], w=[("ps", 2 + dh)])
                for dh in range(2):
                    for c in range(4):
                        S.op("pe", lambda e, c=c, dh=dh: e.matmul(ps[0:P, 4 + dh, :], lhsT=rT[:, c, 0:P],
                                                                  rhs=wro[:, c, dh * 512:(dh + 1) * 512],
                                                                  start=(c == 0), stop=(c == 3)),
                             r=["wro", "rT"], w=[("ps", 4 + dh)])
                if t == 8 and (KX & 128):
                    return
                for gi, goff in enumerate((512, 1536)):
                    for dh in range(2):
                        for kc in range(8):
                            S.op("pe", lambda e, kc=kc, dh=dh, goff=goff: e.matmul(
                                ps[0:P, 6 + dh, :], lhsT=hT[:, kc, c0:c0 + P], rhs=w2[:, kc, goff + dh * 512:goff + (dh + 1) * 512],
                                start=(kc == 0), stop=(kc == 7)), r=["w2b" if gi == 0 else "w2c", ("hT", t)], w=[("ps", 6 + dh)])
                    S.op("act", lambda e: e.activation(out=sa[0:P], in_=PS2(6, P), func=AF.Tanh, scale=0.5),
                         r=[("ps", 6), ("ps", 7)], w=["sa"])
                    if gi == 0:
                        S.op("dve", lambda e: e.scalar_tensor_tensor(out=m1[0:P], in0=sa[0:P], scalar=1.0, in1=PS2(2, P),
                                                                     op0=ALU.add, op1=ALU.mult),
                             r=["sa", ("ps", 2), ("ps", 3)], w=["m1"])
                    else:
                        S.op("dve", lambda e: e.scalar_tensor_tensor(out=sa[0:P], in0=sa[0:P], scalar=1.0, in1=PS2(4, P),
                                                                     op0=ALU.add, op1=ALU.mult),
                             r=["sa", ("ps", 4), ("ps", 5)], w=["sa"])
                        S.op("dve", lambda e: e.tensor_tensor(out=m1[0:P], in0=m1[0:P], in1=sa[0:P], op=ALU.add),
                             r=["sa", "m1"], w=["m1"])
                if t == 8 and (KX & 256):
                    return
                for kc in range(8):
                    S.op("pe", lambda e, kc=kc: e.transpose(out=ps[:, kc // 4, (kc % 4) * 128:(kc % 4) * 128 + P],
                                                            in_=m1[0:P, kc * 128:(kc + 1) * 128], identity=ident[0:P, 0:P]),
                         r=["m1", "cst"], w=[("ps", kc // 4)])
                S.op("act", lambda e: e.copy(out=mT[:, :, 0:P],
                                             in_=ps[:, 0:2, :].rearrange("p a (b c) -> p (a b) c", c=128)[:, :, 0:P]),
                     r=[("ps", 0), ("ps", 1)], w=["mT"])
                for dh in range(2):
                    for kc in range(8):
                        S.op("pe", lambda e, kc=kc, dh=dh: e.matmul(ps[0:P, 2 + dh, :], lhsT=mT[:, kc, 0:P],
                                                                    rhs=wo[:, kc, dh * 512:(dh + 1) * 512],
                                                                    start=(kc == 0), stop=(kc == 7)),
                             r=["wo", "mT"], w=[("ps", 2 + dh)])
                ln_elem(t, PS2(2, P), [("ps", 2), ("ps", 3)], 0.5 / ALPHA, LN_EPS / (ALPHA * ALPHA))
                if debug and blk == 0:
                    S.op("sp", lambda e, t=t: e.dma_start(out=dbg["h2"][t], in_=h_tok[:, t, :]), r=[("h", t)], dma=("dbg2", t))

            pend = None
            for t in tiles:
                if t == 8 and (KX & 32):
                    continue
                S.pe_sync = (t == 8)
                b2_tile(t)
                S.pe_sync = False
                if pend is not None:
                    transpose_to_hT(pend, 6)
                pend = t
            transpose_to_hT(pend, 6)

        stage = 0
        for blk in range(2):
            tiles = list(range(8)) + ([8] if blk == 0 else [])
            if stage >= max_stage:
                break
            stage += 1
            for t in tiles:
                P, c0 = tile_geom(t)
                src = xp[(blk * 8 + t) * 128:(blk * 8 + t + 1) * 128, :] if t < 8 else xs
                S.op("sp", lambda e, t=t, P=P, src=src: e.dma_start(out=h_tok[0:P, t, :], in_=src), w=[("h", t)], dma=("x", t))
            for ti, t in enumerate(tiles):
                transpose_to_hT(t, 4 + (ti % 2) * 2)
            if stage >= max_stage:
                break
            stage += 1
            ffn_phase(blk, 0, 0, final=False)
            if stage >= max_stage:
                break
            stage += 1
            mixer_phase(blk)
            if stage >= max_stage:
                break
            stage += 1
            ffn_phase(blk, 1, 2, final=True)
        S.barrier()
        S.op("sp", lambda e: e.nop(), r=[], w=[])
        S.emit(nc, es)
        build_nc.info = dict(nops=len(S.ops), nsems=S.nsems, sbuf_peak=A.peak)
    return nc


_CACHE = {}
KX = 0


def kernel(x_prompt, x_sample, cache_k_win, cache_v_win, state_ret, rel_bias, w_in, attn_sinks,
           w_attn_out, w_ret_out, w_o, ffn1_w_up, ffn1_w_down, ffn2_w_up, ffn2_w_down,
           ln1_g, ln1_b, ln2_g, ln2_b, ln3_g, ln3_b, _debug=False):
    f = lambda a: np.ascontiguousarray(np.asarray(a, dtype=np.float32))
    import os as _os
    _ms = int(_os.environ.get("K_STAGES", "99"))
    _sub = float(_os.environ.get("K_SUB", "99"))
    global KX
    KX = int(_os.environ.get("K_X", "0"))
    key = ("nc", bool(_debug), _ms, _sub, KX, _os.environ.get("K_PAD", "0"))
    if key not in _CACHE:
        _CACHE[key] = build_nc(debug=_debug, max_stage=_ms, sub=_sub)
    nc = _CACHE[key]
    consts = make_consts()
    shared = dict(relb=f(rel_bias), w_in=f(w_in)[0], sinks=f(attn_sinks)[0], w_ao=f(w_attn_out)[0], w_ro=f(w_ret_out)[0],
                  w_o=f(w_o)[0], f1u=f(ffn1_w_up)[0], f2u=f(ffn2_w_up)[0], f1d=f(ffn1_w_down)[0], f2d=f(ffn2_w_down)[0],
                  ln1g=f(ln1_g)[0], ln1b=f(ln1_b)[0], ln2g=f(ln2_g)[0], ln2b=f(ln2_b)[0], ln3g=f(ln3_g)[0], ln3b=f(ln3_b)[0],
                  consts=consts)
    xp, xs = f(x_prompt), f(x_sample)
    ckf, cvf, stf = f(cache_k_win), f(cache_v_win), f(state_ret)
    in_maps = []
    for c in range(NCORES):
        m = dict(shared)
        sl = slice(c * NS, (c + 1) * NS)
        m["xp"] = xp[c]
        m["xs"] = xs[sl, 0, :]
        m["ck"] = ckf[0, sl].reshape(NS, 128, 128)
        m["cv"] = cvf[0, sl].reshape(NS, 128, 128)
        m["st"] = stf[0, sl]
        in_maps.append(m)
    _nco = int(_os.environ.get("K_CORES", str(NCORES)))
    res = run_bass_kernel_spmd(nc, in_maps[:_nco], core_ids=list(range(_nco)))
    R = list(res.results)
    while len(R) < NCORES:
        R.append({k: np.zeros_like(v) for k, v in R[0].items()})
    y_p = np.stack([R[c]["yp"] for c in range(NCORES)], 0)
    y_s = np.concatenate([R[c]["ys"] for c in range(NCORES)], 0).reshape(128, 1, D)
    kwp = np.stack([R[c]["kwp"] for c in range(NCORES)], 0).reshape(1, 8, 128, 2, 64)
    vwp = np.stack([R[c]["vwp"] for c in range(NCORES)], 0).reshape(1, 8, 128, 2, 64)
    spo = np.stack([R[c]["sp"] for c in range(NCORES)], 0).reshape(1, 8, 4, 128, 128)
    kws = np.concatenate([R[c]["kws"] for c in range(NCORES)], 0).reshape(1, 128, 128, 2, 64)
    vws = np.concatenate([R[c]["vws"] for c in range(NCORES)], 0).reshape(1, 128, 128, 2, 64)
    sso = np.concatenate([R[c]["ss"] for c in range(NCORES)], 0).reshape(1, 128, 4, 128, 128)
    outs = tuple(np.ascontiguousarray(a.astype(np.float32)) for a in (y_p, y_s, kwp, vwp, spo, kws, vws, sso))
    if _debug:
        kernel.dbg = [{k: v for k, v in R[c].items() if k.startswith("dbg")} for c in range(NCORES)]
    return outs
```

```python
import numpy as np
from contextlib import ExitStack
import concourse.bass as bass
import concourse.mybir as mybir
from concourse.bass_utils import run_bass_kernel_spmd

F32 = mybir.dt.float32
BF16 = mybir.dt.bfloat16
AF = mybir.ActivationFunctionType
ALU = mybir.AluOpType

NCORES = 8
D = 1024
SEQ = 2048
DFF = 2816
NJ = DFF // 128
NS = 16
TB = 1024
NCOL = TB + NS
PAST = 16384
ALPHA = 2.0 ** 0.25
LN_EPS = 1e-5
GN_EPS = 1e-6
MASKV = -30000.0
SB_BASE = 16512
SB_END = 229376

C_ID = 0
C_J = 128
C_CAUS = 256
C_DQ = 384
C_DK = C_DQ + 8
C_GC = C_DK + 8
C_EYER = C_GC + 4
C_OH1 = C_EYER + 256
C_OH2 = C_OH1 + 128
C_MH = C_OH2 + 128
C_ONE = C_MH + 1
NC1 = ((C_ONE + 1 + 7) // 8) * 8
C_COS = NC1
C_SIN = C_COS + 17 * 64
C_NSIN = C_SIN + 17 * 64
NROT = 3 * 17 * 64
NCONST = NC1 + NROT
GAMMAS = [1.0 - 2.0 ** (-5.0 - h) for h in range(4)]


def _bucket(d):
    d = np.asarray(d)
    n = np.maximum(d, 0)
    ratio = np.maximum(n, 1).astype(np.float32) / np.float32(16)
    large = 16 + (np.log(np.maximum(ratio, np.float32(1.0))).astype(np.float32)
                  / np.float32(np.log(128 / 16)) * np.float32(16)).astype(np.int32)
    large = np.minimum(large, 31)
    return np.where(n < 16, n, large)


def make_consts():
    c = np.zeros((128, NCONST), np.float32)
    c[:, C_ID:C_ID + 128] = np.eye(128, dtype=np.float32)
    c[:, C_J:C_J + 128] = np.eye(128, dtype=np.float32)[::-1]
    jj = np.arange(128)
    c[:, C_CAUS:C_CAUS + 128] = (jj[None, :] >= jj[:, None]).astype(np.float32)
    inv = (np.float32(10000.0) ** (-(np.arange(64, dtype=np.float32) / np.float32(64)))).astype(np.float32)
    for t in range(17):
        pos = (t * 128 + np.arange(128)) if t < 16 else np.full(128, PAST)
        ang = (pos.astype(np.float32)[:, None] * inv[None, :]).astype(np.float32)
        c[:, C_COS + t * 64:C_COS + (t + 1) * 64] = np.cos(ang.astype(np.float64)).astype(np.float32)
        c[:, C_SIN + t * 64:C_SIN + (t + 1) * 64] = np.sin(ang.astype(np.float64)).astype(np.float32)
        c[:, C_NSIN + t * 64:C_NSIN + (t + 1) * 64] = -np.sin(ang.astype(np.float64)).astype(np.float32)
    p = np.arange(128, dtype=np.float64)
    for h in range(4):
        lg = np.log1p(-(2.0 ** (-5.0 - h)))
        c[:, C_DQ + h] = np.exp((p + 1) * lg)
        c[:, C_DK + h] = (128.0 ** -0.5) * np.exp(-(p + 1) * lg)
        c[:, C_DQ + 4 + h] = 1.0
        c[:, C_DK + 4 + h] = 128.0 ** -0.5
        c[:, C_GC + h] = np.exp(128 * lg)
    c[:, C_EYER:C_EYER + 256] = np.eye(16, dtype=np.float32).reshape(1, 256)
    b1 = _bucket(np.arange(128))
    b2 = _bucket(127 - np.arange(128))
    for r in range(128):
        c[b1[r], C_OH1 + r] = 1.0
        c[b2[r], C_OH2 + r] = 1.0
    c[:, C_MH] = -0.5
    c[:, C_ONE] = 1.0
    return c


def _rnd_tile(x):
    return 32 if x <= 32 else (64 if x <= 64 else 128)


class _FakePE:
    def __init__(self):
        self.mode = None

    def matmul(self, out, lhsT=None, rhs=None, **kw):
        self.mode = (_rnd_tile(lhsT.shape[0]), _rnd_tile(int(np.prod(lhsT.shape[1:]))))

    def transpose(self, out=None, in_=None, identity=None):
        self.mode = (_rnd_tile(in_.shape[0]), _rnd_tile(int(np.prod(in_.shape[1:]))))


class Sched:
    ENGS = ("pe", "act", "dve", "pool", "sp")

    def __init__(self):
        self.ops = []
        self.last_w = {}
        self.readers = {}
        self.dma_cnt = {}
        self.bar = None
        self.bar_passed = set()
        self.last_eng = {}
        self.last_dma = {}
        self.pe_sync = False
        self.pe_sync_once = False

    def _stream(self, idx):
        o = self.ops[idx]
        return ("dma", o["dma"]) if o["dma"] is not None else ("eng", o["eng"])

    def op(self, eng, fn, r=(), w=(), dma=None):
        idx = len(self.ops)
        deps = {}

        def add(d):
            s = self._stream(d)
            if deps.get(s, -1) < d:
                deps[s] = d
        for t in r:
            lw = self.last_w.get(t)
            if lw is not None:
                add(lw)
        for t in w:
            lw = self.last_w.get(t)
            if lw is not None:
                add(lw)
            for d in self.readers.get(t, {}).values():
                add(d)
        if self.bar is not None and eng not in self.bar_passed:
            for d in self.bar:
                add(d)
            self.bar_passed.add(eng)
        force = False
        if eng == "pe" and dma is None and (self.pe_sync or self.pe_sync_once):
            self.pe_sync_once = self.pe_sync
            if "pe" in self.last_eng:
                add(self.last_eng["pe"])
                force = True
        o = dict(eng=eng, fn=fn, deps=deps, dma=dma, need_inc=False, ev=None, force=force)
        if dma is not None:
            self.dma_cnt[dma] = self.dma_cnt.get(dma, 0) + 1
            o["ev"] = ("dma", dma, 16 * self.dma_cnt[dma])
            self.last_dma[dma] = idx
        else:
            self.last_eng[eng] = idx
        self.ops.append(o)
        me = ("dma", dma) if dma is not None else ("eng", eng)
        for t in r:
            self.readers.setdefault(t, {})[me] = idx
        for t in w:
            self.last_w[t] = idx
            self.readers[t] = {}
        return idx

    def barrier(self):
        self.bar = set(self.last_eng.values()) | set(self.last_dma.values())
        self.bar_passed = set()

    def finalize(self):
        ops = self.ops
        for o in ops:
            real = []
            for s, d in o["deps"].items():
                od = ops[d]
                if s == ("eng", "pe") and o["eng"] == "pe" and o["dma"] is None and not o["force"]:
                    continue
                real.append(d)
                if od["dma"] is None:
                    od["need_inc"] = True
            o["deps"] = real
        cnt = {e: 0 for e in self.ENGS}
        for o in ops:
            if o["dma"] is None and o["need_inc"]:
                cnt[o["eng"]] += 1
                o["ev"] = ("eng", o["eng"], cnt[o["eng"]])
        seen = {e: {} for e in self.ENGS}
        for o in ops:
            waits = {}
            for d in o["deps"]:
                kind, key, val = ops[d]["ev"]
                k = (kind, key)
                if seen[o["eng"]].get(k, 0) >= val:
                    continue
                waits[k] = max(waits.get(k, 0), val)
            for k, v in waits.items():
                seen[o["eng"]][k] = v
            o["waits"] = waits

    def emit(self, nc, es):
        self.finalize()
        sems = {}
        n = [0]

        def sem(k):
            if k not in sems:
                n[0] += 1
                sems[k] = es.enter_context(nc.semaphore("s%d" % n[0]))
            return sems[k]
        for o in self.ops:
            for k in o["waits"]:
                sem(k)
            if o["ev"] is not None:
                sem((o["ev"][0], o["ev"][1]))
        block = es.enter_context(nc.Block())
        ops = self.ops

        import os as _os2
        pad = int(_os2.environ.get("K_PAD", "0"))

        def run(engname, eng):
            for _ in range(pad):
                eng.nop()
            for o in ops:
                if o["eng"] != engname:
                    continue
                for k, v in o["waits"].items():
                    eng.wait_ge(sems[k], v)
                ins = o["fn"](eng)
                if o["dma"] is not None:
                    ins.then_inc(sems[("dma", o["dma"])], 16)
                elif o["need_inc"]:
                    ins.then_inc(sems[("eng", engname)], 1)

        @block.tensor
        def _(e):
            run("pe", e)

        @block.scalar
        def _(e):
            run("act", e)

        @block.vector
        def _(e):
            run("dve", e)

        @block.gpsimd
        def _(e):
            run("pool", e)

        @block.sync
        def _(e):
            run("sp", e)
        self.nsems = len(sems)


class Arena:
    def __init__(self, nc, base, end):
        self.nc, self.top, self.end = nc, base, end
        self.n = 0
        self.peak = base

    def alloc(self, shape, dtype):
        sz = int(np.prod(shape[1:])) * (2 if dtype == BF16 else 4)
        off = (self.top + 31) // 32 * 32
        assert off + sz <= self.end, ("SBUF overflow", off + sz, self.end)
        self.top = off + sz
        self.peak = max(self.peak, self.top)
        self.n += 1
        return self.nc.alloc_sbuf_tensor_at("sb%d" % self.n, list(shape), dtype, offset=off)


def build_nc(debug=False, max_stage=99, sub=99):
    nc = bass.Bass("TRN2", target_bir_lowering=False)

    def din(name, shape):
        return nc.dram_tensor(name, list(shape), F32, kind="ExternalInput").ap()

    def dout(name, shape):
        return nc.dram_tensor(name, list(shape), F32, kind="ExternalOutput").ap()
    xp = din("xp", [SEQ, D])
    xs = din("xs", [NS, D])
    ck = din("ck", [NS, 128, 128])
    cv = din("cv", [NS, 128, 128])
    st_in = din("st", [NS, 4, 128, 128])
    relb = din("relb", [32, 8])
    w_in = din("w_in", [D, 4864])
    sinks = din("sinks", [8])
    w_ao = din("w_ao", [512, D])
    w_ro = din("w_ro", [512, D])
    w_o = din("w_o", [D, D])
    wup_d = [din("f1u", [D, 2 * DFF]), din("f2u", [D, 2 * DFF])]
    wdn_d = [din("f1d", [DFF, D]), din("f2d", [DFF, D])]
    lng = [din("ln%dg" % i, [D]) for i in (1, 2, 3)]
    lnb = [din("ln%db" % i, [D]) for i in (1, 2, 3)]
    consts_d = din("consts", [128, NCONST])
    yp = dout("yp", [SEQ, D])
    ys = dout("ys", [NS, D])
    kwp = dout("kwp", [128, 128])
    vwp = dout("vwp", [128, 128])
    sp_out = dout("sp", [4, 128, 128])
    kws = dout("kws", [NS, 128, 128])
    vws = dout("vws", [NS, 128, 128])
    ss_out = dout("ss", [NS, 4, 128, 128])
    scr = nc.dram_tensor("scr", [2, 8, 256], F32, kind="Internal").ap()
    dbg = {}
    if debug:
        dbg["h1"] = dout("dbg_h1", [9, 128, D])
        dbg["h2"] = dout("dbg_h2", [9, 128, D])

    es = ExitStack()
    with es:
        S = Sched()
        A = Arena(nc, SB_BASE, SB_END)
        ps = es.enter_context(nc.psum_tensor("ps", [128, 8, 512], F32))

        h_tok = A.alloc([128, 9, D], F32)
        hT = A.alloc([128, 8, NCOL], BF16)
        cst = A.alloc([128, NC1], F32)
        w1q = A.alloc([128, 8, 512], BF16)
        BT = A.alloc([128, 2, 8, 128], F32)
        lg_t = A.alloc([128, D], F32)
        lb_t = A.alloc([128, D], F32)
        Sst = A.alloc([128, 4, 128], F32)
        Sb = A.alloc([128, 4, 128], BF16)
        bias_s = A.alloc([128, 8], F32)
        sinkexp = A.alloc([128, 8], F32)
        stt = [A.alloc([128, 4, 6], F32) for _ in range(2)]
        mvt = [A.alloc([128, 4, 8], F32) for _ in range(2)]
        kprev = A.alloc([128, 128], BF16)
        vprev = A.alloc([128, 2, 65], BF16)
        phase_base = A.top

        ident = cst[:, C_ID:C_ID + 128]
        Jm = cst[:, C_J:C_J + 128]
        caus = cst[:, C_CAUS:C_CAUS + 128]
        mhalf = cst[:, C_MH:C_MH + 1]

        def PSB(b, P=128, n=512):
            return ps[0:P, b, 0:n]

        def PS2(b, P=128):
            return ps[0:P, b:b + 2, :].rearrange("p a b -> p (a b)")

        S.op("sp", lambda e: e.dma_start(out=cst[:], in_=consts_d[:, 0:NC1]), w=["cst"], dma="cst")
        S.op("sp", lambda e: e.dma_start(out=sinkexp[:], in_=sinks.partition_broadcast(128)), w=["sinkexp"], dma="c2")
        S.op("act", lambda e: e.activation(out=sinkexp[:], in_=sinkexp[:], func=AF.Exp), r=["sinkexp"], w=["sinkexp"])
        S.op("dve", lambda e: e.memset(Sst[:], 0.0), w=["S"])
        S.op("dve", lambda e: e.memset(Sb[:], 0.0), w=["Sb"])
        A0 = A.top
        rb = A.alloc([32, 8], F32)
        TTs = A.alloc([8, 128], F32)
        Lsb = A.alloc([8, 2, 256], F32)
        G = A.alloc([128, 2, 8, 128], F32)
        S.op("sp", lambda e: e.dma_start(out=rb[:], in_=relb), w=["rb"], dma="c3")
        S.op("pe", lambda e: e.matmul(ps[0:8, 0, 0:128], lhsT=rb[:], rhs=cst[0:32, C_OH1:C_OH1 + 128], start=True, stop=True),
             r=["rb", "cst"], w=[("ps", 0)])
        S.op("act", lambda e: e.copy(out=TTs[:], in_=ps[0:8, 0, 0:128]), r=[("ps", 0)], w=["TTs"])
        S.op("dve", lambda e: e.memset(Lsb[:], MASKV), w=["Lsb"])
        S.op("dve", lambda e: e.tensor_copy(out=Lsb[:, 0, 0:127], in_=TTs[:, 1:128]), r=["TTs", "Lsb"], w=["Lsb"])
        S.op("dve", lambda e: e.tensor_copy(out=Lsb[:, 1, 127:255], in_=TTs[:, 0:128]), r=["TTs", "Lsb"], w=["Lsb"])
        S.op("sp", lambda e: e.dma_start(out=scr.rearrange("t h u -> h t u"), in_=Lsb[:]), r=["Lsb"], w=["scr"], dma="c4")
        for tab in range(2):
            hank = bass.AP(tensor=scr.tensor, offset=tab * 2048, ap=[[1, 128], [256, 8], [1, 128]])
            S.op("sp", lambda e, tab=tab, hank=hank: e.dma_start(out=G[:, tab, :, :], in_=hank), r=["scr"], w=["G"], dma="c5")
        for tab in range(2):
            for hf in range(2):
                b = 1 + tab * 2 + hf
                S.op("pe", lambda e, tab=tab, hf=hf, b=b: e.matmul(
                    ps[:, b, :], lhsT=Jm, rhs=G[:, tab, hf * 4:(hf + 1) * 4, :].rearrange("p a b -> p (a b)"),
                    start=True, stop=True), r=["cst", "G"], w=[("ps", b)])
                S.op("act", lambda e, tab=tab, hf=hf, b=b: e.copy(
                    out=BT[:, tab, hf * 4:(hf + 1) * 4, :].rearrange("p a b -> p (a b)"), in_=ps[:, b, :]),
                    r=[("ps", b)], w=["BT"])
        S.op("pe", lambda e: e.matmul(ps[:, 5, 0:8], lhsT=cst[0:32, C_OH2:C_OH2 + 128], rhs=rb[:], start=True, stop=True),
             r=["rb", "cst"], w=[("ps", 5)])
        S.op("act", lambda e: e.copy(out=bias_s[:], in_=ps[:, 5, 0:8]), r=[("ps", 5)], w=["bias_s"])
        A.top = A0

        def tile_geom(t):
            return (128, t * 128) if t < 8 else (NS, TB)

        def hT_tok(c0, n):
            if c0 >= TB:
                return [("hT", 8)]
            return [("hT", t) for t in range(c0 // 128, (c0 + n + 127) // 128)]

        def transpose_to_hT(t, pbank):
            P, c0 = tile_geom(t)
            S.pe_sync = (t == 8)
            for kc in range(8):
                S.op("pe", lambda e, kc=kc: e.transpose(out=ps[:, pbank + kc // 4, (kc % 4) * 128:(kc % 4) * 128 + P],
                                                        in_=h_tok[0:P, t, kc * 128:(kc + 1) * 128], identity=ident[0:P, 0:P]),
                     r=[("h", t), "cst"], w=[("ps", pbank + kc // 4)])
            S.op("act", lambda e: e.copy(out=hT[:, :, c0:c0 + P],
                                         in_=ps[:, pbank:pbank + 2, :].rearrange("p a (b c) -> p (a b) c", c=128)[:, :, 0:P]),
                 r=[("ps", pbank), ("ps", pbank + 1)], w=[("hT", t)])
            S.pe_sync = False

        def load_ln(i):
            S.op("sp", lambda e: e.dma_start(out=lg_t[:], in_=lng[i].partition_broadcast(128)), w=["lng"], dma="lng")
            S.op("sp", lambda e: e.dma_start(out=lb_t[:], in_=lnb[i].partition_broadcast(128)), w=["lnb"], dma="lnb")

        def ln_elem(t, src, src_tok, cscale, eps_eff):
            P, c0 = tile_geom(t)
            h = h_tok[0:P, t, :]
            st, mv = stt[t % 2], mvt[t % 2]
            tk = ("lnt", t % 2)
            S.op("dve", lambda e: e.scalar_tensor_tensor(out=h, in0=src, scalar=cscale, in1=h, op0=ALU.mult, op1=ALU.add),
                 r=list(src_tok) + [("h", t)], w=[("h", t)])
            for a in range(2):
                S.op("dve", lambda e, a=a: e.bn_stats(out=st[0:P, a, :], in_=h_tok[0:P, t, a * 512:(a + 1) * 512]),
                     r=[("h", t)], w=[tk])
            S.op("dve", lambda e: e.bn_aggr(out=mv[0:P, 0, 0:2], in_=st[0:P, 0:2, :].rearrange("p a b -> p (a b)")),
                 r=[tk], w=[tk])
            S.op("dve", lambda e: e.tensor_scalar(out=mv[0:P, 0, 2:3], in0=mv[0:P, 0, 1:2], scalar1=eps_eff, scalar2=None,
                                                  op0=ALU.add), r=[tk], w=[tk])
            S.op("pool", lambda e: e.tensor_tensor(out=mv[0:P, 0, 3:4], in0=mv[0:P, 0, 2:3], in1=mhalf[0:P], op=ALU.pow),
                 r=[tk, "cst"], w=[tk])
            S.op("dve", lambda e: e.scalar_tensor_tensor(out=mv[0:P, 0, 4:5], in0=mv[0:P, 0, 0:1], scalar=-1.0,
                                                         in1=mv[0:P, 0, 3:4], op0=ALU.mult, op1=ALU.mult), r=[tk], w=[tk])
            S.op("act", lambda e: e.activation(out=h, in_=h, func=AF.Identity, scale=mv[0:P, 0, 3:4], bias=mv[0:P, 0, 4:5]),
                 r=[tk, ("h", t)], w=[("h", t)])
            S.op("dve", lambda e: e.tensor_tensor(out=h, in0=h, in1=lg_t[0:P], op=ALU.mult), r=[("h", t), "lng"], w=[("h", t)])
            S.op("dve", lambda e: e.tensor_tensor(out=h, in0=h, in1=lb_t[0:P], op=ALU.add), r=[("h", t), "lnb"], w=[("h", t)])

        w1t_holder = {}

        def ffn_phase(blk, which, ln_i, final):
            tiles = list(range(8)) + ([8] if blk == 0 else [])
            ntiles = [(0, 512), (512, 512)] + ([(TB, NS)] if blk == 0 else [])
            S.barrier()
            A.top = phase_base
            actT = A.alloc([128, NJ, NCOL], BF16)
            wdn = A.alloc([128, NJ, D], BF16)
            if "t" not in w1t_holder:
                off = (A.top + 31) // 32 * 32
                assert off + 8 * 1792 * 2 <= SB_END
                w1t_holder["t"] = nc.alloc_sbuf_tensor_at("w1top", [128, 8, 1792], BF16, offset=off)
                w1t_holder["off"] = off
            wup = [A.alloc([128, 8, 2, 128], BF16) for _ in range(3)]
            sg = [A.alloc([128, 512], F32) for _ in range(2)]
            wu_v = wup_d[which].rearrange("(kc p) (two j c) -> p kc two j c", p=128, two=2, c=128)
            wd_v = wdn_d[which].rearrange("(j p) c -> p j c", p=128)

            def load_wup(j):
                for half in range(2):
                    S.op("pool", lambda e, half=half: e.dma_start(out=wup[j % 3][:, :, half, :], in_=wu_v[:, :, half, j, :]),
                         w=[("wup", j % 3)], dma=("wup", j % 3))
            load_ln(ln_i)
            load_wup(0)
            load_wup(1)
            cnt = 0
            for j in range(NJ):
                if j + 2 < NJ:
                    load_wup(j + 2)
                S.op("pool", lambda e, j=j: e.dma_start(out=wdn[:, j, :], in_=wd_v[:, j, :]), w=["wdn"], dma="wdn")
                for ni, (c0, n) in enumerate(ntiles):
                    S.pe_sync = (c0 >= TB)
                    bg, bu = cnt % 2, 2 + cnt % 2
                    sgt = sg[cnt % 2]
                    sgk = ("sg", cnt % 2)
                    cnt += 1
                    for half, bank in ((0, bg), (1, bu)):
                        for kc in range(8):
                            S.op("pe", lambda e, kc=kc, half=half, bank=bank, j=j, c0=c0, n=n: e.matmul(
                                ps[:, bank, 0:n], lhsT=wup[j % 3][:, kc, half, :], rhs=hT[:, kc, c0:c0 + n],
                                start=(kc == 0), stop=(kc == 7)),
                                r=[("wup", j % 3)] + hT_tok(c0, n), w=[("ps", bank)])
                    S.op("act", lambda e, bg=bg, n=n, sgt=sgt: e.activation(out=sgt[:, 0:n], in_=ps[:, bg, 0:n], func=AF.Silu),
                         r=[("ps", bg)], w=[sgk])
                    S.op("dve", lambda e, bu=bu, n=n, sgt=sgt, j=j, c0=c0: e.tensor_tensor(
                        out=actT[:, j, c0:c0 + n], in0=sgt[:, 0:n], in1=ps[:, bu, 0:n], op=ALU.mult),
                        r=[sgk, ("ps", bu)], w=[("actT", j, ni)])
            if which == 0:
                w1t = w1t_holder["t"]
                alias = [("wup", 0), ("wup", 1), ("wup", 2), ("sg", 0), ("sg", 1)]
                S.op("pool", lambda e: e.dma_start(out=w1t[:, :, 0:256],
                                                   in_=w_in[:, 512:768].rearrange("(kc p) c -> p kc c", p=128)),
                     w=alias + ["w1a"], dma="w1a")
                S.op("pool", lambda e: e.dma_start(out=w1t[:, :, 256:1792],
                                                   in_=w_in[:, 768:2304].rearrange("(kc p) c -> p kc c", p=128)),
                     w=alias + ["w1r"], dma="w1r")
            S.pe_sync = False
            pend = None
            for ti, t in enumerate(tiles):
                P, c0 = tile_geom(t)
                S.pe_sync = (t == 8)
                pb = (ti % 2) * 2
                ni = 2 if t == 8 else t // 4
                for dh in range(2):
                    for j in range(NJ):
                        S.op("pe", lambda e, j=j, dh=dh, pb=pb, P=P, c0=c0: e.matmul(
                            ps[0:P, pb + dh, :], lhsT=actT[:, j, c0:c0 + P], rhs=wdn[:, j, dh * 512:(dh + 1) * 512],
                            start=(j == 0), stop=(j == NJ - 1)),
                            r=[("actT", j, ni), "wdn"], w=[("ps", pb + dh)])
                S.pe_sync = False
                ln_elem(t, PS2(pb, P), [("ps", pb), ("ps", pb + 1)], 0.5 / ALPHA, LN_EPS / (ALPHA * ALPHA))
                if which == 0 and ti < 8:
                    kc = ti
                    for hh in range(4):
                        S.op("pool", lambda e, kc=kc, hh=hh: e.dma_start(
                            out=w1q[:, kc, hh * 128:(hh + 1) * 128].rearrange("p (g d) -> p g d", g=2),
                            in_=w_in[kc * 128:(kc + 1) * 128, 0:512].rearrange("p (g hh d) -> p hh g d", g=2, hh=4)[:, hh, :, :]),
                            w=["w1q"], dma="w1q")
                if final:
                    dst = yp[(blk * 8 + t) * 128:(blk * 8 + t + 1) * 128, :] if t < 8 else ys
                    S.op("sp", lambda e, t=t, P=P, dst=dst: e.dma_start(out=dst, in_=h_tok[0:P, t, :]),
                         r=[("h", t)], dma=("yout", t))
                else:
                    if debug and blk == 0:
                        S.op("sp", lambda e, t=t: e.dma_start(out=dbg["h1"][t], in_=h_tok[:, t, :]), r=[("h", t)],
                             dma=("dbg", t))
                    if pend is not None:
                        transpose_to_hT(pend[0], pend[1])
                    pend = (t, 4 + (ti % 2) * 2)
            if pend is not None and not final:
                transpose_to_hT(pend[0], pend[1])

        def mixer_phase(blk):
            tiles = list(range(8)) + ([8] if blk == 0 else [])
            ntiles = [(0, 512), (512, 512)] + ([(TB, NS)] if blk == 0 else [])
            S.barrier()
            A.top = phase_base
            oaT = A.alloc([128, 4, NCOL], BF16)
            rgn = A.alloc([128, 9, 512], F32)
            b12_base = A.top
            rot = A.alloc([128, NROT], F32)
            w1t = w1t_holder["t"]
            S.op("sp", lambda e: e.dma_start(out=rot[:], in_=consts_d[:, NC1:NCONST]), w=["rot"], dma="rot")
            qaT = A.alloc([128, 4, NCOL], BF16)
            kaT = A.alloc([128, 128 + NCOL], BF16)
            vaug = A.alloc([128, 10, 2, 65], BF16)
            qs = A.alloc([128, 4, 128], F32)
            tA = A.alloc([128, 4, 128], F32)
            tB = A.alloc([128, 4, 128], F32)
            qh = A.alloc([128, 4, 128], F32)
            kh = A.alloc([128, 4, 128], F32)
            khb = A.alloc([128, 4, 128], BF16)
            vrb = A.alloc([128, 4, 128], BF16)
            qhT = A.alloc([128, 4, 128], BF16)
            khT = A.alloc([128, 4, 128], BF16)
            et = [A.alloc([128, 512], F32) for _ in range(2)]
            pT = [A.alloc([128, 2, 4, 128], BF16) for _ in range(2)]
            scm = A.alloc([128, 4, 128], BF16)
            o_n = A.alloc([128, 8, 64], F32)
            den = A.alloc([128, 8], F32)
            kvout = A.alloc([128, 256], F32)
            qhT32 = A.alloc([128, 4, NS], F32)
            vr32 = A.alloc([128, 4, 128], F32)
            assert A.top <= w1t_holder["off"], (A.top, w1t_holder["off"])

            S.op("dve", lambda e: e.memset(vaug[:, :, :, 64:65], 1.0), w=["vaug_ones"])
            if blk == 0:
                S.op("sp", lambda e: e.dma_start(out=kws[:, 0:127, :], in_=ck[:, 1:128, :]), w=["kws_a"], dma="kws_a")
                S.op("sp", lambda e: e.dma_start(out=vws[:, 0:127, :], in_=cv[:, 1:128, :]), w=["vws_a"], dma="vws_a")
            else:
                S.op("dve", lambda e: e.tensor_copy(out=kaT[:, 0:128], in_=kprev[:]), r=["kprev"], w=["kaT_prev"])
                S.op("dve", lambda e: e.tensor_copy(out=vaug[:, 0, :, :], in_=vprev[:]), r=["vprev", "vaug_ones"],
                     w=[("vaug", 0)])

            def b0_all():
              cnt = 0
              for c in range(5):
                    for ni, (c0, n) in enumerate(ntiles):
                        S.pe_sync = (c0 >= TB)
                        bank = 6 + cnt % 2
                        for kc in range(8):
                            S.op("pe", lambda e, c=c, kc=kc, c0=c0, n=n, bank=bank: e.matmul(
                                ps[:, bank, 0:n], lhsT=(w1q[:, kc, c * 128:(c + 1) * 128] if c < 4 else w1t[:, kc, 0:128]),
                                rhs=hT[:, kc, c0:c0 + n],
                                start=(kc == 0), stop=(kc == 7)), r=["w1q" if c < 4 else "w1a"] + hT_tok(c0, n), w=[("ps", bank)])
                        if c < 4:
                            dst, wt = qaT[:, c, c0:c0 + n], [("qaT", c, ni)]
                        else:
                            dst, wt = kaT[:, 128 + c0:128 + c0 + n], [("kaT", ni), "kaT_all"]
                        eng = "act" if cnt % 2 == 0 else "dve"
                        if eng == "act":
                            S.op("act", lambda e, dst=dst, bank=bank, n=n: e.copy(out=dst, in_=ps[:, bank, 0:n]),
                                 r=[("ps", bank)], w=wt)
                        else:
                            S.op("dve", lambda e, dst=dst, bank=bank, n=n: e.tensor_copy(out=dst, in_=ps[:, bank, 0:n]),
                                 r=[("ps", bank)], w=wt)
                        cnt += 1
            S.pe_sync = False
            qa_tok = lambda ni: [("qaT", c, ni) for c in range(4)]

            def gn_store(t, P, heads):
                st, mv = stt[t % 2], mvt[t % 2]
                tk = ("lnt", t % 2)
                for h, (v, vt) in enumerate(heads):
                    S.op("dve", lambda e, h=h, v=v: e.bn_stats(out=st[0:P, h, :], in_=v), r=vt, w=[tk])
                for h in range(4):
                    S.op("dve", lambda e, h=h: e.bn_aggr(out=mv[0:P, h, 0:2], in_=st[0:P, h, :]), r=[tk], w=[tk])
                S.op("dve", lambda e: e.tensor_scalar(out=mv[0:P, :, 2:3], in0=mv[0:P, :, 1:2], scalar1=GN_EPS, scalar2=None,
                                                      op0=ALU.add), r=[tk], w=[tk])
                S.op("pool", lambda e: e.tensor_tensor(out=mv[0:P, :, 3:4], in0=mv[0:P, :, 2:3],
                                                       in1=mhalf[0:P].unsqueeze(1).to_broadcast([P, 4, 1]), op=ALU.pow),
                     r=[tk, "cst"], w=[tk])
                for h, (v, vt) in enumerate(heads):
                    S.op("dve", lambda e, h=h, v=v: e.tensor_scalar(
                        out=rgn[0:P, t, h * 128:(h + 1) * 128], in0=v, scalar1=mv[0:P, h, 0:1], scalar2=mv[0:P, h, 3:4],
                        op0=ALU.subtract, op1=ALU.mult), r=vt + [tk], w=[("rgn", t)])

            def attn_finish(t, P, c0):
                for g, bank in ((0, 0), (1, 3)):
                    ov = ps[0:P, bank, 0:260].rearrange("p (h d) -> p h d", d=65)
                    S.op("dve", lambda e, g=g, ov=ov: e.tensor_tensor(
                        out=den[0:P, g * 4:(g + 1) * 4], in0=ov[:, :, 64], in1=sinkexp[0:P, g * 4:(g + 1) * 4], op=ALU.add),
                        r=[("ps", bank), "sinkexp"], w=[("den", g)])
                    S.op("dve", lambda e, g=g: e.reciprocal(out=den[0:P, g * 4:(g + 1) * 4], in_=den[0:P, g * 4:(g + 1) * 4]),
                         r=[("den", g)], w=[("den", g)])
                    S.op("dve", lambda e, g=g, ov=ov: e.tensor_tensor(
                        out=o_n[0:P, g * 4:(g + 1) * 4, :], in0=ov[:, :, 0:64],
                        in1=den[0:P, g * 4:(g + 1) * 4].unsqueeze(2).to_broadcast([P, 4, 64]), op=ALU.mult),
                        r=[("ps", bank), ("den", g)], w=[("o_n", g)])
                for c in range(4):
                    S.op("pe", lambda e, c=c: e.transpose(out=ps[:, 4, c * 128:c * 128 + P],
                                                          in_=o_n[0:P, 2 * c:2 * c + 2, :].rearrange("p a b -> p (a b)"),
                                                          identity=ident[0:P, 0:P]),
                         r=[("o_n", c // 2), "cst"], w=[("ps", 4)])
                S.op("act", lambda e: e.copy(out=oaT[:, :, c0:c0 + P],
                                             in_=ps[:, 4, :].rearrange("p (a b) -> p a b", b=128)[:, :, 0:P]),
                     r=[("ps", 4)], w=[("oaT", t)])

            def b1_tile(t, part):
                P, c0 = tile_geom(t)
                gt = blk * 8 + t if t < 8 else 16
                smp = (t == 8)
                srow = 4 if smp else 0
                if part == "A":
                    for bank, (co, n) in enumerate(((512, 256), (768, 512), (1280, 512), (1792, 512))):
                        for kc in range(8):
                            S.op("pe", lambda e, kc=kc, bank=bank, co=co, n=n: e.matmul(
                                ps[0:P, bank, 0:n], lhsT=hT[:, kc, c0:c0 + P], rhs=w1t[:, kc, co - 512:co - 512 + n],
                                start=(kc == 0), stop=(kc == 7)), r=["w1a" if bank == 0 else "w1r", ("hT", t)], w=[("ps", bank)])
                    if smp and sub < 4.52:
                        return
                    slot = 1 + t
                    if not (smp and (KX & 1)):
                        S.op("act", lambda e, slot=slot: e.copy(out=vaug[0:P, slot, :, 0:64],
                                                                in_=ps[0:P, 0, 128:256].rearrange("p (g d) -> p g d", g=2)),
                             r=[("ps", 0)], w=[("vaug", slot)])
                    if (smp and not (KX & 2)) or (blk == 1 and t == 7):
                        S.op("act", lambda e: e.copy(out=kvout[0:P, :], in_=ps[0:P, 0, 0:256]), r=[("ps", 0)], w=["kvout"])
                        if smp and not (KX & 4):
                            S.op("sp", lambda e: e.dma_start(out=kws[:, 127, :], in_=kvout[0:NS, 0:128]), r=["kvout"],
                                 w=["kws_b"], dma="kws_b")
                            S.op("sp", lambda e: e.dma_start(out=vws[:, 127, :], in_=kvout[0:NS, 128:256]), r=["kvout"],
                                 w=["vws_b"], dma="vws_b")
                        elif not smp:
                            S.op("sp", lambda e: e.dma_start(out=kwp, in_=kvout[:, 0:128]), r=["kvout"], dma="kwp")
                            S.op("sp", lambda e: e.dma_start(out=vwp, in_=kvout[:, 128:256]), r=["kvout"], dma="vwp")
                    if not (smp and (KX & 8)):
                        S.op("act", lambda e: e.copy(out=vrb[0:P].rearrange("p a b -> p (a b)"), in_=ps[0:P, 3, :]),
                             r=[("ps", 3)], w=["vrb"])
                    if smp and not (KX & 16):
                        S.op("act", lambda e: e.copy(out=vr32[0:NS].rearrange("p a b -> p (a b)"), in_=ps[0:NS, 3, :]),
                             r=[("ps", 3)], w=["vr32"])
                    if smp and sub < 4.53:
                        return
                    cosv = rot[0:P, gt * 64:(gt + 1) * 64].unsqueeze(1).unsqueeze(1).to_broadcast([P, 4, 2, 64])
                    sinv = rot[0:P, 1088 + gt * 64:1088 + (gt + 1) * 64].unsqueeze(1).to_broadcast([P, 4, 64])
                    nsinv = rot[0:P, 2176 + gt * 64:2176 + (gt + 1) * 64].unsqueeze(1).to_broadcast([P, 4, 64])
                    for which, bank, dcol, dsth in (("q", 1, C_DQ, qh), ("k", 2, C_DK, kh)):
                        dv = cst[0:P, dcol + srow:dcol + srow + 4].unsqueeze(2).to_broadcast([P, 4, 128])
                        S.op("dve", lambda e, bank=bank, dv=dv: e.tensor_tensor(
                            out=qs[0:P], in0=ps[0:P, bank, :].rearrange("p (a b) -> p a b", b=128), in1=dv, op=ALU.mult),
                            r=[("ps", bank), "cst"], w=["qs"])
                        S.op("dve", lambda e: e.tensor_tensor(
                            out=tA[0:P].rearrange("p h (two d) -> p h two d", two=2),
                            in0=qs[0:P].rearrange("p h (two d) -> p h two d", two=2), in1=cosv, op=ALU.mult),
                            r=["qs", "rot"], w=["tA"])
                        S.op("dve", lambda e: e.tensor_tensor(out=tB[0:P, :, 0:64], in0=qs[0:P, :, 64:128], in1=nsinv, op=ALU.mult),
                             r=["qs", "rot"], w=["tB0"])
                        S.op("dve", lambda e: e.tensor_tensor(out=tB[0:P, :, 64:128], in0=qs[0:P, :, 0:64], in1=sinv, op=ALU.mult),
                             r=["qs", "rot"], w=["tB1"])
                        S.op("dve", lambda e, dsth=dsth: e.tensor_tensor(out=dsth[0:P], in0=tA[0:P], in1=tB[0:P], op=ALU.add),
                             r=["tA", "tB0", "tB1"], w=[which + "h"])
                    if smp and sub < 4.54:
                        return
                    S.op("act", lambda e: e.copy(out=khb[0:P], in_=kh[0:P]), r=["kh"], w=["khb"])
                    for which, src, bank, dstT in (("q", qh, 4, qhT), ("k", kh, 5, khT)):
                        for h in range(4):
                            S.op("pe", lambda e, h=h, src=src, bank=bank: e.transpose(
                                out=ps[:, bank, h * 128:h * 128 + P], in_=src[0:P, h, :], identity=ident[0:P, 0:P]),
                                r=[which + "h", "cst"], w=[("ps", bank)])
                        S.op("act", lambda e, bank=bank, dstT=dstT: e.copy(
                            out=dstT[:, :, 0:P], in_=ps[:, bank, :].rearrange("p (a b) -> p a b", b=128)[:, :, 0:P]),
                            r=[("ps", bank)], w=[which + "hT"])
                        if smp and which == "q":
                            S.op("act", lambda e, bank=bank: e.copy(
                                out=qhT32[:], in_=ps[:, bank, :].rearrange("p (a b) -> p a b", b=128)[:, :, 0:NS]),
                                r=[("ps", bank)], w=["qhT32"])

                    return
                if sub < 2:
                    return
                if not smp:
                    first = (blk == 0 and t == 0)
                    kbs = [1] if first else [0, 1]
                    sc_cnt = 0
                    for g in range(2):
                        for kb in kbs:
                            kcol = (t + kb) * 128
                            sbank = 6 + sc_cnt % 2
                            e_t = et[sc_cnt % 2]
                            ek = ("et", sc_cnt % 2)
                            sc_cnt += 1
                            ktok = ["kaT_prev"] if (kb == 0 and t == 0) else [("kaT", (t + kb - 1) // 4)]
                            S.op("pe", lambda e, g=g, kcol=kcol, sbank=sbank: e.matmul(
                                ps[:, sbank, :], lhsT=kaT[g * 64:(g + 1) * 64, kcol:kcol + 128],
                                rhs=qaT[g * 64:(g + 1) * 64, :, c0:c0 + 128], start=True, stop=True),
                                r=ktok + qa_tok(t // 4), w=[("ps", sbank)])
                            S.op("dve", lambda e, g=g, kb=kb, sbank=sbank, e_t=e_t: e.scalar_tensor_tensor(
                                out=e_t[:], in0=ps[:, sbank, :], scalar=0.125,
                                in1=BT[:, kb, g * 4:(g + 1) * 4, :].rearrange("p a b -> p (a b)"), op0=ALU.mult, op1=ALU.add),
                                r=[("ps", sbank), "BT"], w=[ek])
                            S.op("act", lambda e, g=g, kb=kb, e_t=e_t: e.activation(
                                out=pT[g][:, kb, :, :].rearrange("p a b -> p (a b)"), in_=e_t[:], func=AF.Exp),
                                r=[ek], w=[("pT", g, kb)])
                        obank = 0 if g == 0 else 3
                        for hh in range(4):
                            for i, kb in enumerate(kbs):
                                vslot = t + kb
                                S.op("pe", lambda e, g=g, hh=hh, kb=kb, vslot=vslot, obank=obank, i=i: e.matmul(
                                    ps[:, obank, hh * 65:(hh + 1) * 65], lhsT=pT[g][:, kb, hh, :], rhs=vaug[:, vslot, g, :],
                                    start=(i == 0), stop=(i == len(kbs) - 1)),
                                    r=[("pT", g, kb), ("vaug", vslot), "vaug_ones"], w=[("ps", obank)])
                    attn_finish(t, P, c0)
                    if sub < 3:
                        return
                    for h in range(4):
                        S.op("pe", lambda e, h=h: e.matmul(ps[:, 1, h * 128:(h + 1) * 128], lhsT=khT[:, h, :], rhs=qhT[:, h, :],
                                                           start=True, stop=True), r=["khT", "qhT"], w=[("ps", 1)])
                    S.op("dve", lambda e: e.tensor_tensor(
                        out=scm[:], in0=ps[:, 1, :].rearrange("p (a b) -> p a b", b=128),
                        in1=caus.unsqueeze(1).to_broadcast([128, 4, 128]), op=ALU.mult), r=[("ps", 1), "cst"], w=["scm"])
                    for h in range(4):
                        S.op("pe", lambda e, h=h: e.matmul(ps[:, 2, h * 128:(h + 1) * 128], lhsT=scm[:, h, :], rhs=vrb[:, h, :],
                                                           start=True, stop=first), r=["scm", "vrb"], w=[("ps", 2)])
                        if not first:
                            S.op("pe", lambda e, h=h: e.matmul(ps[:, 2, h * 128:(h + 1) * 128], lhsT=qhT[:, h, :], rhs=Sb[:, h, :],
                                                               start=False, stop=True), r=["qhT", "Sb"], w=[("ps", 2)])
                    gn_store(t, P, [(ps[:, 2, h * 128:(h + 1) * 128], [("ps", 2)]) for h in range(4)])
                    for h in range(4):
                        S.op("pe", lambda e, h=h: e.matmul(ps[:, 5, h * 128:(h + 1) * 128], lhsT=khb[:, h, :], rhs=vrb[:, h, :],
                                                           start=True, stop=True), r=["khb", "vrb"], w=[("ps", 5)])
                    S.op("dve", lambda e: e.tensor_tensor(out=Sst[:].rearrange("p a b -> p (a b)"), in0=ps[:, 5, :],
                                                          in1=Sst[:].rearrange("p a b -> p (a b)"), op=ALU.add),
                         r=[("ps", 5), "S"], w=["S"])
                    S.op("dve", lambda e: e.tensor_tensor(out=Sst[:], in0=Sst[:],
                                                          in1=cst[:, C_GC:C_GC + 4].unsqueeze(2).to_broadcast([128, 4, 128]),
                                                          op=ALU.mult), r=["S", "cst"], w=["S"])
                    S.op("act", lambda e: e.copy(out=Sb[:], in_=Sst[:]), r=["S"], w=["Sb"])
                    if blk == 1 and t == 7:
                        S.op("sp", lambda e: e.dma_start(out=sp_out.rearrange("h k v -> k h v"), in_=Sst[:]), r=["S"], dma="spout")
                else:
                    if sub < 5:
                        return
                    A_save = A.top
                    Wk32 = A.alloc([128, NS, 128], F32)
                    WkT = A.alloc([128, NS, 128], BF16)
                    Wva = A.alloc([128, NS, 2, 65], BF16)
                    oTs = A.alloc([128, 8, NS], F32)
                    e_s = A.alloc([128, NS, 8], F32)
                    pTs = A.alloc([128, NS, 8], BF16)
                    Sp = [A.alloc([128, 4, 128], F32) for _ in range(2)]
                    Sn = [A.alloc([128, 4, 128], F32) for _ in range(2)]
                    Vm = [A.alloc([128, 4, 128], F32) for _ in range(2)]
                    QTm = A.alloc([128, 4, NS, NS], F32)
                    S.op("sp", lambda e: e.dma_start(out=Wk32[:], in_=kws.rearrange("b r c -> r b c")),
                         r=["kws_a", "kws_b"], w=["Wk32", "w1a", "w1r", "w1q"], dma="wk32")
                    for g in range(2):
                        S.op("pool", lambda e, g=g: e.dma_start(out=Wva[:, :, g, 0:64],
                                                                in_=vws[:, :, g * 64:(g + 1) * 64].rearrange("b r d -> r b d")),
                             r=["vws_a", "vws_b"], w=["Wva", "w1a", "w1r", "w1q"], dma="wva")
                    S.op("dve", lambda e: e.memset(Wva[:, :, :, 64:65], 1.0), w=["Wva1", "w1a", "w1r", "w1q"])
                    for b in range(NS):
                        bank = 6 + (b // 4) % 2
                        S.op("pe", lambda e, b=b, bank=bank: e.transpose(out=ps[:, bank, (b % 4) * 128:(b % 4 + 1) * 128],
                                                                        in_=Wk32[:, b, :], identity=ident),
                             r=["Wk32", "cst"], w=[("ps", bank)])
                        if b % 4 == 3:
                            S.op("act", lambda e, b=b, bank=bank: e.copy(
                                out=WkT[:, b - 3:b + 1, :], in_=ps[:, bank, :].rearrange("p (a b) -> p a b", b=128)),
                                r=[("ps", bank)], w=["WkT", "w1a", "w1r", "w1q"])
                    for b in range(NS):
                        for g in range(2):
                            S.op("pe", lambda e, b=b, g=g: e.matmul(
                                ps[:, 0, b * 8 + g * 4:b * 8 + g * 4 + 4], lhsT=WkT[g * 64:(g + 1) * 64, b, :],
                                rhs=qaT[g * 64:(g + 1) * 64, :, TB + b], start=True, stop=True),
                                r=["WkT"] + qa_tok(2), w=[("ps", 0)])
                    S.op("dve", lambda e: e.scalar_tensor_tensor(
                        out=e_s[:], in0=ps[:, 0, 0:128].rearrange("p (b h) -> p b h", h=8), scalar=0.125,
                        in1=bias_s[:].unsqueeze(1).to_broadcast([128, NS, 8]), op0=ALU.mult, op1=ALU.add),
                        r=[("ps", 0), "bias_s"], w=["e_s", "w1a", "w1r", "w1q"])
                    S.op("act", lambda e: e.activation(out=pTs[:], in_=e_s[:], func=AF.Exp), r=["e_s"], w=["pTs", "w1a", "w1r", "w1q"])
                    if sub < 5.2:
                        A.top = A_save
                        return
                    for b in range(NS):
                        for g in range(2):
                            S.op("pe", lambda e, b=b, g=g: e.matmul(
                                ps[0:65, 3, b * 8 + g * 4:b * 8 + g * 4 + 4], lhsT=Wva[:, b, g, :],
                                rhs=pTs[:, b, g * 4:(g + 1) * 4], start=True, stop=True),
                                r=["Wva", "Wva1", "pTs"], w=[("ps", 3)])
                    S.op("act", lambda e: e.copy(out=oTs[0:65], in_=ps[0:65, 3, 0:128].rearrange("p (b h) -> p h b", h=8)),
                         r=[("ps", 3)], w=["oTs", "w1a", "w1r", "w1q"])
                    for h in range(8):
                        bank = 0 if h < 4 else 3
                        S.op("pe", lambda e, h=h, bank=bank: e.transpose(
                            out=ps[0:NS, bank, (h % 4) * 65:(h % 4 + 1) * 65], in_=oTs[0:65, h, :], identity=ident[0:65, 0:65]),
                            r=["oTs", "cst"], w=[("ps", bank)])
                    attn_finish(t, P, c0)
                    if sub < 5.3:
                        A.top = A_save
                        return
                    S.op("dve", lambda e: e.tensor_tensor(
                        out=QTm[:], in0=qhT32[:].unsqueeze(3).to_broadcast([128, 4, NS, NS]),
                        in1=cst[:, C_EYER:C_EYER + 256].rearrange("p (a b) -> p a b", b=NS).unsqueeze(1).to_broadcast([128, 4, NS, NS]),
                        op=ALU.mult), r=["qhT32", "cst"], w=["QTm", "w1a", "w1r", "w1q"])
                    for b in range(NS):
                        i2 = b % 2
                        S.op("sp", lambda e, b=b, i2=i2: e.dma_start(out=Sp[i2][:], in_=st_in[b].rearrange("h k v -> k h v")),
                             w=[("Sp", i2), "w1a", "w1r", "w1q"], dma=("Sp", i2))
                        S.op("pool", lambda e, b=b, i2=i2: e.tensor_scalar(
                            out=Vm[i2][0:NS].rearrange("p a b -> p (a b)"), in0=vr32[0:NS].rearrange("p a b -> p (a b)"),
                            scalar1=ident[0:NS, b:b + 1], scalar2=None, op0=ALU.mult),
                            r=["vr32", "cst"], w=[("Vm", i2), "w1a", "w1r", "w1q"])
                        ub = 1 + i2
                        for h in range(4):
                            S.op("pe", lambda e, h=h, ub=ub, i2=i2: e.matmul(
                                ps[:, ub, h * 128:(h + 1) * 128], lhsT=kh[0:NS, h, :], rhs=Vm[i2][0:NS, h, :],
                                start=True, stop=True), r=["kh", ("Vm", i2)], w=[("ps", ub)])
                        for h in range(4):
                            S.op("dve", lambda e, h=h, ub=ub, i2=i2: e.scalar_tensor_tensor(
                                out=Sn[i2][:, h, :], in0=Sp[i2][:, h, :], scalar=float(GAMMAS[h]),
                                in1=ps[:, ub, h * 128:(h + 1) * 128], op0=ALU.mult, op1=ALU.add),
                                r=[("Sp", i2), ("ps", ub)], w=[("Sn", i2), "w1a", "w1r", "w1q"])
                        S.op("sp", lambda e, b=b, i2=i2: e.dma_start(out=ss_out[b].rearrange("h k v -> k h v"), in_=Sn[i2][:]),
                             r=[("Sn", i2)], dma=("ssout", i2))
                        for h in range(4):
                            if sub < 5.4:
                                break
                            S.op("pe", lambda e, h=h, b=b, i2=i2: e.matmul(
                                ps[0:NS, 4 + h, 0:128], lhsT=QTm[:, h, b, :], rhs=Sn[i2][:, h, :],
                                start=(b == 0), stop=(b == NS - 1)), r=["QTm", ("Sn", i2)], w=[("ps", 4 + h)])
                    if sub >= 5.4:
                        gn_store(t, P, [(ps[0:NS, 4 + h, 0:128], [("ps", 4 + h)]) for h in range(4)])
                    A.top = A_save

            if sub < 1:
                return
            for t in tiles:
                if t == 8 and (sub < 4.5 or (KX & 32)):
                    continue
                if t > 0 and sub < 4:
                    continue
                S.pe_sync = (t == 8)
                b1_tile(t, "A")
                S.pe_sync = False
                if t == 0:
                    b0_all()
                    S.pe_sync = False
                S.pe_sync = (t == 8)
                b1_tile(t, "B")
                S.pe_sync = False
            if sub < 6:
                return
            if blk == 0:
                S.op("dve", lambda e: e.tensor_copy(out=kprev[:], in_=kaT[:, TB:TB + 128]), r=[("kaT", 1)], w=["kprev"])
                S.op("dve", lambda e: e.tensor_copy(out=vprev[:], in_=vaug[:, 8, :, :]), r=[("vaug", 8), "vaug_ones"],
                     w=["vprev"])

            S.barrier()
            A.top = b12_base
            w2 = A.alloc([128, 8, 2560], BF16)
            wao = A.alloc([128, 4, D], BF16)
            wro = A.alloc([128, 4, D], BF16)
            wo = A.alloc([128, 8, D], BF16)
            sgr = A.alloc([128, 512], F32)
            rr = A.alloc([128, 512], F32)
            rT = A.alloc([128, 4, 128], BF16)
            sa = A.alloc([128, D], F32)
            m1 = A.alloc([128, D], F32)
            mT = A.alloc([128, 8, 128], BF16)
            def ld_w2(lo, hi, tok):
                S.op("pool", lambda e: e.dma_start(out=w2[:, :, lo:hi],
                                                   in_=w_in[:, 2304 + lo:2304 + hi].rearrange("(kc p) c -> p kc c", p=128)),
                     w=[tok], dma=tok)
            ld_w2(0, 512, "w2a")
            S.op("pool", lambda e: e.dma_start(out=wao[:], in_=w_ao.rearrange("(kc p) c -> p kc c", p=128)), w=["wao"], dma="wao")
            S.op("pool", lambda e: e.dma_start(out=wro[:], in_=w_ro.rearrange("(kc p) c -> p kc c", p=128)), w=["wro"], dma="wro")
            ld_w2(512, 1536, "w2b")
            ld_w2(1536, 2560, "w2c")
            S.op("pool", lambda e: e.dma_start(out=wo[:], in_=w_o.rearrange("(kc p) c -> p kc c", p=128)), w=["wo"], dma="wo")
            load_ln(1)
            def b2_tile(t):
                P, c0 = tile_geom(t)
                for kc in range(8):
                    S.op("pe", lambda e, kc=kc: e.matmul(ps[0:P, 0, :], lhsT=hT[:, kc, c0:c0 + P], rhs=w2[:, kc, 0:512],
                                                         start=(kc == 0), stop=(kc == 7)), r=["w2a", ("hT", t)], w=[("ps", 0)])
                S.op("act", lambda e: e.activation(out=sgr[0:P], in_=ps[0:P, 0, :], func=AF.Silu), r=[("ps", 0)], w=["sgr"])
                S.op("dve", lambda e: e.tensor_tensor(out=rr[0:P], in0=sgr[0:P], in1=rgn[0:P, t, :], op=ALU.mult),
                     r=["sgr", ("rgn", t)], w=["rr"])
                for c in range(4):
                    S.op("pe", lambda e, c=c: e.transpose(out=ps[:, 1, c * 128:c * 128 + P], in_=rr[0:P, c * 128:(c + 1) * 128],
                                                          identity=ident[0:P, 0:P]), r=["rr", "cst"], w=[("ps", 1)])
                S.op("act", lambda e: e.copy(out=rT[:, :, 0:P], in_=ps[:, 1, :].rearrange("p (a b) -> p a b", b=128)[:, :, 0:P]),
                     r=[("ps", 1)], w=["rT"])
                if t == 8 and (KX & 64):
                    return
                for dh in range(2):
                    for c in range(4):
                        S.op("pe", lambda e, c=c, dh=dh: e.matmul(ps[0:P, 2 + dh, :], lhsT=oaT[:, c, c0:c0 + P],
                                                                  rhs=wao[:, c, dh * 512:(dh + 1) * 512],
                                                                  start=(c == 0), stop=(c == 3)),
                             r=["wao", ("oaT", t)], w=[("ps", 2 + dh)])
                for dh in range(2):
                    for c in range(4):
                        S.op("pe", lambda e, c=c, dh=dh: e.matmul(ps[0:P, 4 + dh, :], lhsT=rT[:, c, 0:P],
                                                                  rhs=wro[:, c, dh * 512:(dh + 1) * 512],
                                                                  start=(c == 0), stop=(c == 3)),
                             r=["wro", "rT"], w=[("ps", 4 + dh)])
                if t == 8 and (KX & 128):
                    return
                for gi, goff in enumerate((512, 1536)):
                    for dh in range(2):
                        for kc in range(8):
                            S.op("pe", lambda e, kc=kc, dh=dh, goff=goff: e.matmul(
                                ps[0:P, 6 + dh, :], lhsT=hT[:, kc, c0:c0 + P], rhs=w2[:, kc, goff + dh * 512:goff + (dh + 1) * 512],
                                start=(kc == 0), stop=(kc == 7)), r=["w2b" if gi == 0 else "w2c", ("hT", t)], w=[("ps", 6 + dh)])
                    S.op("act", lambda e: e.activation(out=sa[0:P], in_=PS2(6, P), func=AF.Tanh, scale=0.5),
                         r=[("ps", 6), ("ps", 7)], w=["sa"])
                    if gi == 0:
                        S.op("dve", lambda e: e.scalar_tensor_tensor(out=m1[0:P], in0=sa[0:P], scalar=1.0, in1=PS2(2, P),
                                                                     op0=ALU.add, op1=ALU.mult),
                             r=["sa", ("ps", 2), ("ps", 3)], w=["m1"])
                    else:
                        S.op("dve", lambda e: e.scalar_tensor_tensor(out=sa[0:P], in0=sa[0:P], scalar=1.0, in1=PS2(4, P),
                                                                     op0=ALU.add, op1=ALU.mult),
                             r=["sa", ("ps", 4), ("ps", 5)], w=["sa"])
                        S.op("dve", lambda e: e.tensor_tensor(out=m1[0:P], in0=m1[0:P], in1=sa[0:P], op=ALU.add),
                             r=["sa", "m1"], w=["m1"])
                if t == 8 and (KX & 256):
                    return
                for kc in range(8):
                    S.op("pe", lambda e, kc=kc: e.transpose(out=ps[:, kc // 4, (kc % 4) * 128:(kc % 4) * 128 + P],
                                                            in_=m1[0:P, kc * 128:(kc + 1) * 128], identity=ident[0:P, 0:P]),
                         r=["m1", "cst"], w=[("ps", kc // 4)])
                S.op("act", lambda e: e.copy(out=mT[:, :, 0:P],
                                             in_=ps[:, 0:2, :].rearrange("p a (b c) -> p (a b) c", c=128)[:, :, 0:P]),
                     r=[("ps", 0), ("ps", 1)], w=["mT"])
                for dh in range(2):
                    for kc in range(8):
                        S.op("pe", lambda e, kc=kc, dh=dh: e.matmul(ps[0:P, 2 + dh, :], lhsT=mT[:, kc, 0:P],
                                                                    rhs=wo[:, kc, dh * 512:(dh + 1) * 512],
                                                                    start=(kc == 0), stop=(kc == 7)),
                             r=["wo", "mT"], w=[("ps", 2 + dh)])
                ln_elem(t, PS2(2, P), [("ps", 2), ("ps", 3)], 0.5 / ALPHA, LN_EPS / (ALPHA * ALPHA))
                if debug and blk == 0:
                    S.op("sp", lambda e, t=t: e.dma_start(out=dbg["h2"][t], in_=h_tok[:, t, :]), r=[("h", t)], dma=("dbg2", t))

            pend = None
            for t in tiles:
                if t == 8 and (KX & 32):
                    continue
                S.pe_sync = (t == 8)
                b2_tile(t)
                S.pe_sync = False
                if pend is not None:
                    transpose_to_hT(pend, 6)
                pend = t
            transpose_to_hT(pend, 6)

        stage = 0
        for blk in range(2):
            tiles = list(range(8)) + ([8] if blk == 0 else [])
            if stage >= max_stage:
                break
            stage += 1
            for t in tiles:
                P, c0 = tile_geom(t)
                src = xp[(blk * 8 + t) * 128:(blk * 8 + t + 1) * 128, :] if t < 8 else xs
                S.op("sp", lambda e, t=t, P=P, src=src: e.dma_start(out=h_tok[0:P, t, :], in_=src), w=[("h", t)], dma=("x", t))
            for ti, t in enumerate(tiles):
                transpose_to_hT(t, 4 + (ti % 2) * 2)
            if stage >= max_stage:
                break
            stage += 1
            ffn_phase(blk, 0, 0, final=False)
            if stage >= max_stage:
                break
            stage += 1
            mixer_phase(blk)
            if stage >= max_stage:
                break
            stage += 1
            ffn_phase(blk, 1, 2, final=True)
        S.barrier()
        S.op("sp", lambda e: e.nop(), r=[], w=[])
        S.emit(nc, es)
        build_nc.info = dict(nops=len(S.ops), nsems=S.nsems, sbuf_peak=A.peak)
    return nc


_CACHE = {}
KX = 0


def kernel(x_prompt, x_sample, cache_k_win, cache_v_win, state_ret, rel_bias, w_in, attn_sinks,
           w_attn_out, w_ret_out, w_o, ffn1_w_up, ffn1_w_down, ffn2_w_up, ffn2_w_down,
           ln1_g, ln1_b, ln2_g, ln2_b, ln3_g, ln3_b, _debug=False):
    f = lambda a: np.ascontiguousarray(np.asarray(a, dtype=np.float32))
    import os as _os
    _ms = int(_os.environ.get("K_STAGES", "99"))
    _sub = float(_os.environ.get("K_SUB", "99"))
    global KX
    KX = int(_os.environ.get("K_X", "0"))
    key = ("nc", bool(_debug), _ms, _sub, KX, _os.environ.get("K_PAD", "0"))
    if key not in _CACHE:
        _CACHE[key] = build_nc(debug=_debug, max_stage=_ms, sub=_sub)
    nc = _CACHE[key]
    consts = make_consts()
    shared = dict(relb=f(rel_bias), w_in=f(w_in)[0], sinks=f(attn_sinks)[0], w_ao=f(w_attn_out)[0], w_ro=f(w_ret_out)[0],
                  w_o=f(w_o)[0], f1u=f(ffn1_w_up)[0], f2u=f(ffn2_w_up)[0], f1d=f(ffn1_w_down)[0], f2d=f(ffn2_w_down)[0],
                  ln1g=f(ln1_g)[0], ln1b=f(ln1_b)[0], ln2g=f(ln2_g)[0], ln2b=f(ln2_b)[0], ln3g=f(ln3_g)[0], ln3b=f(ln3_b)[0],
                  consts=consts)
    xp, xs = f(x_prompt), f(x_sample)
    ckf, cvf, stf = f(cache_k_win), f(cache_v_win), f(state_ret)
    in_maps = []
    for c in range(NCORES):
        m = dict(shared)
        sl = slice(c * NS, (c + 1) * NS)
        m["xp"] = xp[c]
        m["xs"] = xs[sl, 0, :]
        m["ck"] = ckf[0, sl].reshape(NS, 128, 128)
        m["cv"] = cvf[0, sl].reshape(NS, 128, 128)
        m["st"] = stf[0, sl]
        in_maps.append(m)
    _nco = int(_os.environ.get("K_CORES", str(NCORES)))
    res = run_bass_kernel_spmd(nc, in_maps[:_nco], core_ids=list(range(_nco)))
    R = list(res.results)
    while len(R) < NCORES:
        R.append({k: np.zeros_like(v) for k, v in R[0].items()})
    y_p = np.stack([R[c]["yp"] for c in range(NCORES)], 0)
    y_s = np.concatenate([R[c]["ys"] for c in range(NCORES)], 0).reshape(128, 1, D)
    kwp = np.stack([R[c]["kwp"] for c in range(NCORES)], 0).reshape(1, 8, 128, 2, 64)
    vwp = np.stack([R[c]["vwp"] for c in range(NCORES)], 0).reshape(1, 8, 128, 2, 64)
    spo = np.stack([R[c]["sp"] for c in range(NCORES)], 0).reshape(1, 8, 4, 128, 128)
    kws = np.concatenate([R[c]["kws"] for c in range(NCORES)], 0).reshape(1, 128, 128, 2, 64)
    vws = np.concatenate([R[c]["vws"] for c in range(NCORES)], 0).reshape(1, 128, 128, 2, 64)
    sso = np.concatenate([R[c]["ss"] for c in range(NCORES)], 0).reshape(1, 128, 4, 128, 128)
    outs = tuple(np.ascontiguousarray(a.astype(np.float32)) for a in (y_p, y_s, kwp, vwp, spo, kws, vws, sso))
    if _debug:
        kernel.dbg = [{k: v for k, v in R[c].items() if k.startswith("dbg")} for c in range(NCORES)]
    return outs
```

```python
import numpy as np
from contextlib import ExitStack
import concourse.bass as bass
import concourse.mybir as mybir
from concourse.bass_utils import run_bass_kernel_spmd

F32 = mybir.dt.float32
BF16 = mybir.dt.bfloat16
AF = mybir.ActivationFunctionType
ALU = mybir.AluOpType

NCORES = 8
D = 1024
SEQ = 2048
DFF = 2816
NJ = DFF // 128
NS = 16
TB = 1024
NCOL = TB + NS
PAST = 16384
ALPHA = 2.0 ** 0.25
LN_EPS = 1e-5
GN_EPS = 1e-6
MASKV = -30000.0
SB_BASE = 16512
SB_END = 229376

C_ID = 0
C_J = 128
C_CAUS = 256
C_DQ = 384
C_DK = C_DQ + 8
C_GC = C_DK + 8
C_EYER = C_GC + 4
C_OH1 = C_EYER + 256
C_OH2 = C_OH1 + 128
C_MH = C_OH2 + 128
C_ONE = C_MH + 1
NC1 = ((C_ONE + 1 + 7) // 8) * 8
C_COS = NC1
C_SIN = C_COS + 17 * 64
C_NSIN = C_SIN + 17 * 64
NROT = 3 * 17 * 64
NCONST = NC1 + NROT
GAMMAS = [1.0 - 2.0 ** (-5.0 - h) for h in range(4)]


def _bucket(d):
    d = np.asarray(d)
    n = np.maximum(d, 0)
    ratio = np.maximum(n, 1).astype(np.float32) / np.float32(16)
    large = 16 + (np.log(np.maximum(ratio, np.float32(1.0))).astype(np.float32)
                  / np.float32(np.log(128 / 16)) * np.float32(16)).astype(np.int32)
    large = np.minimum(large, 31)
    return np.where(n < 16, n, large)


def make_consts():
    c = np.zeros((128, NCONST), np.float32)
    c[:, C_ID:C_ID + 128] = np.eye(128, dtype=np.float32)
    c[:, C_J:C_J + 128] = np.eye(128, dtype=np.float32)[::-1]
    jj = np.arange(128)
    c[:, C_CAUS:C_CAUS + 128] = (jj[None, :] >= jj[:, None]).astype(np.float32)
    inv = (np.float32(10000.0) ** (-(np.arange(64, dtype=np.float32) / np.float32(64)))).astype(np.float32)
    for t in range(17):
        pos = (t * 128 + np.arange(128)) if t < 16 else np.full(128, PAST)
        ang = (pos.astype(np.float32)[:, None] * inv[None, :]).astype(np.float32)
        c[:, C_COS + t * 64:C_COS + (t + 1) * 64] = np.cos(ang.astype(np.float64)).astype(np.float32)
        c[:, C_SIN + t * 64:C_SIN + (t + 1) * 64] = np.sin(ang.astype(np.float64)).astype(np.float32)
        c[:, C_NSIN + t * 64:C_NSIN + (t + 1) * 64] = -np.sin(ang.astype(np.float64)).astype(np.float32)
    p = np.arange(128, dtype=np.float64)
    for h in range(4):
        lg = np.log1p(-(2.0 ** (-5.0 - h)))
        c[:, C_DQ + h] = np.exp((p + 1) * lg)
        c[:, C_DK + h] = (128.0 ** -0.5) * np.exp(-(p + 1) * lg)
        c[:, C_DQ + 4 + h] = 1.0
        c[:, C_DK + 4 + h] = 128.0 ** -0.5
        c[:, C_GC + h] = np.exp(128 * lg)
    c[:, C_EYER:C_EYER + 256] = np.eye(16, dtype=np.float32).reshape(1, 256)
    b1 = _bucket(np.arange(128))
    b2 = _bucket(127 - np.arange(128))
    for r in range(128):
        c[b1[r], C_OH1 + r] = 1.0
        c[b2[r], C_OH2 + r] = 1.0
    c[:, C_MH] = -0.5
    c[:, C_ONE] = 1.0
    return c


def _rnd_tile(x):
    return 32 if x <= 32 else (64 if x <= 64 else 128)


class _FakePE:
    def __init__(self):
        self.mode = None

    def matmul(self, out, lhsT=None, rhs=None, **kw):
        self.mode = (_rnd_tile(lhsT.shape[0]), _rnd_tile(int(np.prod(lhsT.shape[1:]))))

    def transpose(self, out=None, in_=None, identity=None):
        self.mode = (_rnd_tile(in_.shape[0]), _rnd_tile(int(np.prod(in_.shape[1:]))))


class Sched:
    ENGS = ("pe", "act", "dve", "pool", "sp")

    def __init__(self):
        self.ops = []
        self.last_w = {}
        self.readers = {}
        self.dma_cnt = {}
        self.bar = None
        self.bar_passed = set()
        self.last_eng = {}
        self.last_dma = {}
        self.pe_sync = False
        self.pe_sync_once = False

    def _stream(self, idx):
        o = self.ops[idx]
        return ("dma", o["dma"]) if o["dma"] is not None else ("eng", o["eng"])

    def op(self, eng, fn, r=(), w=(), dma=None):
        idx = len(self.ops)
        deps = {}

        def add(d):
            s = self._stream(d)
            if deps.get(s, -1) < d:
                deps[s] = d
        for t in r:
            lw = self.last_w.get(t)
            if lw is not None:
                add(lw)
        for t in w:
            lw = self.last_w.get(t)
            if lw is not None:
                add(lw)
            for d in self.readers.get(t, {}).values():
                add(d)
        if self.bar is not None and eng not in self.bar_passed:
            for d in self.bar:
                add(d)
            self.bar_passed.add(eng)
        force = False
        if eng == "pe" and dma is None and (self.pe_sync or self.pe_sync_once):
            self.pe_sync_once = self.pe_sync
            if "pe" in self.last_eng:
                add(self.last_eng["pe"])
                force = True
        o = dict(eng=eng, fn=fn, deps=deps, dma=dma, need_inc=False, ev=None, force=force)
        if dma is not None:
            self.dma_cnt[dma] = self.dma_cnt.get(dma, 0) + 1
            o["ev"] = ("dma", dma, 16 * self.dma_cnt[dma])
            self.last_dma[dma] = idx
        else:
            self.last_eng[eng] = idx
        self.ops.append(o)
        me = ("dma", dma) if dma is not None else ("eng", eng)
        for t in r:
            self.readers.setdefault(t, {})[me] = idx
        for t in w:
            self.last_w[t] = idx
            self.readers[t] = {}
        return idx

    def barrier(self):
        self.bar = set(self.last_eng.values()) | set(self.last_dma.values())
        self.bar_passed = set()

    def finalize(self):
        ops = self.ops
        for o in ops:
            real = []
            for s, d in o["deps"].items():
                od = ops[d]
                if s == ("eng", "pe") and o["eng"] == "pe" and o["dma"] is None and not o["force"]:
                    continue
                real.append(d)
                if od["dma"] is None:
                    od["need_inc"] = True
            o["deps"] = real
        cnt = {e: 0 for e in self.ENGS}
        for o in ops:
            if o["dma"] is None and o["need_inc"]:
                cnt[o["eng"]] += 1
                o["ev"] = ("eng", o["eng"], cnt[o["eng"]])
        seen = {e: {} for e in self.ENGS}
        for o in ops:
            waits = {}
            for d in o["deps"]:
                kind, key, val = ops[d]["ev"]
                k = (kind, key)
                if seen[o["eng"]].get(k, 0) >= val:
                    continue
                waits[k] = max(waits.get(k, 0), val)
            for k, v in waits.items():
                seen[o["eng"]][k] = v
            o["waits"] = waits

    def emit(self, nc, es):
        self.finalize()
        sems = {}
        n = [0]

        def sem(k):
            if k not in sems:
                n[0] += 1
                sems[k] = es.enter_context(nc.semaphore("s%d" % n[0]))
            return sems[k]
        for o in self.ops:
            for k in o["waits"]:
                sem(k)
            if o["ev"] is not None:
                sem((o["ev"][0], o["ev"][1]))
        block = es.enter_context(nc.Block())
        ops = self.ops

        import os as _os2
        pad = int(_os2.environ.get("K_PAD", "0"))

        def run(engname, eng):
            for _ in range(pad):
                eng.nop()
            for o in ops:
                if o["eng"] != engname:
                    continue
                for k, v in o["waits"].items():
                    eng.wait_ge(sems[k], v)
                ins = o["fn"](eng)
                if o["dma"] is not None:
                    ins.then_inc(sems[("dma", o["dma"])], 16)
                elif o["need_inc"]:
                    ins.then_inc(sems[("eng", engname)], 1)

        @block.tensor
        def _(e):
            run("pe", e)

        @block.scalar
        def _(e):
            run("act", e)

        @block.vector
        def _(e):
            run("dve", e)

        @block.gpsimd
        def _(e):
            run("pool", e)

        @block.sync
        def _(e):
            run("sp", e)
        self.nsems = len(sems)


class Arena:
    def __init__(self, nc, base, end):
        self.nc, self.top, self.end = nc, base, end
        self.n = 0
        self.peak = base

    def alloc(self, shape, dtype):
        sz = int(np.prod(shape[1:])) * (2 if dtype == BF16 else 4)
        off = (self.top + 31) // 32 * 32
        assert off + sz <= self.end, ("SBUF overflow", off + sz, self.end)
        self.top = off + sz
        self.peak = max(self.peak, self.top)
        self.n += 1
        return self.nc.alloc_sbuf_tensor_at("sb%d" % self.n, list(shape), dtype, offset=off)


def build_nc(debug=False, max_stage=99, sub=99):
    nc = bass.Bass("TRN2", target_bir_lowering=False)

    def din(name, shape):
        return nc.dram_tensor(name, list(shape), F32, kind="ExternalInput").ap()

    def dout(name, shape):
        return nc.dram_tensor(name, list(shape), F32, kind="ExternalOutput").ap()
    xp = din("xp", [SEQ, D])
    xs = din("xs", [NS, D])
    ck = din("ck", [NS, 128, 128])
    cv = din("cv", [NS, 128, 128])
    st_in = din("st", [NS, 4, 128, 128])
    relb = din("relb", [32, 8])
    w_in = din("w_in", [D, 4864])
    sinks = din("sinks", [8])
    w_ao = din("w_ao", [512, D])
    w_ro = din("w_ro", [512, D])
    w_o = din("w_o", [D, D])
    wup_d = [din("f1u", [D, 2 * DFF]), din("f2u", [D, 2 * DFF])]
    wdn_d = [din("f1d", [DFF, D]), din("f2d", [DFF, D])]
    lng = [din("ln%dg" % i, [D]) for i in (1, 2, 3)]
    lnb = [din("ln%db" % i, [D]) for i in (1, 2, 3)]
    consts_d = din("consts", [128, NCONST])
    yp = dout("yp", [SEQ, D])
    ys = dout("ys", [NS, D])
    kwp = dout("kwp", [128, 128])
    vwp = dout("vwp", [128, 128])
    sp_out = dout("sp", [4, 128, 128])
    kws = dout("kws", [NS, 128, 128])
    vws = dout("vws", [NS, 128, 128])
    ss_out = dout("ss", [NS, 4, 128, 128])
    scr = nc.dram_tensor("scr", [2, 8, 256], F32, kind="Internal").ap()
    dbg = {}
    if debug:
        dbg["h1"] = dout("dbg_h1", [9, 128, D])
        dbg["h2"] = dout("dbg_h2", [9, 128, D])

    es = ExitStack()
    with es:
        S = Sched()
        A = Arena(nc, SB_BASE, SB_END)
        ps = es.enter_context(nc.psum_tensor("ps", [128, 8, 512], F32))

        h_tok = A.alloc([128, 9, D], F32)
        hT = A.alloc([128, 8, NCOL], BF16)
        cst = A.alloc([128, NC1], F32)
        w1q = A.alloc([128, 8, 512], BF16)
        BT = A.alloc([128, 2, 8, 128], F32)
        lg_t = A.alloc([128, D], F32)
        lb_t = A.alloc([128, D], F32)
        Sst = A.alloc([128, 4, 128], F32)
        Sb = A.alloc([128, 4, 128], BF16)
        bias_s = A.alloc([128, 8], F32)
        sinkexp = A.alloc([128, 8], F32)
        stt = [A.alloc([128, 4, 6], F32) for _ in range(2)]
        mvt = [A.alloc([128, 4, 8], F32) for _ in range(2)]
        kprev = A.alloc([128, 128], BF16)
        vprev = A.alloc([128, 2, 65], BF16)
        phase_base = A.top

        ident = cst[:, C_ID:C_ID + 128]
        Jm = cst[:, C_J:C_J + 128]
        caus = cst[:, C_CAUS:C_CAUS + 128]
        mhalf = cst[:, C_MH:C_MH + 1]

        def PSB(b, P=128, n=512):
            return ps[0:P, b, 0:n]

        def PS2(b, P=128):
            return ps[0:P, b:b + 2, :].rearrange("p a b -> p (a b)")

        S.op("sp", lambda e: e.dma_start(out=cst[:], in_=consts_d[:, 0:NC1]), w=["cst"], dma="cst")
        S.op("sp", lambda e: e.dma_start(out=sinkexp[:], in_=sinks.partition_broadcast(128)), w=["sinkexp"], dma="c2")
        S.op("act", lambda e: e.activation(out=sinkexp[:], in_=sinkexp[:], func=AF.Exp), r=["sinkexp"], w=["sinkexp"])
        S.op("dve", lambda e: e.memset(Sst[:], 0.0), w=["S"])
        S.op("dve", lambda e: e.memset(Sb[:], 0.0), w=["Sb"])
        A0 = A.top
        rb = A.alloc([32, 8], F32)
        TTs = A.alloc([8, 128], F32)
        Lsb = A.alloc([8, 2, 256], F32)
        G = A.alloc([128, 2, 8, 128], F32)
        S.op("sp", lambda e: e.dma_start(out=rb[:], in_=relb), w=["rb"], dma="c3")
        S.op("pe", lambda e: e.matmul(ps[0:8, 0, 0:128], lhsT=rb[:], rhs=cst[0:32, C_OH1:C_OH1 + 128], start=True, stop=True),
             r=["rb", "cst"], w=[("ps", 0)])
        S.op("act", lambda e: e.copy(out=TTs[:], in_=ps[0:8, 0, 0:128]), r=[("ps", 0)], w=["TTs"])
        S.op("dve", lambda e: e.memset(Lsb[:], MASKV), w=["Lsb"])
        S.op("dve", lambda e: e.tensor_copy(out=Lsb[:, 0, 0:127], in_=TTs[:, 1:128]), r=["TTs", "Lsb"], w=["Lsb"])
        S.op("dve", lambda e: e.tensor_copy(out=Lsb[:, 1, 127:255], in_=TTs[:, 0:128]), r=["TTs", "Lsb"], w=["Lsb"])
        S.op("sp", lambda e: e.dma_start(out=scr.rearrange("t h u -> h t u"), in_=Lsb[:]), r=["Lsb"], w=["scr"], dma="c4")
        for tab in range(2):
            hank = bass.AP(tensor=scr.tensor, offset=tab * 2048, ap=[[1, 128], [256, 8], [1, 128]])
            S.op("sp", lambda e, tab=tab, hank=hank: e.dma_start(out=G[:, tab, :, :], in_=hank), r=["scr"], w=["G"], dma="c5")
        for tab in range(2):
            for hf in range(2):
                b = 1 + tab * 2 + hf
                S.op("pe", lambda e, tab=tab, hf=hf, b=b: e.matmul(
                    ps[:, b, :], lhsT=Jm, rhs=G[:, tab, hf * 4:(hf + 1) * 4, :].rearrange("p a b -> p (a b)"),
                    start=True, stop=True), r=["cst", "G"], w=[("ps", b)])
                S.op("act", lambda e, tab=tab, hf=hf, b=b: e.copy(
                    out=BT[:, tab, hf * 4:(hf + 1) * 4, :].rearrange("p a b -> p (a b)"), in_=ps[:, b, :]),
                    r=[("ps", b)], w=["BT"])
        S.op("pe", lambda e: e.matmul(ps[:, 5, 0:8], lhsT=cst[0:32, C_OH2:C_OH2 + 128], rhs=rb[:], start=True, stop=True),
             r=["rb", "cst"], w=[("ps", 5)])
        S.op("act", lambda e: e.copy(out=bias_s[:], in_=ps[:, 5, 0:8]), r=[("ps", 5)], w=["bias_s"])
        A.top = A0

        def tile_geom(t):
            return (128, t * 128) if t < 8 else (NS, TB)

        def hT_tok(c0, n):
            toks = [("hT", t) for t in range(c0 // 128, min(8, (c0 + n + 127) // 128))]
            if c0 + n > TB:
                toks.append(("hT", 8))
            return toks

        def nis_of(ntiles, c0, n):
            return [i for i, (a, m) in enumerate(ntiles) if a < c0 + n and a + m > c0]

        def transpose_to_hT(t, pbank):
            P, c0 = tile_geom(t)
            S.pe_sync = (t == 8)
            for kc in range(8):
                S.op("pe", lambda e, kc=kc: e.transpose(out=ps[:, pbank + kc // 4, (kc % 4) * 128:(kc % 4) * 128 + P],
                                                        in_=h_tok[0:P, t, kc * 128:(kc + 1) * 128], identity=ident[0:P, 0:P]),
                     r=[("h", t), "cst"], w=[("ps", pbank + kc // 4)])
            S.op("act", lambda e: e.copy(out=hT[:, :, c0:c0 + P],
                                         in_=ps[:, pbank:pbank + 2, :].rearrange("p a (b c) -> p (a b) c", c=128)[:, :, 0:P]),
                 r=[("ps", pbank), ("ps", pbank + 1)], w=[("hT", t)])
            S.pe_sync = False

        def load_ln(i):
            S.op("sp", lambda e: e.dma_start(out=lg_t[:], in_=lng[i].partition_broadcast(128)), w=["lng"], dma="lng")
            S.op("sp", lambda e: e.dma_start(out=lb_t[:], in_=lnb[i].partition_broadcast(128)), w=["lnb"], dma="lnb")

        def ln_elem(t, src, src_tok, cscale, eps_eff):
            P, c0 = tile_geom(t)
            h = h_tok[0:P, t, :]
            st, mv = stt[t % 2], mvt[t % 2]
            tk = ("lnt", t % 2)
            S.op("dve", lambda e: e.scalar_tensor_tensor(out=h, in0=src, scalar=cscale, in1=h, op0=ALU.mult, op1=ALU.add),
                 r=list(src_tok) + [("h", t)], w=[("h", t)])
            for a in range(2):
                S.op("dve", lambda e, a=a: e.bn_stats(out=st[0:P, a, :], in_=h_tok[0:P, t, a * 512:(a + 1) * 512]),
                     r=[("h", t)], w=[tk])
            S.op("dve", lambda e: e.bn_aggr(out=mv[0:P, 0, 0:2], in_=st[0:P, 0:2, :].rearrange("p a b -> p (a b)")),
                 r=[tk], w=[tk])
            S.op("dve", lambda e: e.tensor_scalar(out=mv[0:P, 0, 2:3], in0=mv[0:P, 0, 1:2], scalar1=eps_eff, scalar2=None,
                                                  op0=ALU.add), r=[tk], w=[tk])
            S.op("pool", lambda e: e.tensor_tensor(out=mv[0:P, 0, 3:4], in0=mv[0:P, 0, 2:3], in1=mhalf[0:P], op=ALU.pow),
                 r=[tk, "cst"], w=[tk])
            S.op("dve", lambda e: e.scalar_tensor_tensor(out=mv[0:P, 0, 4:5], in0=mv[0:P, 0, 0:1], scalar=-1.0,
                                                         in1=mv[0:P, 0, 3:4], op0=ALU.mult, op1=ALU.mult), r=[tk], w=[tk])
            S.op("act", lambda e: e.activation(out=h, in_=h, func=AF.Identity, scale=mv[0:P, 0, 3:4], bias=mv[0:P, 0, 4:5]),
                 r=[tk, ("h", t)], w=[("h", t)])
            S.op("dve", lambda e: e.tensor_tensor(out=h, in0=h, in1=lg_t[0:P], op=ALU.mult), r=[("h", t), "lng"], w=[("h", t)])
            S.op("dve", lambda e: e.tensor_tensor(out=h, in0=h, in1=lb_t[0:P], op=ALU.add), r=[("h", t), "lnb"], w=[("h", t)])

        w1t_holder = {}

        def ffn_phase(blk, which, ln_i, final):
            tiles = list(range(8)) + ([8] if blk == 0 else [])
            ntiles = [(0, 352), (352, 352), (704, 336)] if blk == 0 else [(0, 512), (512, 512)]
            S.barrier()
            A.top = phase_base
            actT = A.alloc([128, NJ, NCOL], BF16)
            wdn = A.alloc([128, NJ, D], BF16)
            if "t" not in w1t_holder:
                off = (A.top + 31) // 32 * 32
                assert off + 8 * 1792 * 2 <= SB_END
                w1t_holder["t"] = nc.alloc_sbuf_tensor_at("w1top", [128, 8, 1792], BF16, offset=off)
                w1t_holder["off"] = off
            wup = [A.alloc([128, 8, 2, 128], BF16) for _ in range(3)]
            sg = [A.alloc([128, 512], F32) for _ in range(2)]
            wu_v = wup_d[which].rearrange("(kc p) (two j c) -> p kc two j c", p=128, two=2, c=128)
            wd_v = wdn_d[which].rearrange("(j p) c -> p j c", p=128)

            def load_wup(j):
                for half in range(2):
                    S.op("pool", lambda e, half=half: e.dma_start(out=wup[j % 3][:, :, half, :], in_=wu_v[:, :, half, j, :]),
                         w=[("wup", j % 3)], dma=("wup", j % 3))
            load_ln(ln_i)
            load_wup(0)
            load_wup(1)
            cnt = 0
            for j in range(NJ):
                if j + 2 < NJ:
                    load_wup(j + 2)
                S.op("pool", lambda e, j=j: e.dma_start(out=wdn[:, j, :], in_=wd_v[:, j, :]), w=["wdn"], dma="wdn")
                for ni, (c0, n) in enumerate(ntiles):
                    bg, bu = cnt % 2, 2 + cnt % 2
                    sgt = sg[cnt % 2]
                    sgk = ("sg", cnt % 2)
                    cnt += 1
                    for half, bank in ((0, bg), (1, bu)):
                        for kc in range(8):
                            S.op("pe", lambda e, kc=kc, half=half, bank=bank, j=j, c0=c0, n=n: e.matmul(
                                ps[:, bank, 0:n], lhsT=wup[j % 3][:, kc, half, :], rhs=hT[:, kc, c0:c0 + n],
                                start=(kc == 0), stop=(kc == 7)),
                                r=[("wup", j % 3)] + hT_tok(c0, n), w=[("ps", bank)])
                    S.op("act", lambda e, bg=bg, n=n, sgt=sgt: e.activation(out=sgt[:, 0:n], in_=ps[:, bg, 0:n], func=AF.Silu),
                         r=[("ps", bg)], w=[sgk])
                    S.op("dve", lambda e, bu=bu, n=n, sgt=sgt, j=j, c0=c0: e.tensor_tensor(
                        out=actT[:, j, c0:c0 + n], in0=sgt[:, 0:n], in1=ps[:, bu, 0:n], op=ALU.mult),
                        r=[sgk, ("ps", bu)], w=[("actT", j, ni)])
            if which == 0:
                w1t = w1t_holder["t"]
                alias = [("wup", 0), ("wup", 1), ("wup", 2), ("sg", 0), ("sg", 1)]
                S.op("pool", lambda e: e.dma_start(out=w1t[:, :, 0:256],
                                                   in_=w_in[:, 512:768].rearrange("(kc p) c -> p kc c", p=128)),
                     w=alias + ["w1a"], dma="w1a")
                S.op("pool", lambda e: e.dma_start(out=w1t[:, :, 256:1792],
                                                   in_=w_in[:, 768:2304].rearrange("(kc p) c -> p kc c", p=128)),
                     w=alias + ["w1r"], dma="w1r")
            S.pe_sync = False
            pend = None
            for ti, t in enumerate(tiles):
                P, c0 = tile_geom(t)
                S.pe_sync = (t == 8)
                pb = (ti % 2) * 2
                nil = nis_of(ntiles, c0, P)
                for dh in range(2):
                    for j in range(NJ):
                        S.op("pe", lambda e, j=j, dh=dh, pb=pb, P=P, c0=c0: e.matmul(
                            ps[0:P, pb + dh, :], lhsT=actT[:, j, c0:c0 + P], rhs=wdn[:, j, dh * 512:(dh + 1) * 512],
                            start=(j == 0), stop=(j == NJ - 1)),
                            r=[("actT", j, i) for i in nil] + ["wdn"], w=[("ps", pb + dh)])
                S.pe_sync = False
                ln_elem(t, PS2(pb, P), [("ps", pb), ("ps", pb + 1)], 0.5 / ALPHA, LN_EPS / (ALPHA * ALPHA))
                if which == 0 and ti < 8:
                    kc = ti
                    for hh in range(4):
                        S.op("pool", lambda e, kc=kc, hh=hh: e.dma_start(
                            out=w1q[:, kc, hh * 128:(hh + 1) * 128].rearrange("p (g d) -> p g d", g=2),
                            in_=w_in[kc * 128:(kc + 1) * 128, 0:512].rearrange("p (g hh d) -> p hh g d", g=2, hh=4)[:, hh, :, :]),
                            w=["w1q"], dma="w1q")
                if final:
                    dst = yp[(blk * 8 + t) * 128:(blk * 8 + t + 1) * 128, :] if t < 8 else ys
                    S.op("sp", lambda e, t=t, P=P, dst=dst: e.dma_start(out=dst, in_=h_tok[0:P, t, :]),
                         r=[("h", t)], dma=("yout", t))
                else:
                    if debug and blk == 0:
                        S.op("sp", lambda e, t=t: e.dma_start(out=dbg["h1"][t], in_=h_tok[:, t, :]), r=[("h", t)],
                             dma=("dbg", t))
                    if pend is not None:
                        transpose_to_hT(pend[0], pend[1])
                    pend = (t, 4 + (ti % 2) * 2)
            if pend is not None and not final:
                transpose_to_hT(pend[0], pend[1])

        def mixer_phase(blk):
            tiles = list(range(8)) + ([8] if blk == 0 else [])
            ntiles = [(0, 352), (352, 352), (704, 336)] if blk == 0 else [(0, 512), (512, 512)]
            S.barrier()
            A.top = phase_base
            oaT = A.alloc([128, 4, NCOL], BF16)
            rgn = A.alloc([128, 9, 512], F32)
            b12_base = A.top
            rot = A.alloc([128, NROT], F32)
            w1t = w1t_holder["t"]
            S.op("sp", lambda e: e.dma_start(out=rot[:], in_=consts_d[:, NC1:NCONST]), w=["rot"], dma="rot")
            qaT = A.alloc([128, 4, NCOL], BF16)
            kaT = A.alloc([128, 128 + NCOL], BF16)
            vaug = A.alloc([128, 10, 2, 65], BF16)
            qs = A.alloc([128, 4, 128], F32)
            tA = A.alloc([128, 4, 128], F32)
            tB = A.alloc([128, 4, 128], F32)
            qh = A.alloc([128, 4, 128], F32)
            kh = A.alloc([128, 4, 128], F32)
            khb = A.alloc([128, 4, 128], BF16)
            vrb = A.alloc([128, 4, 128], BF16)
            qhT = A.alloc([128, 4, 128], BF16)
            khT = A.alloc([128, 4, 128], BF16)
            et = [A.alloc([128, 512], F32) for _ in range(2)]
            pT = [A.alloc([128, 2, 4, 128], BF16) for _ in range(2)]
            scm = A.alloc([128, 4, 128], BF16)
            o_n = A.alloc([128, 8, 64], F32)
            den = A.alloc([128, 8], F32)
            kvout = A.alloc([128, 256], F32)
            qhT32 = A.alloc([128, 4, NS], F32)
            vr32 = A.alloc([128, 4, 128], F32)
            assert A.top <= w1t_holder["off"], (A.top, w1t_holder["off"])

            S.op("dve", lambda e: e.memset(vaug[:, :, :, 64:65], 1.0), w=["vaug_ones"])
            if blk == 0:
                S.op("sp", lambda e: e.dma_start(out=kws[:, 0:127, :], in_=ck[:, 1:128, :]), w=["kws_a"], dma="kws_a")
                S.op("sp", lambda e: e.dma_start(out=vws[:, 0:127, :], in_=cv[:, 1:128, :]), w=["vws_a"], dma="vws_a")
            else:
                S.op("dve", lambda e: e.tensor_copy(out=kaT[:, 0:128], in_=kprev[:]), r=["kprev"], w=["kaT_prev"])
                S.op("dve", lambda e: e.tensor_copy(out=vaug[:, 0, :, :], in_=vprev[:]), r=["vprev", "vaug_ones"],
                     w=[("vaug", 0)])

            def b0_all():
              cnt = 0
              for c in range(5):
                    for ni, (c0, n) in enumerate(ntiles):
                        bank = 6 + cnt % 2
                        for kc in range(8):
                            S.op("pe", lambda e, c=c, kc=kc, c0=c0, n=n, bank=bank: e.matmul(
                                ps[:, bank, 0:n], lhsT=(w1q[:, kc, c * 128:(c + 1) * 128] if c < 4 else w1t[:, kc, 0:128]),
                                rhs=hT[:, kc, c0:c0 + n],
                                start=(kc == 0), stop=(kc == 7)), r=["w1q" if c < 4 else "w1a"] + hT_tok(c0, n), w=[("ps", bank)])
                        if c < 4:
                            dst, wt = qaT[:, c, c0:c0 + n], [("qaT", c, ni)]
                        else:
                            dst, wt = kaT[:, 128 + c0:128 + c0 + n], [("kaT", ni), "kaT_all"]
                        eng = "act" if cnt % 2 == 0 else "dve"
                        if eng == "act":
                            S.op("act", lambda e, dst=dst, bank=bank, n=n: e.copy(out=dst, in_=ps[:, bank, 0:n]),
                                 r=[("ps", bank)], w=wt)
                        else:
                            S.op("dve", lambda e, dst=dst, bank=bank, n=n: e.tensor_copy(out=dst, in_=ps[:, bank, 0:n]),
                                 r=[("ps", bank)], w=wt)
                        cnt += 1
            S.pe_sync = False
            qa_tok = lambda c0_, n_: [("qaT", c, i) for c in range(4) for i in nis_of(ntiles, c0_, n_)]

            def gn_store(t, P, heads):
                st, mv = stt[t % 2], mvt[t % 2]
                tk = ("lnt", t % 2)
                for h, (v, vt) in enumerate(heads):
                    S.op("dve", lambda e, h=h, v=v: e.bn_stats(out=st[0:P, h, :], in_=v), r=vt, w=[tk])
                for h in range(4):
                    S.op("dve", lambda e, h=h: e.bn_aggr(out=mv[0:P, h, 0:2], in_=st[0:P, h, :]), r=[tk], w=[tk])
                S.op("dve", lambda e: e.tensor_scalar(out=mv[0:P, :, 2:3], in0=mv[0:P, :, 1:2], scalar1=GN_EPS, scalar2=None,
                                                      op0=ALU.add), r=[tk], w=[tk])
                S.op("pool", lambda e: e.tensor_tensor(out=mv[0:P, :, 3:4], in0=mv[0:P, :, 2:3],
                                                       in1=mhalf[0:P].unsqueeze(1).to_broadcast([P, 4, 1]), op=ALU.pow),
                     r=[tk, "cst"], w=[tk])
                for h, (v, vt) in enumerate(heads):
                    S.op("dve", lambda e, h=h, v=v: e.tensor_scalar(
                        out=rgn[0:P, t, h * 128:(h + 1) * 128], in0=v, scalar1=mv[0:P, h, 0:1], scalar2=mv[0:P, h, 3:4],
                        op0=ALU.subtract, op1=ALU.mult), r=vt + [tk], w=[("rgn", t)])

            def attn_finish(t, P, c0):
                for g, bank in ((0, 0), (1, 3)):
                    ov = ps[0:P, bank, 0:260].rearrange("p (h d) -> p h d", d=65)
                    S.op("dve", lambda e, g=g, ov=ov: e.tensor_tensor(
                        out=den[0:P, g * 4:(g + 1) * 4], in0=ov[:, :, 64], in1=sinkexp[0:P, g * 4:(g + 1) * 4], op=ALU.add),
                        r=[("ps", bank), "sinkexp"], w=[("den", g)])
                    S.op("dve", lambda e, g=g: e.reciprocal(out=den[0:P, g * 4:(g + 1) * 4], in_=den[0:P, g * 4:(g + 1) * 4]),
                         r=[("den", g)], w=[("den", g)])
                    S.op("dve", lambda e, g=g, ov=ov: e.tensor_tensor(
                        out=o_n[0:P, g * 4:(g + 1) * 4, :], in0=ov[:, :, 0:64],
                        in1=den[0:P, g * 4:(g + 1) * 4].unsqueeze(2).to_broadcast([P, 4, 64]), op=ALU.mult),
                        r=[("ps", bank), ("den", g)], w=[("o_n", g)])
                for c in range(4):
                    S.op("pe", lambda e, c=c: e.transpose(out=ps[:, 4, c * 128:c * 128 + P],
                                                          in_=o_n[0:P, 2 * c:2 * c + 2, :].rearrange("p a b -> p (a b)"),
                                                          identity=ident[0:P, 0:P]),
                         r=[("o_n", c // 2), "cst"], w=[("ps", 4)])
                S.op("act", lambda e: e.copy(out=oaT[:, :, c0:c0 + P],
                                             in_=ps[:, 4, :].rearrange("p (a b) -> p a b", b=128)[:, :, 0:P]),
                     r=[("ps", 4)], w=[("oaT", t)])

            def b1_tile(t, part):
                P, c0 = tile_geom(t)
                gt = blk * 8 + t if t < 8 else 16
                smp = (t == 8)
                srow = 4 if smp else 0
                if part == "A":
                    for bank, (co, n) in enumerate(((512, 256), (768, 512), (1280, 512), (1792, 512))):
                        for kc in range(8):
                            S.op("pe", lambda e, kc=kc, bank=bank, co=co, n=n: e.matmul(
                                ps[0:P, bank, 0:n], lhsT=hT[:, kc, c0:c0 + P], rhs=w1t[:, kc, co - 512:co - 512 + n],
                                start=(kc == 0), stop=(kc == 7)), r=["w1a" if bank == 0 else "w1r", ("hT", t)], w=[("ps", bank)])
                    if smp and sub < 4.52:
                        return
                    slot = 1 + t
                    if not (smp and (KX & 1)):
                        S.op("act", lambda e, slot=slot: e.copy(out=vaug[0:P, slot, :, 0:64],
                                                                in_=ps[0:P, 0, 128:256].rearrange("p (g d) -> p g d", g=2)),
                             r=[("ps", 0)], w=[("vaug", slot)])
                    if (smp and not (KX & 2)) or (blk == 1 and t == 7):
                        S.op("act", lambda e: e.copy(out=kvout[0:P, :], in_=ps[0:P, 0, 0:256]), r=[("ps", 0)], w=["kvout"])
                        if smp and not (KX & 4):
                            S.op("sp", lambda e: e.dma_start(out=kws[:, 127, :], in_=kvout[0:NS, 0:128]), r=["kvout"],
                                 w=["kws_b"], dma="kws_b")
                            S.op("sp", lambda e: e.dma_start(out=vws[:, 127, :], in_=kvout[0:NS, 128:256]), r=["kvout"],
                                 w=["vws_b"], dma="vws_b")
                        elif not smp:
                            S.op("sp", lambda e: e.dma_start(out=kwp, in_=kvout[:, 0:128]), r=["kvout"], dma="kwp")
                            S.op("sp", lambda e: e.dma_start(out=vwp, in_=kvout[:, 128:256]), r=["kvout"], dma="vwp")
                    if not (smp and (KX & 8)):
                        S.op("act", lambda e: e.copy(out=vrb[0:P].rearrange("p a b -> p (a b)"), in_=ps[0:P, 3, :]),
                             r=[("ps", 3)], w=["vrb"])
                    if smp and not (KX & 16):
                        S.op("act", lambda e: e.copy(out=vr32[0:NS].rearrange("p a b -> p (a b)"), in_=ps[0:NS, 3, :]),
                             r=[("ps", 3)], w=["vr32"])
                    if smp and sub < 4.53:
                        return
                    cosv = rot[0:P, gt * 64:(gt + 1) * 64].unsqueeze(1).unsqueeze(1).to_broadcast([P, 4, 2, 64])
                    sinv = rot[0:P, 1088 + gt * 64:1088 + (gt + 1) * 64].unsqueeze(1).to_broadcast([P, 4, 64])
                    nsinv = rot[0:P, 2176 + gt * 64:2176 + (gt + 1) * 64].unsqueeze(1).to_broadcast([P, 4, 64])
                    for which, bank, dcol, dsth in (("q", 1, C_DQ, qh), ("k", 2, C_DK, kh)):
                        dv = cst[0:P, dcol + srow:dcol + srow + 4].unsqueeze(2).to_broadcast([P, 4, 128])
                        S.op("dve", lambda e, bank=bank, dv=dv: e.tensor_tensor(
                            out=qs[0:P], in0=ps[0:P, bank, :].rearrange("p (a b) -> p a b", b=128), in1=dv, op=ALU.mult),
                            r=[("ps", bank), "cst"], w=["qs"])
                        S.op("dve", lambda e: e.tensor_tensor(
                            out=tA[0:P].rearrange("p h (two d) -> p h two d", two=2),
                            in0=qs[0:P].rearrange("p h (two d) -> p h two d", two=2), in1=cosv, op=ALU.mult),
                            r=["qs", "rot"], w=["tA"])
                        S.op("dve", lambda e: e.tensor_tensor(out=tB[0:P, :, 0:64], in0=qs[0:P, :, 64:128], in1=nsinv, op=ALU.mult),
                             r=["qs", "rot"], w=["tB0"])
                        S.op("dve", lambda e: e.tensor_tensor(out=tB[0:P, :, 64:128], in0=qs[0:P, :, 0:64], in1=sinv, op=ALU.mult),
                             r=["qs", "rot"], w=["tB1"])
                        S.op("dve", lambda e, dsth=dsth: e.tensor_tensor(out=dsth[0:P], in0=tA[0:P], in1=tB[0:P], op=ALU.add),
                             r=["tA", "tB0", "tB1"], w=[which + "h"])
                    if smp and sub < 4.54:
                        return
                    S.op("act", lambda e: e.copy(out=khb[0:P], in_=kh[0:P]), r=["kh"], w=["khb"])
                    for which, src, bank, dstT in (("q", qh, 4, qhT), ("k", kh, 5, khT)):
                        for h in range(4):
                            S.op("pe", lambda e, h=h, src=src, bank=bank: e.transpose(
                                out=ps[:, bank, h * 128:h * 128 + P], in_=src[0:P, h, :], identity=ident[0:P, 0:P]),
                                r=[which + "h", "cst"], w=[("ps", bank)])
                        S.op("act", lambda e, bank=bank, dstT=dstT: e.copy(
                            out=dstT[:, :, 0:P], in_=ps[:, bank, :].rearrange("p (a b) -> p a b", b=128)[:, :, 0:P]),
                            r=[("ps", bank)], w=[which + "hT"])
                        if smp and which == "q":
                            S.op("act", lambda e, bank=bank: e.copy(
                                out=qhT32[:], in_=ps[:, bank, :].rearrange("p (a b) -> p a b", b=128)[:, :, 0:NS]),
                                r=[("ps", bank)], w=["qhT32"])

                    return
                if sub < 2:
                    return
                if not smp:
                    first = (blk == 0 and t == 0)
                    kbs = [1] if first else [0, 1]
                    sc_cnt = 0
                    for g in range(2):
                        for kb in kbs:
                            kcol = (t + kb) * 128
                            sbank = 6 + sc_cnt % 2
                            e_t = et[sc_cnt % 2]
                            ek = ("et", sc_cnt % 2)
                            sc_cnt += 1
                            ktok = ["kaT_prev"] if (kb == 0 and t == 0) else [("kaT", i) for i in nis_of(ntiles, kcol - 128, 128)]
                            S.op("pe", lambda e, g=g, kcol=kcol, sbank=sbank: e.matmul(
                                ps[:, sbank, :], lhsT=kaT[g * 64:(g + 1) * 64, kcol:kcol + 128],
                                rhs=qaT[g * 64:(g + 1) * 64, :, c0:c0 + 128], start=True, stop=True),
                                r=ktok + qa_tok(c0, 128), w=[("ps", sbank)])
                            S.op("dve", lambda e, g=g, kb=kb, sbank=sbank, e_t=e_t: e.scalar_tensor_tensor(
                                out=e_t[:], in0=ps[:, sbank, :], scalar=0.125,
                                in1=BT[:, kb, g * 4:(g + 1) * 4, :].rearrange("p a b -> p (a b)"), op0=ALU.mult, op1=ALU.add),
                                r=[("ps", sbank), "BT"], w=[ek])
                            S.op("act", lambda e, g=g, kb=kb, e_t=e_t: e.activation(
                                out=pT[g][:, kb, :, :].rearrange("p a b -> p (a b)"), in_=e_t[:], func=AF.Exp),
                                r=[ek], w=[("pT", g, kb)])
                        obank = 0 if g == 0 else 3
                        for hh in range(4):
                            for i, kb in enumerate(kbs):
                                vslot = t + kb
                                S.op("pe", lambda e, g=g, hh=hh, kb=kb, vslot=vslot, obank=obank, i=i: e.matmul(
                                    ps[:, obank, hh * 65:(hh + 1) * 65], lhsT=pT[g][:, kb, hh, :], rhs=vaug[:, vslot, g, :],
                                    start=(i == 0), stop=(i == len(kbs) - 1)),
                                    r=[("pT", g, kb), ("vaug", vslot), "vaug_ones"], w=[("ps", obank)])
                    attn_finish(t, P, c0)
                    if sub < 3:
                        return
                    for h in range(4):
                        S.op("pe", lambda e, h=h: e.matmul(ps[:, 1, h * 128:(h + 1) * 128], lhsT=khT[:, h, :], rhs=qhT[:, h, :],
                                                           start=True, stop=True), r=["khT", "qhT"], w=[("ps", 1)])
                    S.op("dve", lambda e: e.tensor_tensor(
                        out=scm[:], in0=ps[:, 1, :].rearrange("p (a b) -> p a b", b=128),
                        in1=caus.unsqueeze(1).to_broadcast([128, 4, 128]), op=ALU.mult), r=[("ps", 1), "cst"], w=["scm"])
                    for h in range(4):
                        S.op("pe", lambda e, h=h: e.matmul(ps[:, 2, h * 128:(h + 1) * 128], lhsT=scm[:, h, :], rhs=vrb[:, h, :],
                                                           start=True, stop=first), r=["scm", "vrb"], w=[("ps", 2)])
                        if not first:
                            S.op("pe", lambda e, h=h: e.matmul(ps[:, 2, h * 128:(h + 1) * 128], lhsT=qhT[:, h, :], rhs=Sb[:, h, :],
                                                               start=False, stop=True), r=["qhT", "Sb"], w=[("ps", 2)])
                    gn_store(t, P, [(ps[:, 2, h * 128:(h + 1) * 128], [("ps", 2)]) for h in range(4)])
                    for h in range(4):
                        S.op("pe", lambda e, h=h: e.matmul(ps[:, 5, h * 128:(h + 1) * 128], lhsT=khb[:, h, :], rhs=vrb[:, h, :],
                                                           start=True, stop=True), r=["khb", "vrb"], w=[("ps", 5)])
                    S.op("dve", lambda e: e.tensor_tensor(out=Sst[:].rearrange("p a b -> p (a b)"), in0=ps[:, 5, :],
                                                          in1=Sst[:].rearrange("p a b -> p (a b)"), op=ALU.add),
                         r=[("ps", 5), "S"], w=["S"])
                    S.op("dve", lambda e: e.tensor_tensor(out=Sst[:], in0=Sst[:],
                                                          in1=cst[:, C_GC:C_GC + 4].unsqueeze(2).to_broadcast([128, 4, 128]),
                                                          op=ALU.mult), r=["S", "cst"], w=["S"])
                    S.op("act", lambda e: e.copy(out=Sb[:], in_=Sst[:]), r=["S"], w=["Sb"])
                    if blk == 1 and t == 7:
                        S.op("sp", lambda e: e.dma_start(out=sp_out.rearrange("h k v -> k h v"), in_=Sst[:]), r=["S"], dma="spout")
                else:
                    if sub < 5:
                        return
                    A_save = A.top
                    Wk32 = A.alloc([128, NS, 128], F32)
                    WkT = A.alloc([128, NS, 128], BF16)
                    Wva = A.alloc([128, NS, 2, 65], BF16)
                    oTs = A.alloc([128, 8, NS], F32)
                    e_s = A.alloc([128, NS, 8], F32)
                    pTs = A.alloc([128, NS, 8], BF16)
                    Sp = [A.alloc([128, 4, 128], F32) for _ in range(2)]
                    Sn = [A.alloc([128, 4, 128], F32) for _ in range(2)]
                    Vm = [A.alloc([128, 4, 128], F32) for _ in range(2)]
                    QTm = A.alloc([128, 4, NS, NS], F32)
                    S.op("sp", lambda e: e.dma_start(out=Wk32[:], in_=kws.rearrange("b r c -> r b c")),
                         r=["kws_a", "kws_b"], w=["Wk32", "w1a", "w1r", "w1q"], dma="wk32")
                    for g in range(2):
                        S.op("pool", lambda e, g=g: e.dma_start(out=Wva[:, :, g, 0:64],
                                                                in_=vws[:, :, g * 64:(g + 1) * 64].rearrange("b r d -> r b d")),
                             r=["vws_a", "vws_b"], w=["Wva", "w1a", "w1r", "w1q"], dma="wva")
                    S.op("dve", lambda e: e.memset(Wva[:, :, :, 64:65], 1.0), w=["Wva1", "w1a", "w1r", "w1q"])
                    for b in range(NS):
                        bank = 6 + (b // 4) % 2
                        S.op("pe", lambda e, b=b, bank=bank: e.transpose(out=ps[:, bank, (b % 4) * 128:(b % 4 + 1) * 128],
                                                                        in_=Wk32[:, b, :], identity=ident),
                             r=["Wk32", "cst"], w=[("ps", bank)])
                        if b % 4 == 3:
                            S.op("act", lambda e, b=b, bank=bank: e.copy(
                                out=WkT[:, b - 3:b + 1, :], in_=ps[:, bank, :].rearrange("p (a b) -> p a b", b=128)),
                                r=[("ps", bank)], w=["WkT", "w1a", "w1r", "w1q"])
                    for b in range(NS):
                        for g in range(2):
                            S.op("pe", lambda e, b=b, g=g: e.matmul(
                                ps[:, 0, b * 8 + g * 4:b * 8 + g * 4 + 4], lhsT=WkT[g * 64:(g + 1) * 64, b, :],
                                rhs=qaT[g * 64:(g + 1) * 64, :, TB + b], start=True, stop=True),
                                r=["WkT"] + qa_tok(TB, NS), w=[("ps", 0)])
                    S.op("dve", lambda e: e.scalar_tensor_tensor(
                        out=e_s[:], in0=ps[:, 0, 0:128].rearrange("p (b h) -> p b h", h=8), scalar=0.125,
                        in1=bias_s[:].unsqueeze(1).to_broadcast([128, NS, 8]), op0=ALU.mult, op1=ALU.add),
                        r=[("ps", 0), "bias_s"], w=["e_s", "w1a", "w1r", "w1q"])
                    S.op("act", lambda e: e.activation(out=pTs[:], in_=e_s[:], func=AF.Exp), r=["e_s"], w=["pTs", "w1a", "w1r", "w1q"])
                    if sub < 5.2:
                        A.top = A_save
                        return
                    for b in range(NS):
                        for g in range(2):
                            S.op("pe", lambda e, b=b, g=g: e.matmul(
                                ps[0:65, 3, b * 8 + g * 4:b * 8 + g * 4 + 4], lhsT=Wva[:, b, g, :],
                                rhs=pTs[:, b, g * 4:(g + 1) * 4], start=True, stop=True),
                                r=["Wva", "Wva1", "pTs"], w=[("ps", 3)])
                    S.op("act", lambda e: e.copy(out=oTs[0:65], in_=ps[0:65, 3, 0:128].rearrange("p (b h) -> p h b", h=8)),
                         r=[("ps", 3)], w=["oTs", "w1a", "w1r", "w1q"])
                    for h in range(8):
                        bank = 0 if h < 4 else 3
                        S.op("pe", lambda e, h=h, bank=bank: e.transpose(
                            out=ps[0:NS, bank, (h % 4) * 65:(h % 4 + 1) * 65], in_=oTs[0:65, h, :], identity=ident[0:65, 0:65]),
                            r=["oTs", "cst"], w=[("ps", bank)])
                    attn_finish(t, P, c0)
                    if sub < 5.3:
                        A.top = A_save
                        return
                    S.op("dve", lambda e: e.tensor_tensor(
                        out=QTm[:], in0=qhT32[:].unsqueeze(3).to_broadcast([128, 4, NS, NS]),
                        in1=cst[:, C_EYER:C_EYER + 256].rearrange("p (a b) -> p a b", b=NS).unsqueeze(1).to_broadcast([128, 4, NS, NS]),
                        op=ALU.mult), r=["qhT32", "cst"], w=["QTm", "w1a", "w1r", "w1q"])
                    for b in range(NS):
                        i2 = b % 2
                        S.op("sp", lambda e, b=b, i2=i2: e.dma_start(out=Sp[i2][:], in_=st_in[b].rearrange("h k v -> k h v")),
                             w=[("Sp", i2), "w1a", "w1r", "w1q"], dma=("Sp", i2))
                        S.op("pool", lambda e, b=b, i2=i2: e.tensor_scalar(
                            out=Vm[i2][0:NS].rearrange("p a b -> p (a b)"), in0=vr32[0:NS].rearrange("p a b -> p (a b)"),
                            scalar1=ident[0:NS, b:b + 1], scalar2=None, op0=ALU.mult),
                            r=["vr32", "cst"], w=[("Vm", i2), "w1a", "w1r", "w1q"])
                        ub = 1 + i2
                        for h in range(4):
                            S.op("pe", lambda e, h=h, ub=ub, i2=i2: e.matmul(
                                ps[:, ub, h * 128:(h + 1) * 128], lhsT=kh[0:NS, h, :], rhs=Vm[i2][0:NS, h, :],
                                start=True, stop=True), r=["kh", ("Vm", i2)], w=[("ps", ub)])
                        for h in range(4):
                            S.op("dve", lambda e, h=h, ub=ub, i2=i2: e.scalar_tensor_tensor(
                                out=Sn[i2][:, h, :], in0=Sp[i2][:, h, :], scalar=float(GAMMAS[h]),
                                in1=ps[:, ub, h * 128:(h + 1) * 128], op0=ALU.mult, op1=ALU.add),
                                r=[("Sp", i2), ("ps", ub)], w=[("Sn", i2), "w1a", "w1r", "w1q"])
                        S.op("sp", lambda e, b=b, i2=i2: e.dma_start(out=ss_out[b].rearrange("h k v -> k h v"), in_=Sn[i2][:]),
                             r=[("Sn", i2)], dma=("ssout", i2))
                        for h in range(4):
                            if sub < 5.4:
                                break
                            S.op("pe", lambda e, h=h, b=b, i2=i2: e.matmul(
                                ps[0:NS, 4 + h, 0:128], lhsT=QTm[:, h, b, :], rhs=Sn[i2][:, h, :],
                                start=(b == 0), stop=(b == NS - 1)), r=["QTm", ("Sn", i2)], w=[("ps", 4 + h)])
                    if sub >= 5.4:
                        gn_store(t, P, [(ps[0:NS, 4 + h, 0:128], [("ps", 4 + h)]) for h in range(4)])
                    A.top = A_save

            if sub < 1:
                return
            for t in tiles:
                if t == 8 and (sub < 4.5 or (KX & 32)):
                    continue
                if t > 0 and sub < 4:
                    continue
                S.pe_sync = (t == 8)
                b1_tile(t, "A")
                S.pe_sync = False
                if t == 0:
                    b0_all()
                    S.pe_sync = False
                S.pe_sync = (t == 8)
                b1_tile(t, "B")
                S.pe_sync = False
            if sub < 6:
                return
            if blk == 0:
                S.op("dve", lambda e: e.tensor_copy(out=kprev[:], in_=kaT[:, TB:TB + 128]), r=[("kaT", i) for i in nis_of(ntiles, TB - 128, 128)], w=["kprev"])
                S.op("dve", lambda e: e.tensor_copy(out=vprev[:], in_=vaug[:, 8, :, :]), r=[("vaug", 8), "vaug_ones"],
                     w=["vprev"])

            S.barrier()
            A.top = b12_base
            w2 = A.alloc([128, 8, 2560], BF16)
            wao = A.alloc([128, 4, D], BF16)
            wro = A.alloc([128, 4, D], BF16)
            wo = A.alloc([128, 8, D], BF16)
            sgr = A.alloc([128, 512], F32)
            rr = A.alloc([128, 512], F32)
            rT = A.alloc([128, 4, 128], BF16)
            sa = A.alloc([128, D], F32)
            m1 = A.alloc([128, D], F32)
            mT = A.alloc([128, 8, 128], BF16)
            def ld_w2(lo, hi, tok):
                S.op("pool", lambda e: e.dma_start(out=w2[:, :, lo:hi],
                                                   in_=w_in[:, 2304 + lo:2304 + hi].rearrange("(kc p) c -> p kc c", p=128)),
                     w=[tok], dma=tok)
            ld_w2(0, 512, "w2a")
            S.op("pool", lambda e: e.dma_start(out=wao[:], in_=w_ao.rearrange("(kc p) c -> p kc c", p=128)), w=["wao"], dma="wao")
            S.op("pool", lambda e: e.dma_start(out=wro[:], in_=w_ro.rearrange("(kc p) c -> p kc c", p=128)), w=["wro"], dma="wro")
            ld_w2(512, 1536, "w2b")
            ld_w2(1536, 2560, "w2c")
            S.op("pool", lambda e: e.dma_start(out=wo[:], in_=w_o.rearrange("(kc p) c -> p kc c", p=128)), w=["wo"], dma="wo")
            load_ln(1)
            def b2_tile(t):
                P, c0 = tile_geom(t)
                for kc in range(8):
                    S.op("pe", lambda e, kc=kc: e.matmul(ps[0:P, 0, :], lhsT=hT[:, kc, c0:c0 + P], rhs=w2[:, kc, 0:512],
                                                         start=(kc == 0), stop=(kc == 7)), r=["w2a", ("hT", t)], w=[("ps", 0)])
                S.op("act", lambda e: e.activation(out=sgr[0:P], in_=ps[0:P, 0, :], func=AF.Silu), r=[("ps", 0)], w=["sgr"])
                S.op("dve", lambda e: e.tensor_tensor(out=rr[0:P], in0=sgr[0:P], in1=rgn[0:P, t, :], op=ALU.mult),
                     r=["sgr", ("rgn", t)], w=["rr"])
                for c in range(4):
                    S.op("pe", lambda e, c=c: e.transpose(out=ps[:, 1, c * 128:c * 128 + P], in_=rr[0:P, c * 128:(c + 1) * 128],
                                                          identity=ident[0:P, 0:P]), r=["rr", "cst"], w=[("ps", 1)])
                S.op("act", lambda e: e.copy(out=rT[:, :, 0:P], in_=ps[:, 1, :].rearrange("p (a b) -> p a b", b=128)[:, :, 0:P]),
                     r=[("ps", 1)], w=["rT"])
                if t == 8 and (KX & 64):
                    return
                for dh in range(2):
                    for c in range(4):
                        S.op("pe", lambda e, c=c, dh=dh: e.matmul(ps[0:P, 2 + dh, :], lhsT=oaT[:, c, c0:c0 + P],
                                                                  rhs=wao[:, c, dh * 512:(dh + 1) * 512],
                                                                  start=(c == 0), stop=(c == 3)),
                             r=["wao", ("oaT", t)], w=[("ps", 2 + dh)])
                for dh in range(2):
                    for c in range(4):
                        S.op("pe", lambda e, c=c, dh=dh: e.matmul(ps[0:P, 4 + dh, :], lhsT=rT[:, c, 0:P],
                                                                  rhs=wro[:, c, dh * 512:(dh + 1) * 512],
                                                                  start=(c == 0), stop=(c == 3)),
                             r=["wro", "rT"], w=[("ps", 4 + dh)])
                if t == 8 and (KX & 128):
                    return
                for gi, goff in enumerate((512, 1536)):
                    for dh in range(2):
                        for kc in range(8):
                            S.op("pe", lambda e, kc=kc, dh=dh, goff=goff: e.matmul(
                                ps[0:P, 6 + dh, :], lhsT=hT[:, kc, c0:c0 + P], rhs=w2[:, kc, goff + dh * 512:goff + (dh + 1) * 512],
                                start=(kc == 0), stop=(kc == 7)), r=["w2b" if gi == 0 else "w2c", ("hT", t)], w=[("ps", 6 + dh)])
                    S.op("act", lambda e: e.activation(out=sa[0:P], in_=PS2(6, P), func=AF.Tanh, scale=0.5),
                         r=[("ps", 6), ("ps", 7)], w=["sa"])
                    if gi == 0:
                        S.op("dve", lambda e: e.scalar_tensor_tensor(out=m1[0:P], in0=sa[0:P], scalar=1.0, in1=PS2(2, P),
                                                                     op0=ALU.add, op1=ALU.mult),
                             r=["sa", ("ps", 2), ("ps", 3)], w=["m1"])
                    else:
                        S.op("dve", lambda e: e.scalar_tensor_tensor(out=sa[0:P], in0=sa[0:P], scalar=1.0, in1=PS2(4, P),
                                                                     op0=ALU.add, op1=ALU.mult),
                             r=["sa", ("ps", 4), ("ps", 5)], w=["sa"])
                        S.op("dve", lambda e: e.tensor_tensor(out=m1[0:P], in0=m1[0:P], in1=sa[0:P], op=ALU.add),
                             r=["sa", "m1"], w=["m1"])
                if t == 8 and (KX & 256):
                    return
                for kc in range(8):
                    S.op("pe", lambda e, kc=kc: e.transpose(out=ps[:, kc // 4, (kc % 4) * 128:(kc % 4) * 128 + P],
                                                            in_=m1[0:P, kc * 128:(kc + 1) * 128], identity=ident[0:P, 0:P]),
                         r=["m1", "cst"], w=[("ps", kc // 4)])
                S.op("act", lambda e: e.copy(out=mT[:, :, 0:P],
                                             in_=ps[:, 0:2, :].rearrange("p a (b c) -> p (a b) c", c=128)[:, :, 0:P]),
                     r=[("ps", 0), ("ps", 1)], w=["mT"])
                for dh in range(2):
                    for kc in range(8):
                        S.op("pe", lambda e, kc=kc, dh=dh: e.matmul(ps[0:P, 2 + dh, :], lhsT=mT[:, kc, 0:P],
                                                                    rhs=wo[:, kc, dh * 512:(dh + 1) * 512],
                                                                    start=(kc == 0), stop=(kc == 7)),
                             r=["wo", "mT"], w=[("ps", 2 + dh)])
                ln_elem(t, PS2(2, P), [("ps", 2), ("ps", 3)], 0.5 / ALPHA, LN_EPS / (ALPHA * ALPHA))
                if debug and blk == 0:
                    S.op("sp", lambda e, t=t: e.dma_start(out=dbg["h2"][t], in_=h_tok[:, t, :]), r=[("h", t)], dma=("dbg2", t))

            pend = None
            for t in tiles:
                if t == 8 and (KX & 32):
                    continue
                S.pe_sync = (t == 8)
                b2_tile(t)
                S.pe_sync = False
                if pend is not None:
                    transpose_to_hT(pend, 6)
                pend = t
            transpose_to_hT(pend, 6)

        stage = 0
        for blk in range(2):
            tiles = list(range(8)) + ([8] if blk == 0 else [])
            if stage >= max_stage:
                break
            stage += 1
            for t in tiles:
                P, c0 = tile_geom(t)
                src = xp[(blk * 8 + t) * 128:(blk * 8 + t + 1) * 128, :] if t < 8 else xs
                S.op("sp", lambda e, t=t, P=P, src=src: e.dma_start(out=h_tok[0:P, t, :], in_=src), w=[("h", t)], dma=("x", t))
            for ti, t in enumerate(tiles):
                transpose_to_hT(t, 4 + (ti % 2) * 2)
            if stage >= max_stage:
                break
            stage += 1
            ffn_phase(blk, 0, 0, final=False)
            if stage >= max_stage:
                break
            stage += 1
            mixer_phase(blk)
            if stage >= max_stage:
                break
            stage += 1
            ffn_phase(blk, 1, 2, final=True)
        S.barrier()
        S.op("sp", lambda e: e.nop(), r=[], w=[])
        S.emit(nc, es)
        build_nc.info = dict(nops=len(S.ops), nsems=S.nsems, sbuf_peak=A.peak)
    return nc


_CACHE = {}
KX = 0


def kernel(x_prompt, x_sample, cache_k_win, cache_v_win, state_ret, rel_bias, w_in, attn_sinks,
           w_attn_out, w_ret_out, w_o, ffn1_w_up, ffn1_w_down, ffn2_w_up, ffn2_w_down,
           ln1_g, ln1_b, ln2_g, ln2_b, ln3_g, ln3_b, _debug=False):
    f = lambda a: np.ascontiguousarray(np.asarray(a, dtype=np.float32))
    import os as _os
    _ms = int(_os.environ.get("K_STAGES", "99"))
    _sub = float(_os.environ.get("K_SUB", "99"))
    global KX
    KX = int(_os.environ.get("K_X", "0"))
    key = ("nc", bool(_debug), _ms, _sub, KX, _os.environ.get("K_PAD", "0"))
    if key not in _CACHE:
        _CACHE[key] = build_nc(debug=_debug, max_stage=_ms, sub=_sub)
    nc = _CACHE[key]
    consts = make_consts()
    shared = dict(relb=f(rel_bias), w_in=f(w_in)[0], sinks=f(attn_sinks)[0], w_ao=f(w_attn_out)[0], w_ro=f(w_ret_out)[0],
                  w_o=f(w_o)[0], f1u=f(ffn1_w_up)[0], f2u=f(ffn2_w_up)[0], f1d=f(ffn1_w_down)[0], f2d=f(ffn2_w_down)[0],
                  ln1g=f(ln1_g)[0], ln1b=f(ln1_b)[0], ln2g=f(ln2_g)[0], ln2b=f(ln2_b)[0], ln3g=f(ln3_g)[0], ln3b=f(ln3_b)[0],
                  consts=consts)
    xp, xs = f(x_prompt), f(x_sample)
    ckf, cvf, stf = f(cache_k_win), f(cache_v_win), f(state_ret)
    in_maps = []
    for c in range(NCORES):
        m = dict(shared)
        sl = slice(c * NS, (c + 1) * NS)
        m["xp"] = xp[c]
        m["xs"] = xs[sl, 0, :]
        m["ck"] = ckf[0, sl].reshape(NS, 128, 128)
        m["cv"] = cvf[0, sl].reshape(NS, 128, 128)
        m["st"] = stf[0, sl]
        in_maps.append(m)
    _nco = int(_os.environ.get("K_CORES", str(NCORES)))
    res = run_bass_kernel_spmd(nc, in_maps[:_nco], core_ids=list(range(_nco)))
    R = list(res.results)
    while len(R) < NCORES:
        R.append({k: np.zeros_like(v) for k, v in R[0].items()})
    y_p = np.stack([R[c]["yp"] for c in range(NCORES)], 0)
    y_s = np.concatenate([R[c]["ys"] for c in range(NCORES)], 0).reshape(128, 1, D)
    kwp = np.stack([R[c]["kwp"] for c in range(NCORES)], 0).reshape(1, 8, 128, 2, 64)
    vwp = np.stack([R[c]["vwp"] for c in range(NCORES)], 0).reshape(1, 8, 128, 2, 64)
    spo = np.stack([R[c]["sp"] for c in range(NCORES)], 0).reshape(1, 8, 4, 128, 128)
    kws = np.concatenate([R[c]["kws"] for c in range(NCORES)], 0).reshape(1, 128, 128, 2, 64)
    vws = np.concatenate([R[c]["vws"] for c in range(NCORES)], 0).reshape(1, 128, 128, 2, 64)
    sso = np.concatenate([R[c]["ss"] for c in range(NCORES)], 0).reshape(1, 128, 4, 128, 128)
    outs = tuple(np.ascontiguousarray(a.astype(np.float32)) for a in (y_p, y_s, kwp, vwp, spo, kws, vws, sso))
    if _debug:
        kernel.dbg = [{k: v for k, v in R[c].items() if k.startswith("dbg")} for c in range(NCORES)]
    return outs
```

```python
import numpy as np
from contextlib import ExitStack
import concourse.bass as bass
import concourse.mybir as mybir
from concourse.bass_utils import run_bass_kernel_spmd

F32 = mybir.dt.float32
BF16 = mybir.dt.bfloat16
AF = mybir.ActivationFunctionType
ALU = mybir.AluOpType

NCORES = 8
D = 1024
SEQ = 2048
DFF = 2816
NJ = DFF // 128
NS = 16
TB = 1024
NCOL = TB + NS
PAST = 16384
ALPHA = 2.0 ** 0.25
LN_EPS = 1e-5
GN_EPS = 1e-6
MASKV = -30000.0
SB_BASE = 16512
SB_END = 229376

C_ID = 0
C_J = 128
C_CAUS = 256
C_DQ = 384
C_DK = C_DQ + 8
C_GC = C_DK + 8
C_EYER = C_GC + 4
C_OH1 = C_EYER + 256
C_OH2 = C_OH1 + 128
C_MH = C_OH2 + 128
C_ONE = C_MH + 1
NC1 = ((C_ONE + 1 + 7) // 8) * 8
C_COS = NC1
C_SIN = C_COS + 17 * 64
C_NSIN = C_SIN + 17 * 64
NROT = 3 * 17 * 64
NCONST = NC1 + NROT
GAMMAS = [1.0 - 2.0 ** (-5.0 - h) for h in range(4)]


def _bucket(d):
    d = np.asarray(d)
    n = np.maximum(d, 0)
    ratio = np.maximum(n, 1).astype(np.float32) / np.float32(16)
    large = 16 + (np.log(np.maximum(ratio, np.float32(1.0))).astype(np.float32)
                  / np.float32(np.log(128 / 16)) * np.float32(16)).astype(np.int32)
    large = np.minimum(large, 31)
    return np.where(n < 16, n, large)


def make_consts():
    c = np.zeros((128, NCONST), np.float32)
    c[:, C_ID:C_ID + 128] = np.eye(128, dtype=np.float32)
    c[:, C_J:C_J + 128] = np.eye(128, dtype=np.float32)[::-1]
    jj = np.arange(128)
    c[:, C_CAUS:C_CAUS + 128] = (jj[None, :] >= jj[:, None]).astype(np.float32)
    inv = (np.float32(10000.0) ** (-(np.arange(64, dtype=np.float32) / np.float32(64)))).astype(np.float32)
    for t in range(17):
        pos = (t * 128 + np.arange(128)) if t < 16 else np.full(128, PAST)
        ang = (pos.astype(np.float32)[:, None] * inv[None, :]).astype(np.float32)
        c[:, C_COS + t * 64:C_COS + (t + 1) * 64] = np.cos(ang.astype(np.float64)).astype(np.float32)
        c[:, C_SIN + t * 64:C_SIN + (t + 1) * 64] = np.sin(ang.astype(np.float64)).astype(np.float32)
        c[:, C_NSIN + t * 64:C_NSIN + (t + 1) * 64] = -np.sin(ang.astype(np.float64)).astype(np.float32)
    p = np.arange(128, dtype=np.float64)
    for h in range(4):
        lg = np.log1p(-(2.0 ** (-5.0 - h)))
        c[:, C_DQ + h] = np.exp((p + 1) * lg)
        c[:, C_DK + h] = (128.0 ** -0.5) * np.exp(-(p + 1) * lg)
        c[:, C_DQ + 4 + h] = 1.0
        c[:, C_DK + 4 + h] = 128.0 ** -0.5
        c[:, C_GC + h] = np.exp(128 * lg)
    c[:, C_EYER:C_EYER + 256] = np.eye(16, dtype=np.float32).reshape(1, 256)
    b1 = _bucket(np.arange(128))
    b2 = _bucket(127 - np.arange(128))
    for r in range(128):
        c[b1[r], C_OH1 + r] = 1.0
        c[b2[r], C_OH2 + r] = 1.0
    c[:, C_MH] = -0.5
    c[:, C_ONE] = 1.0
    return c


def _rnd_tile(x):
    return 32 if x <= 32 else (64 if x <= 64 else 128)


class _FakePE:
    def __init__(self):
        self.mode = None

    def matmul(self, out, lhsT=None, rhs=None, **kw):
        self.mode = (_rnd_tile(lhsT.shape[0]), _rnd_tile(int(np.prod(lhsT.shape[1:]))))

    def transpose(self, out=None, in_=None, identity=None):
        self.mode = (_rnd_tile(in_.shape[0]), _rnd_tile(int(np.prod(in_.shape[1:]))))


class Sched:
    ENGS = ("pe", "act", "dve", "pool", "sp")

    def __init__(self):
        self.ops = []
        self.last_w = {}
        self.readers = {}
        self.dma_cnt = {}
        self.bar = None
        self.bar_passed = set()
        self.last_eng = {}
        self.last_dma = {}
        self.pe_sync = False
        self.pe_sync_once = False
        self.pe_prev_small = False

    def _stream(self, idx):
        o = self.ops[idx]
        return ("dma", o["dma"]) if o["dma"] is not None else ("eng", o["eng"])

    def op(self, eng, fn, r=(), w=(), dma=None):
        idx = len(self.ops)
        deps = {}

        def add(d):
            s = self._stream(d)
            if deps.get(s, -1) < d:
                deps[s] = d
        for t in r:
            lw = self.last_w.get(t)
            if lw is not None:
                add(lw)
        for t in w:
            lw = self.last_w.get(t)
            if lw is not None:
                add(lw)
            for d in self.readers.get(t, {}).values():
                add(d)
        if self.bar is not None and eng not in self.bar_passed:
            for d in self.bar:
                add(d)
            self.bar_passed.add(eng)
        force = False
        if eng == "pe" and dma is None and (self.pe_sync or self.pe_sync_once):
            self.pe_sync_once = self.pe_sync
            if "pe" in self.last_eng:
                add(self.last_eng["pe"])
                force = True
        o = dict(eng=eng, fn=fn, deps=deps, dma=dma, need_inc=False, ev=None, force=force)
        if dma is not None:
            self.dma_cnt[dma] = self.dma_cnt.get(dma, 0) + 1
            o["ev"] = ("dma", dma, 16 * self.dma_cnt[dma])
            self.last_dma[dma] = idx
        else:
            self.last_eng[eng] = idx
        self.ops.append(o)
        me = ("dma", dma) if dma is not None else ("eng", eng)
        for t in r:
            self.readers.setdefault(t, {})[me] = idx
        for t in w:
            self.last_w[t] = idx
            self.readers[t] = {}
        return idx

    def barrier(self):
        self.bar = set(self.last_eng.values()) | set(self.last_dma.values())
        self.bar_passed = set()

    def finalize(self):
        ops = self.ops
        for o in ops:
            real = []
            for s, d in o["deps"].items():
                od = ops[d]
                if s == ("eng", "pe") and o["eng"] == "pe" and o["dma"] is None and not o["force"]:
                    continue
                real.append(d)
                if od["dma"] is None:
                    od["need_inc"] = True
            o["deps"] = real
        cnt = {e: 0 for e in self.ENGS}
        for o in ops:
            if o["dma"] is None and o["need_inc"]:
                cnt[o["eng"]] += 1
                o["ev"] = ("eng", o["eng"], cnt[o["eng"]])
        seen = {e: {} for e in self.ENGS}
        for o in ops:
            waits = {}
            for d in o["deps"]:
                kind, key, val = ops[d]["ev"]
                k = (kind, key)
                if seen[o["eng"]].get(k, 0) >= val:
                    continue
                waits[k] = max(waits.get(k, 0), val)
            for k, v in waits.items():
                seen[o["eng"]][k] = v
            o["waits"] = waits

    def emit(self, nc, es):
        self.finalize()
        sems = {}
        n = [0]

        def sem(k):
            if k not in sems:
                n[0] += 1
                sems[k] = es.enter_context(nc.semaphore("s%d" % n[0]))
            return sems[k]
        for o in self.ops:
            for k in o["waits"]:
                sem(k)
            if o["ev"] is not None:
                sem((o["ev"][0], o["ev"][1]))
        block = es.enter_context(nc.Block())
        ops = self.ops

        import os as _os2
        pad = int(_os2.environ.get("K_PAD", "0"))

        def run(engname, eng):
            for _ in range(pad):
                eng.nop()
            for o in ops:
                if o["eng"] != engname:
                    continue
                for k, v in o["waits"].items():
                    eng.wait_ge(sems[k], v)
                ins = o["fn"](eng)
                if o["dma"] is not None:
                    ins.then_inc(sems[("dma", o["dma"])], 16)
                elif o["need_inc"]:
                    ins.then_inc(sems[("eng", engname)], 1)

        @block.tensor
        def _(e):
            run("pe", e)

        @block.scalar
        def _(e):
            run("act", e)

        @block.vector
        def _(e):
            run("dve", e)

        @block.gpsimd
        def _(e):
            run("pool", e)

        @block.sync
        def _(e):
            run("sp", e)
        self.nsems = len(sems)


class Arena:
    def __init__(self, nc, base, end):
        self.nc, self.top, self.end = nc, base, end
        self.n = 0
        self.peak = base

    def alloc(self, shape, dtype):
        sz = int(np.prod(shape[1:])) * (2 if dtype == BF16 else 4)
        off = (self.top + 31) // 32 * 32
        assert off + sz <= self.end, ("SBUF overflow", off + sz, self.end)
        self.top = off + sz
        self.peak = max(self.peak, self.top)
        self.n += 1
        return self.nc.alloc_sbuf_tensor_at("sb%d" % self.n, list(shape), dtype, offset=off)


def build_nc(debug=False, max_stage=99, sub=99):
    nc = bass.Bass("TRN2", target_bir_lowering=False)

    def din(name, shape):
        return nc.dram_tensor(name, list(shape), F32, kind="ExternalInput").ap()

    def dout(name, shape):
        return nc.dram_tensor(name, list(shape), F32, kind="ExternalOutput").ap()
    xp = din("xp", [SEQ, D])
    xs = din("xs", [NS, D])
    ck = din("ck", [NS, 128, 128])
    cv = din("cv", [NS, 128, 128])
    st_in = din("st", [NS, 4, 128, 128])
    relb = din("relb", [32, 8])
    w_in = din("w_in", [D, 4864])
    sinks = din("sinks", [8])
    w_ao = din("w_ao", [512, D])
    w_ro = din("w_ro", [512, D])
    w_o = din("w_o", [D, D])
    wup_d = [din("f1u", [D, 2 * DFF]), din("f2u", [D, 2 * DFF])]
    wdn_d = [din("f1d", [DFF, D]), din("f2d", [DFF, D])]
    lng = [din("ln%dg" % i, [D]) for i in (1, 2, 3)]
    lnb = [din("ln%db" % i, [D]) for i in (1, 2, 3)]
    consts_d = din("consts", [128, NCONST])
    yp = dout("yp", [SEQ, D])
    ys = dout("ys", [NS, D])
    kwp = dout("kwp", [128, 128])
    vwp = dout("vwp", [128, 128])
    sp_out = dout("sp", [4, 128, 128])
    kws = dout("kws", [NS, 128, 128])
    vws = dout("vws", [NS, 128, 128])
    ss_out = dout("ss", [NS, 4, 128, 128])
    scr = nc.dram_tensor("scr", [2, 8, 256], F32, kind="Internal").ap()
    dbg = {}
    if debug:
        dbg["h1"] = dout("dbg_h1", [9, 128, D])
        dbg["h2"] = dout("dbg_h2", [9, 128, D])

    es = ExitStack()
    with es:
        S = Sched()
        A = Arena(nc, SB_BASE, SB_END)
        ps = es.enter_context(nc.psum_tensor("ps", [128, 8, 512], F32))

        h_tok = A.alloc([128, 9, D], F32)
        hT = A.alloc([128, 8, NCOL], BF16)
        cst = A.alloc([128, NC1], F32)
        w1q = A.alloc([128, 8, 512], BF16)
        BT = A.alloc([128, 2, 8, 128], F32)
        lg_t = A.alloc([128, D], F32)
        lb_t = A.alloc([128, D], F32)
        Sst = A.alloc([128, 4, 128], F32)
        Sb = A.alloc([128, 4, 128], BF16)
        bias_s = A.alloc([128, 8], F32)
        sinkexp = A.alloc([128, 8], F32)
        stt = [A.alloc([128, 4, 6], F32) for _ in range(2)]
        mvt = [A.alloc([128, 4, 8], F32) for _ in range(2)]
        kprev = A.alloc([128, 128], BF16)
        vprev = A.alloc([128, 2, 65], BF16)
        phase_base = A.top

        ident = cst[:, C_ID:C_ID + 128]
        Jm = cst[:, C_J:C_J + 128]
        caus = cst[:, C_CAUS:C_CAUS + 128]
        mhalf = cst[:, C_MH:C_MH + 1]

        def PSB(b, P=128, n=512):
            return ps[0:P, b, 0:n]

        def PS2(b, P=128):
            return ps[0:P, b:b + 2, :].rearrange("p a b -> p (a b)")

        S.op("sp", lambda e: e.dma_start(out=cst[:], in_=consts_d[:, 0:NC1]), w=["cst"], dma="cst")
        S.op("sp", lambda e: e.dma_start(out=sinkexp[:], in_=sinks.partition_broadcast(128)), w=["sinkexp"], dma="c2")
        S.op("act", lambda e: e.activation(out=sinkexp[:], in_=sinkexp[:], func=AF.Exp), r=["sinkexp"], w=["sinkexp"])
        S.op("dve", lambda e: e.memset(Sst[:], 0.0), w=["S"])
        S.op("dve", lambda e: e.memset(Sb[:], 0.0), w=["Sb"])
        A0 = A.top
        rb = A.alloc([32, 8], F32)
        TTs = A.alloc([8, 128], F32)
        Lsb = A.alloc([8, 2, 256], F32)
        G = A.alloc([128, 2, 8, 128], F32)
        S.op("sp", lambda e: e.dma_start(out=rb[:], in_=relb), w=["rb"], dma="c3")
        S.op("pe", lambda e: e.matmul(ps[0:8, 0, 0:128], lhsT=rb[:], rhs=cst[0:32, C_OH1:C_OH1 + 128], start=True, stop=True),
             r=["rb", "cst"], w=[("ps", 0)])
        S.op("act", lambda e: e.copy(out=TTs[:], in_=ps[0:8, 0, 0:128]), r=[("ps", 0)], w=["TTs"])
        S.op("dve", lambda e: e.memset(Lsb[:], MASKV), w=["Lsb"])
        S.op("dve", lambda e: e.tensor_copy(out=Lsb[:, 0, 0:127], in_=TTs[:, 1:128]), r=["TTs", "Lsb"], w=["Lsb"])
        S.op("dve", lambda e: e.tensor_copy(out=Lsb[:, 1, 127:255], in_=TTs[:, 0:128]), r=["TTs", "Lsb"], w=["Lsb"])
        S.op("sp", lambda e: e.dma_start(out=scr.rearrange("t h u -> h t u"), in_=Lsb[:]), r=["Lsb"], w=["scr"], dma="c4")
        for tab in range(2):
            hank = bass.AP(tensor=scr.tensor, offset=tab * 2048, ap=[[1, 128], [256, 8], [1, 128]])
            S.op("sp", lambda e, tab=tab, hank=hank: e.dma_start(out=G[:, tab, :, :], in_=hank), r=["scr"], w=["G"], dma="c5")
        for tab in range(2):
            for hf in range(2):
                b = 1 + tab * 2 + hf
                S.op("pe", lambda e, tab=tab, hf=hf, b=b: e.matmul(
                    ps[:, b, :], lhsT=Jm, rhs=G[:, tab, hf * 4:(hf + 1) * 4, :].rearrange("p a b -> p (a b)"),
                    start=True, stop=True), r=["cst", "G"], w=[("ps", b)])
                S.op("act", lambda e, tab=tab, hf=hf, b=b: e.copy(
                    out=BT[:, tab, hf * 4:(hf + 1) * 4, :].rearrange("p a b -> p (a b)"), in_=ps[:, b, :]),
                    r=[("ps", b)], w=["BT"])
        S.op("pe", lambda e: e.matmul(ps[:, 5, 0:8], lhsT=cst[0:32, C_OH2:C_OH2 + 128], rhs=rb[:], start=True, stop=True),
             r=["rb", "cst"], w=[("ps", 5)])
        S.op("act", lambda e: e.copy(out=bias_s[:], in_=ps[:, 5, 0:8]), r=[("ps", 5)], w=["bias_s"])
        A.top = A0

        def tile_geom(t):
            return (128, t * 128) if t < 8 else (NS, TB)

        def hT_tok(c0, n):
            toks = [("hT", t) for t in range(c0 // 128, min(8, (c0 + n + 127) // 128))]
            if c0 + n > TB:
                toks.append(("hT", 8))
            return toks

        def nis_of(ntiles, c0, n):
            return [i for i, (a, m) in enumerate(ntiles) if a < c0 + n and a + m > c0]

        def transpose_to_hT(t, pbank):
            P, c0 = tile_geom(t)
            S.pe_sync = (t == 8)
            for kc in range(8):
                S.op("pe", lambda e, kc=kc: e.transpose(out=ps[:, pbank + kc // 4, (kc % 4) * 128:(kc % 4) * 128 + P],
                                                        in_=h_tok[0:P, t, kc * 128:(kc + 1) * 128], identity=ident[0:P, 0:P]),
                     r=[("h", t), "cst"], w=[("ps", pbank + kc // 4)])
            S.op("act", lambda e: e.copy(out=hT[:, :, c0:c0 + P],
                                         in_=ps[:, pbank:pbank + 2, :].rearrange("p a (b c) -> p (a b) c", c=128)[:, :, 0:P]),
                 r=[("ps", pbank), ("ps", pbank + 1)], w=[("hT", t)])
            S.pe_sync = False

        def load_ln(i):
            S.op("sp", lambda e: e.dma_start(out=lg_t[:], in_=lng[i].partition_broadcast(128)), w=["lng"], dma="lng")
            S.op("sp", lambda e: e.dma_start(out=lb_t[:], in_=lnb[i].partition_broadcast(128)), w=["lnb"], dma="lnb")

        def ln_elem(t, src, src_tok, cscale, eps_eff):
            P, c0 = tile_geom(t)
            h = h_tok[0:P, t, :]
            st, mv = stt[t % 2], mvt[t % 2]
            tk = ("lnt", t % 2)
            S.op("dve", lambda e: e.scalar_tensor_tensor(out=h, in0=src, scalar=cscale, in1=h, op0=ALU.mult, op1=ALU.add),
                 r=list(src_tok) + [("h", t)], w=[("h", t)])
            for a in range(2):
                S.op("dve", lambda e, a=a: e.bn_stats(out=st[0:P, a, :], in_=h_tok[0:P, t, a * 512:(a + 1) * 512]),
                     r=[("h", t)], w=[tk])
            S.op("dve", lambda e: e.bn_aggr(out=mv[0:P, 0, 0:2], in_=st[0:P, 0:2, :].rearrange("p a b -> p (a b)")),
                 r=[tk], w=[tk])
            S.op("dve", lambda e: e.tensor_scalar(out=mv[0:P, 0, 2:3], in0=mv[0:P, 0, 1:2], scalar1=eps_eff, scalar2=None,
                                                  op0=ALU.add), r=[tk], w=[tk])
            S.op("pool", lambda e: e.tensor_tensor(out=mv[0:P, 0, 3:4], in0=mv[0:P, 0, 2:3], in1=mhalf[0:P], op=ALU.pow),
                 r=[tk, "cst"], w=[tk])
            S.op("dve", lambda e: e.scalar_tensor_tensor(out=mv[0:P, 0, 4:5], in0=mv[0:P, 0, 0:1], scalar=-1.0,
                                                         in1=mv[0:P, 0, 3:4], op0=ALU.mult, op1=ALU.mult), r=[tk], w=[tk])
            S.op("act", lambda e: e.activation(out=h, in_=h, func=AF.Identity, scale=mv[0:P, 0, 3:4], bias=mv[0:P, 0, 4:5]),
                 r=[tk, ("h", t)], w=[("h", t)])
            S.op("dve", lambda e: e.tensor_tensor(out=h, in0=h, in1=lg_t[0:P], op=ALU.mult), r=[("h", t), "lng"], w=[("h", t)])
            S.op("dve", lambda e: e.tensor_tensor(out=h, in0=h, in1=lb_t[0:P], op=ALU.add), r=[("h", t), "lnb"], w=[("h", t)])

        w1t_holder = {}

        def ffn_phase(blk, which, ln_i, final):
            tiles = list(range(8)) + ([8] if blk == 0 else [])
            ntiles = [(0, 352), (352, 352), (704, 336)] if blk == 0 else [(0, 512), (512, 512)]
            S.barrier()
            A.top = phase_base
            actT = A.alloc([128, NJ, NCOL], BF16)
            wdn = A.alloc([128, NJ, D], BF16)
            if "t" not in w1t_holder:
                off = (A.top + 31) // 32 * 32
                assert off + 8 * 1792 * 2 <= SB_END
                w1t_holder["t"] = nc.alloc_sbuf_tensor_at("w1top", [128, 8, 1792], BF16, offset=off)
                w1t_holder["off"] = off
            wup = [A.alloc([128, 8, 2, 128], BF16) for _ in range(3)]
            sg = [A.alloc([128, 512], F32) for _ in range(2)]
            wu_v = wup_d[which].rearrange("(kc p) (two j c) -> p kc two j c", p=128, two=2, c=128)
            wd_v = wdn_d[which].rearrange("(j p) c -> p j c", p=128)

            def load_wup(j):
                for half in range(2):
                    S.op("pool", lambda e, half=half: e.dma_start(out=wup[j % 3][:, :, half, :], in_=wu_v[:, :, half, j, :]),
                         w=[("wup", j % 3)], dma=("wup", j % 3))
            load_ln(ln_i)
            load_wup(0)
            load_wup(1)
            cnt = 0
            for j in range(NJ):
                if j + 2 < NJ:
                    load_wup(j + 2)
                S.op("pool", lambda e, j=j: e.dma_start(out=wdn[:, j, :], in_=wd_v[:, j, :]), w=["wdn"], dma="wdn")
                for ni, (c0, n) in enumerate(ntiles):
                    bg, bu = cnt % 2, 2 + cnt % 2
                    sgt = sg[cnt % 2]
                    sgk = ("sg", cnt % 2)
                    cnt += 1
                    for half, bank in ((0, bg), (1, bu)):
                        for kc in range(8):
                            S.op("pe", lambda e, kc=kc, half=half, bank=bank, j=j, c0=c0, n=n: e.matmul(
                                ps[:, bank, 0:n], lhsT=wup[j % 3][:, kc, half, :], rhs=hT[:, kc, c0:c0 + n],
                                start=(kc == 0), stop=(kc == 7)),
                                r=[("wup", j % 3)] + hT_tok(c0, n), w=[("ps", bank)])
                    S.op("act", lambda e, bg=bg, n=n, sgt=sgt: e.activation(out=sgt[:, 0:n], in_=ps[:, bg, 0:n], func=AF.Silu),
                         r=[("ps", bg)], w=[sgk])
                    S.op("dve", lambda e, bu=bu, n=n, sgt=sgt, j=j, c0=c0: e.tensor_tensor(
                        out=actT[:, j, c0:c0 + n], in0=sgt[:, 0:n], in1=ps[:, bu, 0:n], op=ALU.mult),
                        r=[sgk, ("ps", bu)], w=[("actT", j, ni)])
            if which == 0:
                w1t = w1t_holder["t"]
                alias = [("wup", 0), ("wup", 1), ("wup", 2), ("sg", 0), ("sg", 1)]
                S.op("pool", lambda e: e.dma_start(out=w1t[:, :, 0:256],
                                                   in_=w_in[:, 512:768].rearrange("(kc p) c -> p kc c", p=128)),
                     w=alias + ["w1a"], dma="w1a")
                S.op("pool", lambda e: e.dma_start(out=w1t[:, :, 256:1792],
                                                   in_=w_in[:, 768:2304].rearrange("(kc p) c -> p kc c", p=128)),
                     w=alias + ["w1r"], dma="w1r")
            S.pe_sync = False
            pend = None
            for ti, t in enumerate(tiles):
                P, c0 = tile_geom(t)
                S.pe_sync = (t == 8)
                pb = (ti % 2) * 2
                nil = nis_of(ntiles, c0, P)
                for dh in range(2):
                    for j in range(NJ):
                        S.op("pe", lambda e, j=j, dh=dh, pb=pb, P=P, c0=c0: e.matmul(
                            ps[0:P, pb + dh, :], lhsT=actT[:, j, c0:c0 + P], rhs=wdn[:, j, dh * 512:(dh + 1) * 512],
                            start=(j == 0), stop=(j == NJ - 1)),
                            r=[("actT", j, i) for i in nil] + ["wdn"], w=[("ps", pb + dh)])
                S.pe_sync = False
                ln_elem(t, PS2(pb, P), [("ps", pb), ("ps", pb + 1)], 0.5 / ALPHA, LN_EPS / (ALPHA * ALPHA))
                if which == 0 and ti < 8:
                    kc = ti
                    for hh in range(4):
                        S.op("pool", lambda e, kc=kc, hh=hh: e.dma_start(
                            out=w1q[:, kc, hh * 128:(hh + 1) * 128].rearrange("p (g d) -> p g d", g=2),
                            in_=w_in[kc * 128:(kc + 1) * 128, 0:512].rearrange("p (g hh d) -> p hh g d", g=2, hh=4)[:, hh, :, :]),
                            w=["w1q"], dma="w1q")
                if final:
                    dst = yp[(blk * 8 + t) * 128:(blk * 8 + t + 1) * 128, :] if t < 8 else ys
                    S.op("sp", lambda e, t=t, P=P, dst=dst: e.dma_start(out=dst, in_=h_tok[0:P, t, :]),
                         r=[("h", t)], dma=("yout", t))
                else:
                    if debug and blk == 0:
                        S.op("sp", lambda e, t=t: e.dma_start(out=dbg["h1"][t], in_=h_tok[:, t, :]), r=[("h", t)],
                             dma=("dbg", t))
                    if pend is not None:
                        transpose_to_hT(pend[0], pend[1])
                    pend = (t, 4 + (ti % 2) * 2)
            if pend is not None and not final:
                transpose_to_hT(pend[0], pend[1])

        def mixer_phase(blk):
            tiles = list(range(8)) + ([8] if blk == 0 else [])
            ntiles = [(0, 352), (352, 352), (704, 336)] if blk == 0 else [(0, 512), (512, 512)]
            S.barrier()
            A.top = phase_base
            oaT = A.alloc([128, 4, NCOL], BF16)
            rgn = A.alloc([128, 9, 512], F32)
            b12_base = A.top
            rot = A.alloc([128, NROT], F32)
            w1t = w1t_holder["t"]
            S.op("sp", lambda e: e.dma_start(out=rot[:], in_=consts_d[:, NC1:NCONST]), w=["rot"], dma="rot")
            qaT = A.alloc([128, 4, NCOL], BF16)
            kaT = A.alloc([128, 128 + NCOL], BF16)
            vaug = A.alloc([128, 10, 2, 65], BF16)
            qs = A.alloc([128, 4, 128], F32)
            tA = A.alloc([128, 4, 128], F32)
            tB = A.alloc([128, 4, 128], F32)
            qh = A.alloc([128, 4, 128], F32)
            kh = A.alloc([128, 4, 128], F32)
            khb = A.alloc([128, 4, 128], BF16)
            vrb = A.alloc([128, 4, 128], BF16)
            qhT = A.alloc([128, 4, 128], BF16)
            khT = A.alloc([128, 4, 128], BF16)
            et = [A.alloc([128, 512], F32) for _ in range(2)]
            pT = [A.alloc([128, 2, 4, 128], BF16) for _ in range(2)]
            scm = A.alloc([128, 4, 128], BF16)
            o_n = A.alloc([128, 8, 64], F32)
            den = A.alloc([128, 8], F32)
            kvout = A.alloc([128, 256], F32)
            qhT32 = A.alloc([128, 4, NS], F32)
            vr32 = A.alloc([128, 4, 128], F32)
            assert A.top <= w1t_holder["off"], (A.top, w1t_holder["off"])

            S.op("dve", lambda e: e.memset(vaug[:, :, :, 64:65], 1.0), w=["vaug_ones"])
            if blk == 0:
                S.op("sp", lambda e: e.dma_start(out=kws[:, 0:127, :], in_=ck[:, 1:128, :]), w=["kws_a"], dma="kws_a")
                S.op("sp", lambda e: e.dma_start(out=vws[:, 0:127, :], in_=cv[:, 1:128, :]), w=["vws_a"], dma="vws_a")
            else:
                S.op("dve", lambda e: e.tensor_copy(out=kaT[:, 0:128], in_=kprev[:]), r=["kprev"], w=["kaT_prev"])
                S.op("dve", lambda e: e.tensor_copy(out=vaug[:, 0, :, :], in_=vprev[:]), r=["vprev", "vaug_ones"],
                     w=[("vaug", 0)])

            def b0_all():
              cnt = 0
              for c in range(5):
                    for ni, (c0, n) in enumerate(ntiles):
                        bank = 6 + cnt % 2
                        for kc in range(8):
                            S.op("pe", lambda e, c=c, kc=kc, c0=c0, n=n, bank=bank: e.matmul(
                                ps[:, bank, 0:n], lhsT=(w1q[:, kc, c * 128:(c + 1) * 128] if c < 4 else w1t[:, kc, 0:128]),
                                rhs=hT[:, kc, c0:c0 + n],
                                start=(kc == 0), stop=(kc == 7)), r=["w1q" if c < 4 else "w1a"] + hT_tok(c0, n), w=[("ps", bank)])
                        if c < 4:
                            dst, wt = qaT[:, c, c0:c0 + n], [("qaT", c, ni)]
                        else:
                            dst, wt = kaT[:, 128 + c0:128 + c0 + n], [("kaT", ni), "kaT_all"]
                        eng = "act" if cnt % 2 == 0 else "dve"
                        if eng == "act":
                            S.op("act", lambda e, dst=dst, bank=bank, n=n: e.copy(out=dst, in_=ps[:, bank, 0:n]),
                                 r=[("ps", bank)], w=wt)
                        else:
                            S.op("dve", lambda e, dst=dst, bank=bank, n=n: e.tensor_copy(out=dst, in_=ps[:, bank, 0:n]),
                                 r=[("ps", bank)], w=wt)
                        cnt += 1
            S.pe_sync = False
            qa_tok = lambda c0_, n_: [("qaT", c, i) for c in range(4) for i in nis_of(ntiles, c0_, n_)]

            def gn_store(t, P, heads):
                st, mv = stt[t % 2], mvt[t % 2]
                tk = ("lnt", t % 2)
                for h, (v, vt) in enumerate(heads):
                    S.op("dve", lambda e, h=h, v=v: e.bn_stats(out=st[0:P, h, :], in_=v), r=vt, w=[tk])
                for h in range(4):
                    S.op("dve", lambda e, h=h: e.bn_aggr(out=mv[0:P, h, 0:2], in_=st[0:P, h, :]), r=[tk], w=[tk])
                S.op("dve", lambda e: e.tensor_scalar(out=mv[0:P, :, 2:3], in0=mv[0:P, :, 1:2], scalar1=GN_EPS, scalar2=None,
                                                      op0=ALU.add), r=[tk], w=[tk])
                S.op("pool", lambda e: e.tensor_tensor(out=mv[0:P, :, 3:4], in0=mv[0:P, :, 2:3],
                                                       in1=mhalf[0:P].unsqueeze(1).to_broadcast([P, 4, 1]), op=ALU.pow),
                     r=[tk, "cst"], w=[tk])
                for h, (v, vt) in enumerate(heads):
                    S.op("dve", lambda e, h=h, v=v: e.tensor_scalar(
                        out=rgn[0:P, t, h * 128:(h + 1) * 128], in0=v, scalar1=mv[0:P, h, 0:1], scalar2=mv[0:P, h, 3:4],
                        op0=ALU.subtract, op1=ALU.mult), r=vt + [tk], w=[("rgn", t)])

            def attn_finish(t, P, c0):
                for g, bank in ((0, 0), (1, 3)):
                    ov = ps[0:P, bank, 0:260].rearrange("p (h d) -> p h d", d=65)
                    S.op("dve", lambda e, g=g, ov=ov: e.tensor_tensor(
                        out=den[0:P, g * 4:(g + 1) * 4], in0=ov[:, :, 64], in1=sinkexp[0:P, g * 4:(g + 1) * 4], op=ALU.add),
                        r=[("ps", bank), "sinkexp"], w=[("den", g)])
                    S.op("dve", lambda e, g=g: e.reciprocal(out=den[0:P, g * 4:(g + 1) * 4], in_=den[0:P, g * 4:(g + 1) * 4]),
                         r=[("den", g)], w=[("den", g)])
                    S.op("dve", lambda e, g=g, ov=ov: e.tensor_tensor(
                        out=o_n[0:P, g * 4:(g + 1) * 4, :], in0=ov[:, :, 0:64],
                        in1=den[0:P, g * 4:(g + 1) * 4].unsqueeze(2).to_broadcast([P, 4, 64]), op=ALU.mult),
                        r=[("ps", bank), ("den", g)], w=[("o_n", g)])
                for c in range(4):
                    S.op("pe", lambda e, c=c: e.transpose(out=ps[:, 4, c * 128:c * 128 + P],
                                                          in_=o_n[0:P, 2 * c:2 * c + 2, :].rearrange("p a b -> p (a b)"),
                                                          identity=ident[0:P, 0:P]),
                         r=[("o_n", c // 2), "cst"], w=[("ps", 4)])
                S.op("act", lambda e: e.copy(out=oaT[:, :, c0:c0 + P],
                                             in_=ps[:, 4, :].rearrange("p (a b) -> p a b", b=128)[:, :, 0:P]),
                     r=[("ps", 4)], w=[("oaT", t)])

            def b1_tile(t, part):
                P, c0 = tile_geom(t)
                gt = blk * 8 + t if t < 8 else 16
                smp = (t == 8)
                srow = 4 if smp else 0
                if part == "A":
                    for bank, (co, n) in enumerate(((512, 256), (768, 512), (1280, 512), (1792, 512))):
                        for kc in range(8):
                            S.op("pe", lambda e, kc=kc, bank=bank, co=co, n=n: e.matmul(
                                ps[0:P, bank, 0:n], lhsT=hT[:, kc, c0:c0 + P], rhs=w1t[:, kc, co - 512:co - 512 + n],
                                start=(kc == 0), stop=(kc == 7)), r=["w1a" if bank == 0 else "w1r", ("hT", t)], w=[("ps", bank)])
                    if smp and sub < 4.52:
                        return
                    slot = 1 + t
                    if not (smp and (KX & 1)):
                        S.op("act", lambda e, slot=slot: e.copy(out=vaug[0:P, slot, :, 0:64],
                                                                in_=ps[0:P, 0, 128:256].rearrange("p (g d) -> p g d", g=2)),
                             r=[("ps", 0)], w=[("vaug", slot)])
                    if (smp and not (KX & 2)) or (blk == 1 and t == 7):
                        S.op("act", lambda e: e.copy(out=kvout[0:P, :], in_=ps[0:P, 0, 0:256]), r=[("ps", 0)], w=["kvout"])
                        if smp and not (KX & 4):
                            S.op("sp", lambda e: e.dma_start(out=kws[:, 127, :], in_=kvout[0:NS, 0:128]), r=["kvout"],
                                 w=["kws_b"], dma="kws_b")
                            S.op("sp", lambda e: e.dma_start(out=vws[:, 127, :], in_=kvout[0:NS, 128:256]), r=["kvout"],
                                 w=["vws_b"], dma="vws_b")
                        elif not smp:
                            S.op("sp", lambda e: e.dma_start(out=kwp, in_=kvout[:, 0:128]), r=["kvout"], dma="kwp")
                            S.op("sp", lambda e: e.dma_start(out=vwp, in_=kvout[:, 128:256]), r=["kvout"], dma="vwp")
                    if not (smp and (KX & 8)):
                        S.op("act", lambda e: e.copy(out=vrb[0:P].rearrange("p a b -> p (a b)"), in_=ps[0:P, 3, :]),
                             r=[("ps", 3)], w=["vrb"])
                    if smp and not (KX & 16):
                        S.op("act", lambda e: e.copy(out=vr32[0:NS].rearrange("p a b -> p (a b)"), in_=ps[0:NS, 3, :]),
                             r=[("ps", 3)], w=["vr32"])
                    if smp and sub < 4.53:
                        return
                    cosv = rot[0:P, gt * 64:(gt + 1) * 64].unsqueeze(1).unsqueeze(1).to_broadcast([P, 4, 2, 64])
                    sinv = rot[0:P, 1088 + gt * 64:1088 + (gt + 1) * 64].unsqueeze(1).to_broadcast([P, 4, 64])
                    nsinv = rot[0:P, 2176 + gt * 64:2176 + (gt + 1) * 64].unsqueeze(1).to_broadcast([P, 4, 64])
                    for which, bank, dcol, dsth in (("q", 1, C_DQ, qh), ("k", 2, C_DK, kh)):
                        dv = cst[0:P, dcol + srow:dcol + srow + 4].unsqueeze(2).to_broadcast([P, 4, 128])
                        S.op("dve", lambda e, bank=bank, dv=dv: e.tensor_tensor(
                            out=qs[0:P], in0=ps[0:P, bank, :].rearrange("p (a b) -> p a b", b=128), in1=dv, op=ALU.mult),
                            r=[("ps", bank), "cst"], w=["qs"])
                        S.op("dve", lambda e: e.tensor_tensor(
                            out=tA[0:P].rearrange("p h (two d) -> p h two d", two=2),
                            in0=qs[0:P].rearrange("p h (two d) -> p h two d", two=2), in1=cosv, op=ALU.mult),
                            r=["qs", "rot"], w=["tA"])
                        S.op("dve", lambda e: e.tensor_tensor(out=tB[0:P, :, 0:64], in0=qs[0:P, :, 64:128], in1=nsinv, op=ALU.mult),
                             r=["qs", "rot"], w=["tB0"])
                        S.op("dve", lambda e: e.tensor_tensor(out=tB[0:P, :, 64:128], in0=qs[0:P, :, 0:64], in1=sinv, op=ALU.mult),
                             r=["qs", "rot"], w=["tB1"])
                        S.op("dve", lambda e, dsth=dsth: e.tensor_tensor(out=dsth[0:P], in0=tA[0:P], in1=tB[0:P], op=ALU.add),
                             r=["tA", "tB0", "tB1"], w=[which + "h"])
                    if smp and sub < 4.54:
                        return
                    S.op("act", lambda e: e.copy(out=khb[0:P], in_=kh[0:P]), r=["kh"], w=["khb"])
                    for which, src, bank, dstT in (("q", qh, 4, qhT), ("k", kh, 5, khT)):
                        for h in range(4):
                            S.op("pe", lambda e, h=h, src=src, bank=bank: e.transpose(
                                out=ps[:, bank, h * 128:h * 128 + P], in_=src[0:P, h, :], identity=ident[0:P, 0:P]),
                                r=[which + "h", "cst"], w=[("ps", bank)])
                        S.op("act", lambda e, bank=bank, dstT=dstT: e.copy(
                            out=dstT[:, :, 0:P], in_=ps[:, bank, :].rearrange("p (a b) -> p a b", b=128)[:, :, 0:P]),
                            r=[("ps", bank)], w=[which + "hT"])
                        if smp and which == "q":
                            S.op("act", lambda e, bank=bank: e.copy(
                                out=qhT32[:], in_=ps[:, bank, :].rearrange("p (a b) -> p a b", b=128)[:, :, 0:NS]),
                                r=[("ps", bank)], w=["qhT32"])

                    return
                if sub < 2:
                    return
                if not smp:
                    first = (blk == 0 and t == 0)
                    kbs = [1] if first else [0, 1]
                    sc_cnt = 0
                    for g in range(2):
                        for kb in kbs:
                            kcol = (t + kb) * 128
                            sbank = 6 + sc_cnt % 2
                            e_t = et[sc_cnt % 2]
                            ek = ("et", sc_cnt % 2)
                            sc_cnt += 1
                            ktok = ["kaT_prev"] if (kb == 0 and t == 0) else [("kaT", i) for i in nis_of(ntiles, kcol - 128, 128)]
                            S.op("pe", lambda e, g=g, kcol=kcol, sbank=sbank: e.matmul(
                                ps[:, sbank, :], lhsT=kaT[g * 64:(g + 1) * 64, kcol:kcol + 128],
                                rhs=qaT[g * 64:(g + 1) * 64, :, c0:c0 + 128], start=True, stop=True),
                                r=ktok + qa_tok(c0, 128), w=[("ps", sbank)])
                            S.op("dve", lambda e, g=g, kb=kb, sbank=sbank, e_t=e_t: e.scalar_tensor_tensor(
                                out=e_t[:], in0=ps[:, sbank, :], scalar=0.125,
                                in1=BT[:, kb, g * 4:(g + 1) * 4, :].rearrange("p a b -> p (a b)"), op0=ALU.mult, op1=ALU.add),
                                r=[("ps", sbank), "BT"], w=[ek])
                            S.op("act", lambda e, g=g, kb=kb, e_t=e_t: e.activation(
                                out=pT[g][:, kb, :, :].rearrange("p a b -> p (a b)"), in_=e_t[:], func=AF.Exp),
                                r=[ek], w=[("pT", g, kb)])
                        obank = 0 if g == 0 else 3
                        for hh in range(4):
                            for i, kb in enumerate(kbs):
                                vslot = t + kb
                                S.op("pe", lambda e, g=g, hh=hh, kb=kb, vslot=vslot, obank=obank, i=i: e.matmul(
                                    ps[:, obank, hh * 65:(hh + 1) * 65], lhsT=pT[g][:, kb, hh, :], rhs=vaug[:, vslot, g, :],
                                    start=(i == 0), stop=(i == len(kbs) - 1)),
                                    r=[("pT", g, kb), ("vaug", vslot), "vaug_ones"], w=[("ps", obank)])
                    attn_finish(t, P, c0)
                    if sub < 3:
                        return
                    for h in range(4):
                        S.op("pe", lambda e, h=h: e.matmul(ps[:, 1, h * 128:(h + 1) * 128], lhsT=khT[:, h, :], rhs=qhT[:, h, :],
                                                           start=True, stop=True), r=["khT", "qhT"], w=[("ps", 1)])
                    S.op("dve", lambda e: e.tensor_tensor(
                        out=scm[:], in0=ps[:, 1, :].rearrange("p (a b) -> p a b", b=128),
                        in1=caus.unsqueeze(1).to_broadcast([128, 4, 128]), op=ALU.mult), r=[("ps", 1), "cst"], w=["scm"])
                    for h in range(4):
                        S.op("pe", lambda e, h=h: e.matmul(ps[:, 2, h * 128:(h + 1) * 128], lhsT=scm[:, h, :], rhs=vrb[:, h, :],
                                                           start=True, stop=first), r=["scm", "vrb"], w=[("ps", 2)])
                        if not first:
                            S.op("pe", lambda e, h=h: e.matmul(ps[:, 2, h * 128:(h + 1) * 128], lhsT=qhT[:, h, :], rhs=Sb[:, h, :],
                                                               start=False, stop=True), r=["qhT", "Sb"], w=[("ps", 2)])
                    gn_store(t, P, [(ps[:, 2, h * 128:(h + 1) * 128], [("ps", 2)]) for h in range(4)])
                    for h in range(4):
                        S.op("pe", lambda e, h=h: e.matmul(ps[:, 5, h * 128:(h + 1) * 128], lhsT=khb[:, h, :], rhs=vrb[:, h, :],
                                                           start=True, stop=True), r=["khb", "vrb"], w=[("ps", 5)])
                    S.op("dve", lambda e: e.tensor_tensor(out=Sst[:].rearrange("p a b -> p (a b)"), in0=ps[:, 5, :],
                                                          in1=Sst[:].rearrange("p a b -> p (a b)"), op=ALU.add),
                         r=[("ps", 5), "S"], w=["S"])
                    S.op("dve", lambda e: e.tensor_tensor(out=Sst[:], in0=Sst[:],
                                                          in1=cst[:, C_GC:C_GC + 4].unsqueeze(2).to_broadcast([128, 4, 128]),
                                                          op=ALU.mult), r=["S", "cst"], w=["S"])
                    S.op("act", lambda e: e.copy(out=Sb[:], in_=Sst[:]), r=["S"], w=["Sb"])
                    if blk == 1 and t == 7:
                        S.op("sp", lambda e: e.dma_start(out=sp_out.rearrange("h k v -> k h v"), in_=Sst[:]), r=["S"], dma="spout")
                else:
                    if sub < 5:
                        return
                    A_save = A.top
                    Wk32 = A.alloc([128, NS, 128], F32)
                    WkT = A.alloc([128, NS, 128], BF16)
                    Wva = A.alloc([128, NS, 2, 65], BF16)
                    oTs = A.alloc([128, 8, NS], F32)
                    e_s = A.alloc([128, NS, 8], F32)
                    pTs = A.alloc([128, NS, 8], BF16)
                    Sp = [A.alloc([128, 4, 128], F32) for _ in range(2)]
                    Sn = [A.alloc([128, 4, 128], F32) for _ in range(2)]
                    Vm = [A.alloc([128, 4, 128], F32) for _ in range(2)]
                    QTm = A.alloc([128, 4, NS, NS], F32)
                    S.op("sp", lambda e: e.dma_start(out=Wk32[:], in_=kws.rearrange("b r c -> r b c")),
                         r=["kws_a", "kws_b"], w=["Wk32", "w1a", "w1r", "w1q"], dma="wk32")
                    for g in range(2):
                        S.op("pool", lambda e, g=g: e.dma_start(out=Wva[:, :, g, 0:64],
                                                                in_=vws[:, :, g * 64:(g + 1) * 64].rearrange("b r d -> r b d")),
                             r=["vws_a", "vws_b"], w=["Wva", "w1a", "w1r", "w1q"], dma="wva")
                    S.op("dve", lambda e: e.memset(Wva[:, :, :, 64:65], 1.0), w=["Wva1", "w1a", "w1r", "w1q"])
                    for b in range(NS):
                        bank = 6 + (b // 4) % 2
                        S.op("pe", lambda e, b=b, bank=bank: e.transpose(out=ps[:, bank, (b % 4) * 128:(b % 4 + 1) * 128],
                                                                        in_=Wk32[:, b, :], identity=ident),
                             r=["Wk32", "cst"], w=[("ps", bank)])
                        if b % 4 == 3:
                            S.op("act", lambda e, b=b, bank=bank: e.copy(
                                out=WkT[:, b - 3:b + 1, :], in_=ps[:, bank, :].rearrange("p (a b) -> p a b", b=128)),
                                r=[("ps", bank)], w=["WkT", "w1a", "w1r", "w1q"])
                    for b in range(NS):
                        for g in range(2):
                            S.op("pe", lambda e, b=b, g=g: e.matmul(
                                ps[:, 0, b * 8 + g * 4:b * 8 + g * 4 + 4], lhsT=WkT[g * 64:(g + 1) * 64, b, :],
                                rhs=qaT[g * 64:(g + 1) * 64, :, TB + b], start=True, stop=True),
                                r=["WkT"] + qa_tok(TB, NS), w=[("ps", 0)])
                    S.op("dve", lambda e: e.scalar_tensor_tensor(
                        out=e_s[:], in0=ps[:, 0, 0:128].rearrange("p (b h) -> p b h", h=8), scalar=0.125,
                        in1=bias_s[:].unsqueeze(1).to_broadcast([128, NS, 8]), op0=ALU.mult, op1=ALU.add),
                        r=[("ps", 0), "bias_s"], w=["e_s", "w1a", "w1r", "w1q"])
                    S.op("act", lambda e: e.activation(out=pTs[:], in_=e_s[:], func=AF.Exp), r=["e_s"], w=["pTs", "w1a", "w1r", "w1q"])
                    if sub < 5.2:
                        A.top = A_save
                        return
                    for b in range(NS):
                        for g in range(2):
                            S.op("pe", lambda e, b=b, g=g: e.matmul(
                                ps[0:65, 3, b * 8 + g * 4:b * 8 + g * 4 + 4], lhsT=Wva[:, b, g, :],
                                rhs=pTs[:, b, g * 4:(g + 1) * 4], start=True, stop=True),
                                r=["Wva", "Wva1", "pTs"], w=[("ps", 3)])
                    S.op("act", lambda e: e.copy(out=oTs[0:65], in_=ps[0:65, 3, 0:128].rearrange("p (b h) -> p h b", h=8)),
                         r=[("ps", 3)], w=["oTs", "w1a", "w1r", "w1q"])
                    for h in range(8):
                        bank = 0 if h < 4 else 3
                        S.op("pe", lambda e, h=h, bank=bank: e.transpose(
                            out=ps[0:NS, bank, (h % 4) * 65:(h % 4 + 1) * 65], in_=oTs[0:65, h, :], identity=ident[0:65, 0:65]),
                            r=["oTs", "cst"], w=[("ps", bank)])
                    attn_finish(t, P, c0)
                    if sub < 5.3:
                        A.top = A_save
                        return
                    S.op("dve", lambda e: e.tensor_tensor(
                        out=QTm[:], in0=qhT32[:].unsqueeze(3).to_broadcast([128, 4, NS, NS]),
                        in1=cst[:, C_EYER:C_EYER + 256].rearrange("p (a b) -> p a b", b=NS).unsqueeze(1).to_broadcast([128, 4, NS, NS]),
                        op=ALU.mult), r=["qhT32", "cst"], w=["QTm", "w1a", "w1r", "w1q"])
                    for b in range(NS):
                        i2 = b % 2
                        S.op("sp", lambda e, b=b, i2=i2: e.dma_start(out=Sp[i2][:], in_=st_in[b].rearrange("h k v -> k h v")),
                             w=[("Sp", i2), "w1a", "w1r", "w1q"], dma=("Sp", i2))
                        S.op("pool", lambda e, b=b, i2=i2: e.tensor_scalar(
                            out=Vm[i2][0:NS].rearrange("p a b -> p (a b)"), in0=vr32[0:NS].rearrange("p a b -> p (a b)"),
                            scalar1=ident[0:NS, b:b + 1], scalar2=None, op0=ALU.mult),
                            r=["vr32", "cst"], w=[("Vm", i2), "w1a", "w1r", "w1q"])
                        ub = 1 + i2
                        for h in range(4):
                            S.op("pe", lambda e, h=h, ub=ub, i2=i2: e.matmul(
                                ps[:, ub, h * 128:(h + 1) * 128], lhsT=kh[0:NS, h, :], rhs=Vm[i2][0:NS, h, :],
                                start=True, stop=True), r=["kh", ("Vm", i2)], w=[("ps", ub)])
                        for h in range(4):
                            S.op("dve", lambda e, h=h, ub=ub, i2=i2: e.scalar_tensor_tensor(
                                out=Sn[i2][:, h, :], in0=Sp[i2][:, h, :], scalar=float(GAMMAS[h]),
                                in1=ps[:, ub, h * 128:(h + 1) * 128], op0=ALU.mult, op1=ALU.add),
                                r=[("Sp", i2), ("ps", ub)], w=[("Sn", i2), "w1a", "w1r", "w1q"])
                        S.op("sp", lambda e, b=b, i2=i2: e.dma_start(out=ss_out[b].rearrange("h k v -> k h v"), in_=Sn[i2][:]),
                             r=[("Sn", i2)], dma=("ssout", i2))
                        for h in range(4):
                            if sub < 5.4:
                                break
                            S.op("pe", lambda e, h=h, b=b, i2=i2: e.matmul(
                                ps[0:NS, 4 + h, 0:128], lhsT=QTm[:, h, b, :], rhs=Sn[i2][:, h, :],
                                start=(b == 0), stop=(b == NS - 1)), r=["QTm", ("Sn", i2)], w=[("ps", 4 + h)])
                    if sub >= 5.4:
                        gn_store(t, P, [(ps[0:NS, 4 + h, 0:128], [("ps", 4 + h)]) for h in range(4)])
                    A.top = A_save

            if sub < 1:
                return
            for t in tiles:
                if t == 8 and (sub < 4.5 or (KX & 32)):
                    continue
                if t > 0 and sub < 4:
                    continue
                S.pe_sync = (t == 8)
                b1_tile(t, "A")
                S.pe_sync = False
                if t == 0:
                    b0_all()
                    S.pe_sync = False
                S.pe_sync = (t == 8)
                b1_tile(t, "B")
                S.pe_sync = False
            if sub < 6:
                return
            if blk == 0:
                S.op("dve", lambda e: e.tensor_copy(out=kprev[:], in_=kaT[:, TB:TB + 128]), r=[("kaT", i) for i in nis_of(ntiles, TB - 128, 128)], w=["kprev"])
                S.op("dve", lambda e: e.tensor_copy(out=vprev[:], in_=vaug[:, 8, :, :]), r=[("vaug", 8), "vaug_ones"],
                     w=["vprev"])

            S.barrier()
            A.top = b12_base
            w2 = A.alloc([128, 8, 2560], BF16)
            wao = A.alloc([128, 4, D], BF16)
            wro = A.alloc([128, 4, D], BF16)
            wo = A.alloc([128, 8, D], BF16)
            sgr = A.alloc([128, 512], F32)
            rr = A.alloc([128, 512], F32)
            rT = A.alloc([128, 4, 128], BF16)
            sa = A.alloc([128, D], F32)
            m1 = A.alloc([128, D], F32)
            mT = A.alloc([128, 8, 128], BF16)
            def ld_w2(lo, hi, tok):
                S.op("pool", lambda e: e.dma_start(out=w2[:, :, lo:hi],
                                                   in_=w_in[:, 2304 + lo:2304 + hi].rearrange("(kc p) c -> p kc c", p=128)),
                     w=[tok], dma=tok)
            ld_w2(0, 512, "w2a")
            S.op("pool", lambda e: e.dma_start(out=wao[:], in_=w_ao.rearrange("(kc p) c -> p kc c", p=128)), w=["wao"], dma="wao")
            S.op("pool", lambda e: e.dma_start(out=wro[:], in_=w_ro.rearrange("(kc p) c -> p kc c", p=128)), w=["wro"], dma="wro")
            ld_w2(512, 1536, "w2b")
            ld_w2(1536, 2560, "w2c")
            S.op("pool", lambda e: e.dma_start(out=wo[:], in_=w_o.rearrange("(kc p) c -> p kc c", p=128)), w=["wo"], dma="wo")
            load_ln(1)
            def b2_tile(t):
                P, c0 = tile_geom(t)
                for kc in range(8):
                    S.op("pe", lambda e, kc=kc: e.matmul(ps[0:P, 0, :], lhsT=hT[:, kc, c0:c0 + P], rhs=w2[:, kc, 0:512],
                                                         start=(kc == 0), stop=(kc == 7)), r=["w2a", ("hT", t)], w=[("ps", 0)])
                for dh in range(2):
                    for c in range(4):
                        S.op("pe", lambda e, c=c, dh=dh: e.matmul(ps[0:P, 2 + dh, :], lhsT=oaT[:, c, c0:c0 + P],
                                                                  rhs=wao[:, c, dh * 512:(dh + 1) * 512],
                                                                  start=(c == 0), stop=(c == 3)),
                             r=["wao", ("oaT", t)], w=[("ps", 2 + dh)])
                S.op("act", lambda e: e.activation(out=sgr[0:P], in_=ps[0:P, 0, :], func=AF.Silu), r=[("ps", 0)], w=["sgr"])
                S.op("dve", lambda e: e.tensor_tensor(out=rr[0:P], in0=sgr[0:P], in1=rgn[0:P, t, :], op=ALU.mult),
                     r=["sgr", ("rgn", t)], w=["rr"])
                for c in range(4):
                    S.op("pe", lambda e, c=c: e.transpose(out=ps[:, 1, c * 128:c * 128 + P], in_=rr[0:P, c * 128:(c + 1) * 128],
                                                          identity=ident[0:P, 0:P]), r=["rr", "cst"], w=[("ps", 1)])
                S.op("act", lambda e: e.copy(out=rT[:, :, 0:P], in_=ps[:, 1, :].rearrange("p (a b) -> p a b", b=128)[:, :, 0:P]),
                     r=[("ps", 1)], w=["rT"])
                if t == 8 and (KX & 64):
                    return
                for dh in range(2):
                    for c in range(4):
                        S.op("pe", lambda e, c=c, dh=dh: e.matmul(ps[0:P, 4 + dh, :], lhsT=rT[:, c, 0:P],
                                                                  rhs=wro[:, c, dh * 512:(dh + 1) * 512],
                                                                  start=(c == 0), stop=(c == 3)),
                             r=["wro", "rT"], w=[("ps", 4 + dh)])
                if t == 8 and (KX & 128):
                    return
                for gi, goff in enumerate((512, 1536)):
                    for dh in range(2):
                        for kc in range(8):
                            S.op("pe", lambda e, kc=kc, dh=dh, goff=goff: e.matmul(
                                ps[0:P, 6 + dh, :], lhsT=hT[:, kc, c0:c0 + P], rhs=w2[:, kc, goff + dh * 512:goff + (dh + 1) * 512],
                                start=(kc == 0), stop=(kc == 7)), r=["w2b" if gi == 0 else "w2c", ("hT", t)], w=[("ps", 6 + dh)])
                    S.op("act", lambda e: e.activation(out=sa[0:P], in_=PS2(6, P), func=AF.Tanh, scale=0.5),
                         r=[("ps", 6), ("ps", 7)], w=["sa"])
                    if gi == 0:
                        S.op("dve", lambda e: e.scalar_tensor_tensor(out=m1[0:P], in0=sa[0:P], scalar=1.0, in1=PS2(2, P),
                                                                     op0=ALU.add, op1=ALU.mult),
                             r=["sa", ("ps", 2), ("ps", 3)], w=["m1"])
                    else:
                        S.op("dve", lambda e: e.scalar_tensor_tensor(out=sa[0:P], in0=sa[0:P], scalar=1.0, in1=PS2(4, P),
                                                                     op0=ALU.add, op1=ALU.mult),
                             r=["sa", ("ps", 4), ("ps", 5)], w=["sa"])
                        S.op("dve", lambda e: e.tensor_tensor(out=m1[0:P], in0=m1[0:P], in1=sa[0:P], op=ALU.add),
                             r=["sa", "m1"], w=["m1"])
                if t == 8 and (KX & 256):
                    return
                for kc in range(8):
                    S.op("pe", lambda e, kc=kc: e.transpose(out=ps[:, kc // 4, (kc % 4) * 128:(kc % 4) * 128 + P],
                                                            in_=m1[0:P, kc * 128:(kc + 1) * 128], identity=ident[0:P, 0:P]),
                         r=["m1", "cst"], w=[("ps", kc // 4)])
                S.op("act", lambda e: e.copy(out=mT[:, :, 0:P],
                                             in_=ps[:, 0:2, :].rearrange("p a (b c) -> p (a b) c", c=128)[:, :, 0:P]),
                     r=[("ps", 0), ("ps", 1)], w=["mT"])
                for dh in range(2):
                    for kc in range(8):
                        S.op("pe", lambda e, kc=kc, dh=dh: e.matmul(ps[0:P, 2 + dh, :], lhsT=mT[:, kc, 0:P],
                                                                    rhs=wo[:, kc, dh * 512:(dh + 1) * 512],
                                                                    start=(kc == 0), stop=(kc == 7)),
                             r=["wo", "mT"], w=[("ps", 2 + dh)])
                ln_elem(t, PS2(2, P), [("ps", 2), ("ps", 3)], 0.5 / ALPHA, LN_EPS / (ALPHA * ALPHA))
                if debug and blk == 0:
                    S.op("sp", lambda e, t=t: e.dma_start(out=dbg["h2"][t], in_=h_tok[:, t, :]), r=[("h", t)], dma=("dbg2", t))

            pend = None
            for t in tiles:
                if t == 8 and (KX & 32):
                    continue
                S.pe_sync = (t == 8)
                b2_tile(t)
                S.pe_sync = False
                if pend is not None:
                    transpose_to_hT(pend, 6)
                pend = t
            transpose_to_hT(pend, 6)

        stage = 0
        for blk in range(2):
            tiles = list(range(8)) + ([8] if blk == 0 else [])
            if stage >= max_stage:
                break
            stage += 1
            for t in tiles:
                P, c0 = tile_geom(t)
                src = xp[(blk * 8 + t) * 128:(blk * 8 + t + 1) * 128, :] if t < 8 else xs
                S.op("sp", lambda e, t=t, P=P, src=src: e.dma_start(out=h_tok[0:P, t, :], in_=src), w=[("h", t)], dma=("x", t))
            for ti, t in enumerate(tiles):
                transpose_to_hT(t, 4 + (ti % 2) * 2)
            if stage >= max_stage:
                break
            stage += 1
            ffn_phase(blk, 0, 0, final=False)
            if stage >= max_stage:
                break
            stage += 1
            mixer_phase(blk)
            if stage >= max_stage:
                break
            stage += 1
            ffn_phase(blk, 1, 2, final=True)
        S.barrier()
        S.op("sp", lambda e: e.nop(), r=[], w=[])
        S.emit(nc, es)
        build_nc.info = dict(nops=len(S.ops), nsems=S.nsems, sbuf_peak=A.peak)
    return nc


_CACHE = {}
KX = 0


def kernel(x_prompt, x_sample, cache_k_win, cache_v_win, state_ret, rel_bias, w_in, attn_sinks,
           w_attn_out, w_ret_out, w_o, ffn1_w_up, ffn1_w_down, ffn2_w_up, ffn2_w_down,
           ln1_g, ln1_b, ln2_g, ln2_b, ln3_g, ln3_b, _debug=False):
    f = lambda a: np.ascontiguousarray(np.asarray(a, dtype=np.float32))
    import os as _os
    _ms = int(_os.environ.get("K_STAGES", "99"))
    _sub = float(_os.environ.get("K_SUB", "99"))
    global KX
    KX = int(_os.environ.get("K_X", "0"))
    key = ("nc", bool(_debug), _ms, _sub, KX, _os.environ.get("K_PAD", "0"))
    if key not in _CACHE:
        _CACHE[key] = build_nc(debug=_debug, max_stage=_ms, sub=_sub)
    nc = _CACHE[key]
    consts = make_consts()
    shared = dict(relb=f(rel_bias), w_in=f(w_in)[0], sinks=f(attn_sinks)[0], w_ao=f(w_attn_out)[0], w_ro=f(w_ret_out)[0],
                  w_o=f(w_o)[0], f1u=f(ffn1_w_up)[0], f2u=f(ffn2_w_up)[0], f1d=f(ffn1_w_down)[0], f2d=f(ffn2_w_down)[0],
                  ln1g=f(ln1_g)[0], ln1b=f(ln1_b)[0], ln2g=f(ln2_g)[0], ln2b=f(ln2_b)[0], ln3g=f(ln3_g)[0], ln3b=f(ln3_b)[0],
                  consts=consts)
    xp, xs = f(x_prompt), f(x_sample)
    ckf, cvf, stf = f(cache_k_win), f(cache_v_win), f(state_ret)
    in_maps = []
    for c in range(NCORES):
        m = dict(shared)
        sl = slice(c * NS, (c + 1) * NS)
        m["xp"] = xp[c]
        m["xs"] = xs[sl, 0, :]
        m["ck"] = ckf[0, sl].reshape(NS, 128, 128)
        m["cv"] = cvf[0, sl].reshape(NS, 128, 128)
        m["st"] = stf[0, sl]
        in_maps.append(m)
    _nco = int(_os.environ.get("K_CORES", str(NCORES)))
    res = run_bass_kernel_spmd(nc, in_maps[:_nco], core_ids=list(range(_nco)))
    R = list(res.results)
    while len(R) < NCORES:
        R.append({k: np.zeros_like(v) for k, v in R[0].items()})
    y_p = np.stack([R[c]["yp"] for c in range(NCORES)], 0)
    y_s = np.concatenate([R[c]["ys"] for c in range(NCORES)], 0).reshape(128, 1, D)
    kwp = np.stack([R[c]["kwp"] for c in range(NCORES)], 0).reshape(1, 8, 128, 2, 64)
    vwp = np.stack([R[c]["vwp"] for c in range(NCORES)], 0).reshape(1, 8, 128, 2, 64)
    spo = np.stack([R[c]["sp"] for c in range(NCORES)], 0).reshape(1, 8, 4, 128, 128)
    kws = np.concatenate([R[c]["kws"] for c in range(NCORES)], 0).reshape(1, 128, 128, 2, 64)
    vws = np.concatenate([R[c]["vws"] for c in range(NCORES)], 0).reshape(1, 128, 128, 2, 64)
    sso = np.concatenate([R[c]["ss"] for c in range(NCORES)], 0).reshape(1, 128, 4, 128, 128)
    outs = tuple(np.ascontiguousarray(a.astype(np.float32)) for a in (y_p, y_s, kwp, vwp, spo, kws, vws, sso))
    if _debug:
        kernel.dbg = [{k: v for k, v in R[c].items() if k.startswith("dbg")} for c in range(NCORES)]
    return outs
```

```python
import numpy as np
from contextlib import ExitStack
import concourse.bass as bass
import concourse.mybir as mybir
from concourse.bass_utils import run_bass_kernel_spmd

F32 = mybir.dt.float32
BF16 = mybir.dt.bfloat16
AF = mybir.ActivationFunctionType
ALU = mybir.AluOpType

NCORES = 8
D = 1024
SEQ = 2048
DFF = 2816
NJ = DFF // 128
NS = 16
TB = 1024
NCOL = TB + NS
PAST = 16384
ALPHA = 2.0 ** 0.25
LN_EPS = 1e-5
GN_EPS = 1e-6
MASKV = -30000.0
SB_BASE = 16512
SB_END = 229376

C_ID = 0
C_J = 128
C_CAUS = 256
C_DQ = 384
C_DK = C_DQ + 8
C_GC = C_DK + 8
C_EYER = C_GC + 4
C_OH1 = C_EYER + 256
C_OH2 = C_OH1 + 128
C_MH = C_OH2 + 128
C_ONE = C_MH + 1
NC1 = ((C_ONE + 1 + 7) // 8) * 8
C_COS = NC1
C_SIN = C_COS + 17 * 64
C_NSIN = C_SIN + 17 * 64
NROT = 3 * 17 * 64
NCONST = NC1 + NROT
GAMMAS = [1.0 - 2.0 ** (-5.0 - h) for h in range(4)]


def _bucket(d):
    d = np.asarray(d)
    n = np.maximum(d, 0)
    ratio = np.maximum(n, 1).astype(np.float32) / np.float32(16)
    large = 16 + (np.log(np.maximum(ratio, np.float32(1.0))).astype(np.float32)
                  / np.float32(np.log(128 / 16)) * np.float32(16)).astype(np.int32)
    large = np.minimum(large, 31)
    return np.where(n < 16, n, large)


def make_consts():
    c = np.zeros((128, NCONST), np.float32)
    c[:, C_ID:C_ID + 128] = np.eye(128, dtype=np.float32)
    c[:, C_J:C_J + 128] = np.eye(128, dtype=np.float32)[::-1]
    jj = np.arange(128)
    c[:, C_CAUS:C_CAUS + 128] = (jj[None, :] >= jj[:, None]).astype(np.float32)
    inv = (np.float32(10000.0) ** (-(np.arange(64, dtype=np.float32) / np.float32(64)))).astype(np.float32)
    for t in range(17):
        pos = (t * 128 + np.arange(128)) if t < 16 else np.full(128, PAST)
        ang = (pos.astype(np.float32)[:, None] * inv[None, :]).astype(np.float32)
        c[:, C_COS + t * 64:C_COS + (t + 1) * 64] = np.cos(ang.astype(np.float64)).astype(np.float32)
        c[:, C_SIN + t * 64:C_SIN + (t + 1) * 64] = np.sin(ang.astype(np.float64)).astype(np.float32)
        c[:, C_NSIN + t * 64:C_NSIN + (t + 1) * 64] = -np.sin(ang.astype(np.float64)).astype(np.float32)
    p = np.arange(128, dtype=np.float64)
    for h in range(4):
        lg = np.log1p(-(2.0 ** (-5.0 - h)))
        c[:, C_DQ + h] = np.exp((p + 1) * lg)
        c[:, C_DK + h] = (128.0 ** -0.5) * np.exp(-(p + 1) * lg)
        c[:, C_DQ + 4 + h] = 1.0
        c[:, C_DK + 4 + h] = 128.0 ** -0.5
        c[:, C_GC + h] = np.exp(128 * lg)
    c[:, C_EYER:C_EYER + 256] = np.eye(16, dtype=np.float32).reshape(1, 256)
    b1 = _bucket(np.arange(128))
    b2 = _bucket(127 - np.arange(128))
    for r in range(128):
        c[b1[r], C_OH1 + r] = 1.0
        c[b2[r], C_OH2 + r] = 1.0
    c[:, C_MH] = -0.5
    c[:, C_ONE] = 1.0
    return c


def _rnd_tile(x):
    return 32 if x <= 32 else (64 if x <= 64 else 128)


class _FakePE:
    def __init__(self):
        self.mode = None

    def matmul(self, out, lhsT=None, rhs=None, **kw):
        self.mode = (_rnd_tile(lhsT.shape[0]), _rnd_tile(int(np.prod(lhsT.shape[1:]))))

    def transpose(self, out=None, in_=None, identity=None):
        self.mode = (_rnd_tile(in_.shape[0]), _rnd_tile(int(np.prod(in_.shape[1:]))))


class Sched:
    ENGS = ("pe", "act", "dve", "pool", "sp")

    def __init__(self):
        self.ops = []
        self.last_w = {}
        self.readers = {}
        self.dma_cnt = {}
        self.bar = None
        self.bar_passed = set()
        self.last_eng = {}
        self.last_dma = {}
        self.pe_sync = False
        self.pe_sync_once = False
        self.pe_prev_small = False

    def _stream(self, idx):
        o = self.ops[idx]
        return ("dma", o["dma"]) if o["dma"] is not None else ("eng", o["eng"])

    def op(self, eng, fn, r=(), w=(), dma=None):
        idx = len(self.ops)
        deps = {}

        def add(d):
            s = self._stream(d)
            if deps.get(s, -1) < d:
                deps[s] = d
        for t in r:
            lw = self.last_w.get(t)
            if lw is not None:
                add(lw)
        for t in w:
            lw = self.last_w.get(t)
            if lw is not None:
                add(lw)
            for d in self.readers.get(t, {}).values():
                add(d)
        if self.bar is not None and eng not in self.bar_passed:
            for d in self.bar:
                add(d)
            self.bar_passed.add(eng)
        force = False
        if eng == "pe" and dma is None and (self.pe_sync or self.pe_sync_once):
            self.pe_sync_once = self.pe_sync
            if "pe" in self.last_eng:
                add(self.last_eng["pe"])
                force = True
        o = dict(eng=eng, fn=fn, deps=deps, dma=dma, need_inc=False, ev=None, force=force)
        if dma is not None:
            self.dma_cnt[dma] = self.dma_cnt.get(dma, 0) + 1
            o["ev"] = ("dma", dma, 16 * self.dma_cnt[dma])
            self.last_dma[dma] = idx
        else:
            self.last_eng[eng] = idx
        self.ops.append(o)
        me = ("dma", dma) if dma is not None else ("eng", eng)
        for t in r:
            self.readers.setdefault(t, {})[me] = idx
        for t in w:
            self.last_w[t] = idx
            self.readers[t] = {}
        return idx

    def barrier(self):
        self.bar = set(self.last_eng.values()) | set(self.last_dma.values())
        self.bar_passed = set()

    def finalize(self):
        ops = self.ops
        for o in ops:
            real = []
            for s, d in o["deps"].items():
                od = ops[d]
                if s == ("eng", "pe") and o["eng"] == "pe" and o["dma"] is None and not o["force"]:
                    continue
                real.append(d)
                if od["dma"] is None:
                    od["need_inc"] = True
            o["deps"] = real
        cnt = {e: 0 for e in self.ENGS}
        for o in ops:
            if o["dma"] is None and o["need_inc"]:
                cnt[o["eng"]] += 1
                o["ev"] = ("eng", o["eng"], cnt[o["eng"]])
        seen = {e: {} for e in self.ENGS}
        for o in ops:
            waits = {}
            for d in o["deps"]:
                kind, key, val = ops[d]["ev"]
                k = (kind, key)
                if seen[o["eng"]].get(k, 0) >= val:
                    continue
                waits[k] = max(waits.get(k, 0), val)
            for k, v in waits.items():
                seen[o["eng"]][k] = v
            o["waits"] = waits

    def emit(self, nc, es):
        self.finalize()
        sems = {}
        n = [0]

        def sem(k):
            if k not in sems:
                n[0] += 1
                sems[k] = es.enter_context(nc.semaphore("s%d" % n[0]))
            return sems[k]
        for o in self.ops:
            for k in o["waits"]:
                sem(k)
            if o["ev"] is not None:
                sem((o["ev"][0], o["ev"][1]))
        block = es.enter_context(nc.Block())
        ops = self.ops

        import os as _os2
        pad = int(_os2.environ.get("K_PAD", "0"))

        def run(engname, eng):
            for _ in range(pad):
                eng.nop()
            for o in ops:
                if o["eng"] != engname:
                    continue
                for k, v in o["waits"].items():
                    eng.wait_ge(sems[k], v)
                ins = o["fn"](eng)
                if o["dma"] is not None:
                    ins.then_inc(sems[("dma", o["dma"])], 16)
                elif o["need_inc"]:
                    ins.then_inc(sems[("eng", engname)], 1)

        @block.tensor
        def _(e):
            run("pe", e)

        @block.scalar
        def _(e):
            run("act", e)

        @block.vector
        def _(e):
            run("dve", e)

        @block.gpsimd
        def _(e):
            run("pool", e)

        @block.sync
        def _(e):
            run("sp", e)
        self.nsems = len(sems)


class Arena:
    def __init__(self, nc, base, end):
        self.nc, self.top, self.end = nc, base, end
        self.n = 0
        self.peak = base

    def alloc(self, shape, dtype):
        sz = int(np.prod(shape[1:])) * (2 if dtype == BF16 else 4)
        off = (self.top + 31) // 32 * 32
        assert off + sz <= self.end, ("SBUF overflow", off + sz, self.end)
        self.top = off + sz
        self.peak = max(self.peak, self.top)
        self.n += 1
        return self.nc.alloc_sbuf_tensor_at("sb%d" % self.n, list(shape), dtype, offset=off)


def build_nc(debug=False, max_stage=99, sub=99):
    nc = bass.Bass("TRN2", target_bir_lowering=False)

    def din(name, shape):
        return nc.dram_tensor(name, list(shape), F32, kind="ExternalInput").ap()

    def dout(name, shape):
        return nc.dram_tensor(name, list(shape), F32, kind="ExternalOutput").ap()
    xp = din("xp", [SEQ, D])
    xs = din("xs", [NS, D])
    ck = din("ck", [NS, 128, 128])
    cv = din("cv", [NS, 128, 128])
    st_in = din("st", [NS, 4, 128, 128])
    relb = din("relb", [32, 8])
    w_in = din("w_in", [D, 4864])
    sinks = din("sinks", [8])
    w_ao = din("w_ao", [512, D])
    w_ro = din("w_ro", [512, D])
    w_o = din("w_o", [D, D])
    wup_d = [din("f1u", [D, 2 * DFF]), din("f2u", [D, 2 * DFF])]
    wdn_d = [din("f1d", [DFF, D]), din("f2d", [DFF, D])]
    lng = [din("ln%dg" % i, [D]) for i in (1, 2, 3)]
    lnb = [din("ln%db" % i, [D]) for i in (1, 2, 3)]
    consts_d = din("consts", [128, NCONST])
    yp = dout("yp", [SEQ, D])
    ys = dout("ys", [NS, D])
    kwp = dout("kwp", [128, 128])
    vwp = dout("vwp", [128, 128])
    sp_out = dout("sp", [4, 128, 128])
    kws = dout("kws", [NS, 128, 128])
    vws = dout("vws", [NS, 128, 128])
    ss_out = dout("ss", [NS, 4, 128, 128])
    scr = nc.dram_tensor("scr", [2, 8, 256], F32, kind="Internal").ap()
    dbg = {}
    if debug:
        dbg["h1"] = dout("dbg_h1", [9, 128, D])
        dbg["h2"] = dout("dbg_h2", [9, 128, D])

    es = ExitStack()
    with es:
        S = Sched()
        A = Arena(nc, SB_BASE, SB_END)
        ps = es.enter_context(nc.psum_tensor("ps", [128, 8, 512], F32))

        h_tok = A.alloc([128, 9, D], F32)
        hT = A.alloc([128, 8, NCOL], BF16)
        cst = A.alloc([128, NC1], F32)
        w1q = A.alloc([128, 8, 512], BF16)
        BT = A.alloc([128, 2, 8, 128], F32)
        lg_t = A.alloc([128, D], F32)
        lb_t = A.alloc([128, D], F32)
        Sst = A.alloc([128, 4, 128], F32)
        Sb = A.alloc([128, 4, 128], BF16)
        bias_s = A.alloc([128, 8], F32)
        sinkexp = A.alloc([128, 8], F32)
        stt = [A.alloc([128, 4, 6], F32) for _ in range(2)]
        mvt = [A.alloc([128, 4, 8], F32) for _ in range(2)]
        kprev = A.alloc([128, 128], BF16)
        vprev = A.alloc([128, 2, 65], BF16)
        phase_base = A.top

        ident = cst[:, C_ID:C_ID + 128]
        Jm = cst[:, C_J:C_J + 128]
        caus = cst[:, C_CAUS:C_CAUS + 128]
        mhalf = cst[:, C_MH:C_MH + 1]

        def PSB(b, P=128, n=512):
            return ps[0:P, b, 0:n]

        def PS2(b, P=128):
            return ps[0:P, b:b + 2, :].rearrange("p a b -> p (a b)")

        S.op("sp", lambda e: e.dma_start(out=cst[:], in_=consts_d[:, 0:NC1]), w=["cst"], dma="cst")
        S.op("sp", lambda e: e.dma_start(out=sinkexp[:], in_=sinks.partition_broadcast(128)), w=["sinkexp"], dma="c2")
        S.op("act", lambda e: e.activation(out=sinkexp[:], in_=sinkexp[:], func=AF.Exp), r=["sinkexp"], w=["sinkexp"])
        S.op("dve", lambda e: e.memset(Sst[:], 0.0), w=["S"])
        S.op("dve", lambda e: e.memset(Sb[:], 0.0), w=["Sb"])
        A0 = A.top
        rb = A.alloc([32, 8], F32)
        TTs = A.alloc([8, 128], F32)
        Lsb = A.alloc([8, 2, 256], F32)
        G = A.alloc([128, 2, 8, 128], F32)
        S.op("sp", lambda e: e.dma_start(out=rb[:], in_=relb), w=["rb"], dma="c3")
        S.op("pe", lambda e: e.matmul(ps[0:8, 0, 0:128], lhsT=rb[:], rhs=cst[0:32, C_OH1:C_OH1 + 128], start=True, stop=True),
             r=["rb", "cst"], w=[("ps", 0)])
        S.op("act", lambda e: e.copy(out=TTs[:], in_=ps[0:8, 0, 0:128]), r=[("ps", 0)], w=["TTs"])
        S.op("dve", lambda e: e.memset(Lsb[:], MASKV), w=["Lsb"])
        S.op("dve", lambda e: e.tensor_copy(out=Lsb[:, 0, 0:127], in_=TTs[:, 1:128]), r=["TTs", "Lsb"], w=["Lsb"])
        S.op("dve", lambda e: e.tensor_copy(out=Lsb[:, 1, 127:255], in_=TTs[:, 0:128]), r=["TTs", "Lsb"], w=["Lsb"])
        S.op("sp", lambda e: e.dma_start(out=scr.rearrange("t h u -> h t u"), in_=Lsb[:]), r=["Lsb"], w=["scr"], dma="c4")
        for tab in range(2):
            hank = bass.AP(tensor=scr.tensor, offset=tab * 2048, ap=[[1, 128], [256, 8], [1, 128]])
            S.op("sp", lambda e, tab=tab, hank=hank: e.dma_start(out=G[:, tab, :, :], in_=hank), r=["scr"], w=["G"], dma="c5")
        for tab in range(2):
            for hf in range(2):
                b = 1 + tab * 2 + hf
                S.op("pe", lambda e, tab=tab, hf=hf, b=b: e.matmul(
                    ps[:, b, :], lhsT=Jm, rhs=G[:, tab, hf * 4:(hf + 1) * 4, :].rearrange("p a b -> p (a b)"),
                    start=True, stop=True), r=["cst", "G"], w=[("ps", b)])
                S.op("act", lambda e, tab=tab, hf=hf, b=b: e.copy(
                    out=BT[:, tab, hf * 4:(hf + 1) * 4, :].rearrange("p a b -> p (a b)"), in_=ps[:, b, :]),
                    r=[("ps", b)], w=["BT"])
        S.op("pe", lambda e: e.matmul(ps[:, 5, 0:8], lhsT=cst[0:32, C_OH2:C_OH2 + 128], rhs=rb[:], start=True, stop=True),
             r=["rb", "cst"], w=[("ps", 5)])
        S.op("act", lambda e: e.copy(out=bias_s[:], in_=ps[:, 5, 0:8]), r=[("ps", 5)], w=["bias_s"])
        A.top = A0

        def tile_geom(t):
            return (128, t * 128) if t < 8 else (NS, TB)

        def hT_tok(c0, n):
            toks = [("hT", t) for t in range(c0 // 128, min(8, (c0 + n + 127) // 128))]
            if c0 + n > TB:
                toks.append(("hT", 8))
            return toks

        def nis_of(ntiles, c0, n):
            return [i for i, (a, m) in enumerate(ntiles) if a < c0 + n and a + m > c0]

        def transpose_to_hT(t, pbank):
            P, c0 = tile_geom(t)
            S.pe_sync = (t == 8)
            for kc in range(8):
                S.op("pe", lambda e, kc=kc: e.transpose(out=ps[:, pbank + kc // 4, (kc % 4) * 128:(kc % 4) * 128 + P],
                                                        in_=h_tok[0:P, t, kc * 128:(kc + 1) * 128], identity=ident[0:P, 0:P]),
                     r=[("h", t), "cst"], w=[("ps", pbank + kc // 4)])
            S.op("act", lambda e: e.copy(out=hT[:, :, c0:c0 + P],
                                         in_=ps[:, pbank:pbank + 2, :].rearrange("p a (b c) -> p (a b) c", c=128)[:, :, 0:P]),
                 r=[("ps", pbank), ("ps", pbank + 1)], w=[("hT", t)])
            S.pe_sync = False

        def load_ln(i):
            S.op("sp", lambda e: e.dma_start(out=lg_t[:], in_=lng[i].partition_broadcast(128)), w=["lng"], dma="lng")
            S.op("sp", lambda e: e.dma_start(out=lb_t[:], in_=lnb[i].partition_broadcast(128)), w=["lnb"], dma="lnb")

        def ln_elem(t, src, src_tok, cscale, eps_eff):
            P, c0 = tile_geom(t)
            h = h_tok[0:P, t, :]
            st, mv = stt[t % 2], mvt[t % 2]
            tk = ("lnt", t % 2)
            S.op("dve", lambda e: e.scalar_tensor_tensor(out=h, in0=src, scalar=cscale, in1=h, op0=ALU.mult, op1=ALU.add),
                 r=list(src_tok) + [("h", t)], w=[("h", t)])
            for a in range(2):
                S.op("dve", lambda e, a=a: e.bn_stats(out=st[0:P, a, :], in_=h_tok[0:P, t, a * 512:(a + 1) * 512]),
                     r=[("h", t)], w=[tk])
            S.op("dve", lambda e: e.bn_aggr(out=mv[0:P, 0, 0:2], in_=st[0:P, 0:2, :].rearrange("p a b -> p (a b)")),
                 r=[tk], w=[tk])
            S.op("dve", lambda e: e.tensor_scalar(out=mv[0:P, 0, 2:3], in0=mv[0:P, 0, 1:2], scalar1=eps_eff, scalar2=None,
                                                  op0=ALU.add), r=[tk], w=[tk])
            S.op("pool", lambda e: e.tensor_tensor(out=mv[0:P, 0, 3:4], in0=mv[0:P, 0, 2:3], in1=mhalf[0:P], op=ALU.pow),
                 r=[tk, "cst"], w=[tk])
            S.op("dve", lambda e: e.scalar_tensor_tensor(out=mv[0:P, 0, 4:5], in0=mv[0:P, 0, 0:1], scalar=-1.0,
                                                         in1=mv[0:P, 0, 3:4], op0=ALU.mult, op1=ALU.mult), r=[tk], w=[tk])
            S.op("act", lambda e: e.activation(out=h, in_=h, func=AF.Identity, scale=mv[0:P, 0, 3:4], bias=mv[0:P, 0, 4:5]),
                 r=[tk, ("h", t)], w=[("h", t)])
            S.op("dve", lambda e: e.tensor_tensor(out=h, in0=h, in1=lg_t[0:P], op=ALU.mult), r=[("h", t), "lng"], w=[("h", t)])
            S.op("dve", lambda e: e.tensor_tensor(out=h, in0=h, in1=lb_t[0:P], op=ALU.add), r=[("h", t), "lnb"], w=[("h", t)])

        w1t_holder = {}

        def ffn_phase(blk, which, ln_i, final):
            tiles = list(range(8)) + ([8] if blk == 0 else [])
            ntiles = [(0, 352), (352, 352), (704, 336)] if blk == 0 else [(0, 512), (512, 512)]
            S.barrier()
            A.top = phase_base
            actT = A.alloc([128, NJ, NCOL], BF16)
            wdn = A.alloc([128, NJ, D], BF16)
            if "t" not in w1t_holder:
                off = (A.top + 31) // 32 * 32
                assert off + 8 * 1792 * 2 <= SB_END
                w1t_holder["t"] = nc.alloc_sbuf_tensor_at("w1top", [128, 8, 1792], BF16, offset=off)
                w1t_holder["off"] = off
            wup = [A.alloc([128, 8, 2, 128], BF16) for _ in range(3)]
            sg = [A.alloc([128, 512], F32) for _ in range(2)]
            wu_v = wup_d[which].rearrange("(kc p) (two j c) -> p kc two j c", p=128, two=2, c=128)
            wd_v = wdn_d[which].rearrange("(j p) c -> p j c", p=128)

            def load_wup(j):
                for half in range(2):
                    S.op("pool", lambda e, half=half: e.dma_start(out=wup[j % 3][:, :, half, :], in_=wu_v[:, :, half, j, :]),
                         w=[("wup", j % 3)], dma=("wup", j % 3))
            load_ln(ln_i)
            load_wup(0)
            load_wup(1)
            cnt = 0
            for j in range(NJ):
                if j + 2 < NJ:
                    load_wup(j + 2)
                S.op("pool", lambda e, j=j: e.dma_start(out=wdn[:, j, :], in_=wd_v[:, j, :]), w=["wdn"], dma="wdn")
                for ni, (c0, n) in enumerate(ntiles):
                    bg, bu = cnt % 2, 2 + cnt % 2
                    sgt = sg[cnt % 2]
                    sgk = ("sg", cnt % 2)
                    cnt += 1
                    for half, bank in ((0, bg), (1, bu)):
                        for kc in range(8):
                            S.op("pe", lambda e, kc=kc, half=half, bank=bank, j=j, c0=c0, n=n: e.matmul(
                                ps[:, bank, 0:n], lhsT=wup[j % 3][:, kc, half, :], rhs=hT[:, kc, c0:c0 + n],
                                start=(kc == 0), stop=(kc == 7)),
                                r=[("wup", j % 3)] + hT_tok(c0, n), w=[("ps", bank)])
                    S.op("act", lambda e, bg=bg, n=n, sgt=sgt: e.activation(out=sgt[:, 0:n], in_=ps[:, bg, 0:n], func=AF.Silu),
                         r=[("ps", bg)], w=[sgk])
                    S.op("dve", lambda e, bu=bu, n=n, sgt=sgt, j=j, c0=c0: e.tensor_tensor(
                        out=actT[:, j, c0:c0 + n], in0=sgt[:, 0:n], in1=ps[:, bu, 0:n], op=ALU.mult),
                        r=[sgk, ("ps", bu)], w=[("actT", j, ni)])
            if which == 0:
                w1t = w1t_holder["t"]
                alias = [("wup", 0), ("wup", 1), ("wup", 2), ("sg", 0), ("sg", 1)]
                S.op("pool", lambda e: e.dma_start(out=w1t[:, :, 0:256],
                                                   in_=w_in[:, 512:768].rearrange("(kc p) c -> p kc c", p=128)),
                     w=alias + ["w1a"], dma="w1a")
                S.op("pool", lambda e: e.dma_start(out=w1t[:, :, 256:1792],
                                                   in_=w_in[:, 768:2304].rearrange("(kc p) c -> p kc c", p=128)),
                     w=alias + ["w1r"], dma="w1r")
            S.pe_sync = False
            pend = None
            for ti, t in enumerate(tiles):
                P, c0 = tile_geom(t)
                S.pe_sync = (t == 8)
                pb = (ti % 2) * 2
                nil = nis_of(ntiles, c0, P)
                for dh in range(2):
                    for j in range(NJ):
                        S.op("pe", lambda e, j=j, dh=dh, pb=pb, P=P, c0=c0: e.matmul(
                            ps[0:P, pb + dh, :], lhsT=actT[:, j, c0:c0 + P], rhs=wdn[:, j, dh * 512:(dh + 1) * 512],
                            start=(j == 0), stop=(j == NJ - 1)),
                            r=[("actT", j, i) for i in nil] + ["wdn"], w=[("ps", pb + dh)])
                S.pe_sync = False
                ln_elem(t, PS2(pb, P), [("ps", pb), ("ps", pb + 1)], 0.5 / ALPHA, LN_EPS / (ALPHA * ALPHA))
                if which == 0 and ti < 8:
                    kc = ti
                    for hh in range(4):
                        S.op("pool", lambda e, kc=kc, hh=hh: e.dma_start(
                            out=w1q[:, kc, hh * 128:(hh + 1) * 128].rearrange("p (g d) -> p g d", g=2),
                            in_=w_in[kc * 128:(kc + 1) * 128, 0:512].rearrange("p (g hh d) -> p hh g d", g=2, hh=4)[:, hh, :, :]),
                            w=["w1q"], dma="w1q")
                if final:
                    dst = yp[(blk * 8 + t) * 128:(blk * 8 + t + 1) * 128, :] if t < 8 else ys
                    S.op("sp", lambda e, t=t, P=P, dst=dst: e.dma_start(out=dst, in_=h_tok[0:P, t, :]),
                         r=[("h", t)], dma=("yout", t))
                else:
                    if debug and blk == 0:
                        S.op("sp", lambda e, t=t: e.dma_start(out=dbg["h1"][t], in_=h_tok[:, t, :]), r=[("h", t)],
                             dma=("dbg", t))
                    if pend is not None:
                        transpose_to_hT(pend[0], pend[1])
                    pend = (t, 4 + (ti % 2) * 2)
            if pend is not None and not final:
                transpose_to_hT(pend[0], pend[1])

        def mixer_phase(blk):
            tiles = list(range(8)) + ([8] if blk == 0 else [])
            ntiles = [(0, 352), (352, 352), (704, 336)] if blk == 0 else [(0, 512), (512, 512)]
            S.barrier()
            A.top = phase_base
            oaT = A.alloc([128, 4, NCOL], BF16)
            rgn = A.alloc([128, 9, 512], F32)
            b12_base = A.top
            rot = A.alloc([128, NROT], F32)
            w1t = w1t_holder["t"]
            S.op("sp", lambda e: e.dma_start(out=rot[:], in_=consts_d[:, NC1:NCONST]), w=["rot"], dma="rot")
            qaT = A.alloc([128, 4, NCOL], BF16)
            kaT = A.alloc([128, 128 + NCOL], BF16)
            vaug = A.alloc([128, 10, 2, 65], BF16)
            qs = A.alloc([128, 4, 128], F32)
            tA = A.alloc([128, 4, 128], F32)
            tB = A.alloc([128, 4, 128], F32)
            qh = A.alloc([128, 4, 128], F32)
            kh = A.alloc([128, 4, 128], F32)
            khb = A.alloc([128, 4, 128], BF16)
            vrb = A.alloc([128, 4, 128], BF16)
            qhT = A.alloc([128, 4, 128], BF16)
            khT = A.alloc([128, 4, 128], BF16)
            et = [A.alloc([128, 512], F32) for _ in range(2)]
            pT = [A.alloc([128, 2, 4, 128], BF16) for _ in range(2)]
            scm = A.alloc([128, 4, 128], BF16)
            o_n = A.alloc([128, 8, 64], F32)
            den = A.alloc([128, 8], F32)
            kvout = A.alloc([128, 256], F32)
            qhT32 = A.alloc([128, 4, NS], F32)
            vr32 = A.alloc([128, 4, 128], F32)
            assert A.top <= w1t_holder["off"], (A.top, w1t_holder["off"])

            S.op("dve", lambda e: e.memset(vaug[:, :, :, 64:65], 1.0), w=["vaug_ones"])
            if blk == 0:
                S.op("sp", lambda e: e.dma_start(out=kws[:, 0:127, :], in_=ck[:, 1:128, :]), w=["kws_a"], dma="kws_a")
                S.op("sp", lambda e: e.dma_start(out=vws[:, 0:127, :], in_=cv[:, 1:128, :]), w=["vws_a"], dma="vws_a")
            else:
                S.op("dve", lambda e: e.tensor_copy(out=kaT[:, 0:128], in_=kprev[:]), r=["kprev"], w=["kaT_prev"])
                S.op("dve", lambda e: e.tensor_copy(out=vaug[:, 0, :, :], in_=vprev[:]), r=["vprev", "vaug_ones"],
                     w=[("vaug", 0)])

            def b0_all():
              cnt = 0
              for c in range(5):
                    for ni, (c0, n) in enumerate(ntiles):
                        bank = 6 + cnt % 2
                        for kc in range(8):
                            S.op("pe", lambda e, c=c, kc=kc, c0=c0, n=n, bank=bank: e.matmul(
                                ps[:, bank, 0:n], lhsT=(w1q[:, kc, c * 128:(c + 1) * 128] if c < 4 else w1t[:, kc, 0:128]),
                                rhs=hT[:, kc, c0:c0 + n],
                                start=(kc == 0), stop=(kc == 7)), r=["w1q" if c < 4 else "w1a"] + hT_tok(c0, n), w=[("ps", bank)])
                        if c < 4:
                            dst, wt = qaT[:, c, c0:c0 + n], [("qaT", c, ni)]
                        else:
                            dst, wt = kaT[:, 128 + c0:128 + c0 + n], [("kaT", ni), "kaT_all"]
                        eng = "act" if cnt % 2 == 0 else "dve"
                        if eng == "act":
                            S.op("act", lambda e, dst=dst, bank=bank, n=n: e.copy(out=dst, in_=ps[:, bank, 0:n]),
                                 r=[("ps", bank)], w=wt)
                        else:
                            S.op("dve", lambda e, dst=dst, bank=bank, n=n: e.tensor_copy(out=dst, in_=ps[:, bank, 0:n]),
                                 r=[("ps", bank)], w=wt)
                        cnt += 1
            S.pe_sync = False
            qa_tok = lambda c0_, n_: [("qaT", c, i) for c in range(4) for i in nis_of(ntiles, c0_, n_)]

            def gn_store(t, P, heads):
                st, mv = stt[t % 2], mvt[t % 2]
                tk = ("lnt", t % 2)
                for h, (v, vt) in enumerate(heads):
                    S.op("dve", lambda e, h=h, v=v: e.bn_stats(out=st[0:P, h, :], in_=v), r=vt, w=[tk])
                for h in range(4):
                    S.op("dve", lambda e, h=h: e.bn_aggr(out=mv[0:P, h, 0:2], in_=st[0:P, h, :]), r=[tk], w=[tk])
                S.op("dve", lambda e: e.tensor_scalar(out=mv[0:P, :, 2:3], in0=mv[0:P, :, 1:2], scalar1=GN_EPS, scalar2=None,
                                                      op0=ALU.add), r=[tk], w=[tk])
                S.op("pool", lambda e: e.tensor_tensor(out=mv[0:P, :, 3:4], in0=mv[0:P, :, 2:3],
                                                       in1=mhalf[0:P].unsqueeze(1).to_broadcast([P, 4, 1]), op=ALU.pow),
                     r=[tk, "cst"], w=[tk])
                for h, (v, vt) in enumerate(heads):
                    S.op("dve", lambda e, h=h, v=v: e.tensor_scalar(
                        out=rgn[0:P, t, h * 128:(h + 1) * 128], in0=v, scalar1=mv[0:P, h, 0:1], scalar2=mv[0:P, h, 3:4],
                        op0=ALU.subtract, op1=ALU.mult), r=vt + [tk], w=[("rgn", t)])

            def attn_finish(t, P, c0):
                for g, bank in ((0, 0), (1, 3)):
                    ov = ps[0:P, bank, 0:260].rearrange("p (h d) -> p h d", d=65)
                    S.op("dve", lambda e, g=g, ov=ov: e.tensor_tensor(
                        out=den[0:P, g * 4:(g + 1) * 4], in0=ov[:, :, 64], in1=sinkexp[0:P, g * 4:(g + 1) * 4], op=ALU.add),
                        r=[("ps", bank), "sinkexp"], w=[("den", g)])
                    S.op("dve", lambda e, g=g: e.reciprocal(out=den[0:P, g * 4:(g + 1) * 4], in_=den[0:P, g * 4:(g + 1) * 4]),
                         r=[("den", g)], w=[("den", g)])
                    S.op("dve", lambda e, g=g, ov=ov: e.tensor_tensor(
                        out=o_n[0:P, g * 4:(g + 1) * 4, :], in0=ov[:, :, 0:64],
                        in1=den[0:P, g * 4:(g + 1) * 4].unsqueeze(2).to_broadcast([P, 4, 64]), op=ALU.mult),
                        r=[("ps", bank), ("den", g)], w=[("o_n", g)])
                for c in range(4):
                    S.op("pe", lambda e, c=c: e.transpose(out=ps[:, 4, c * 128:c * 128 + P],
                                                          in_=o_n[0:P, 2 * c:2 * c + 2, :].rearrange("p a b -> p (a b)"),
                                                          identity=ident[0:P, 0:P]),
                         r=[("o_n", c // 2), "cst"], w=[("ps", 4)])
                S.op("act", lambda e: e.copy(out=oaT[:, :, c0:c0 + P],
                                             in_=ps[:, 4, :].rearrange("p (a b) -> p a b", b=128)[:, :, 0:P]),
                     r=[("ps", 4)], w=[("oaT", t)])

            def qk_transposes(t, P, smp):
                S.op("act", lambda e: e.copy(out=khb[0:P], in_=kh[0:P]), r=["kh"], w=["khb"])
                for which, src, bank, dstT in (("q", qh, 4, qhT), ("k", kh, 5, khT)):
                    for h in range(4):
                        S.op("pe", lambda e, h=h, src=src, bank=bank: e.transpose(
                            out=ps[:, bank, h * 128:h * 128 + P], in_=src[0:P, h, :], identity=ident[0:P, 0:P]),
                            r=[which + "h", "cst"], w=[("ps", bank)])
                    S.op("act", lambda e, bank=bank, dstT=dstT: e.copy(
                        out=dstT[:, :, 0:P], in_=ps[:, bank, :].rearrange("p (a b) -> p a b", b=128)[:, :, 0:P]),
                        r=[("ps", bank)], w=[which + "hT"])
                    if smp and which == "q":
                        S.op("act", lambda e, bank=bank: e.copy(
                            out=qhT32[:], in_=ps[:, bank, :].rearrange("p (a b) -> p a b", b=128)[:, :, 0:NS]),
                            r=[("ps", bank)], w=["qhT32"])

            def b1_tile(t, part):
                P, c0 = tile_geom(t)
                gt = blk * 8 + t if t < 8 else 16
                smp = (t == 8)
                srow = 4 if smp else 0
                if part == "A":
                    for bank, (co, n) in enumerate(((512, 256), (768, 512), (1280, 512), (1792, 512))):
                        for kc in range(8):
                            S.op("pe", lambda e, kc=kc, bank=bank, co=co, n=n: e.matmul(
                                ps[0:P, bank, 0:n], lhsT=hT[:, kc, c0:c0 + P], rhs=w1t[:, kc, co - 512:co - 512 + n],
                                start=(kc == 0), stop=(kc == 7)), r=["w1a" if bank == 0 else "w1r", ("hT", t)], w=[("ps", bank)])
                    if smp and sub < 4.52:
                        return
                    slot = 1 + t
                    if not (smp and (KX & 1)):
                        S.op("act", lambda e, slot=slot: e.copy(out=vaug[0:P, slot, :, 0:64],
                                                                in_=ps[0:P, 0, 128:256].rearrange("p (g d) -> p g d", g=2)),
                             r=[("ps", 0)], w=[("vaug", slot)])
                    if (smp and not (KX & 2)) or (blk == 1 and t == 7):
                        S.op("act", lambda e: e.copy(out=kvout[0:P, :], in_=ps[0:P, 0, 0:256]), r=[("ps", 0)], w=["kvout"])
                        if smp and not (KX & 4):
                            S.op("sp", lambda e: e.dma_start(out=kws[:, 127, :], in_=kvout[0:NS, 0:128]), r=["kvout"],
                                 w=["kws_b"], dma="kws_b")
                            S.op("sp", lambda e: e.dma_start(out=vws[:, 127, :], in_=kvout[0:NS, 128:256]), r=["kvout"],
                                 w=["vws_b"], dma="vws_b")
                        elif not smp:
                            S.op("sp", lambda e: e.dma_start(out=kwp, in_=kvout[:, 0:128]), r=["kvout"], dma="kwp")
                            S.op("sp", lambda e: e.dma_start(out=vwp, in_=kvout[:, 128:256]), r=["kvout"], dma="vwp")
                    if not (smp and (KX & 8)):
                        S.op("act", lambda e: e.copy(out=vrb[0:P].rearrange("p a b -> p (a b)"), in_=ps[0:P, 3, :]),
                             r=[("ps", 3)], w=["vrb"])
                    if smp and not (KX & 16):
                        S.op("act", lambda e: e.copy(out=vr32[0:NS].rearrange("p a b -> p (a b)"), in_=ps[0:NS, 3, :]),
                             r=[("ps", 3)], w=["vr32"])
                    if smp and sub < 4.53:
                        return
                    cosv = rot[0:P, gt * 64:(gt + 1) * 64].unsqueeze(1).unsqueeze(1).to_broadcast([P, 4, 2, 64])
                    sinv = rot[0:P, 1088 + gt * 64:1088 + (gt + 1) * 64].unsqueeze(1).to_broadcast([P, 4, 64])
                    nsinv = rot[0:P, 2176 + gt * 64:2176 + (gt + 1) * 64].unsqueeze(1).to_broadcast([P, 4, 64])
                    for which, bank, dcol, dsth in (("q", 1, C_DQ, qh), ("k", 2, C_DK, kh)):
                        dv = cst[0:P, dcol + srow:dcol + srow + 4].unsqueeze(2).to_broadcast([P, 4, 128])
                        S.op("dve", lambda e, bank=bank, dv=dv: e.tensor_tensor(
                            out=qs[0:P], in0=ps[0:P, bank, :].rearrange("p (a b) -> p a b", b=128), in1=dv, op=ALU.mult),
                            r=[("ps", bank), "cst"], w=["qs"])
                        S.op("pool", lambda e: e.tensor_tensor(
                            out=tA[0:P].rearrange("p h (two d) -> p h two d", two=2),
                            in0=qs[0:P].rearrange("p h (two d) -> p h two d", two=2), in1=cosv, op=ALU.mult),
                            r=["qs", "rot"], w=["tA"])
                        S.op("pool", lambda e: e.tensor_tensor(out=tB[0:P, :, 0:64], in0=qs[0:P, :, 64:128], in1=nsinv, op=ALU.mult),
                             r=["qs", "rot"], w=["tB0"])
                        S.op("pool", lambda e: e.tensor_tensor(out=tB[0:P, :, 64:128], in0=qs[0:P, :, 0:64], in1=sinv, op=ALU.mult),
                             r=["qs", "rot"], w=["tB1"])
                        S.op("pool", lambda e, dsth=dsth: e.tensor_tensor(out=dsth[0:P], in0=tA[0:P], in1=tB[0:P], op=ALU.add),
                             r=["tA", "tB0", "tB1"], w=[which + "h"])
                    if smp and sub < 4.54:
                        return

                    return
                if sub < 2:
                    return
                if not smp:
                    first = (blk == 0 and t == 0)
                    kbs = [1] if first else [0, 1]
                    sc_cnt = 0
                    for g in range(2):
                        for kb in kbs:
                            kcol = (t + kb) * 128
                            sbank = 6 + sc_cnt % 2
                            e_t = et[sc_cnt % 2]
                            ek = ("et", sc_cnt % 2)
                            sc_cnt += 1
                            ktok = ["kaT_prev"] if (kb == 0 and t == 0) else [("kaT", i) for i in nis_of(ntiles, kcol - 128, 128)]
                            S.op("pe", lambda e, g=g, kcol=kcol, sbank=sbank: e.matmul(
                                ps[:, sbank, :], lhsT=kaT[g * 64:(g + 1) * 64, kcol:kcol + 128],
                                rhs=qaT[g * 64:(g + 1) * 64, :, c0:c0 + 128], start=True, stop=True),
                                r=ktok + qa_tok(c0, 128), w=[("ps", sbank)])
                            S.op("dve", lambda e, g=g, kb=kb, sbank=sbank, e_t=e_t: e.scalar_tensor_tensor(
                                out=e_t[:], in0=ps[:, sbank, :], scalar=0.125,
                                in1=BT[:, kb, g * 4:(g + 1) * 4, :].rearrange("p a b -> p (a b)"), op0=ALU.mult, op1=ALU.add),
                                r=[("ps", sbank), "BT"], w=[ek])
                            S.op("act", lambda e, g=g, kb=kb, e_t=e_t: e.activation(
                                out=pT[g][:, kb, :, :].rearrange("p a b -> p (a b)"), in_=e_t[:], func=AF.Exp),
                                r=[ek], w=[("pT", g, kb)])
                        obank = 0 if g == 0 else 3
                        for hh in range(4):
                            for i, kb in enumerate(kbs):
                                vslot = t + kb
                                S.op("pe", lambda e, g=g, hh=hh, kb=kb, vslot=vslot, obank=obank, i=i: e.matmul(
                                    ps[:, obank, hh * 65:(hh + 1) * 65], lhsT=pT[g][:, kb, hh, :], rhs=vaug[:, vslot, g, :],
                                    start=(i == 0), stop=(i == len(kbs) - 1)),
                                    r=[("pT", g, kb), ("vaug", vslot), "vaug_ones"], w=[("ps", obank)])
                    attn_finish(t, P, c0)
                    qk_transposes(t, P, smp)
                    if sub < 3:
                        return
                    for h in range(4):
                        S.op("pe", lambda e, h=h: e.matmul(ps[:, 1, h * 128:(h + 1) * 128], lhsT=khT[:, h, :], rhs=qhT[:, h, :],
                                                           start=True, stop=True), r=["khT", "qhT"], w=[("ps", 1)])
                    S.op("dve", lambda e: e.tensor_tensor(
                        out=scm[:], in0=ps[:, 1, :].rearrange("p (a b) -> p a b", b=128),
                        in1=caus.unsqueeze(1).to_broadcast([128, 4, 128]), op=ALU.mult), r=[("ps", 1), "cst"], w=["scm"])
                    for h in range(4):
                        S.op("pe", lambda e, h=h: e.matmul(ps[:, 2, h * 128:(h + 1) * 128], lhsT=scm[:, h, :], rhs=vrb[:, h, :],
                                                           start=True, stop=first), r=["scm", "vrb"], w=[("ps", 2)])
                        if not first:
                            S.op("pe", lambda e, h=h: e.matmul(ps[:, 2, h * 128:(h + 1) * 128], lhsT=qhT[:, h, :], rhs=Sb[:, h, :],
                                                               start=False, stop=True), r=["qhT", "Sb"], w=[("ps", 2)])
                    gn_store(t, P, [(ps[:, 2, h * 128:(h + 1) * 128], [("ps", 2)]) for h in range(4)])
                    for h in range(4):
                        S.op("pe", lambda e, h=h: e.matmul(ps[:, 5, h * 128:(h + 1) * 128], lhsT=khb[:, h, :], rhs=vrb[:, h, :],
                                                           start=True, stop=True), r=["khb", "vrb"], w=[("ps", 5)])
                    S.op("dve", lambda e: e.tensor_tensor(out=Sst[:].rearrange("p a b -> p (a b)"), in0=ps[:, 5, :],
                                                          in1=Sst[:].rearrange("p a b -> p (a b)"), op=ALU.add),
                         r=[("ps", 5), "S"], w=["S"])
                    S.op("dve", lambda e: e.tensor_tensor(out=Sst[:], in0=Sst[:],
                                                          in1=cst[:, C_GC:C_GC + 4].unsqueeze(2).to_broadcast([128, 4, 128]),
                                                          op=ALU.mult), r=["S", "cst"], w=["S"])
                    S.op("act", lambda e: e.copy(out=Sb[:], in_=Sst[:]), r=["S"], w=["Sb"])
                    if blk == 1 and t == 7:
                        S.op("sp", lambda e: e.dma_start(out=sp_out.rearrange("h k v -> k h v"), in_=Sst[:]), r=["S"], dma="spout")
                else:
                    if sub < 5:
                        return
                    A_save = A.top
                    Wk32 = A.alloc([128, NS, 128], F32)
                    WkT = A.alloc([128, NS, 128], BF16)
                    Wva = A.alloc([128, NS, 2, 65], BF16)
                    oTs = A.alloc([128, 8, NS], F32)
                    e_s = A.alloc([128, NS, 8], F32)
                    pTs = A.alloc([128, NS, 8], BF16)
                    Sp = [A.alloc([128, 4, 128], F32) for _ in range(2)]
                    Sn = [A.alloc([128, 4, 128], F32) for _ in range(2)]
                    Vm = [A.alloc([128, 4, 128], F32) for _ in range(2)]
                    QTm = A.alloc([128, 4, NS, NS], F32)
                    S.op("sp", lambda e: e.dma_start(out=Wk32[:], in_=kws.rearrange("b r c -> r b c")),
                         r=["kws_a", "kws_b"], w=["Wk32", "w1a", "w1r", "w1q"], dma="wk32")
                    for g in range(2):
                        S.op("pool", lambda e, g=g: e.dma_start(out=Wva[:, :, g, 0:64],
                                                                in_=vws[:, :, g * 64:(g + 1) * 64].rearrange("b r d -> r b d")),
                             r=["vws_a", "vws_b"], w=["Wva", "w1a", "w1r", "w1q"], dma="wva")
                    S.op("dve", lambda e: e.memset(Wva[:, :, :, 64:65], 1.0), w=["Wva1", "w1a", "w1r", "w1q"])
                    for b in range(NS):
                        bank = 6 + (b // 4) % 2
                        S.op("pe", lambda e, b=b, bank=bank: e.transpose(out=ps[:, bank, (b % 4) * 128:(b % 4 + 1) * 128],
                                                                        in_=Wk32[:, b, :], identity=ident),
                             r=["Wk32", "cst"], w=[("ps", bank)])
                        if b % 4 == 3:
                            S.op("act", lambda e, b=b, bank=bank: e.copy(
                                out=WkT[:, b - 3:b + 1, :], in_=ps[:, bank, :].rearrange("p (a b) -> p a b", b=128)),
                                r=[("ps", bank)], w=["WkT", "w1a", "w1r", "w1q"])
                    for b in range(NS):
                        for g in range(2):
                            S.op("pe", lambda e, b=b, g=g: e.matmul(
                                ps[:, 0, b * 8 + g * 4:b * 8 + g * 4 + 4], lhsT=WkT[g * 64:(g + 1) * 64, b, :],
                                rhs=qaT[g * 64:(g + 1) * 64, :, TB + b], start=True, stop=True),
                                r=["WkT"] + qa_tok(TB, NS), w=[("ps", 0)])
                    S.op("dve", lambda e: e.scalar_tensor_tensor(
                        out=e_s[:], in0=ps[:, 0, 0:128].rearrange("p (b h) -> p b h", h=8), scalar=0.125,
                        in1=bias_s[:].unsqueeze(1).to_broadcast([128, NS, 8]), op0=ALU.mult, op1=ALU.add),
                        r=[("ps", 0), "bias_s"], w=["e_s", "w1a", "w1r", "w1q"])
                    S.op("act", lambda e: e.activation(out=pTs[:], in_=e_s[:], func=AF.Exp), r=["e_s"], w=["pTs", "w1a", "w1r", "w1q"])
                    if sub < 5.2:
                        A.top = A_save
                        return
                    for b in range(NS):
                        for g in range(2):
                            S.op("pe", lambda e, b=b, g=g: e.matmul(
                                ps[0:65, 3, b * 8 + g * 4:b * 8 + g * 4 + 4], lhsT=Wva[:, b, g, :],
                                rhs=pTs[:, b, g * 4:(g + 1) * 4], start=True, stop=True),
                                r=["Wva", "Wva1", "pTs"], w=[("ps", 3)])
                    S.op("act", lambda e: e.copy(out=oTs[0:65], in_=ps[0:65, 3, 0:128].rearrange("p (b h) -> p h b", h=8)),
                         r=[("ps", 3)], w=["oTs", "w1a", "w1r", "w1q"])
                    for h in range(8):
                        bank = 0 if h < 4 else 3
                        S.op("pe", lambda e, h=h, bank=bank: e.transpose(
                            out=ps[0:NS, bank, (h % 4) * 65:(h % 4 + 1) * 65], in_=oTs[0:65, h, :], identity=ident[0:65, 0:65]),
                            r=["oTs", "cst"], w=[("ps", bank)])
                    attn_finish(t, P, c0)
                    qk_transposes(t, P, smp)
                    if sub < 5.3:
                        A.top = A_save
                        return
                    S.op("dve", lambda e: e.tensor_tensor(
                        out=QTm[:], in0=qhT32[:].unsqueeze(3).to_broadcast([128, 4, NS, NS]),
                        in1=cst[:, C_EYER:C_EYER + 256].rearrange("p (a b) -> p a b", b=NS).unsqueeze(1).to_broadcast([128, 4, NS, NS]),
                        op=ALU.mult), r=["qhT32", "cst"], w=["QTm", "w1a", "w1r", "w1q"])
                    for b in range(NS):
                        i2 = b % 2
                        S.op("sp", lambda e, b=b, i2=i2: e.dma_start(out=Sp[i2][:], in_=st_in[b].rearrange("h k v -> k h v")),
                             w=[("Sp", i2), "w1a", "w1r", "w1q"], dma=("Sp", i2))
                        S.op("pool", lambda e, b=b, i2=i2: e.tensor_scalar(
                            out=Vm[i2][0:NS].rearrange("p a b -> p (a b)"), in0=vr32[0:NS].rearrange("p a b -> p (a b)"),
                            scalar1=ident[0:NS, b:b + 1], scalar2=None, op0=ALU.mult),
                            r=["vr32", "cst"], w=[("Vm", i2), "w1a", "w1r", "w1q"])
                        ub = 1 + i2
                        for h in range(4):
                            S.op("pe", lambda e, h=h, ub=ub, i2=i2: e.matmul(
                                ps[:, ub, h * 128:(h + 1) * 128], lhsT=kh[0:NS, h, :], rhs=Vm[i2][0:NS, h, :],
                                start=True, stop=True), r=["kh", ("Vm", i2)], w=[("ps", ub)])
                        for h in range(4):
                            S.op("dve", lambda e, h=h, ub=ub, i2=i2: e.scalar_tensor_tensor(
                                out=Sn[i2][:, h, :], in0=Sp[i2][:, h, :], scalar=float(GAMMAS[h]),
                                in1=ps[:, ub, h * 128:(h + 1) * 128], op0=ALU.mult, op1=ALU.add),
                                r=[("Sp", i2), ("ps", ub)], w=[("Sn", i2), "w1a", "w1r", "w1q"])
                        S.op("sp", lambda e, b=b, i2=i2: e.dma_start(out=ss_out[b].rearrange("h k v -> k h v"), in_=Sn[i2][:]),
                             r=[("Sn", i2)], dma=("ssout", i2))
                        for h in range(4):
                            if sub < 5.4:
                                break
                            S.op("pe", lambda e, h=h, b=b, i2=i2: e.matmul(
                                ps[0:NS, 4 + h, 0:128], lhsT=QTm[:, h, b, :], rhs=Sn[i2][:, h, :],
                                start=(b == 0), stop=(b == NS - 1)), r=["QTm", ("Sn", i2)], w=[("ps", 4 + h)])
                    if sub >= 5.4:
                        gn_store(t, P, [(ps[0:NS, 4 + h, 0:128], [("ps", 4 + h)]) for h in range(4)])
                    A.top = A_save

            if sub < 1:
                return
            for t in tiles:
                if t == 8 and (sub < 4.5 or (KX & 32)):
                    continue
                if t > 0 and sub < 4:
                    continue
                S.pe_sync = (t == 8)
                b1_tile(t, "A")
                S.pe_sync = False
                if t == 0:
                    b0_all()
                    S.pe_sync = False
                S.pe_sync = (t == 8)
                b1_tile(t, "B")
                S.pe_sync = False
            if sub < 6:
                return
            if blk == 0:
                S.op("dve", lambda e: e.tensor_copy(out=kprev[:], in_=kaT[:, TB:TB + 128]), r=[("kaT", i) for i in nis_of(ntiles, TB - 128, 128)], w=["kprev"])
                S.op("dve", lambda e: e.tensor_copy(out=vprev[:], in_=vaug[:, 8, :, :]), r=[("vaug", 8), "vaug_ones"],
                     w=["vprev"])

            S.barrier()
            A.top = b12_base
            w2 = A.alloc([128, 8, 2560], BF16)
            wao = A.alloc([128, 4, D], BF16)
            wro = A.alloc([128, 4, D], BF16)
            wo = A.alloc([128, 8, D], BF16)
            sgr = A.alloc([128, 512], F32)
            rr = A.alloc([128, 512], F32)
            rT = A.alloc([128, 4, 128], BF16)
            sa = A.alloc([128, D], F32)
            m1 = A.alloc([128, D], F32)
            mT = A.alloc([128, 8, 128], BF16)
            def ld_w2(lo, hi, tok):
                S.op("pool", lambda e: e.dma_start(out=w2[:, :, lo:hi],
                                                   in_=w_in[:, 2304 + lo:2304 + hi].rearrange("(kc p) c -> p kc c", p=128)),
                     w=[tok], dma=tok)
            ld_w2(0, 512, "w2a")
            S.op("pool", lambda e: e.dma_start(out=wao[:], in_=w_ao.rearrange("(kc p) c -> p kc c", p=128)), w=["wao"], dma="wao")
            S.op("pool", lambda e: e.dma_start(out=wro[:], in_=w_ro.rearrange("(kc p) c -> p kc c", p=128)), w=["wro"], dma="wro")
            ld_w2(512, 1536, "w2b")
            ld_w2(1536, 2560, "w2c")
            S.op("pool", lambda e: e.dma_start(out=wo[:], in_=w_o.rearrange("(kc p) c -> p kc c", p=128)), w=["wo"], dma="wo")
            load_ln(1)
            def b2_tile(t):
                P, c0 = tile_geom(t)
                for kc in range(8):
                    S.op("pe", lambda e, kc=kc: e.matmul(ps[0:P, 0, :], lhsT=hT[:, kc, c0:c0 + P], rhs=w2[:, kc, 0:512],
                                                         start=(kc == 0), stop=(kc == 7)), r=["w2a", ("hT", t)], w=[("ps", 0)])
                for dh in range(2):
                    for c in range(4):
                        S.op("pe", lambda e, c=c, dh=dh: e.matmul(ps[0:P, 2 + dh, :], lhsT=oaT[:, c, c0:c0 + P],
                                                                  rhs=wao[:, c, dh * 512:(dh + 1) * 512],
                                                                  start=(c == 0), stop=(c == 3)),
                             r=["wao", ("oaT", t)], w=[("ps", 2 + dh)])
                S.op("act", lambda e: e.activation(out=sgr[0:P], in_=ps[0:P, 0, :], func=AF.Silu), r=[("ps", 0)], w=["sgr"])
                S.op("dve", lambda e: e.tensor_tensor(out=rr[0:P], in0=sgr[0:P], in1=rgn[0:P, t, :], op=ALU.mult),
                     r=["sgr", ("rgn", t)], w=["rr"])
                for c in range(4):
                    S.op("pe", lambda e, c=c: e.transpose(out=ps[:, 1, c * 128:c * 128 + P], in_=rr[0:P, c * 128:(c + 1) * 128],
                                                          identity=ident[0:P, 0:P]), r=["rr", "cst"], w=[("ps", 1)])
                S.op("act", lambda e: e.copy(out=rT[:, :, 0:P], in_=ps[:, 1, :].rearrange("p (a b) -> p a b", b=128)[:, :, 0:P]),
                     r=[("ps", 1)], w=["rT"])
                if t == 8 and (KX & 64):
                    return
                for dh in range(2):
                    for c in range(4):
                        S.op("pe", lambda e, c=c, dh=dh: e.matmul(ps[0:P, 4 + dh, :], lhsT=rT[:, c, 0:P],
                                                                  rhs=wro[:, c, dh * 512:(dh + 1) * 512],
                                                                  start=(c == 0), stop=(c == 3)),
                             r=["wro", "rT"], w=[("ps", 4 + dh)])
                if t == 8 and (KX & 128):
                    return
                for gi, goff in enumerate((512, 1536)):
                    for dh in range(2):
                        for kc in range(8):
                            S.op("pe", lambda e, kc=kc, dh=dh, goff=goff: e.matmul(
                                ps[0:P, 6 + dh, :], lhsT=hT[:, kc, c0:c0 + P], rhs=w2[:, kc, goff + dh * 512:goff + (dh + 1) * 512],
                                start=(kc == 0), stop=(kc == 7)), r=["w2b" if gi == 0 else "w2c", ("hT", t)], w=[("ps", 6 + dh)])
                    S.op("act", lambda e: e.activation(out=sa[0:P], in_=PS2(6, P), func=AF.Tanh, scale=0.5),
                         r=[("ps", 6), ("ps", 7)], w=["sa"])
                    if gi == 0:
                        S.op("dve", lambda e: e.scalar_tensor_tensor(out=m1[0:P], in0=sa[0:P], scalar=1.0, in1=PS2(2, P),
                                                                     op0=ALU.add, op1=ALU.mult),
                             r=["sa", ("ps", 2), ("ps", 3)], w=["m1"])
                    else:
                        S.op("dve", lambda e: e.scalar_tensor_tensor(out=sa[0:P], in0=sa[0:P], scalar=1.0, in1=PS2(4, P),
                                                                     op0=ALU.add, op1=ALU.mult),
                             r=["sa", ("ps", 4), ("ps", 5)], w=["sa"])
                        S.op("dve", lambda e: e.tensor_tensor(out=m1[0:P], in0=m1[0:P], in1=sa[0:P], op=ALU.add),
                             r=["sa", "m1"], w=["m1"])
                if t == 8 and (KX & 256):
                    return
                for kc in range(8):
                    S.op("pe", lambda e, kc=kc: e.transpose(out=ps[:, kc // 4, (kc % 4) * 128:(kc % 4) * 128 + P],
                                                            in_=m1[0:P, kc * 128:(kc + 1) * 128], identity=ident[0:P, 0:P]),
                         r=["m1", "cst"], w=[("ps", kc // 4)])
                S.op("act", lambda e: e.copy(out=mT[:, :, 0:P],
                                             in_=ps[:, 0:2, :].rearrange("p a (b c) -> p (a b) c", c=128)[:, :, 0:P]),
                     r=[("ps", 0), ("ps", 1)], w=["mT"])
                for dh in range(2):
                    for kc in range(8):
                        S.op("pe", lambda e, kc=kc, dh=dh: e.matmul(ps[0:P, 2 + dh, :], lhsT=mT[:, kc, 0:P],
                                                                    rhs=wo[:, kc, dh * 512:(dh + 1) * 512],
                                                                    start=(kc == 0), stop=(kc == 7)),
                             r=["wo", "mT"], w=[("ps", 2 + dh)])
                ln_elem(t, PS2(2, P), [("ps", 2), ("ps", 3)], 0.5 / ALPHA, LN_EPS / (ALPHA * ALPHA))
                if debug and blk == 0:
                    S.op("sp", lambda e, t=t: e.dma_start(out=dbg["h2"][t], in_=h_tok[:, t, :]), r=[("h", t)], dma=("dbg2", t))

            pend = None
            for t in tiles:
                if t == 8 and (KX & 32):
                    continue
                S.pe_sync = (t == 8)
                b2_tile(t)
                S.pe_sync = False
                if pend is not None:
                    transpose_to_hT(pend, 6)
                pend = t
            transpose_to_hT(pend, 6)

        stage = 0
        for blk in range(2):
            tiles = list(range(8)) + ([8] if blk == 0 else [])
            if stage >= max_stage:
                break
            stage += 1
            for t in tiles:
                P, c0 = tile_geom(t)
                src = xp[(blk * 8 + t) * 128:(blk * 8 + t + 1) * 128, :] if t < 8 else xs
                S.op("sp", lambda e, t=t, P=P, src=src: e.dma_start(out=h_tok[0:P, t, :], in_=src), w=[("h", t)], dma=("x", t))
            for ti, t in enumerate(tiles):
                transpose_to_hT(t, 4 + (ti % 2) * 2)
            if stage >= max_stage:
                break
            stage += 1
            ffn_phase(blk, 0, 0, final=False)
            if stage >= max_stage:
                break
            stage += 1
            mixer_phase(blk)
            if stage >= max_stage:
                break
            stage += 1
            ffn_phase(blk, 1, 2, final=True)
        S.barrier()
        S.op("sp", lambda e: e.nop(), r=[], w=[])
        S.emit(nc, es)
        build_nc.info = dict(nops=len(S.ops), nsems=S.nsems, sbuf_peak=A.peak)
    return nc


_CACHE = {}
KX = 0


def kernel(x_prompt, x_sample, cache_k_win, cache_v_win, state_ret, rel_bias, w_in, attn_sinks,
           w_attn_out, w_ret_out, w_o, ffn1_w_up, ffn1_w_down, ffn2_w_up, ffn2_w_down,
           ln1_g, ln1_b, ln2_g, ln2_b, ln3_g, ln3_b, _debug=False):
    f = lambda a: np.ascontiguousarray(np.asarray(a, dtype=np.float32))
    import os as _os
    _ms = int(_os.environ.get("K_STAGES", "99"))
    _sub = float(_os.environ.get("K_SUB", "99"))
    global KX
    KX = int(_os.environ.get("K_X", "0"))
    key = ("nc", bool(_debug), _ms, _sub, KX, _os.environ.get("K_PAD", "0"))
    if key not in _CACHE:
        _CACHE[key] = build_nc(debug=_debug, max_stage=_ms, sub=_sub)
    nc = _CACHE[key]
    consts = make_consts()
    shared = dict(relb=f(rel_bias), w_in=f(w_in)[0], sinks=f(attn_sinks)[0], w_ao=f(w_attn_out)[0], w_ro=f(w_ret_out)[0],
                  w_o=f(w_o)[0], f1u=f(ffn1_w_up)[0], f2u=f(ffn2_w_up)[0], f1d=f(ffn1_w_down)[0], f2d=f(ffn2_w_down)[0],
                  ln1g=f(ln1_g)[0], ln1b=f(ln1_b)[0], ln2g=f(ln2_g)[0], ln2b=f(ln2_b)[0], ln3g=f(ln3_g)[0], ln3b=f(ln3_b)[0],
                  consts=consts)
    xp, xs = f(x_prompt), f(x_sample)
    ckf, cvf, stf = f(cache_k_win), f(cache_v_win), f(state_ret)
    in_maps = []
    for c in range(NCORES):
        m = dict(shared)
        sl = slice(c * NS, (c + 1) * NS)
        m["xp"] = xp[c]
        m["xs"] = xs[sl, 0, :]
        m["ck"] = ckf[0, sl].reshape(NS, 128, 128)
        m["cv"] = cvf[0, sl].reshape(NS, 128, 128)
        m["st"] = stf[0, sl]
        in_maps.append(m)
    _nco = int(_os.environ.get("K_CORES", str(NCORES)))
    res = run_bass_kernel_spmd(nc, in_maps[:_nco], core_ids=list(range(_nco)))
    R = list(res.results)
    while len(R) < NCORES:
        R.append({k: np.zeros_like(v) for k, v in R[0].items()})
    y_p = np.stack([R[c]["yp"] for c in range(NCORES)], 0)
    y_s = np.concatenate([R[c]["ys"] for c in range(NCORES)], 0).reshape(128, 1, D)
    kwp = np.stack([R[c]["kwp"] for c in range(NCORES)], 0).reshape(1, 8, 128, 2, 64)
    vwp = np.stack([R[c]["vwp"] for c in range(NCORES)], 0).reshape(1, 8, 128, 2, 64)
    spo = np.stack([R[c]["sp"] for c in range(NCORES)], 0).reshape(1, 8, 4, 128, 128)
    kws = np.concatenate([R[c]["kws"] for c in range(NCORES)], 0).reshape(1, 128, 128, 2, 64)
    vws = np.concatenate([R[c]["vws"] for c in range(NCORES)], 0).reshape(1, 128, 128, 2, 64)
    sso = np.concatenate([R[c]["ss"] for c in range(NCORES)], 0).reshape(1, 128, 4, 128, 128)
    outs = tuple(np.ascontiguousarray(a.astype(np.float32)) for a in (y_p, y_s, kwp, vwp, spo, kws, vws, sso))
    if _debug:
        kernel.dbg = [{k: v for k, v in R[c].items() if k.startswith("dbg")} for c in range(NCORES)]
    return outs
```

```python
import numpy as np
from contextlib import ExitStack
import concourse.bass as bass
import concourse.mybir as mybir
from concourse.bass_utils import run_bass_kernel_spmd

F32 = mybir.dt.float32
BF16 = mybir.dt.bfloat16
AF = mybir.ActivationFunctionType
ALU = mybir.AluOpType

NCORES = 8
D = 1024
SEQ = 2048
DFF = 2816
NJ = DFF // 128
NS = 16
TB = 1024
NCOL = TB + NS
PAST = 16384
ALPHA = 2.0 ** 0.25
LN_EPS = 1e-5
GN_EPS = 1e-6
MASKV = -30000.0
SB_BASE = 16512
SB_END = 229376

C_ID = 0
C_J = 128
C_CAUS = 256
C_DQ = 384
C_DK = C_DQ + 8
C_GC = C_DK + 8
C_EYER = C_GC + 4
C_OH1 = C_EYER + 256
C_OH2 = C_OH1 + 128
C_MH = C_OH2 + 128
C_ONE = C_MH + 1
NC1 = ((C_ONE + 1 + 7) // 8) * 8
C_COS = NC1
C_SIN = C_COS + 17 * 64
C_NSIN = C_SIN + 17 * 64
NROT = 3 * 17 * 64
NCONST = NC1 + NROT
GAMMAS = [1.0 - 2.0 ** (-5.0 - h) for h in range(4)]


def _bucket(d):
    d = np.asarray(d)
    n = np.maximum(d, 0)
    ratio = np.maximum(n, 1).astype(np.float32) / np.float32(16)
    large = 16 + (np.log(np.maximum(ratio, np.float32(1.0))).astype(np.float32)
                  / np.float32(np.log(128 / 16)) * np.float32(16)).astype(np.int32)
    large = np.minimum(large, 31)
    return np.where(n < 16, n, large)


def make_consts():
    c = np.zeros((128, NCONST), np.float32)
    c[:, C_ID:C_ID + 128] = np.eye(128, dtype=np.float32)
    c[:, C_J:C_J + 128] = np.eye(128, dtype=np.float32)[::-1]
    jj = np.arange(128)
    c[:, C_CAUS:C_CAUS + 128] = (jj[None, :] >= jj[:, None]).astype(np.float32)
    inv = (np.float32(10000.0) ** (-(np.arange(64, dtype=np.float32) / np.float32(64)))).astype(np.float32)
    for t in range(17):
        pos = (t * 128 + np.arange(128)) if t < 16 else np.full(128, PAST)
        ang = (pos.astype(np.float32)[:, None] * inv[None, :]).astype(np.float32)
        c[:, C_COS + t * 64:C_COS + (t + 1) * 64] = np.cos(ang.astype(np.float64)).astype(np.float32)
        c[:, C_SIN + t * 64:C_SIN + (t + 1) * 64] = np.sin(ang.astype(np.float64)).astype(np.float32)
        c[:, C_NSIN + t * 64:C_NSIN + (t + 1) * 64] = -np.sin(ang.astype(np.float64)).astype(np.float32)
    p = np.arange(128, dtype=np.float64)
    for h in range(4):
        lg = np.log1p(-(2.0 ** (-5.0 - h)))
        c[:, C_DQ + h] = np.exp((p + 1) * lg)
        c[:, C_DK + h] = (128.0 ** -0.5) * np.exp(-(p + 1) * lg)
        c[:, C_DQ + 4 + h] = 1.0
        c[:, C_DK + 4 + h] = 128.0 ** -0.5
        c[:, C_GC + h] = np.exp(128 * lg)
    c[:, C_EYER:C_EYER + 256] = np.eye(16, dtype=np.float32).reshape(1, 256)
    b1 = _bucket(np.arange(128))
    b2 = _bucket(127 - np.arange(128))
    for r in range(128):
        c[b1[r], C_OH1 + r] = 1.0
        c[b2[r], C_OH2 + r] = 1.0
    c[:, C_MH] = -0.5
    c[:, C_ONE] = 1.0
    return c


def _rnd_tile(x):
    return 32 if x <= 32 else (64 if x <= 64 else 128)


class _FakePE:
    def __init__(self):
        self.mode = None

    def matmul(self, out, lhsT=None, rhs=None, **kw):
        self.mode = (_rnd_tile(lhsT.shape[0]), _rnd_tile(int(np.prod(lhsT.shape[1:]))))

    def transpose(self, out=None, in_=None, identity=None):
        self.mode = (_rnd_tile(in_.shape[0]), _rnd_tile(int(np.prod(in_.shape[1:]))))


class Sched:
    ENGS = ("pe", "act", "dve", "pool", "sp")

    def __init__(self):
        self.ops = []
        self.last_w = {}
        self.readers = {}
        self.dma_cnt = {}
        self.bar = None
        self.bar_passed = set()
        self.last_eng = {}
        self.last_dma = {}
        self.pe_sync = False
        self.pe_sync_once = False
        self.pe_prev_small = False

    def _stream(self, idx):
        o = self.ops[idx]
        return ("dma", o["dma"]) if o["dma"] is not None else ("eng", o["eng"])

    def op(self, eng, fn, r=(), w=(), dma=None):
        idx = len(self.ops)
        deps = {}

        def add(d):
            s = self._stream(d)
            if deps.get(s, -1) < d:
                deps[s] = d
        for t in r:
            lw = self.last_w.get(t)
            if lw is not None:
                add(lw)
        for t in w:
            lw = self.last_w.get(t)
            if lw is not None:
                add(lw)
            for d in self.readers.get(t, {}).values():
                add(d)
        if self.bar is not None and eng not in self.bar_passed:
            for d in self.bar:
                add(d)
            self.bar_passed.add(eng)
        force = False
        if eng == "pe" and dma is None and (self.pe_sync or self.pe_sync_once):
            self.pe_sync_once = self.pe_sync
            if "pe" in self.last_eng:
                add(self.last_eng["pe"])
                force = True
        o = dict(eng=eng, fn=fn, deps=deps, dma=dma, need_inc=False, ev=None, force=force)
        if dma is not None:
            self.dma_cnt[dma] = self.dma_cnt.get(dma, 0) + 1
            o["ev"] = ("dma", dma, 16 * self.dma_cnt[dma])
            self.last_dma[dma] = idx
        else:
            self.last_eng[eng] = idx
        self.ops.append(o)
        me = ("dma", dma) if dma is not None else ("eng", eng)
        for t in r:
            self.readers.setdefault(t, {})[me] = idx
        for t in w:
            self.last_w[t] = idx
            self.readers[t] = {}
        return idx

    def barrier(self):
        self.bar = set(self.last_eng.values()) | set(self.last_dma.values())
        self.bar_passed = set()

    def finalize(self):
        ops = self.ops
        for o in ops:
            real = []
            for s, d in o["deps"].items():
                od = ops[d]
                if s == ("eng", "pe") and o["eng"] == "pe" and o["dma"] is None and not o["force"]:
                    continue
                real.append(d)
                if od["dma"] is None:
                    od["need_inc"] = True
            o["deps"] = real
        cnt = {e: 0 for e in self.ENGS}
        for o in ops:
            if o["dma"] is None and o["need_inc"]:
                cnt[o["eng"]] += 1
                o["ev"] = ("eng", o["eng"], cnt[o["eng"]])
        seen = {e: {} for e in self.ENGS}
        for o in ops:
            waits = {}
            for d in o["deps"]:
                kind, key, val = ops[d]["ev"]
                k = (kind, key)
                if seen[o["eng"]].get(k, 0) >= val:
                    continue
                waits[k] = max(waits.get(k, 0), val)
            for k, v in waits.items():
                seen[o["eng"]][k] = v
            o["waits"] = waits

    def emit(self, nc, es):
        self.finalize()
        sems = {}
        n = [0]

        def sem(k):
            if k not in sems:
                n[0] += 1
                sems[k] = es.enter_context(nc.semaphore("s%d" % n[0]))
            return sems[k]
        for o in self.ops:
            for k in o["waits"]:
                sem(k)
            if o["ev"] is not None:
                sem((o["ev"][0], o["ev"][1]))
        block = es.enter_context(nc.Block())
        ops = self.ops

        import os as _os2
        pad = int(_os2.environ.get("K_PAD", "0"))

        def run(engname, eng):
            for _ in range(pad):
                eng.nop()
            for o in ops:
                if o["eng"] != engname:
                    continue
                for k, v in o["waits"].items():
                    eng.wait_ge(sems[k], v)
                ins = o["fn"](eng)
                if o["dma"] is not None:
                    ins.then_inc(sems[("dma", o["dma"])], 16)
                elif o["need_inc"]:
                    ins.then_inc(sems[("eng", engname)], 1)

        @block.tensor
        def _(e):
            run("pe", e)

        @block.scalar
        def _(e):
            run("act", e)

        @block.vector
        def _(e):
            run("dve", e)

        @block.gpsimd
        def _(e):
            run("pool", e)

        @block.sync
        def _(e):
            run("sp", e)
        self.nsems = len(sems)


class Arena:
    def __init__(self, nc, base, end):
        self.nc, self.top, self.end = nc, base, end
        self.n = 0
        self.peak = base

    def alloc(self, shape, dtype):
        sz = int(np.prod(shape[1:])) * (2 if dtype == BF16 else 4)
        off = (self.top + 31) // 32 * 32
        assert off + sz <= self.end, ("SBUF overflow", off + sz, self.end)
        self.top = off + sz
        self.peak = max(self.peak, self.top)
        self.n += 1
        return self.nc.alloc_sbuf_tensor_at("sb%d" % self.n, list(shape), dtype, offset=off)


def build_nc(debug=False, max_stage=99, sub=99):
    nc = bass.Bass("TRN2", target_bir_lowering=False)

    def din(name, shape):
        return nc.dram_tensor(name, list(shape), F32, kind="ExternalInput").ap()

    def dout(name, shape):
        return nc.dram_tensor(name, list(shape), F32, kind="ExternalOutput").ap()
    xp = din("xp", [SEQ, D])
    xs = din("xs", [NS, D])
    ck = din("ck", [NS, 128, 128])
    cv = din("cv", [NS, 128, 128])
    st_in = din("st", [NS, 4, 128, 128])
    relb = din("relb", [32, 8])
    w_in = din("w_in", [D, 4864])
    sinks = din("sinks", [8])
    w_ao = din("w_ao", [512, D])
    w_ro = din("w_ro", [512, D])
    w_o = din("w_o", [D, D])
    wup_d = [din("f1u", [D, 2 * DFF]), din("f2u", [D, 2 * DFF])]
    wdn_d = [din("f1d", [DFF, D]), din("f2d", [DFF, D])]
    lng = [din("ln%dg" % i, [D]) for i in (1, 2, 3)]
    lnb = [din("ln%db" % i, [D]) for i in (1, 2, 3)]
    consts_d = din("consts", [128, NCONST])
    yp = dout("yp", [SEQ, D])
    ys = dout("ys", [NS, D])
    kwp = dout("kwp", [128, 128])
    vwp = dout("vwp", [128, 128])
    sp_out = dout("sp", [4, 128, 128])
    kws = dout("kws", [NS, 128, 128])
    vws = dout("vws", [NS, 128, 128])
    ss_out = dout("ss", [NS, 4, 128, 128])
    scr = nc.dram_tensor("scr", [2, 8, 256], F32, kind="Internal").ap()
    dbg = {}
    if debug:
        dbg["h1"] = dout("dbg_h1", [9, 128, D])
        dbg["h2"] = dout("dbg_h2", [9, 128, D])

    es = ExitStack()
    with es:
        S = Sched()
        A = Arena(nc, SB_BASE, SB_END)
        ps = es.enter_context(nc.psum_tensor("ps", [128, 8, 512], F32))

        h_tok = A.alloc([128, 9, D], F32)
        hT = A.alloc([128, 8, NCOL], BF16)
        cst = A.alloc([128, NC1], F32)
        w1q = A.alloc([128, 8, 512], BF16)
        BT = A.alloc([128, 2, 8, 128], F32)
        lg_t = A.alloc([128, D], F32)
        lb_t = A.alloc([128, D], F32)
        Sst = A.alloc([128, 4, 128], F32)
        Sb = A.alloc([128, 4, 128], BF16)
        bias_s = A.alloc([128, 8], F32)
        sinkexp = A.alloc([128, 8], F32)
        stt = [A.alloc([128, 4, 6], F32) for _ in range(2)]
        mvt = [A.alloc([128, 4, 8], F32) for _ in range(2)]
        kprev = A.alloc([128, 128], BF16)
        vprev = A.alloc([128, 2, 65], BF16)
        phase_base = A.top

        ident = cst[:, C_ID:C_ID + 128]
        Jm = cst[:, C_J:C_J + 128]
        caus = cst[:, C_CAUS:C_CAUS + 128]
        mhalf = cst[:, C_MH:C_MH + 1]

        def PSB(b, P=128, n=512):
            return ps[0:P, b, 0:n]

        def PS2(b, P=128):
            return ps[0:P, b:b + 2, :].rearrange("p a b -> p (a b)")

        S.op("sp", lambda e: e.dma_start(out=cst[:], in_=consts_d[:, 0:NC1]), w=["cst"], dma="cst")
        S.op("sp", lambda e: e.dma_start(out=sinkexp[:], in_=sinks.partition_broadcast(128)), w=["sinkexp"], dma="c2")
        S.op("act", lambda e: e.activation(out=sinkexp[:], in_=sinkexp[:], func=AF.Exp), r=["sinkexp"], w=["sinkexp"])
        S.op("dve", lambda e: e.memset(Sst[:], 0.0), w=["S"])
        S.op("dve", lambda e: e.memset(Sb[:], 0.0), w=["Sb"])
        A0 = A.top
        rb = A.alloc([32, 8], F32)
        TTs = A.alloc([8, 128], F32)
        Lsb = A.alloc([8, 2, 256], F32)
        G = A.alloc([128, 2, 8, 128], F32)
        S.op("sp", lambda e: e.dma_start(out=rb[:], in_=relb), w=["rb"], dma="c3")
        S.op("pe", lambda e: e.matmul(ps[0:8, 0, 0:128], lhsT=rb[:], rhs=cst[0:32, C_OH1:C_OH1 + 128], start=True, stop=True),
             r=["rb", "cst"], w=[("ps", 0)])
        S.op("act", lambda e: e.copy(out=TTs[:], in_=ps[0:8, 0, 0:128]), r=[("ps", 0)], w=["TTs"])
        S.op("dve", lambda e: e.memset(Lsb[:], MASKV), w=["Lsb"])
        S.op("dve", lambda e: e.tensor_copy(out=Lsb[:, 0, 0:127], in_=TTs[:, 1:128]), r=["TTs", "Lsb"], w=["Lsb"])
        S.op("dve", lambda e: e.tensor_copy(out=Lsb[:, 1, 127:255], in_=TTs[:, 0:128]), r=["TTs", "Lsb"], w=["Lsb"])
        S.op("sp", lambda e: e.dma_start(out=scr.rearrange("t h u -> h t u"), in_=Lsb[:]), r=["Lsb"], w=["scr"], dma="c4")
        for tab in range(2):
            hank = bass.AP(tensor=scr.tensor, offset=tab * 2048, ap=[[1, 128], [256, 8], [1, 128]])
            S.op("sp", lambda e, tab=tab, hank=hank: e.dma_start(out=G[:, tab, :, :], in_=hank), r=["scr"], w=["G"], dma="c5")
        for tab in range(2):
            for hf in range(2):
                b = 1 + tab * 2 + hf
                S.op("pe", lambda e, tab=tab, hf=hf, b=b: e.matmul(
                    ps[:, b, :], lhsT=Jm, rhs=G[:, tab, hf * 4:(hf + 1) * 4, :].rearrange("p a b -> p (a b)"),
                    start=True, stop=True), r=["cst", "G"], w=[("ps", b)])
                S.op("act", lambda e, tab=tab, hf=hf, b=b: e.copy(
                    out=BT[:, tab, hf * 4:(hf + 1) * 4, :].rearrange("p a b -> p (a b)"), in_=ps[:, b, :]),
                    r=[("ps", b)], w=["BT"])
        S.op("pe", lambda e: e.matmul(ps[:, 5, 0:8], lhsT=cst[0:32, C_OH2:C_OH2 + 128], rhs=rb[:], start=True, stop=True),
             r=["rb", "cst"], w=[("ps", 5)])
        S.op("act", lambda e: e.copy(out=bias_s[:], in_=ps[:, 5, 0:8]), r=[("ps", 5)], w=["bias_s"])
        A.top = A0

        def tile_geom(t):
            return (128, t * 128) if t < 8 else (NS, TB)

        def hT_tok(c0, n):
            toks = [("hT", t) for t in range(c0 // 128, min(8, (c0 + n + 127) // 128))]
            if c0 + n > TB:
                toks.append(("hT", 8))
            return toks

        def nis_of(ntiles, c0, n):
            return [i for i, (a, m) in enumerate(ntiles) if a < c0 + n and a + m > c0]

        def transpose_to_hT(t, pbank):
            P, c0 = tile_geom(t)
            S.pe_sync = (t == 8)
            for kc in range(8):
                S.op("pe", lambda e, kc=kc: e.transpose(out=ps[:, pbank + kc // 4, (kc % 4) * 128:(kc % 4) * 128 + P],
                                                        in_=h_tok[0:P, t, kc * 128:(kc + 1) * 128], identity=ident[0:P, 0:P]),
                     r=[("h", t), "cst"], w=[("ps", pbank + kc // 4)])
            S.op("act", lambda e: e.copy(out=hT[:, :, c0:c0 + P],
                                         in_=ps[:, pbank:pbank + 2, :].rearrange("p a (b c) -> p (a b) c", c=128)[:, :, 0:P]),
                 r=[("ps", pbank), ("ps", pbank + 1)], w=[("hT", t)])
            S.pe_sync = False

        def load_ln(i):
            S.op("sp", lambda e: e.dma_start(out=lg_t[:], in_=lng[i].partition_broadcast(128)), w=["lng"], dma="lng")
            S.op("sp", lambda e: e.dma_start(out=lb_t[:], in_=lnb[i].partition_broadcast(128)), w=["lnb"], dma="lnb")

        def ln_elem(t, src, src_tok, cscale, eps_eff):
            P, c0 = tile_geom(t)
            h = h_tok[0:P, t, :]
            st, mv = stt[t % 2], mvt[t % 2]
            tk = ("lnt", t % 2)
            S.op("dve", lambda e: e.scalar_tensor_tensor(out=h, in0=src, scalar=cscale, in1=h, op0=ALU.mult, op1=ALU.add),
                 r=list(src_tok) + [("h", t)], w=[("h", t)])
            for a in range(2):
                S.op("dve", lambda e, a=a: e.bn_stats(out=st[0:P, a, :], in_=h_tok[0:P, t, a * 512:(a + 1) * 512]),
                     r=[("h", t)], w=[tk])
            S.op("dve", lambda e: e.bn_aggr(out=mv[0:P, 0, 0:2], in_=st[0:P, 0:2, :].rearrange("p a b -> p (a b)")),
                 r=[tk], w=[tk])
            S.op("dve", lambda e: e.tensor_scalar(out=mv[0:P, 0, 2:3], in0=mv[0:P, 0, 1:2], scalar1=eps_eff, scalar2=None,
                                                  op0=ALU.add), r=[tk], w=[tk])
            S.op("pool", lambda e: e.tensor_tensor(out=mv[0:P, 0, 3:4], in0=mv[0:P, 0, 2:3], in1=mhalf[0:P], op=ALU.pow),
                 r=[tk, "cst"], w=[tk])
            S.op("dve", lambda e: e.scalar_tensor_tensor(out=mv[0:P, 0, 4:5], in0=mv[0:P, 0, 0:1], scalar=-1.0,
                                                         in1=mv[0:P, 0, 3:4], op0=ALU.mult, op1=ALU.mult), r=[tk], w=[tk])
            S.op("act", lambda e: e.activation(out=h, in_=h, func=AF.Identity, scale=mv[0:P, 0, 3:4], bias=mv[0:P, 0, 4:5]),
                 r=[tk, ("h", t)], w=[("h", t)])
            S.op("dve", lambda e: e.tensor_tensor(out=h, in0=h, in1=lg_t[0:P], op=ALU.mult), r=[("h", t), "lng"], w=[("h", t)])
            S.op("dve", lambda e: e.tensor_tensor(out=h, in0=h, in1=lb_t[0:P], op=ALU.add), r=[("h", t), "lnb"], w=[("h", t)])

        w1t_holder = {}

        def ffn_phase(blk, which, ln_i, final):
            tiles = list(range(8)) + ([8] if blk == 0 else [])
            ntiles = [(0, 352), (352, 352), (704, 336)] if blk == 0 else [(0, 512), (512, 512)]
            S.barrier()
            A.top = phase_base
            actT = A.alloc([128, NJ, NCOL], BF16)
            wdn = A.alloc([128, NJ, D], BF16)
            if "t" not in w1t_holder:
                off = (A.top + 31) // 32 * 32
                assert off + 8 * 1792 * 2 <= SB_END
                w1t_holder["t"] = nc.alloc_sbuf_tensor_at("w1top", [128, 8, 1792], BF16, offset=off)
                w1t_holder["off"] = off
            wup = [A.alloc([128, 8, 2, 128], BF16) for _ in range(3)]
            sg = [A.alloc([128, 512], F32) for _ in range(2)]
            wu_v = wup_d[which].rearrange("(kc p) (two j c) -> p kc two j c", p=128, two=2, c=128)
            wd_v = wdn_d[which].rearrange("(j p) c -> p j c", p=128)

            def load_wup(j):
                for half in range(2):
                    S.op("pool", lambda e, half=half: e.dma_start(out=wup[j % 3][:, :, half, :], in_=wu_v[:, :, half, j, :]),
                         w=[("wup", j % 3)], dma=("wup", j % 3))
            load_ln(ln_i)
            load_wup(0)
            load_wup(1)
            cnt = 0
            for j in range(NJ):
                if j + 2 < NJ:
                    load_wup(j + 2)
                S.op("pool", lambda e, j=j: e.dma_start(out=wdn[:, j, :], in_=wd_v[:, j, :]), w=["wdn"], dma="wdn")
                for ni, (c0, n) in enumerate(ntiles):
                    bg, bu = cnt % 2, 2 + cnt % 2
                    sgt = sg[cnt % 2]
                    sgk = ("sg", cnt % 2)
                    cnt += 1
                    for half, bank in ((0, bg), (1, bu)):
                        for kc in range(8):
                            S.op("pe", lambda e, kc=kc, half=half, bank=bank, j=j, c0=c0, n=n: e.matmul(
                                ps[:, bank, 0:n], lhsT=wup[j % 3][:, kc, half, :], rhs=hT[:, kc, c0:c0 + n],
                                start=(kc == 0), stop=(kc == 7)),
                                r=[("wup", j % 3)] + hT_tok(c0, n), w=[("ps", bank)])
                    S.op("act", lambda e, bg=bg, n=n, sgt=sgt: e.activation(out=sgt[:, 0:n], in_=ps[:, bg, 0:n], func=AF.Silu),
                         r=[("ps", bg)], w=[sgk])
                    S.op("dve", lambda e, bu=bu, n=n, sgt=sgt, j=j, c0=c0: e.tensor_tensor(
                        out=actT[:, j, c0:c0 + n], in0=sgt[:, 0:n], in1=ps[:, bu, 0:n], op=ALU.mult),
                        r=[sgk, ("ps", bu)], w=[("actT", j, ni)])
            if which == 0:
                w1t = w1t_holder["t"]
                alias = [("wup", 0), ("wup", 1), ("wup", 2), ("sg", 0), ("sg", 1)]
                S.op("pool", lambda e: e.dma_start(out=w1t[:, :, 0:256],
                                                   in_=w_in[:, 512:768].rearrange("(kc p) c -> p kc c", p=128)),
                     w=alias + ["w1a"], dma="w1a")
                S.op("pool", lambda e: e.dma_start(out=w1t[:, :, 256:1792],
                                                   in_=w_in[:, 768:2304].rearrange("(kc p) c -> p kc c", p=128)),
                     w=alias + ["w1r"], dma="w1r")
            S.pe_sync = False
            pend = None
            for ti, t in enumerate(tiles):
                P, c0 = tile_geom(t)
                S.pe_sync = (t == 8)
                pb = (ti % 2) * 2
                nil = nis_of(ntiles, c0, P)
                for dh in range(2):
                    for j in range(NJ):
                        S.op("pe", lambda e, j=j, dh=dh, pb=pb, P=P, c0=c0: e.matmul(
                            ps[0:P, pb + dh, :], lhsT=actT[:, j, c0:c0 + P], rhs=wdn[:, j, dh * 512:(dh + 1) * 512],
                            start=(j == 0), stop=(j == NJ - 1)),
                            r=[("actT", j, i) for i in nil] + ["wdn"], w=[("ps", pb + dh)])
                S.pe_sync = False
                ln_elem(t, PS2(pb, P), [("ps", pb), ("ps", pb + 1)], 0.5 / ALPHA, LN_EPS / (ALPHA * ALPHA))
                if which == 0 and ti < 8:
                    kc = ti
                    for hh in range(4):
                        S.op("pool", lambda e, kc=kc, hh=hh: e.dma_start(
                            out=w1q[:, kc, hh * 128:(hh + 1) * 128].rearrange("p (g d) -> p g d", g=2),
                            in_=w_in[kc * 128:(kc + 1) * 128, 0:512].rearrange("p (g hh d) -> p hh g d", g=2, hh=4)[:, hh, :, :]),
                            w=["w1q"], dma="w1q")
                if final:
                    dst = yp[(blk * 8 + t) * 128:(blk * 8 + t + 1) * 128, :] if t < 8 else ys
                    S.op("sp", lambda e, t=t, P=P, dst=dst: e.dma_start(out=dst, in_=h_tok[0:P, t, :]),
                         r=[("h", t)], dma=("yout", t))
                else:
                    if debug and blk == 0:
                        S.op("sp", lambda e, t=t: e.dma_start(out=dbg["h1"][t], in_=h_tok[:, t, :]), r=[("h", t)],
                             dma=("dbg", t))
                    if pend is not None:
                        transpose_to_hT(pend[0], pend[1])
                    pend = (t, 4 + (ti % 2) * 2)
            if pend is not None and not final:
                transpose_to_hT(pend[0], pend[1])

        def mixer_phase(blk):
            tiles = list(range(8)) + ([8] if blk == 0 else [])
            ntiles = [(0, 352), (352, 352), (704, 336)] if blk == 0 else [(0, 512), (512, 512)]
            S.barrier()
            A.top = phase_base
            oaT = A.alloc([128, 4, NCOL], BF16)
            rgn = A.alloc([128, 9, 512], F32)
            b12_base = A.top
            rot = A.alloc([128, NROT], F32)
            w1t = w1t_holder["t"]
            S.op("sp", lambda e: e.dma_start(out=rot[:], in_=consts_d[:, NC1:NCONST]), w=["rot"], dma="rot")
            qaT = A.alloc([128, 4, NCOL], BF16)
            kaT = A.alloc([128, 128 + NCOL], BF16)
            vaug = A.alloc([128, 10, 2, 65], BF16)
            qs = A.alloc([128, 4, 128], F32)
            tA = A.alloc([128, 4, 128], F32)
            tB = A.alloc([128, 4, 128], F32)
            qh = A.alloc([128, 4, 128], F32)
            kh = A.alloc([128, 4, 128], F32)
            khb = A.alloc([128, 4, 128], BF16)
            vrb = A.alloc([128, 4, 128], BF16)
            qhT = A.alloc([128, 4, 128], BF16)
            khT = A.alloc([128, 4, 128], BF16)
            et = [A.alloc([128, 512], F32) for _ in range(2)]
            pT = [A.alloc([128, 2, 4, 128], BF16) for _ in range(2)]
            scm = A.alloc([128, 4, 128], BF16)
            o_n = A.alloc([128, 8, 64], F32)
            den = A.alloc([128, 8], F32)
            kvout = A.alloc([128, 256], F32)
            qhT32 = A.alloc([128, 4, NS], F32)
            vr32 = A.alloc([128, 4, 128], F32)
            assert A.top <= w1t_holder["off"], (A.top, w1t_holder["off"])

            S.op("dve", lambda e: e.memset(vaug[:, :, :, 64:65], 1.0), w=["vaug_ones"])
            if blk == 0:
                S.op("sp", lambda e: e.dma_start(out=kws[:, 0:127, :], in_=ck[:, 1:128, :]), w=["kws_a"], dma="kws_a")
                S.op("sp", lambda e: e.dma_start(out=vws[:, 0:127, :], in_=cv[:, 1:128, :]), w=["vws_a"], dma="vws_a")
            else:
                S.op("dve", lambda e: e.tensor_copy(out=kaT[:, 0:128], in_=kprev[:]), r=["kprev"], w=["kaT_prev"])
                S.op("dve", lambda e: e.tensor_copy(out=vaug[:, 0, :, :], in_=vprev[:]), r=["vprev", "vaug_ones"],
                     w=[("vaug", 0)])

            def b0_all():
              cnt = 0
              for c in range(5):
                    for ni, (c0, n) in enumerate(ntiles):
                        bank = 6 + cnt % 2
                        for kc in range(8):
                            S.op("pe", lambda e, c=c, kc=kc, c0=c0, n=n, bank=bank: e.matmul(
                                ps[:, bank, 0:n], lhsT=(w1q[:, kc, c * 128:(c + 1) * 128] if c < 4 else w1t[:, kc, 0:128]),
                                rhs=hT[:, kc, c0:c0 + n],
                                start=(kc == 0), stop=(kc == 7)), r=["w1q" if c < 4 else "w1a"] + hT_tok(c0, n), w=[("ps", bank)])
                        if c < 4:
                            dst, wt = qaT[:, c, c0:c0 + n], [("qaT", c, ni)]
                        else:
                            dst, wt = kaT[:, 128 + c0:128 + c0 + n], [("kaT", ni), "kaT_all"]
                        eng = "act" if cnt % 2 == 0 else "dve"
                        if eng == "act":
                            S.op("act", lambda e, dst=dst, bank=bank, n=n: e.copy(out=dst, in_=ps[:, bank, 0:n]),
                                 r=[("ps", bank)], w=wt)
                        else:
                            S.op("dve", lambda e, dst=dst, bank=bank, n=n: e.tensor_copy(out=dst, in_=ps[:, bank, 0:n]),
                                 r=[("ps", bank)], w=wt)
                        cnt += 1
            S.pe_sync = False
            qa_tok = lambda c0_, n_: [("qaT", c, i) for c in range(4) for i in nis_of(ntiles, c0_, n_)]

            def gn_store(t, P, heads):
                st, mv = stt[t % 2], mvt[t % 2]
                tk = ("lnt", t % 2)
                for h, (v, vt) in enumerate(heads):
                    S.op("dve", lambda e, h=h, v=v: e.bn_stats(out=st[0:P, h, :], in_=v), r=vt, w=[tk])
                for h in range(4):
                    S.op("dve", lambda e, h=h: e.bn_aggr(out=mv[0:P, h, 0:2], in_=st[0:P, h, :]), r=[tk], w=[tk])
                S.op("dve", lambda e: e.tensor_scalar(out=mv[0:P, :, 2:3], in0=mv[0:P, :, 1:2], scalar1=GN_EPS, scalar2=None,
                                                      op0=ALU.add), r=[tk], w=[tk])
                S.op("pool", lambda e: e.tensor_tensor(out=mv[0:P, :, 3:4], in0=mv[0:P, :, 2:3],
                                                       in1=mhalf[0:P].unsqueeze(1).to_broadcast([P, 4, 1]), op=ALU.pow),
                     r=[tk, "cst"], w=[tk])
                for h, (v, vt) in enumerate(heads):
                    S.op("dve", lambda e, h=h, v=v: e.tensor_scalar(
                        out=rgn[0:P, t, h * 128:(h + 1) * 128], in0=v, scalar1=mv[0:P, h, 0:1], scalar2=mv[0:P, h, 3:4],
                        op0=ALU.subtract, op1=ALU.mult), r=vt + [tk], w=[("rgn", t)])

            def attn_finish(t, P, c0):
                for g, bank in ((0, 0), (1, 3)):
                    ov = ps[0:P, bank, 0:260].rearrange("p (h d) -> p h d", d=65)
                    S.op("dve", lambda e, g=g, ov=ov: e.tensor_tensor(
                        out=den[0:P, g * 4:(g + 1) * 4], in0=ov[:, :, 64], in1=sinkexp[0:P, g * 4:(g + 1) * 4], op=ALU.add),
                        r=[("ps", bank), "sinkexp"], w=[("den", g)])
                    S.op("dve", lambda e, g=g: e.reciprocal(out=den[0:P, g * 4:(g + 1) * 4], in_=den[0:P, g * 4:(g + 1) * 4]),
                         r=[("den", g)], w=[("den", g)])
                    S.op("dve", lambda e, g=g, ov=ov: e.tensor_tensor(
                        out=o_n[0:P, g * 4:(g + 1) * 4, :], in0=ov[:, :, 0:64],
                        in1=den[0:P, g * 4:(g + 1) * 4].unsqueeze(2).to_broadcast([P, 4, 64]), op=ALU.mult),
                        r=[("ps", bank), ("den", g)], w=[("o_n", g)])
                for c in range(4):
                    S.op("pe", lambda e, c=c: e.transpose(out=ps[:, 4, c * 128:c * 128 + P],
                                                          in_=o_n[0:P, 2 * c:2 * c + 2, :].rearrange("p a b -> p (a b)"),
                                                          identity=ident[0:P, 0:P]),
                         r=[("o_n", c // 2), "cst"], w=[("ps", 4)])
                S.op("act", lambda e: e.copy(out=oaT[:, :, c0:c0 + P],
                                             in_=ps[:, 4, :].rearrange("p (a b) -> p a b", b=128)[:, :, 0:P]),
                     r=[("ps", 4)], w=[("oaT", t)])

            def qk_transposes(t, P, smp):
                S.op("act", lambda e: e.copy(out=khb[0:P], in_=kh[0:P]), r=["kh"], w=["khb"])
                for which, src, bank, dstT in (("q", qh, 4, qhT), ("k", kh, 5, khT)):
                    for h in range(4):
                        S.op("pe", lambda e, h=h, src=src, bank=bank: e.transpose(
                            out=ps[:, bank, h * 128:h * 128 + P], in_=src[0:P, h, :], identity=ident[0:P, 0:P]),
                            r=[which + "h", "cst"], w=[("ps", bank)])
                    S.op("act", lambda e, bank=bank, dstT=dstT: e.copy(
                        out=dstT[:, :, 0:P], in_=ps[:, bank, :].rearrange("p (a b) -> p a b", b=128)[:, :, 0:P]),
                        r=[("ps", bank)], w=[which + "hT"])
                    if smp and which == "q":
                        S.op("act", lambda e, bank=bank: e.copy(
                            out=qhT32[:], in_=ps[:, bank, :].rearrange("p (a b) -> p a b", b=128)[:, :, 0:NS]),
                            r=[("ps", bank)], w=["qhT32"])

            def b1_tile(t, part):
                P, c0 = tile_geom(t)
                gt = blk * 8 + t if t < 8 else 16
                smp = (t == 8)
                srow = 4 if smp else 0
                if part == "A":
                    for bank, (co, n) in enumerate(((512, 256), (768, 512), (1280, 512), (1792, 512))):
                        for kc in range(8):
                            S.op("pe", lambda e, kc=kc, bank=bank, co=co, n=n: e.matmul(
                                ps[0:P, bank, 0:n], lhsT=hT[:, kc, c0:c0 + P], rhs=w1t[:, kc, co - 512:co - 512 + n],
                                start=(kc == 0), stop=(kc == 7)), r=["w1a" if bank == 0 else "w1r", ("hT", t)], w=[("ps", bank)])
                    if smp and sub < 4.52:
                        return
                    slot = 1 + t
                    if not (smp and (KX & 1)):
                        S.op("act", lambda e, slot=slot: e.copy(out=vaug[0:P, slot, :, 0:64],
                                                                in_=ps[0:P, 0, 128:256].rearrange("p (g d) -> p g d", g=2)),
                             r=[("ps", 0)], w=[("vaug", slot)])
                    if (smp and not (KX & 2)) or (blk == 1 and t == 7):
                        S.op("act", lambda e: e.copy(out=kvout[0:P, :], in_=ps[0:P, 0, 0:256]), r=[("ps", 0)], w=["kvout"])
                        if smp and not (KX & 4):
                            S.op("sp", lambda e: e.dma_start(out=kws[:, 127, :], in_=kvout[0:NS, 0:128]), r=["kvout"],
                                 w=["kws_b"], dma="kws_b")
                            S.op("sp", lambda e: e.dma_start(out=vws[:, 127, :], in_=kvout[0:NS, 128:256]), r=["kvout"],
                                 w=["vws_b"], dma="vws_b")
                        elif not smp:
                            S.op("sp", lambda e: e.dma_start(out=kwp, in_=kvout[:, 0:128]), r=["kvout"], dma="kwp")
                            S.op("sp", lambda e: e.dma_start(out=vwp, in_=kvout[:, 128:256]), r=["kvout"], dma="vwp")
                    if not (smp and (KX & 8)):
                        S.op("act", lambda e: e.copy(out=vrb[0:P].rearrange("p a b -> p (a b)"), in_=ps[0:P, 3, :]),
                             r=[("ps", 3)], w=["vrb"])
                    if smp and not (KX & 16):
                        S.op("act", lambda e: e.copy(out=vr32[0:NS].rearrange("p a b -> p (a b)"), in_=ps[0:NS, 3, :]),
                             r=[("ps", 3)], w=["vr32"])
                    if smp and sub < 4.53:
                        return
                    cosv = rot[0:P, gt * 64:(gt + 1) * 64].unsqueeze(1).unsqueeze(1).to_broadcast([P, 4, 2, 64])
                    sinv = rot[0:P, 1088 + gt * 64:1088 + (gt + 1) * 64].unsqueeze(1).to_broadcast([P, 4, 64])
                    nsinv = rot[0:P, 2176 + gt * 64:2176 + (gt + 1) * 64].unsqueeze(1).to_broadcast([P, 4, 64])
                    for which, bank, dcol, dsth in (("q", 1, C_DQ, qh), ("k", 2, C_DK, kh)):
                        dv = cst[0:P, dcol + srow:dcol + srow + 4].unsqueeze(2).to_broadcast([P, 4, 128])
                        S.op("dve", lambda e, bank=bank, dv=dv: e.tensor_tensor(
                            out=qs[0:P], in0=ps[0:P, bank, :].rearrange("p (a b) -> p a b", b=128), in1=dv, op=ALU.mult),
                            r=[("ps", bank), "cst"], w=["qs"])
                        S.op("pool", lambda e: e.tensor_tensor(
                            out=tA[0:P].rearrange("p h (two d) -> p h two d", two=2),
                            in0=qs[0:P].rearrange("p h (two d) -> p h two d", two=2), in1=cosv, op=ALU.mult),
                            r=["qs", "rot"], w=["tA"])
                        S.op("pool", lambda e: e.tensor_tensor(out=tB[0:P, :, 0:64], in0=qs[0:P, :, 64:128], in1=nsinv, op=ALU.mult),
                             r=["qs", "rot"], w=["tB0"])
                        S.op("pool", lambda e: e.tensor_tensor(out=tB[0:P, :, 64:128], in0=qs[0:P, :, 0:64], in1=sinv, op=ALU.mult),
                             r=["qs", "rot"], w=["tB1"])
                        S.op("pool", lambda e, dsth=dsth: e.tensor_tensor(out=dsth[0:P], in0=tA[0:P], in1=tB[0:P], op=ALU.add),
                             r=["tA", "tB0", "tB1"], w=[which + "h"])
                    if smp and sub < 4.54:
                        return

                    return
                if sub < 2:
                    return
                if not smp:
                    first = (blk == 0 and t == 0)
                    kbs = [1] if first else [0, 1]
                    sc_cnt = 0
                    for g in range(2):
                        for kb in kbs:
                            kcol = (t + kb) * 128
                            sbank = 6 + sc_cnt % 2
                            e_t = et[sc_cnt % 2]
                            ek = ("et", sc_cnt % 2)
                            sc_cnt += 1
                            ktok = ["kaT_prev"] if (kb == 0 and t == 0) else [("kaT", i) for i in nis_of(ntiles, kcol - 128, 128)]
                            S.op("pe", lambda e, g=g, kcol=kcol, sbank=sbank: e.matmul(
                                ps[:, sbank, :], lhsT=kaT[g * 64:(g + 1) * 64, kcol:kcol + 128],
                                rhs=qaT[g * 64:(g + 1) * 64, :, c0:c0 + 128], start=True, stop=True),
                                r=ktok + qa_tok(c0, 128), w=[("ps", sbank)])
                            S.op("dve", lambda e, g=g, kb=kb, sbank=sbank, e_t=e_t: e.scalar_tensor_tensor(
                                out=e_t[:], in0=ps[:, sbank, :], scalar=0.125,
                                in1=BT[:, kb, g * 4:(g + 1) * 4, :].rearrange("p a b -> p (a b)"), op0=ALU.mult, op1=ALU.add),
                                r=[("ps", sbank), "BT"], w=[ek])
                            S.op("act", lambda e, g=g, kb=kb, e_t=e_t: e.activation(
                                out=pT[g][:, kb, :, :].rearrange("p a b -> p (a b)"), in_=e_t[:], func=AF.Exp),
                                r=[ek], w=[("pT", g, kb)])
                        obank = 0 if g == 0 else 3
                        for hh in range(4):
                            for i, kb in enumerate(kbs):
                                vslot = t + kb
                                S.op("pe", lambda e, g=g, hh=hh, kb=kb, vslot=vslot, obank=obank, i=i: e.matmul(
                                    ps[:, obank, hh * 65:(hh + 1) * 65], lhsT=pT[g][:, kb, hh, :], rhs=vaug[:, vslot, g, :],
                                    start=(i == 0), stop=(i == len(kbs) - 1)),
                                    r=[("pT", g, kb), ("vaug", vslot), "vaug_ones"], w=[("ps", obank)])
                    attn_finish(t, P, c0)
                    qk_transposes(t, P, smp)
                    if sub < 3:
                        return
                    for h in range(4):
                        S.op("pe", lambda e, h=h: e.matmul(ps[:, 1, h * 128:(h + 1) * 128], lhsT=khT[:, h, :], rhs=qhT[:, h, :],
                                                           start=True, stop=True), r=["khT", "qhT"], w=[("ps", 1)])
                    S.op("dve", lambda e: e.tensor_tensor(
                        out=scm[:], in0=ps[:, 1, :].rearrange("p (a b) -> p a b", b=128),
                        in1=caus.unsqueeze(1).to_broadcast([128, 4, 128]), op=ALU.mult), r=[("ps", 1), "cst"], w=["scm"])
                    for h in range(4):
                        S.op("pe", lambda e, h=h: e.matmul(ps[:, 2, h * 128:(h + 1) * 128], lhsT=scm[:, h, :], rhs=vrb[:, h, :],
                                                           start=True, stop=first), r=["scm", "vrb"], w=[("ps", 2)])
                        if not first:
                            S.op("pe", lambda e, h=h: e.matmul(ps[:, 2, h * 128:(h + 1) * 128], lhsT=qhT[:, h, :], rhs=Sb[:, h, :],
                                                               start=False, stop=True), r=["qhT", "Sb"], w=[("ps", 2)])
                    gn_store(t, P, [(ps[:, 2, h * 128:(h + 1) * 128], [("ps", 2)]) for h in range(4)])
                    for h in range(4):
                        S.op("pe", lambda e, h=h: e.matmul(ps[:, 5, h * 128:(h + 1) * 128], lhsT=khb[:, h, :], rhs=vrb[:, h, :],
                                                           start=True, stop=True), r=["khb", "vrb"], w=[("ps", 5)])
                    S.op("dve", lambda e: e.tensor_tensor(out=Sst[:].rearrange("p a b -> p (a b)"), in0=ps[:, 5, :],
                                                          in1=Sst[:].rearrange("p a b -> p (a b)"), op=ALU.add),
                         r=[("ps", 5), "S"], w=["S"])
                    S.op("dve", lambda e: e.tensor_tensor(out=Sst[:], in0=Sst[:],
                                                          in1=cst[:, C_GC:C_GC + 4].unsqueeze(2).to_broadcast([128, 4, 128]),
                                                          op=ALU.mult), r=["S", "cst"], w=["S"])
                    S.op("act", lambda e: e.copy(out=Sb[:], in_=Sst[:]), r=["S"], w=["Sb"])
                    if blk == 1 and t == 7:
                        S.op("sp", lambda e: e.dma_start(out=sp_out.rearrange("h k v -> k h v"), in_=Sst[:]), r=["S"], dma="spout")
                else:
                    if sub < 5:
                        return
                    A_save = A.top
                    Wk32 = A.alloc([128, NS, 128], F32)
                    WkT = A.alloc([128, NS, 128], BF16)
                    Wva = A.alloc([128, NS, 2, 65], BF16)
                    oTs = A.alloc([128, 8, NS], F32)
                    e_s = A.alloc([128, NS, 8], F32)
                    pTs = A.alloc([128, NS, 8], BF16)
                    Sp = [A.alloc([128, 4, 128], F32) for _ in range(2)]
                    Sn = [A.alloc([128, 4, 128], F32) for _ in range(2)]
                    Vm = [A.alloc([128, 4, 128], F32) for _ in range(2)]
                    QTm = A.alloc([128, 4, NS, NS], F32)
                    S.op("sp", lambda e: e.dma_start(out=Wk32[:], in_=kws.rearrange("b r c -> r b c")),
                         r=["kws_a", "kws_b"], w=["Wk32", "w1a", "w1r", "w1q"], dma="wk32")
                    for g in range(2):
                        S.op("pool", lambda e, g=g: e.dma_start(out=Wva[:, :, g, 0:64],
                                                                in_=vws[:, :, g * 64:(g + 1) * 64].rearrange("b r d -> r b d")),
                             r=["vws_a", "vws_b"], w=["Wva", "w1a", "w1r", "w1q"], dma="wva")
                    S.op("dve", lambda e: e.memset(Wva[:, :, :, 64:65], 1.0), w=["Wva1", "w1a", "w1r", "w1q"])
                    for b in range(NS):
                        bank = 6 + (b // 4) % 2
                        S.op("pe", lambda e, b=b, bank=bank: e.transpose(out=ps[:, bank, (b % 4) * 128:(b % 4 + 1) * 128],
                                                                        in_=Wk32[:, b, :], identity=ident),
                             r=["Wk32", "cst"], w=[("ps", bank)])
                        if b % 4 == 3:
                            S.op("act", lambda e, b=b, bank=bank: e.copy(
                                out=WkT[:, b - 3:b + 1, :], in_=ps[:, bank, :].rearrange("p (a b) -> p a b", b=128)),
                                r=[("ps", bank)], w=["WkT", "w1a", "w1r", "w1q"])
                    for b in range(NS):
                        for g in range(2):
                            S.op("pe", lambda e, b=b, g=g: e.matmul(
                                ps[:, 0, b * 8 + g * 4:b * 8 + g * 4 + 4], lhsT=WkT[g * 64:(g + 1) * 64, b, :],
                                rhs=qaT[g * 64:(g + 1) * 64, :, TB + b], start=True, stop=True),
                                r=["WkT"] + qa_tok(TB, NS), w=[("ps", 0)])
                    S.op("dve", lambda e: e.scalar_tensor_tensor(
                        out=e_s[:], in0=ps[:, 0, 0:128].rearrange("p (b h) -> p b h", h=8), scalar=0.125,
                        in1=bias_s[:].unsqueeze(1).to_broadcast([128, NS, 8]), op0=ALU.mult, op1=ALU.add),
                        r=[("ps", 0), "bias_s"], w=["e_s", "w1a", "w1r", "w1q"])
                    S.op("act", lambda e: e.activation(out=pTs[:], in_=e_s[:], func=AF.Exp), r=["e_s"], w=["pTs", "w1a", "w1r", "w1q"])
                    if sub < 5.2:
                        A.top = A_save
                        return
                    for b in range(NS):
                        for g in range(2):
                            S.op("pe", lambda e, b=b, g=g: e.matmul(
                                ps[0:65, 3, b * 8 + g * 4:b * 8 + g * 4 + 4], lhsT=Wva[:, b, g, :],
                                rhs=pTs[:, b, g * 4:(g + 1) * 4], start=True, stop=True),
                                r=["Wva", "Wva1", "pTs"], w=[("ps", 3)])
                    S.op("act", lambda e: e.copy(out=oTs[0:65], in_=ps[0:65, 3, 0:128].rearrange("p (b h) -> p h b", h=8)),
                         r=[("ps", 3)], w=["oTs", "w1a", "w1r", "w1q"])
                    for h in range(8):
                        bank = 0 if h < 4 else 3
                        S.op("pe", lambda e, h=h, bank=bank: e.transpose(
                            out=ps[0:NS, bank, (h % 4) * 65:(h % 4 + 1) * 65], in_=oTs[0:65, h, :], identity=ident[0:65, 0:65]),
                            r=["oTs", "cst"], w=[("ps", bank)])
                    attn_finish(t, P, c0)
                    qk_transposes(t, P, smp)
                    if sub < 5.3:
                        A.top = A_save
                        return
                    S.op("dve", lambda e: e.tensor_tensor(
                        out=QTm[:], in0=qhT32[:].unsqueeze(3).to_broadcast([128, 4, NS, NS]),
                        in1=cst[:, C_EYER:C_EYER + 256].rearrange("p (a b) -> p a b", b=NS).unsqueeze(1).to_broadcast([128, 4, NS, NS]),
                        op=ALU.mult), r=["qhT32", "cst"], w=["QTm", "w1a", "w1r", "w1q"])
                    for b in range(NS):
                        i2 = b % 2
                        S.op("sp", lambda e, b=b, i2=i2: e.dma_start(out=Sp[i2][:], in_=st_in[b].rearrange("h k v -> k h v")),
                             w=[("Sp", i2), "w1a", "w1r", "w1q"], dma=("Sp", i2))
                        S.op("pool", lambda e, b=b, i2=i2: e.tensor_scalar(
                            out=Vm[i2][0:NS].rearrange("p a b -> p (a b)"), in0=vr32[0:NS].rearrange("p a b -> p (a b)"),
                            scalar1=ident[0:NS, b:b + 1], scalar2=None, op0=ALU.mult),
                            r=["vr32", "cst"], w=[("Vm", i2), "w1a", "w1r", "w1q"])
                        ub = 1 + i2
                        for h in range(4):
                            S.op("pe", lambda e, h=h, ub=ub, i2=i2: e.matmul(
                                ps[:, ub, h * 128:(h + 1) * 128], lhsT=kh[0:NS, h, :], rhs=Vm[i2][0:NS, h, :],
                                start=True, stop=True), r=["kh", ("Vm", i2)], w=[("ps", ub)])
                        for h in range(4):
                            S.op("dve", lambda e, h=h, ub=ub, i2=i2: e.scalar_tensor_tensor(
                                out=Sn[i2][:, h, :], in0=Sp[i2][:, h, :], scalar=float(GAMMAS[h]),
                                in1=ps[:, ub, h * 128:(h + 1) * 128], op0=ALU.mult, op1=ALU.add),
                                r=[("Sp", i2), ("ps", ub)], w=[("Sn", i2), "w1a", "w1r", "w1q"])
                        S.op("sp", lambda e, b=b, i2=i2: e.dma_start(out=ss_out[b].rearrange("h k v -> k h v"), in_=Sn[i2][:]),
                             r=[("Sn", i2)], dma=("ssout", i2))
                        for h in range(4):
                            if sub < 5.4:
                                break
                            S.op("pe", lambda e, h=h, b=b, i2=i2: e.matmul(
                                ps[0:NS, 4 + h, 0:128], lhsT=QTm[:, h, b, :], rhs=Sn[i2][:, h, :],
                                start=(b == 0), stop=(b == NS - 1)), r=["QTm", ("Sn", i2)], w=[("ps", 4 + h)])
                    if sub >= 5.4:
                        gn_store(t, P, [(ps[0:NS, 4 + h, 0:128], [("ps", 4 + h)]) for h in range(4)])
                    A.top = A_save

            if sub < 1:
                return
            for t in tiles:
                if t == 8 and (sub < 4.5 or (KX & 32)):
                    continue
                if t > 0 and sub < 4:
                    continue
                S.pe_sync = (t == 8)
                b1_tile(t, "A")
                S.pe_sync = False
                if t == 0:
                    b0_all()
                    S.pe_sync = False
                S.pe_sync = (t == 8)
                b1_tile(t, "B")
                S.pe_sync = False
            if sub < 6:
                return
            if blk == 0:
                S.op("dve", lambda e: e.tensor_copy(out=kprev[:], in_=kaT[:, TB:TB + 128]), r=[("kaT", i) for i in nis_of(ntiles, TB - 128, 128)], w=["kprev"])
                S.op("dve", lambda e: e.tensor_copy(out=vprev[:], in_=vaug[:, 8, :, :]), r=[("vaug", 8), "vaug_ones"],
                     w=["vprev"])

            S.barrier()
            A.top = b12_base
            w2 = A.alloc([128, 8, 2560], BF16)
            wao = A.alloc([128, 4, D], BF16)
            wro = A.alloc([128, 4, D], BF16)
            wo = A.alloc([128, 8, D], BF16)
            sgr = A.alloc([128, 512], F32)
            rr = A.alloc([128, 512], F32)
            rT = A.alloc([128, 4, 128], BF16)
            sa = A.alloc([128, D], F32)
            m1 = A.alloc([128, D], F32)
            mT = A.alloc([128, 8, 128], BF16)
            def ld_w2(lo, hi, tok):
                S.op("pool", lambda e: e.dma_start(out=w2[:, :, lo:hi],
                                                   in_=w_in[:, 2304 + lo:2304 + hi].rearrange("(kc p) c -> p kc c", p=128)),
                     w=[tok], dma=tok)
            ld_w2(0, 512, "w2a")
            S.op("pool", lambda e: e.dma_start(out=wao[:], in_=w_ao.rearrange("(kc p) c -> p kc c", p=128)), w=["wao"], dma="wao")
            S.op("pool", lambda e: e.dma_start(out=wro[:], in_=w_ro.rearrange("(kc p) c -> p kc c", p=128)), w=["wro"], dma="wro")
            ld_w2(512, 1536, "w2b")
            ld_w2(1536, 2560, "w2c")
            S.op("pool", lambda e: e.dma_start(out=wo[:], in_=w_o.rearrange("(kc p) c -> p kc c", p=128)), w=["wo"], dma="wo")
            load_ln(1)
            def b2_head(t):
                P, c0 = tile_geom(t)
                for kc in range(8):
                    S.op("pe", lambda e, kc=kc: e.matmul(ps[0:P, 0, :], lhsT=hT[:, kc, c0:c0 + P], rhs=w2[:, kc, 0:512],
                                                         start=(kc == 0), stop=(kc == 7)), r=["w2a", ("hT", t)], w=[("ps", 0)])
                for dh in range(2):
                    for c in range(4):
                        S.op("pe", lambda e, c=c, dh=dh: e.matmul(ps[0:P, 2 + dh, :], lhsT=oaT[:, c, c0:c0 + P],
                                                                  rhs=wao[:, c, dh * 512:(dh + 1) * 512],
                                                                  start=(c == 0), stop=(c == 3)),
                             r=["wao", ("oaT", t)], w=[("ps", 2 + dh)])
                S.op("act", lambda e: e.activation(out=sgr[0:P], in_=ps[0:P, 0, :], func=AF.Silu), r=[("ps", 0)], w=["sgr"])
                S.op("dve", lambda e: e.tensor_tensor(out=rr[0:P], in0=sgr[0:P], in1=rgn[0:P, t, :], op=ALU.mult),
                     r=["sgr", ("rgn", t)], w=["rr"])

            def b2_rest(t, nxt):
                P, c0 = tile_geom(t)
                for c in range(4):
                    S.op("pe", lambda e, c=c: e.transpose(out=ps[:, 1, c * 128:c * 128 + P], in_=rr[0:P, c * 128:(c + 1) * 128],
                                                          identity=ident[0:P, 0:P]), r=["rr", "cst"], w=[("ps", 1)])
                S.op("act", lambda e: e.copy(out=rT[:, :, 0:P], in_=ps[:, 1, :].rearrange("p (a b) -> p a b", b=128)[:, :, 0:P]),
                     r=[("ps", 1)], w=["rT"])
                for dh in range(2):
                    for c in range(4):
                        S.op("pe", lambda e, c=c, dh=dh: e.matmul(ps[0:P, 4 + dh, :], lhsT=rT[:, c, 0:P],
                                                                  rhs=wro[:, c, dh * 512:(dh + 1) * 512],
                                                                  start=(c == 0), stop=(c == 3)),
                             r=["wro", "rT"], w=[("ps", 4 + dh)])
                for gi, goff in enumerate((512, 1536)):
                    for dh in range(2):
                        for kc in range(8):
                            S.op("pe", lambda e, kc=kc, dh=dh, goff=goff: e.matmul(
                                ps[0:P, 6 + dh, :], lhsT=hT[:, kc, c0:c0 + P], rhs=w2[:, kc, goff + dh * 512:goff + (dh + 1) * 512],
                                start=(kc == 0), stop=(kc == 7)), r=["w2b" if gi == 0 else "w2c", ("hT", t)], w=[("ps", 6 + dh)])
                    S.op("act", lambda e: e.activation(out=sa[0:P], in_=PS2(6, P), func=AF.Tanh, scale=0.5),
                         r=[("ps", 6), ("ps", 7)], w=["sa"])
                    if gi == 0:
                        S.op("dve", lambda e: e.scalar_tensor_tensor(out=m1[0:P], in0=sa[0:P], scalar=1.0, in1=PS2(2, P),
                                                                     op0=ALU.add, op1=ALU.mult),
                             r=["sa", ("ps", 2), ("ps", 3)], w=["m1"])
                    else:
                        S.op("dve", lambda e: e.scalar_tensor_tensor(out=sa[0:P], in0=sa[0:P], scalar=1.0, in1=PS2(4, P),
                                                                     op0=ALU.add, op1=ALU.mult),
                             r=["sa", ("ps", 4), ("ps", 5)], w=["sa"])
                        S.op("dve", lambda e: e.tensor_tensor(out=m1[0:P], in0=m1[0:P], in1=sa[0:P], op=ALU.add),
                             r=["sa", "m1"], w=["m1"])
                if nxt is not None:
                    save = S.pe_sync
                    S.pe_sync = save or (nxt == 8)
                    b2_head(nxt)
                    S.pe_sync = save
                for kc in range(8):
                    S.op("pe", lambda e, kc=kc: e.transpose(out=ps[:, 6 + kc // 4, (kc % 4) * 128:(kc % 4) * 128 + P],
                                                            in_=m1[0:P, kc * 128:(kc + 1) * 128], identity=ident[0:P, 0:P]),
                         r=["m1", "cst"], w=[("ps", 6 + kc // 4)])
                S.op("act", lambda e: e.copy(out=mT[:, :, 0:P],
                                             in_=ps[:, 6:8, :].rearrange("p a (b c) -> p (a b) c", c=128)[:, :, 0:P]),
                     r=[("ps", 6), ("ps", 7)], w=["mT"])
                for dh in range(2):
                    for kc in range(8):
                        S.op("pe", lambda e, kc=kc, dh=dh: e.matmul(ps[0:P, 4 + dh, :], lhsT=mT[:, kc, 0:P],
                                                                    rhs=wo[:, kc, dh * 512:(dh + 1) * 512],
                                                                    start=(kc == 0), stop=(kc == 7)),
                             r=["wo", "mT"], w=[("ps", 4 + dh)])
                ln_elem(t, PS2(4, P), [("ps", 4), ("ps", 5)], 0.5 / ALPHA, LN_EPS / (ALPHA * ALPHA))
                if debug and blk == 0:
                    S.op("sp", lambda e, t=t: e.dma_start(out=dbg["h2"][t], in_=h_tok[:, t, :]), r=[("h", t)], dma=("dbg2", t))


            tl = [t for t in tiles if not (t == 8 and (KX & 32))]
            S.pe_sync = (tl[0] == 8)
            b2_head(tl[0])
            S.pe_sync = False
            pend = None
            for i, t in enumerate(tl):
                S.pe_sync = (t == 8)
                b2_rest(t, tl[i + 1] if i + 1 < len(tl) else None)
                S.pe_sync = False
                if pend is not None:
                    transpose_to_hT(pend, 6)
                pend = t
            transpose_to_hT(pend, 6)

        stage = 0
        for blk in range(2):
            tiles = list(range(8)) + ([8] if blk == 0 else [])
            if stage >= max_stage:
                break
            stage += 1
            for t in tiles:
                P, c0 = tile_geom(t)
                src = xp[(blk * 8 + t) * 128:(blk * 8 + t + 1) * 128, :] if t < 8 else xs
                S.op("sp", lambda e, t=t, P=P, src=src: e.dma_start(out=h_tok[0:P, t, :], in_=src), w=[("h", t)], dma=("x", t))
            for ti, t in enumerate(tiles):
                transpose_to_hT(t, 4 + (ti % 2) * 2)
            if stage >= max_stage:
                break
            stage += 1
            ffn_phase(blk, 0, 0, final=False)
            if stage >= max_stage:
                break
            stage += 1
            mixer_phase(blk)
            if stage >= max_stage:
                break
            stage += 1
            ffn_phase(blk, 1, 2, final=True)
        S.barrier()
        S.op("sp", lambda e: e.nop(), r=[], w=[])
        S.emit(nc, es)
        build_nc.info = dict(nops=len(S.ops), nsems=S.nsems, sbuf_peak=A.peak)
    return nc


_CACHE = {}
KX = 0


def kernel(x_prompt, x_sample, cache_k_win, cache_v_win, state_ret, rel_bias, w_in, attn_sinks,
           w_attn_out, w_ret_out, w_o, ffn1_w_up, ffn1_w_down, ffn2_w_up, ffn2_w_down,
           ln1_g, ln1_b, ln2_g, ln2_b, ln3_g, ln3_b, _debug=False):
    f = lambda a: np.ascontiguousarray(np.asarray(a, dtype=np.float32))
    import os as _os
    _ms = int(_os.environ.get("K_STAGES", "99"))
    _sub = float(_os.environ.get("K_SUB", "99"))
    global KX
    KX = int(_os.environ.get("K_X", "0"))
    key = ("nc", bool(_debug), _ms, _sub, KX, _os.environ.get("K_PAD", "0"))
    if key not in _CACHE:
        _CACHE[key] = build_nc(debug=_debug, max_stage=_ms, sub=_sub)
    nc = _CACHE[key]
    consts = make_consts()
    shared = dict(relb=f(rel_bias), w_in=f(w_in)[0], sinks=f(attn_sinks)[0], w_ao=f(w_attn_out)[0], w_ro=f(w_ret_out)[0],
                  w_o=f(w_o)[0], f1u=f(ffn1_w_up)[0], f2u=f(ffn2_w_up)[0], f1d=f(ffn1_w_down)[0], f2d=f(ffn2_w_down)[0],
                  ln1g=f(ln1_g)[0], ln1b=f(ln1_b)[0], ln2g=f(ln2_g)[0], ln2b=f(ln2_b)[0], ln3g=f(ln3_g)[0], ln3b=f(ln3_b)[0],
                  consts=consts)
    xp, xs = f(x_prompt), f(x_sample)
    ckf, cvf, stf = f(cache_k_win), f(cache_v_win), f(state_ret)
    in_maps = []
    for c in range(NCORES):
        m = dict(shared)
        sl = slice(c * NS, (c + 1) * NS)
        m["xp"] = xp[c]
        m["xs"] = xs[sl, 0, :]
        m["ck"] = ckf[0, sl].reshape(NS, 128, 128)
        m["cv"] = cvf[0, sl].reshape(NS, 128, 128)
        m["st"] = stf[0, sl]
        in_maps.append(m)
    _nco = int(_os.environ.get("K_CORES", str(NCORES)))
    res = run_bass_kernel_spmd(nc, in_maps[:_nco], core_ids=list(range(_nco)))
    R = list(res.results)
    while len(R) < NCORES:
        R.append({k: np.zeros_like(v) for k, v in R[0].items()})
    y_p = np.stack([R[c]["yp"] for c in range(NCORES)], 0)
    y_s = np.concatenate([R[c]["ys"] for c in range(NCORES)], 0).reshape(128, 1, D)
    kwp = np.stack([R[c]["kwp"] for c in range(NCORES)], 0).reshape(1, 8, 128, 2, 64)
    vwp = np.stack([R[c]["vwp"] for c in range(NCORES)], 0).reshape(1, 8, 128, 2, 64)
    spo = np.stack([R[c]["sp"] for c in range(NCORES)], 0).reshape(1, 8, 4, 128, 128)
    kws = np.concatenate([R[c]["kws"] for c in range(NCORES)], 0).reshape(1, 128, 128, 2, 64)
    vws = np.concatenate([R[c]["vws"] for c in range(NCORES)], 0).reshape(1, 128, 128, 2, 64)
    sso = np.concatenate([R[c]["ss"] for c in range(NCORES)], 0).reshape(1, 128, 4, 128, 128)
    outs = tuple(np.ascontiguousarray(a.astype(np.float32)) for a in (y_p, y_s, kwp, vwp, spo, kws, vws, sso))
    if _debug:
        kernel.dbg = [{k: v for k, v in R[c].items() if k.startswith("dbg")} for c in range(NCORES)]
    return outs
```
